# Optimizing a Trainium2 kernel written in Bass

```python
import math
import jax, jax.numpy as jnp
from jax import lax
import numpy as np

D_MODEL = 2048
BATCH = 16
SEQ = 256
DEPTH = 1
DEC_BATCH = 2
DEC_SEQ = 4096
PAST_LEN = 256

GRID_W = 64
NORM_EPS = 1e-6
N_MOD = 6
GDN_HEADS = 8
GDN_DK = 128
GDN_DV = 128
GDN_CONV = 3
GDN_CHUNK = 64
MLA_HEADS = 8
MLA_Q_LORA = 512
MLA_KV_LORA = 512
MLA_NOPE = 128
MLA_ROPE = 64
MLA_DV = 128
MLA_QK = MLA_NOPE + MLA_ROPE
ROPE_THETA = 10000.0
Q_BLOCK = 128
D_FF = 5632
FFN_CONV = 3
GDN_QKV_W = GDN_HEADS * (2 * GDN_DK + GDN_DV)
GDN_OUT_W = GDN_HEADS * GDN_DV
MLA_OUT_W = MLA_HEADS * MLA_DV
MIX_W = GDN_OUT_W + MLA_OUT_W
IN_SPLITS = (GDN_QKV_W, GDN_OUT_W, 2 * GDN_HEADS, 2 * GDN_HEADS, MLA_Q_LORA, MLA_KV_LORA, MLA_ROPE)
IN_COLS = GDN_QKV_W + GDN_OUT_W + 4 * GDN_HEADS + MLA_Q_LORA + MLA_KV_LORA + MLA_ROPE

kernel_name = 'hybrid_gdn_mla_prefix_dit_step'


def rms_norm(x, g):
    xf = x.astype(jnp.float32)
    y = xf * lax.rsqrt(jnp.mean(xf * xf, axis=-1, keepdims=True) + NORM_EPS)
    return (y * g.astype(jnp.float32)).astype(x.dtype)


def l2_normalize(x):
    xf = x.astype(jnp.float32)
    return xf * lax.rsqrt(jnp.sum(xf * xf, axis=-1, keepdims=True) + NORM_EPS)


def split_cols(x, sizes):
    parts, start = [], 0
    for size in sizes:
        parts.append(x[..., start:start + size])
        start += size
    return parts


def depthwise_conv_centred(x, w):
    k = w.shape[0]
    pad = k // 2
    t = x.shape[1]
    xp = jnp.pad(x, ((0, 0), (pad, pad), (0, 0)))
    y = xp[:, 0:t] * w[0]
    for i in range(1, k):
        y = y + xp[:, i:i + t] * w[i]
    return y


def ada_modulation(cond, w_ada, b_ada):
    m = jax.nn.silu(cond) @ w_ada + b_ada
    return jnp.split(m[:, None, :], N_MOD, axis=-1)


def modulate(h, shift, scale):
    return h * (1 + scale) + shift


def axial_rope(t):
    rows = t // GRID_W
    row = jnp.repeat(jnp.arange(rows, dtype=jnp.float32), GRID_W)
    col = jnp.tile(jnp.arange(GRID_W, dtype=jnp.float32), rows)
    n_freq = MLA_ROPE // 4
    inv_freq = ROPE_THETA ** (-jnp.arange(n_freq, dtype=jnp.float32) / n_freq)
    ang = jnp.concatenate([row[:, None] * inv_freq, col[:, None] * inv_freq], axis=-1)
    return jnp.cos(ang), jnp.sin(ang)


def rotate_half(x, cos, sin):
    x1, x2 = jnp.split(x.astype(jnp.float32), 2, axis=-1)
    return jnp.concatenate([x1 * cos - x2 * sin, x1 * sin + x2 * cos], axis=-1).astype(x.dtype)


def gdn_chunked(q, k, v, g, beta, s0):
    bsz, t, h, dk = q.shape
    dv = v.shape[-1]
    c = GDN_CHUNK
    n = t // c

    def chunks(x):
        return x.reshape(bsz, n, c, h, x.shape[-1]).transpose(1, 0, 3, 2, 4)

    q = chunks(q) * (dk ** -0.5)
    k = chunks(k)
    v = chunks(v)
    g = jnp.cumsum(g.reshape(bsz, n, c, h).transpose(1, 0, 3, 2), axis=-1)
    beta = beta.reshape(bsz, n, c, h).transpose(1, 0, 3, 2)
    k_beta = k * beta[..., None]
    v_beta = v * beta[..., None]
    causal = jnp.tril(jnp.ones((c, c), dtype=bool))
    strict = jnp.tril(jnp.ones((c, c), dtype=bool), k=-1)
    decay = jnp.exp(jnp.where(causal, g[..., :, None] - g[..., None, :], -jnp.inf))
    a_mat = jnp.where(strict, jnp.einsum('nbhid,nbhjd->nbhij', k_beta, k) * decay, 0.0)
    eye = jnp.eye(c, dtype=jnp.float32)
    t_mat = lax.linalg.triangular_solve(eye + a_mat, jnp.broadcast_to(eye, a_mat.shape),
                                        left_side=True, lower=True, unit_diagonal=True)
    u = jnp.einsum('nbhij,nbhjd->nbhid', t_mat, v_beta)
    w = jnp.einsum('nbhij,nbhjd->nbhid', t_mat, k_beta * jnp.exp(g)[..., None])
    intra = jnp.where(causal, jnp.einsum('nbhid,nbhjd->nbhij', q, k) * decay, 0.0)

    def step(state, inp):
        q_c, k_c, u_c, w_c, g_c, intra_c = inp
        v_new = u_c - jnp.einsum('bhck,bhkv->bhcv', w_c, state)
        o_c = (jnp.einsum('bhck,bhkv->bhcv', q_c * jnp.exp(g_c)[..., None], state)
               + jnp.einsum('bhcj,bhjv->bhcv', intra_c, v_new))
        g_last = g_c[..., -1:]
        state = (state * jnp.exp(g_last)[..., None]
                 + jnp.einsum('bhck,bhcv->bhkv', k_c * jnp.exp(g_last - g_c)[..., None], v_new))
        return state, o_c

    s_fin, o = lax.scan(step, s0, (q, k, u, w, g, intra))
    return o.transpose(1, 0, 3, 2, 4).reshape(bsz, t, h, dv), s_fin


def gdn_mix(qkv, z, a, b, conv_w, a_log, dt_bias, norm_g, s0):
    bsz, t, _ = qkv.shape
    qkv = jax.nn.silu(depthwise_conv_centred(qkv, conv_w))
    q, k, v = split_cols(qkv, (GDN_HEADS * GDN_DK, GDN_HEADS * GDN_DK, GDN_HEADS * GDN_DV))
    q = l2_normalize(q.reshape(bsz, t, GDN_HEADS, GDN_DK))
    k = l2_normalize(k.reshape(bsz, t, GDN_HEADS, GDN_DK))
    v = v.reshape(bsz, t, GDN_HEADS, GDN_DV).astype(jnp.float32)
    a = a.astype(jnp.float32).reshape(bsz, t, 2, GDN_HEADS)
    g = -jnp.exp(a_log.astype(jnp.float32)) * jax.nn.softplus(a + dt_bias.astype(jnp.float32))
    beta = jax.nn.sigmoid(b.astype(jnp.float32).reshape(bsz, t, 2, GDN_HEADS))
    s0 = s0.astype(jnp.float32)
    o_f, s_f = gdn_chunked(q, k, v, g[:, :, 0], beta[:, :, 0], s0[:, 0])
    o_b, s_b = gdn_chunked(q[:, ::-1], k[:, ::-1], v[:, ::-1], g[:, ::-1, 1], beta[:, ::-1, 1], s0[:, 1])
    o = o_f + o_b[:, ::-1]
    o = rms_norm(o, norm_g) * jax.nn.silu(z.reshape(bsz, t, GDN_HEADS, GDN_DV).astype(jnp.float32))
    return o.reshape(bsz, t, GDN_OUT_W).astype(z.dtype), jnp.stack([s_f, s_b], axis=1)


def mla_expand(ckv, k_pe, w_kv_b):
    bsz, t, _ = ckv.shape
    kv = (ckv @ w_kv_b).reshape(bsz, t, MLA_HEADS, MLA_NOPE + MLA_DV)
    k_nope, v = kv[..., :MLA_NOPE], kv[..., MLA_NOPE:]
    k_rope = jnp.broadcast_to(k_pe[:, :, None, :], (bsz, t, MLA_HEADS, MLA_ROPE)).astype(k_nope.dtype)
    return jnp.concatenate([k_nope, k_rope], axis=-1), v


def softmax_attend(q, k, v):
    s = jnp.einsum('bqhd,bkhd->bhqk', q, k).astype(jnp.float32) * (MLA_QK ** -0.5)
    p = jax.nn.softmax(s, axis=-1).astype(v.dtype)
    return jnp.einsum('bhqk,bkhd->bqhd', p, v)


def blockwise_attend(q, k, v):
    bsz, t, h, d = q.shape
    nblk = t // Q_BLOCK
    qb = q.reshape(bsz, nblk, Q_BLOCK, h, d).transpose(1, 0, 2, 3, 4)
    ob = lax.map(lambda qi: softmax_attend(qi, k, v), qb)
    return ob.transpose(1, 0, 2, 3, 4).reshape(bsz, t, h, v.shape[-1])


def conv_ffn(h, w_up, conv_w, conv_b, w_down):
    u = depthwise_conv_centred(h @ w_up, conv_w) + conv_b
    a, gate = jnp.split(u, 2, axis=-1)
    return (jax.nn.silu(a) * gate) @ w_down


def trunk_layer(x, cond, p, ctx):
    bsz, t, _ = x.shape
    shift1, scale1, gate1, shift2, scale2, gate2 = ada_modulation(cond, p['w_ada'], p['b_ada'])
    h = modulate(rms_norm(x, p['norm1_g']), shift1, scale1)
    qkv, z, a, b, q_a, kv_a, k_pe = split_cols(h @ p['w_in'], IN_SPLITS)
    if ctx is None:
        s0 = jnp.zeros((bsz, 2, GDN_HEADS, GDN_DK, GDN_DV), jnp.float32)
    else:
        s0 = ctx[0]
    gdn_o, s_fin = gdn_mix(qkv, z, a, b, p['gdn_conv_w'], p['gdn_a_log'], p['gdn_dt_bias'],
                           p['gdn_norm_g'], s0)
    q = (rms_norm(q_a, p['mla_q_norm_g']) @ p['mla_w_q_b']).reshape(bsz, t, MLA_HEADS, MLA_QK)
    ckv = rms_norm(kv_a, p['mla_kv_norm_g'])
    if ctx is None:
        k, v = mla_expand(ckv, k_pe, p['mla_w_kv_b'])
        mla_o = softmax_attend(q, k, v)
    else:
        cos, sin = axial_rope(t)
        q = jnp.concatenate([q[..., :MLA_NOPE],
                             rotate_half(q[..., MLA_NOPE:], cos[None, :, None, :], sin[None, :, None, :])], axis=-1)
        k_pe = rotate_half(k_pe, cos[None], sin[None])
        k_lat, v_lat = mla_expand(ckv, k_pe, p['mla_w_kv_b'])
        k_ctx, v_ctx = mla_expand(ctx[1].astype(ckv.dtype), ctx[2], p['mla_w_kv_b'])
        k = jnp.concatenate([k_lat, k_ctx], axis=1)
        v = jnp.concatenate([v_lat, v_ctx], axis=1)
        mla_o = blockwise_attend(q, k, v)
    mix = jnp.concatenate([gdn_o, mla_o.reshape(bsz, t, MLA_OUT_W)], axis=-1) @ p['w_out']
    x = x + gate1 * mix
    h2 = modulate(rms_norm(x, p['norm2_g']), shift2, scale2)
    x = x + gate2 * conv_ffn(h2, p['w_up'], p['ffn_conv_w'], p['ffn_conv_b'], p['w_down'])
    if ctx is None:
        return x, s_fin, ckv, k_pe
    return x


def setup_inputs(seed: int = 0) -> dict:
    key = jax.random.key(seed)
    ks = jax.random.split(key, 28)
    f32 = jnp.float32
    L, D = DEPTH, D_MODEL

    def nrm(k, shape, scale):
        return jax.random.normal(k, shape, f32) * scale

    def gain(k, shape):
        return 1.0 + 0.02 * jax.random.normal(k, shape, f32)

    a_init = jax.random.uniform(ks[9], (L, 2, GDN_HEADS), f32, 1.0, 16.0)
    dt = jnp.exp(jax.random.uniform(ks[10], (L, 2, GDN_HEADS), f32, math.log(1e-3), math.log(1e-1)))
    return {
        'x_prompt': nrm(ks[0], (BATCH, SEQ, D), 1.0),
        'x_sample': nrm(ks[1], (DEC_BATCH, DEC_SEQ, D), 1.0),
        'state_gdn': nrm(ks[2], (DEC_BATCH, L, 2, GDN_HEADS, GDN_DK, GDN_DV), 0.1),
        'cache_mla_ckv': nrm(ks[3], (DEC_BATCH, L, PAST_LEN, MLA_KV_LORA), 1.0),
        'cache_mla_kpe': nrm(ks[4], (DEC_BATCH, L, PAST_LEN, MLA_ROPE), 1.0),
        'c': nrm(ks[5], (DEC_BATCH, D), 1.0),
        'c_ctx': nrm(ks[6], (D,), 1.0),
        'w_ada': nrm(ks[7], (L, D, N_MOD * D), 0.5 * D ** -0.5),
        'b_ada': nrm(ks[8], (L, N_MOD * D), 0.01),
        'norm1_g': gain(ks[11], (L, D)),
        'w_in': nrm(ks[12], (L, D, IN_COLS), D ** -0.5),
        'gdn_conv_w': nrm(ks[13], (L, GDN_CONV, GDN_QKV_W), GDN_CONV ** -0.5),
        'gdn_a_log': jnp.log(a_init),
        'gdn_dt_bias': dt + jnp.log(-jnp.expm1(-dt)),
        'gdn_norm_g': gain(ks[14], (L, GDN_DV)),
        'mla_q_norm_g': gain(ks[15], (L, MLA_Q_LORA)),
        'mla_w_q_b': nrm(ks[16], (L, MLA_Q_LORA, MLA_HEADS * MLA_QK), MLA_Q_LORA ** -0.5),
        'mla_kv_norm_g': gain(ks[17], (L, MLA_KV_LORA)),
        'mla_w_kv_b': nrm(ks[18], (L, MLA_KV_LORA, MLA_HEADS * (MLA_NOPE + MLA_DV)), MLA_KV_LORA ** -0.5),
        'w_out': nrm(ks[19], (L, MIX_W, D), MIX_W ** -0.5),
        'norm2_g': gain(ks[20], (L, D)),
        'w_up': nrm(ks[21], (L, D, 2 * D_FF), D ** -0.5),
        'ffn_conv_w': nrm(ks[22], (L, FFN_CONV, 2 * D_FF), FFN_CONV ** -0.5),
        'ffn_conv_b': nrm(ks[23], (L, 2 * D_FF), 0.01),
        'w_down': nrm(ks[24], (L, D_FF, D), D_FF ** -0.5),
        'final_norm_g': gain(ks[25], (D,)),
    }


def reference(x_prompt, x_sample, state_gdn, cache_mla_ckv, cache_mla_kpe, c, c_ctx,
              w_ada, b_ada, norm1_g, w_in, gdn_conv_w, gdn_a_log, gdn_dt_bias, gdn_norm_g,
              mla_q_norm_g, mla_w_q_b, mla_kv_norm_g, mla_w_kv_b, w_out, norm2_g,
              w_up, ffn_conv_w, ffn_conv_b, w_down, final_norm_g):
    cond_ctx = jnp.broadcast_to(c_ctx[None, :], (x_prompt.shape[0], c_ctx.shape[-1]))
    yp, ys = x_prompt, x_sample
    states, ckvs, kpes = [], [], []
    for l in range(DEPTH):
        p = {
            'w_ada': w_ada[l], 'b_ada': b_ada[l], 'norm1_g': norm1_g[l], 'w_in': w_in[l],
            'gdn_conv_w': gdn_conv_w[l], 'gdn_a_log': gdn_a_log[l], 'gdn_dt_bias': gdn_dt_bias[l],
            'gdn_norm_g': gdn_norm_g[l], 'mla_q_norm_g': mla_q_norm_g[l], 'mla_w_q_b': mla_w_q_b[l],
            'mla_kv_norm_g': mla_kv_norm_g[l], 'mla_w_kv_b': mla_w_kv_b[l], 'w_out': w_out[l],
            'norm2_g': norm2_g[l], 'w_up': w_up[l], 'ffn_conv_w': ffn_conv_w[l],
            'ffn_conv_b': ffn_conv_b[l], 'w_down': w_down[l],
        }
        yp, s_l, ckv_l, kpe_l = trunk_layer(yp, cond_ctx, p, None)
        states.append(s_l)
        ckvs.append(ckv_l)
        kpes.append(kpe_l)
        ys = trunk_layer(ys, c, p, (state_gdn[:, l], cache_mla_ckv[:, l], cache_mla_kpe[:, l]))
    y_prompt = rms_norm(yp, final_norm_g)
    y_sample = rms_norm(ys, final_norm_g)
    new_state_gdn = jnp.stack(states, axis=1)
    new_cache_mla_ckv = jnp.stack(ckvs, axis=1)
    new_cache_mla_kpe = jnp.stack(kpes, axis=1)
    return (y_prompt, y_sample, new_state_gdn, new_cache_mla_ckv, new_cache_mla_kpe)
```

```python
import numpy as np
from contextlib import ExitStack
import concourse.bass as bass
import concourse.mybir as mybir
from concourse.bass_utils import run_bass_kernel_spmd

F32 = mybir.dt.float32
BF16 = mybir.dt.bfloat16
ALU = mybir.AluOpType
AF = mybir.ActivationFunctionType
AX = mybir.AxisListType
ENGS = ("pe", "act", "dve", "pool", "sp")
NCORES = 8
PENG = "dve"
GSTOP = {"v": 99}
EPS = 1e-6


class View:
    __slots__ = ("tile", "ap", "gen")

    def __init__(self, tile, ap, gen=None):
        self.tile = tile
        self.ap = ap
        self.gen = gen

    def __getitem__(self, idx):
        return View(self.tile, self.ap[idx], self.gen)

    def bc(self, shape):
        return View(self.tile, self.ap.to_broadcast(list(shape)), self.gen)

    def rearrange(self, s, **kw):
        return View(self.tile, self.ap.rearrange(s, **kw), self.gen)


class BankRef:
    def __init__(self, tile, gen):
        self.tile = tile
        self.gen = gen

    def __getitem__(self, idx):
        return View(self.tile, self.tile.h[idx], self.gen)


class Tile:
    def __init__(self, ap, name):
        self.h = ap
        self.name = name
        self.last_write = None
        self.reads = []
        self.dma_sem = None
        self.dma_count = 0

    def __getitem__(self, idx):
        return View(self, self.h[idx])

    @property
    def v(self):
        return View(self, self.h)


class MK:
    ARENA_F32 = 50688
    N_DMA_SEMS = 16

    def __init__(self, nc, stack):
        self.nc = nc
        self.stack = stack
        self.ops = {e: [] for e in ENGS}
        self.seq = {e: 0 for e in ENGS}
        self.sem = {e: stack.enter_context(nc.semaphore("sem_" + e)) for e in ("pe", "act", "dve", "pool")}
        self.waited = {e: {} for e in ENGS}
        self.dma_tiles = []
        self.n_inst = 0
        self.arena = stack.enter_context(nc.sbuf_tensor("arena", [128, self.ARENA_F32], F32))
        self.top = 0
        self.banks = [Tile(stack.enter_context(nc.psum_tensor(f"bank{i}", [128, 512], F32)), f"bank{i}")
                      for i in range(8)]
        for b in self.banks:
            b.is_bank = True
        self.bank_rr = 0
        self.reserved = []
        self.dma_sem_pool = {}
        self.dma_sem_rr = {}
        self.tcount = 0

    def alloc(self, free_shape, dtype=F32, name=None, parts=128):
        n = int(np.prod(free_shape))
        words = n if dtype == F32 else (n + 1) // 2
        words = (words + 7) // 8 * 8
        assert self.top + words <= self.ARENA_F32, f"SBUF arena overflow allocating {name} {free_shape}"
        ap = self.arena[0:parts, self.top:self.top + words]
        self.top += words
        if dtype != F32:
            ap = ap.bitcast(dtype)
        ap = ap[:, 0:n]
        if len(free_shape) == 2:
            ap = ap.rearrange("p (a b) -> p a b", a=free_shape[0])
        elif len(free_shape) == 3:
            ap = ap.rearrange("p (a b c) -> p a b c", a=free_shape[0], b=free_shape[1])
        self.tcount += 1
        return Tile(ap, name or f"t{self.tcount}")

    def mark(self):
        return self.top

    def release(self, mark):
        self.barrier()
        self.top = mark

    def bank(self):
        while True:
            b = self.banks[self.bank_rr % 8]
            self.bank_rr += 1
            if b not in self.reserved:
                b.gen = getattr(b, "gen", 0) + 1
                return BankRef(b, b.gen)

    def reserve(self):
        b = self.bank()
        self.reserved.append(b.tile)
        return b

    def unreserve(self, b):
        self.reserved.remove(b.tile)

    def _resolve(self, ev):
        if ev[0] == "dma":
            t = ev[1]
            return (t.dma_sem, t.dma_count, None)
        return (self.sem[ev[0]], ev[1], ev[0])

    def _wait(self, eng, ev):
        sem, val, src = self._resolve(ev)
        if src == eng and eng == "pe":
            return
        key = id(sem)
        if self.waited[eng].get(key, 0) >= val:
            return
        self.waited[eng][key] = val
        self.ops[eng].append(("w", sem, val))

    def _deps(self, eng, reads, writes):
        for t in reads:
            if t.last_write is not None:
                self._wait(eng, t.last_write)
            if getattr(t, "is_bank", False):
                for ev in t.reads:
                    if ev[0] != eng:
                        self._wait(eng, ev)
        for t in writes:
            if t.last_write is not None:
                self._wait(eng, t.last_write)
            for ev in t.reads:
                self._wait(eng, ev)

    def _commit(self, ev, reads, writes):
        for t in writes:
            t.last_write = ev
            t.reads = []
        for t in reads:
            if t in writes:
                continue
            t.reads.append(ev)
            if len(t.reads) > 24:
                best = {}
                for e in t.reads:
                    k = e[0] if e[0] != "dma" else ("dma", id(e[1]))
                    if k not in best or (e[0] != "dma" and e[1] > best[k][1]):
                        best[k] = e
                t.reads = list(best.values())

    def I(self, eng, meth, *args, reads=(), writes=(), **kw):
        rd, wr = list(reads), list(writes)
        real = {}
        for k, v in kw.items():
            if isinstance(v, View):
                assert v.gen is None or v.gen == v.tile.gen, f"stale PSUM bank handle used by {meth} ({k})"
                (wr if k in ("out", "accum_out") else rd).append(v.tile)
                real[k] = v.ap
            else:
                real[k] = v
        rargs = []
        for v in args:
            if isinstance(v, View):
                rd.append(v.tile)
                rargs.append(v.ap)
            else:
                rargs.append(v)
        self._deps(eng, rd, wr)
        self.seq[eng] += 1
        ev = (eng, self.seq[eng])
        self.ops[eng].append(("i", meth, rargs, real))
        self._commit(ev, rd, wr)
        self.n_inst += 1
        return ev

    def dma(self, q, out, in_, **kw):
        rd, wr = [], []
        st = None
        if isinstance(out, View):
            wr.append(out.tile)
            o = out.ap
            st = out.tile
        else:
            o = out
        if isinstance(in_, View):
            rd.append(in_.tile)
            i = in_.ap
            if st is None:
                st = in_.tile
        else:
            i = in_
        if st.dma_sem is None:
            st.dma_sem = {}
        if q not in st.dma_sem:
            pool = self.dma_sem_pool.setdefault(q, [])
            if len(pool) < self.N_DMA_SEMS:
                ds = Tile(None, "dsem_%s%d" % (q, len(pool)))
                ds.dma_sem = self.stack.enter_context(self.nc.semaphore("ds_%s%d" % (q, len(pool))))
                pool.append(ds)
                self.dma_tiles.append(ds)
            rr = self.dma_sem_rr.get(q, 0)
            self.dma_sem_rr[q] = rr + 1
            st.dma_sem[q] = pool[rr % self.N_DMA_SEMS]
        dsem = st.dma_sem[q]
        self._deps(q, rd, wr)
        if dsem.dma_count:
            self._wait(q, ("dma", dsem))
        dsem.dma_count += 16
        self.ops[q].append(("d", o, i, kw, dsem.dma_sem))
        ev = ("dma", dsem)
        self._commit(ev, rd, wr)
        self.n_inst += 1
        return ev

    def barrier(self):
        for e in ENGS:
            for src in ("pe", "act", "dve", "pool"):
                if src != e and self.seq[src] > 0:
                    self._wait(e, (src, self.seq[src]))
            for t in self.dma_tiles:
                if t.dma_count:
                    self._wait(e, ("dma", t))

    def finalize(self):
        for t in self.dma_tiles:
            self.ops["sp"].append(("w", t.dma_sem, t.dma_count))
        sem = self.sem

        def run(eng_name):
            def f(e):
                for op in self.ops[eng_name]:
                    if op[0] == "w":
                        e.wait_ge(op[1], op[2])
                    elif op[0] == "i":
                        ins = getattr(e, op[1])(*op[2], **op[3])
                        if eng_name in sem:
                            ins.then_inc(sem[eng_name], 1)
                    else:
                        e.dma_start(out=op[1], in_=op[2], **op[3]).then_inc(op[4], 16)
            return f

        with self.nc.Block() as block:
            block.tensor(run("pe"))
            block.scalar(run("act"))
            block.vector(run("dve"))
            block.gpsimd(run("pool"))
            block.sync(run("sp"))


class Ctx:
    pass


DEBUG = {"on": False, "nc": None, "done": set()}


def dump(mk, name, view, shape):
    if not DEBUG["on"] or name in DEBUG["done"]:
        return
    DEBUG["done"].add(name)
    ap = DEBUG["nc"].dram_tensor("dbg_" + name, list(shape), F32, kind="ExternalOutput").ap()
    stg = mk.alloc(list(shape[1:]), F32, "dbgs_" + name, parts=shape[0]) if False else None
    mk.dma("sp", ap, view)


def halves(W):
    if W <= 512:
        return [(0, W)]
    h = (W + 1) // 2
    return [(0, h), (h, W - h)]


def mm_group(mk, W, steps):
    outs = []
    for (c0, n) in halves(W):
        b = mk.bank()
        for i, (lhsT, rhs_fn) in enumerate(steps):
            mk.I("pe", "matmul", out=b[:, 0:n], lhsT=lhsT, rhs=rhs_fn(c0, n),
                 start=(i == 0), stop=(i == len(steps) - 1))
        outs.append((b, c0, n))
    return outs


def load_weight_tile(mk, wt, src_ap, KC, ncols):
    mk.dma("pool", wt[:, 0:KC, 0:ncols], src_ap.rearrange("(kc p) n -> p kc n", p=128))


def load_xT(mk, C, x_rows, W, xT, eng_rr):
    nsub = (W + 127) // 128
    for s in range(nsub):
        r0 = s * 128
        n = min(128, W - r0)
        xs = C.xstage[s % 2]
        mk.dma("sp", xs[0:n, :], x_rows[r0:r0 + n, :])
        for g in range(4):
            b = mk.bank()
            for q in range(4):
                c = g * 4 + q
                mk.I("pe", "transpose", out=b[:, q * 128:q * 128 + n], in_=xs[0:n, c * 128:(c + 1) * 128],
                     identity=C.ident[0:n, 0:n])
            src = b[:, 0:512].rearrange("p (q t) -> p q t", q=4)[:, :, 0:n]
            dst = xT[:, g * 4:(g + 1) * 4, r0:r0 + n]
            if (s * 4 + g) % 2 == 0:
                mk.I("dve", "tensor_copy", out=dst, in_=src)
            else:
                mk.I("act", "activation", out=dst, in_=src, func=AF.Copy)


def rms_rstd(mk, C, XT, nch, W, col0, dim, rstd):
    pieces = halves(W)
    banks = [mk.bank() for _ in pieces]
    for c in range(nch):
        sq = C.sq[c % 2]
        mk.I("act", "activation", out=sq[:, 0:W], in_=XT[:, c, col0:col0 + W], func=AF.Square)
        for (b, (c0, n)) in zip(banks, pieces):
            mk.I("pe", "matmul", out=b[:, 0:n], lhsT=C.ones.v, rhs=sq[:, c0:c0 + n], start=(c == 0), stop=(c == nch - 1))
    for (b, (c0, n)) in zip(banks, pieces):
        mk.I("act", "activation", out=C.tmpn[:, c0:c0 + n], in_=b[:, 0:n], func=AF.Sqrt, scale=1.0 / dim, bias=C.epsb[:, 0:1])
        mk.I("dve", "reciprocal", out=rstd[:, c0:c0 + n], in_=C.tmpn[:, c0:c0 + n])


def adaln(mk, C, w_ada, which_list, ncond):
    b = mk.bank()
    for wh in which_list:
        for blk in range(4):
            col0 = wh * 2048 + blk * 512
            wt = C.wt[C.wt_rr % len(C.wt)]
            C.wt_rr += 1
            load_weight_tile(mk, wt, w_ada[:, col0:col0 + 512], 16, 512)
            for q in range(4):
                cc = wh * 16 + blk * 4 + q
                for kc in range(16):
                    mk.I("pe", "matmul", out=b[:, cc * ncond:(cc + 1) * ncond], lhsT=wt[:, kc, q * 128:(q + 1) * 128],
                         rhs=C.scond[:, kc, 0:ncond], start=(kc == 0), stop=(kc == 15))
    for wh in which_list:
        sl = slice(wh * 16, (wh + 1) * 16)
        mk.I("dve", "tensor_tensor", out=C.mod[:, sl, :],
             in0=b[:, wh * 16 * ncond:(wh + 1) * 16 * ncond].rearrange("p (c n) -> p c n", n=ncond),
             in1=C.b_ada[:, sl].rearrange("p (c o) -> p c o", o=1).bc([128, 16, ncond]), op=ALU.add)


P2_TILES = [dict(W=512, cond=0, segs=[(0, 256, 0), (256, 256, 0)], out0=0),
            dict(W=514, cond=1, segs=[(0, 514, 1)], out0=1),
            dict(W=514, cond=1, segs=[(0, 514, 1)], out0=1)]


def phase2_tiles(mk, C, x2, y, load_mix, n2g, fng, fcw, fcb, hm, gm2, xT, actin, actT, Ra, Rg, ta, tg, tmpx, rstd,
                 w_out, w_up, w_down):
    for ti, T in enumerate(P2_TILES):
        W, ci, out0 = T["W"], T["cond"], T["out0"]
        load_xT(mk, C, x2[ti], W, xT, 0)
        load_mix(ti, actin, W)
        for blk in range(4):
            wt = C.wt[C.wt_rr % len(C.wt)]
            C.wt_rr += 1
            load_weight_tile(mk, wt, w_out[:, blk * 512:(blk + 1) * 512], 16, 512)
            for q in range(4):
                cc = blk * 4 + q
                outs = mm_group(mk, W, [(wt[:, kc, q * 128:(q + 1) * 128],
                                        (lambda c0, n, kc=kc: actin[:, kc, c0:c0 + n])) for kc in range(16)])
                for (b, c0, n) in outs:
                    mk.I("dve", "scalar_tensor_tensor", out=xT[:, cc, c0:c0 + n], in0=b[:, 0:n],
                         scalar=C.mod[:, 32 + cc, ci:ci + 1], in1=xT[:, cc, c0:c0 + n], op0=ALU.mult, op1=ALU.add)
        rms_rstd(mk, C, xT, 16, W, 0, 2048.0, rstd)
        for cc in range(16):
            tx = tmpx[cc % 2]
            mk.I("dve", "scalar_tensor_tensor", out=tx[:, 0:W], in0=xT[:, cc, 0:W], scalar=gm2[:, cc, ci:ci + 1],
                 in1=rstd[:, 0:W], op0=ALU.mult, op1=ALU.mult)
            mk.I("act", "activation", out=actin[:, cc, 0:W], in_=tx[:, 0:W], func=AF.Identity,
                 bias=C.mod[:, 48 + cc, ci:ci + 1], scale=1.0)
        for jb in range(11):
            wa = C.wt[C.wt_rr % len(C.wt)]
            C.wt_rr += 1
            load_weight_tile(mk, wa, w_up[:, jb * 512:(jb + 1) * 512], 16, 512)
            wg = C.wt[C.wt_rr % len(C.wt)]
            C.wt_rr += 1
            load_weight_tile(mk, wg, w_up[:, 5632 + jb * 512:5632 + (jb + 1) * 512], 16, 512)
            for q in range(4):
                j = jb * 4 + q
                res = []
                for (wtile, R, chunk) in ((wa, Ra[j % 2], j), (wg, Rg[j % 2], 44 + j)):
                    outs = mm_group(mk, W, [(wtile[:, kc, q * 128:(q + 1) * 128],
                                            (lambda c0, n, kc=kc: actin[:, kc, c0:c0 + n])) for kc in range(16)])
                    for (b, c0, n) in outs:
                        mk.I("act", "activation", out=R[:, c0:c0 + n], in_=b[:, 0:n], func=AF.Copy)
                    res.append((R, chunk))
                for (R, chunk), tt in ((res[0], ta[j % 2]), (res[1], tg[j % 2])):
                    for (s0, L, halo) in T["segs"]:
                        if halo:
                            mk.I("dve", "tensor_scalar", out=R[:, s0:s0 + 1], in0=R[:, s0:s0 + 1],
                                 scalar1=hm[:, 2 * (ti - 1):2 * (ti - 1) + 1], scalar2=None, op0=ALU.mult)
                            mk.I("dve", "tensor_scalar", out=R[:, s0 + L - 1:s0 + L], in0=R[:, s0 + L - 1:s0 + L],
                                 scalar1=hm[:, 2 * (ti - 1) + 1:2 * (ti - 1) + 2], scalar2=None, op0=ALU.mult)
                            o0, n = s0 + 1, L - 2
                            mk.I("act", "activation", out=tt[:, 0:n], in_=R[:, o0:o0 + n], func=AF.Identity,
                                 scale=fcw[:, chunk, 1:2], bias=fcb[:, chunk:chunk + 1])
                            mk.I("dve", "scalar_tensor_tensor", out=tt[:, 0:n], in0=R[:, o0 - 1:o0 - 1 + n],
                                 scalar=fcw[:, chunk, 0:1], in1=tt[:, 0:n], op0=ALU.mult, op1=ALU.add)
                            mk.I("dve", "scalar_tensor_tensor", out=tt[:, 0:n], in0=R[:, o0 + 1:o0 + 1 + n],
                                 scalar=fcw[:, chunk, 2:3], in1=tt[:, 0:n], op0=ALU.mult, op1=ALU.add)
                        else:
                            mk.I("act", "activation", out=tt[:, s0:s0 + L], in_=R[:, s0:s0 + L], func=AF.Identity,
                                 scale=fcw[:, chunk, 1:2], bias=fcb[:, chunk:chunk + 1])
                            mk.I("dve", "scalar_tensor_tensor", out=tt[:, s0 + 1:s0 + L], in0=R[:, s0:s0 + L - 1],
                                 scalar=fcw[:, chunk, 0:1], in1=tt[:, s0 + 1:s0 + L], op0=ALU.mult, op1=ALU.add)
                            mk.I("dve", "scalar_tensor_tensor", out=tt[:, s0:s0 + L - 1], in0=R[:, s0 + 1:s0 + L],
                                 scalar=fcw[:, chunk, 2:3], in1=tt[:, s0:s0 + L - 1], op0=ALU.mult, op1=ALU.add)
                mk.I("act", "activation", out=ta[j % 2].v, in_=ta[j % 2].v, func=AF.Silu)
                mk.I("dve", "tensor_tensor", out=actT[:, j, :], in0=ta[j % 2].v, in1=tg[j % 2].v, op=ALU.mult)
        for cc in range(16):
            wt = C.wt[C.wt_rr % len(C.wt)]
            C.wt_rr += 1
            wv = wt.v.rearrange("p a b -> p (a b)")[:, 0:44 * 128].rearrange("p (k n) -> p k n", k=44)
            mk.dma("pool", wv, w_down[:, cc * 128:(cc + 1) * 128].rearrange("(kc p) n -> p kc n", p=128))
            b = mk.bank()
            for j in range(44):
                mk.I("pe", "matmul", out=b[:, 0:512], lhsT=wv[:, j, :], rhs=actT[:, j, :], start=(j == 0), stop=(j == 43))
            mk.I("dve", "scalar_tensor_tensor", out=xT[:, cc, out0:out0 + 512], in0=b[:, 0:512],
                 scalar=C.mod[:, 80 + cc, ci:ci + 1], in1=xT[:, cc, out0:out0 + 512], op0=ALU.mult, op1=ALU.add)
        rms_rstd(mk, C, xT, 16, 512, out0, 2048.0, rstd)
        for cc in range(16):
            mk.I("dve", "scalar_tensor_tensor", out=xT[:, cc, out0:out0 + 512], in0=xT[:, cc, out0:out0 + 512],
                 scalar=fng[:, cc:cc + 1], in1=rstd[:, 0:512], op0=ALU.mult, op1=ALU.mult)
        for s in range(4):
            ys = C.xstage[s % 2]
            for g in range(4):
                b = mk.bank()
                for q in range(4):
                    c = g * 4 + q
                    mk.I("pe", "transpose", out=b[:, q * 128:(q + 1) * 128],
                         in_=xT[:, c, out0 + s * 128:out0 + (s + 1) * 128], identity=C.ident.v)
                if g % 2 == 0:
                    mk.I("dve", "tensor_copy", out=ys[:, g * 512:(g + 1) * 512], in_=b[:, 0:512])
                else:
                    mk.I("act", "activation", out=ys[:, g * 512:(g + 1) * 512], in_=b[:, 0:512], func=AF.Copy)
            mk.dma("sp", y[ti, s * 128:(s + 1) * 128, :], ys.v)


def build_phase2():
    nc = bass.Bass("TRN2", target_bir_lowering=False)
    D = {}

    def din(name, shape, dt=F32):
        D[name] = nc.dram_tensor(name, list(shape), dt, kind="ExternalInput").ap()
        return D[name]

    x2 = din("x2", [3, 514, 2048])
    mix = din("mix", [3, 16, 128, 514])
    condT = din("condT", [128, 16, 2])
    hmask = din("hmask", [128, 4])
    w_ada = din("w_ada", [2048, 12288])
    b_adaT = din("b_adaT", [128, 96])
    w_out = din("w_out", [2048, 2048])
    norm2T = din("norm2T", [128, 16])
    w_up = din("w_up", [2048, 11264])
    fcwT = din("fcwT", [128, 88, 3])
    fcbT = din("fcbT", [128, 88])
    w_down = din("w_down", [5632, 2048])
    fnormT = din("fnormT", [128, 16])
    identD = din("ident", [128, 128])
    y = nc.dram_tensor("y", [3, 512, 2048], F32, kind="ExternalOutput").ap()

    with ExitStack() as st:
        mk = MK(nc, st)
        C = Ctx()
        C.ident = mk.alloc([128], F32, "ident")
        C.ones = mk.alloc([128], F32, "ones")
        C.epsb = mk.alloc([1], F32, "epsb")
        C.scond = mk.alloc([16, 2], BF16, "scond")
        condf = mk.alloc([16, 2], F32, "condf")
        C.b_ada = mk.alloc([96], F32, "b_ada")
        C.mod = mk.alloc([96, 2], F32, "mod")
        n2g = mk.alloc([16], F32, "n2g")
        fng = mk.alloc([16], F32, "fng")
        fcw = mk.alloc([88, 3], F32, "fcw")
        fcb = mk.alloc([88], F32, "fcb")
        hm = mk.alloc([4], F32, "hm")
        gm2 = mk.alloc([16, 2], F32, "gm2")
        C.xstage = [mk.alloc([2048], F32, f"xs{i}") for i in range(2)]
        C.sq = [mk.alloc([514], F32, f"sq{i}") for i in range(2)]
        C.tmpn = mk.alloc([514], F32, "tmpn")
        rstd = mk.alloc([514], F32, "rstd")
        C.wt = [mk.alloc([16, 512], BF16, f"wt{i}") for i in range(3)]
        C.wt_rr = 0
        xT = mk.alloc([16, 514], F32, "xT")
        actin = mk.alloc([16, 514], BF16, "actin")
        actT = mk.alloc([44, 512], BF16, "actT")
        Ra = [mk.alloc([514], F32, f"Ra{i}") for i in range(2)]
        Rg = [mk.alloc([514], F32, f"Rg{i}") for i in range(2)]
        ta = [mk.alloc([512], F32, f"ta{i}") for i in range(2)]
        tg = [mk.alloc([512], F32, f"tg{i}") for i in range(2)]
        tmpx = [mk.alloc([514], F32, f"tmpx{i}") for i in range(2)]

        mk.dma("sp", C.ident.v, identD)
        mk.I("dve", "memset", C.ones.v.ap, 1.0, writes=[C.ones])
        mk.I("dve", "memset", C.epsb.v.ap, EPS, writes=[C.epsb])
        mk.dma("sp", condf.v, condT)
        mk.dma("sp", C.b_ada.v, b_adaT)
        mk.dma("sp", n2g.v, norm2T)
        mk.dma("sp", fng.v, fnormT)
        mk.dma("sp", fcw.v, fcwT)
        mk.dma("sp", fcb.v, fcbT)
        mk.dma("sp", hm.v, hmask)
        mk.I("act", "activation", out=C.scond.v, in_=condf.v, func=AF.Silu)
        adaln(mk, C, w_ada, [2, 3, 4, 5], 2)
        mk.I("dve", "tensor_scalar", out=gm2.v, in0=C.mod[:, 64:80, :], scalar1=1.0, scalar2=None, op0=ALU.add)
        mk.I("dve", "tensor_tensor", out=gm2.v, in0=gm2.v,
             in1=n2g.v.rearrange("p (c o) -> p c o", o=1).bc([128, 16, 2]), op=ALU.mult)

        phase2_tiles(mk, C, x2, y, lambda ti, actin, W: mk.dma("pool", actin[:, :, 0:W], mix[ti, :, :, 0:W].rearrange("c p w -> p c w")),
                     n2g, fng, fcw, fcb, hm, gm2, xT, actin, actT, Ra, Rg, ta, tg, tmpx, rstd, w_out, w_up, w_down)
        mk.finalize()
        print("phase2 instructions:", mk.n_inst, {e: len(mk.ops[e]) for e in ENGS}, "sbuf words", mk.top)
    return nc


def colT(v, n):
    return np.ascontiguousarray(np.asarray(v, np.float32).reshape(n, 128).T)


def phase2_inputs(core, x_prompt, x_sample, mix_p, mix_s, c, c_ctx, w_ada, b_ada, w_out, norm2_g, w_up,
                  ffn_conv_w, ffn_conv_b, w_down, final_norm_g):
    b, j = core // 4, core % 4
    x2 = np.zeros((3, 514, 2048), np.float32)
    mix = np.zeros((3, 514, 2048), np.float32)
    x2[0, :512] = x_prompt[2 * core:2 * core + 2].reshape(512, 2048)
    mix[0, :512] = mix_p[2 * core:2 * core + 2].reshape(512, 2048)
    hmask = np.zeros((128, 4), np.float32)
    for g in range(2):
        lo = 1024 * j + 512 * g - 1
        hi = lo + 514
        a, e = max(lo, 0), min(hi, 4096)
        x2[1 + g, a - lo:e - lo] = x_sample[b, a:e]
        mix[1 + g, a - lo:e - lo] = mix_s[b, a:e]
        hmask[:, 2 * g] = 1.0 if lo >= 0 else 0.0
        hmask[:, 2 * g + 1] = 1.0 if hi <= 4096 else 0.0
    mixT = np.ascontiguousarray(mix.reshape(3, 514, 16, 128).transpose(0, 2, 3, 1))
    cond = np.stack([c_ctx, c[b]], axis=0)
    condT = np.ascontiguousarray(cond.reshape(2, 16, 128).transpose(2, 1, 0))
    return dict(x2=x2, mix=mixT, condT=condT, hmask=hmask, w_ada=w_ada, b_adaT=colT(b_ada, 96), w_out=w_out,
                norm2T=colT(norm2_g, 16), w_up=w_up,
                fcwT=np.ascontiguousarray(ffn_conv_w.reshape(3, 88, 128).transpose(2, 1, 0)),
                fcbT=colT(ffn_conv_b, 88), w_down=w_down, fnormT=colT(final_norm_g, 16),
                ident=np.eye(128, dtype=np.float32))


def norm_mod_tile(mk, C, x_rows, W, xT, hT, rstd, gm1, ci):
    load_xT(mk, C, x_rows, W, xT, 0)
    rms_rstd(mk, C, xT, 16, W, 0, 2048.0, rstd)
    for cc in range(16):
        tx = C.tmpx[cc % 2]
        mk.I("dve", "scalar_tensor_tensor", out=tx[:, 0:W], in0=xT[:, cc, 0:W], scalar=gm1[:, cc, ci:ci + 1],
             in1=rstd[:, 0:W], op0=ALU.mult, op1=ALU.mult)
        mk.I("act", "activation", out=hT[:, cc, 0:W], in_=tx[:, 0:W], func=AF.Identity,
             bias=C.mod[:, cc, ci:ci + 1], scale=1.0)


def bcast_sum_rstd(mk, C, srcs, W, dim, rstd, eps=EPS):
    b = mk.bank()
    for i, (s, P) in enumerate(srcs):
        sq = C.sq[i % 2]
        mk.I("act", "activation", out=sq[0:P, 0:W], in_=s, func=AF.Square)
        mk.I("pe", "matmul", out=b[:, 0:W], lhsT=C.ones[0:P, :], rhs=sq[0:P, 0:W], start=(i == 0), stop=(i == len(srcs) - 1))
    mk.I("act", "activation", out=C.tmpn[:, 0:W], in_=b[:, 0:W], func=AF.Sqrt, scale=1.0 / dim, bias=C.epsb[:, 0:1] if eps == EPS else C.zerob[:, 0:1])
    mk.I("dve", "reciprocal", out=rstd[:, 0:W], in_=C.tmpn[:, 0:W])


def gdn_unit(mk, C, G, h, d, c):
    NH = G.NH
    cols = slice(c * 128, (c + 1) * 128)
    gi = d * NH + h
    g_col = G.Gt[:, c, gi:gi + 1]
    beta_col = G.Bt[:, c, gi:gi + 1]
    nbeta_col = G.NBt[:, c, gi:gi + 1]
    TRI = C.tri[d]
    MS = C.ms[d]
    S = G.S[h][d]
    u = C.gu
    k = u.k = (u.k + 1) % 2
    kT = G.KT[h][:, cols]
    qT = G.QT[h][:, cols]
    bk = mk.bank()
    mk.I("pe", "matmul", out=bk[:, 0:128], lhsT=kT, rhs=C.identb.v, start=True, stop=True)
    bv = mk.bank()
    mk.I("pe", "matmul", out=bv[:, 0:128], lhsT=G.VT[h][:, cols], rhs=C.identb.v, start=True, stop=True)
    ktm = u.ktm[k]
    vb = u.vb[k]
    mk.I("act", "activation", out=ktm.v, in_=bk[:, 0:128], func=AF.Copy)
    mk.I("dve", "tensor_scalar", out=vb.v, in0=bv[:, 0:128], scalar1=beta_col, scalar2=None, op0=ALU.mult)
    bg = mk.bank()
    mk.I("pe", "matmul", out=bg[:, 0:1], lhsT=TRI.v, rhs=g_col, start=True, stop=True)
    mk.I("pe", "matmul", out=bg[:, 128:256], lhsT=g_col.bc([128, 128]), rhs=TRI.v, start=True, stop=True)
    Gc = u.Gc[k]
    Gb = u.Gb[k]
    mk.I("dve", "tensor_copy", out=Gc[:, 0:1], in_=bg[:, 0:1])
    mk.I("act", "activation", out=Gb.v, in_=bg[:, 128:256], func=AF.Copy)
    gtot = Gb[:, 127:128] if d == 0 else Gb[:, 0:1]
    if GSTOP['v'] < 3:
        return
    Dm, DTm = u.Dm[k], u.DTm[k]
    mk.I("dve", "tensor_scalar", out=Dm.v, in0=Gb.v, scalar1=Gc[:, 0:1], scalar2=0.0, op0=ALU.subtract, op1=ALU.max)
    mk.I("act", "activation", out=Dm.v, in_=Dm.v, func=AF.Exp, scale=-1.0)
    mk.I(PENG, "tensor_tensor", out=Dm.v, in0=Dm.v, in1=MS.v, op=ALU.mult)
    mk.I("dve", "tensor_scalar", out=DTm.v, in0=Gb.v, scalar1=Gc[:, 0:1], scalar2=0.0, op0=ALU.subtract, op1=ALU.min)
    mk.I("act", "activation", out=DTm.v, in_=DTm.v, func=AF.Exp)
    mk.I(PENG, "tensor_tensor", out=DTm.v, in0=DTm.v, in1=TRI.v, op=ALU.mult)
    if GSTOP['v'] < 4:
        return
    bkk = mk.bank()
    kTc = u.kTc[k]
    mk.I("dve", "tensor_copy", out=kTc.v, in_=kT)
    mk.I("pe", "matmul", out=bkk[:, 0:128], lhsT=kT, rhs=kTc.v, start=True, stop=True)
    P, PT = u.P[k], u.PT[k]
    mk.I("dve", "scalar_tensor_tensor", out=P[0].v, in0=bkk[:, 0:128], scalar=nbeta_col, in1=Dm.v, op0=ALU.mult, op1=ALU.mult)
    bt = mk.bank()
    mk.I("pe", "transpose", out=bt[:, 0:128], in_=P[0].v, identity=C.ident.v)
    mk.I("act", "activation", out=PT[0].v, in_=bt[:, 0:128], func=AF.Copy)
    TT = u.TT[k]
    mk.I("dve", "tensor_tensor", out=TT.v, in0=bt[:, 0:128], in1=C.ident.v, op=ALU.add)
    if GSTOP['v'] < 5:
        return
    cur = 0
    for lev in range(1, 7):
        nxt = 1 - cur
        b1 = mk.bank()
        mk.I("pe", "matmul", out=b1[:, 0:128], lhsT=PT[cur].v, rhs=P[cur].v, start=True, stop=True)
        if lev < 6:
            mk.I("pe", "matmul", out=b1[:, 128:256], lhsT=P[cur].v, rhs=PT[cur].v, start=True, stop=True)
        mk.I("act", "activation", out=P[nxt].v, in_=b1[:, 0:128], func=AF.Copy)
        if lev < 6:
            mk.I("dve", "tensor_copy", out=PT[nxt].v, in_=b1[:, 128:256])
        b2 = mk.bank()
        mk.I("pe", "matmul", out=b2[:, 0:128], lhsT=P[nxt].v, rhs=TT.v, start=True, stop=True)
        mk.I("dve", "tensor_tensor", out=TT.v, in0=b2[:, 0:128], in1=TT.v, op=ALU.add)
        cur = nxt
    if GSTOP['v'] < 6:
        return
    sc = u.sc[k]
    mk.I("act", "activation", out=sc[:, 0:1], in_=Gc[:, 0:1], func=AF.Exp)
    mk.I("dve", "tensor_tensor", out=sc[:, 1:2], in0=sc[:, 0:1], in1=beta_col, op=ALU.mult)
    mk.I("act", "activation", out=sc[:, 2:3], in_=Gc[:, 0:1], func=AF.Exp, scale=-1.0, bias=gtot)
    mk.I("act", "activation", out=sc[:, 3:4], in_=gtot, func=AF.Exp)
    kbg, kdec = u.kbg[k], u.kdec[k]
    mk.I("act", "activation", out=kbg.v, in_=ktm.v, func=AF.Identity, scale=sc[:, 1:2])
    mk.I("dve", "tensor_scalar", out=kdec.v, in0=ktm.v, scalar1=sc[:, 2:3], scalar2=None, op0=ALU.mult)
    bu = mk.bank()
    mk.I("pe", "matmul", out=bu[:, 0:128], lhsT=TT.v, rhs=vb.v, start=True, stop=True)
    mk.I("pe", "matmul", out=bu[:, 128:256], lhsT=kbg.v, rhs=TT.v, start=True, stop=True)
    uu, wT = u.uu[k], u.wT[k]
    mk.I("act", "activation", out=uu.v, in_=bu[:, 0:128], func=AF.Copy)
    mk.I("dve", "tensor_copy", out=wT.v, in_=bu[:, 128:256])
    if GSTOP['v'] < 7:
        return
    bq = mk.bank()
    mk.I("pe", "matmul", out=bq[:, 0:128], lhsT=kT, rhs=qT, start=True, stop=True)
    intraT, qgT, eGb = u.intraT[k], u.qgT[k], u.eGb[k]
    mk.I("dve", "tensor_tensor", out=intraT.v, in0=bq[:, 0:128], in1=DTm.v, op=ALU.mult)
    mk.I("act", "activation", out=eGb.v, in_=Gb.v, func=AF.Exp)
    mk.I(PENG, "tensor_tensor", out=qgT.v, in0=qT, in1=eGb.v, op=ALU.mult)
    if GSTOP['v'] < 8:
        return
    b3 = mk.bank()
    mk.I("pe", "matmul", out=b3[:, 0:128], lhsT=wT.v, rhs=S.v, start=True, stop=True)
    vnew = u.vnew[k]
    mk.I("dve", "tensor_tensor", out=vnew.v, in0=uu.v, in1=b3[:, 0:128], op=ALU.subtract)
    b4 = mk.bank()
    mk.I("pe", "matmul", out=b4[:, 0:128], lhsT=S.v, rhs=qgT.v, start=True, stop=False)
    mk.I("pe", "matmul", out=b4[:, 0:128], lhsT=vnew.v, rhs=intraT.v, start=False, stop=True)
    mk.I("pe", "matmul", out=b4[:, 128:256], lhsT=kdec.v, rhs=vnew.v, start=True, stop=True)
    ocols = slice(c * 128 + G.pad, (c + 1) * 128 + G.pad)
    mk.I("dve", "tensor_tensor", out=G.OT[h][:, ocols], in0=G.OT[h][:, ocols], in1=b4[:, 0:128], op=ALU.add)
    mk.I("dve", "scalar_tensor_tensor", out=S.v, in0=S.v, scalar=sc[:, 3:4], in1=b4[:, 128:256], op0=ALU.mult, op1=ALU.add)
    if h == 0 and c == 0 and d == 0:
        for nm, vv in (("Gb", Gb), ("Dm", Dm), ("DTm", DTm), ("X", P[0] if False else None), ("TT", TT), ("vb", vb), ("kbg", kbg), ("kdec", kdec),
                       ("uu", uu), ("wT", wT), ("intraT", intraT), ("qgT", qgT), ("vnew", vnew), ("S", S)):
            if vv is not None:
                dump(mk, nm, vv.v, [128, 128])
        dump(mk, "sc", sc[:, 0:4], [128, 4])
        dump(mk, "Gc", Gc.v, [128, 1])


def conv_out(mk, C, G, h, comp, R, n, tok0):
    t = C.ct[C.ct_rr % 2]
    C.ct_rr += 1
    ch = h * 3 + comp
    mk.I("act", "activation", out=t[:, 0:n], in_=R[:, 1:1 + n], func=AF.Identity, scale=G.cw[:, ch, 1:2])
    mk.I("dve", "scalar_tensor_tensor", out=t[:, 0:n], in0=R[:, 0:n], scalar=G.cw[:, ch, 0:1], in1=t[:, 0:n], op0=ALU.mult, op1=ALU.add)
    mk.I("dve", "scalar_tensor_tensor", out=t[:, 0:n], in0=R[:, 2:2 + n], scalar=G.cw[:, ch, 2:3], in1=t[:, 0:n], op0=ALU.mult, op1=ALU.add)
    j0 = 1 if tok0 < 0 else 0
    if n - j0 <= 0:
        return
    dsl = slice(tok0 + j0, tok0 + n)
    if comp == 2:
        mk.I("act", "activation", out=G.VT[h][:, dsl], in_=t[:, j0:n], func=AF.Silu)
        return
    mk.I("act", "activation", out=t[:, 0:n], in_=t[:, 0:n], func=AF.Silu)
    bcast_sum_rstd(mk, C, [(t[:, 0:n], 128)], n, 1.0, C.rstd)
    dst = (G.QT if comp == 0 else G.KT)[h]
    mk.I("dve", "scalar_tensor_tensor", out=dst[:, dsl], in0=t[:, j0:n], scalar=(128.0 ** -0.5 if comp == 0 else 1.0),
         in1=C.rstd[:, j0:n], op0=ALU.mult, op1=ALU.mult)


def gdn_phase(mk, C, x_rows, T, NH, ci, gm1, Wd, s0_ap, mix_out, state_out, win=None):
    mark = mk.mark()
    G = Ctx()
    G.NH = NH
    pad = G.pad = 1 if win is not None else 0
    u = C.gu = Ctx()
    u.k = 0
    for nm in ("ktm", "Gb", "Dm", "DTm", "TT", "vb", "kbg", "kdec", "uu", "wT", "intraT", "qgT", "eGb", "vnew"):
        setattr(u, nm, [mk.alloc([128], F32, f"{nm}{k}") for k in range(2)])
    u.P = [[mk.alloc([128], F32, f"P{k}{i}") for i in range(2)] for k in range(2)]
    u.PT = [[mk.alloc([128], F32, f"PT{k}{i}") for i in range(2)] for k in range(2)]
    u.Gc = [mk.alloc([1], F32, f"Gc{k}") for k in range(2)]
    u.kTc = [mk.alloc([128], BF16, f"kTc{k}") for k in range(2)]
    u.sc = [mk.alloc([8], F32, f"sc{k}") for k in range(2)]
    TTs = min(512, T)
    ntile = T // TTs
    nch = T // 128
    G.QT = [mk.alloc([T], BF16, f"QT{h}") for h in range(NH)]
    G.KT = [mk.alloc([T], BF16, f"KT{h}") for h in range(NH)]
    G.VT = [mk.alloc([T], BF16, f"VT{h}") for h in range(NH)]
    G.ZT = [mk.alloc([T + 2 * pad], BF16, f"ZT{h}") for h in range(NH)]
    G.OT = [mk.alloc([T + 2 * pad], F32, f"OT{h}") for h in range(NH)]
    G.Gt = mk.alloc([nch, 2 * NH], F32, "Gt")
    G.Bt = mk.alloc([nch, 2 * NH], F32, "Bt")
    G.NBt = mk.alloc([nch, 2 * NH], F32, "NBt")
    G.cw = mk.alloc([NH * 3, 3], F32, "cw")
    G.halo = [[mk.alloc([2], F32, f"halo{h}_{c}") for c in range(3)] for h in range(NH)]
    G.S = [[mk.alloc([128], F32, f"S{h}_{d}") for d in range(2)] for h in range(NH)]
    wab = mk.alloc([16, 4 * NH], BF16, "wab")
    dtb = mk.alloc([2 * NH], F32, "dtb")
    nA = mk.alloc([2 * NH], F32, "nA")
    gng = mk.alloc([1], F32, "gng")
    sm = [mk.alloc([2 * NH], F32, f"sm{i}") for i in range(4)]
    hT = mk.alloc([16, TTs], BF16, "hT")
    xT = mk.alloc([16, TTs], F32, "xTg")
    mk.dma("sp", G.cw.v, Wd["cw"])
    mk.dma("sp", dtb.v, Wd["dtb"])
    mk.dma("sp", nA.v, Wd["alog"])
    mk.dma("sp", gng.v, Wd["gng"])
    mk.dma("pool", wab.v, Wd["w_ab"].rearrange("(kc p) n -> p kc n", p=128))
    mk.I("act", "activation", out=nA.v, in_=nA.v, func=AF.Exp)
    mk.I("dve", "tensor_scalar", out=nA.v, in0=nA.v, scalar1=-1.0, scalar2=None, op0=ALU.mult)
    for h in range(NH):
        mk.I(PENG, "memset", G.OT[h].v.ap, 0.0, writes=[G.OT[h]])
        if pad:
            mk.I(PENG, "memset", G.ZT[h].v.ap, 0.0, writes=[G.ZT[h]])
        for c in range(3):
            mk.I(PENG, "memset", G.halo[h][c].v.ap, 0.0, writes=[G.halo[h][c]])
        for d in range(2):
            mk.dma("sp", G.S[h][d].v, s0_ap[d, h])
    for it in range(ntile):
        t0 = it * TTs
        norm_mod_tile(mk, C, x_rows[t0:t0 + TTs, :], TTs, xT, hT, C.rstd, gm1, ci)
        for h in range(NH):
            wt = C.wt[C.wt_rr % len(C.wt)]
            C.wt_rr += 1
            load_weight_tile(mk, wt, Wd["w_g"][:, h * 512:(h + 1) * 512], 16, 512)
            for comp in range(4):
                b = mk.bank()
                for kc in range(16):
                    mk.I("pe", "matmul", out=b[:, 0:TTs], lhsT=wt[:, kc, comp * 128:(comp + 1) * 128], rhs=hT[:, kc, 0:TTs],
                         start=(kc == 0), stop=(kc == 15))
                if comp == 3:
                    mk.I("act", "activation", out=G.ZT[h][:, pad + t0:pad + t0 + TTs], in_=b[:, 0:TTs], func=AF.Silu)
                    continue
                R = C.R[C.R_rr % 2]
                C.R_rr += 1
                halo = G.halo[h][comp]
                mk.I("dve", "tensor_copy", out=R[:, 0:2], in_=halo.v)
                mk.I("act", "activation", out=R[:, 2:2 + TTs], in_=b[:, 0:TTs], func=AF.Copy)
                mk.I("dve", "tensor_copy", out=halo.v, in_=R[:, TTs:TTs + 2])
                conv_out(mk, C, G, h, comp, R, TTs, t0 - 1)
                if it == ntile - 1:
                    R2 = C.R[C.R_rr % 2]
                    C.R_rr += 1
                    mk.I("dve", "tensor_copy", out=R2[:, 0:2], in_=halo.v)
                    mk.I("dve", "memset", R2[:, 2:3].ap, 0.0, writes=[R2])
                    conv_out(mk, C, G, h, comp, R2, 1, T - 1)
        for s in range(TTs // 128 if GSTOP['v'] >= 1 else 0):
            c = (t0 + s * 128) // 128
            b = mk.bank()
            for kc in range(16):
                mk.I("pe", "matmul", out=b[:, 0:4 * NH], lhsT=hT[:, kc, s * 128:(s + 1) * 128], rhs=wab[:, kc, :],
                     start=(kc == 0), stop=(kc == 15))
            xs, ax, ee, rr_ = sm
            mk.I("dve", "tensor_tensor", out=xs.v, in0=b[:, 0:2 * NH], in1=dtb.v, op=ALU.add)
            mk.I("dve", "tensor_scalar", out=ax.v, in0=xs.v, scalar1=-1.0, scalar2=None, op0=ALU.mult)
            mk.I("dve", "tensor_tensor", out=ax.v, in0=ax.v, in1=xs.v, op=ALU.max)
            mk.I("act", "activation", out=ee.v, in_=ax.v, func=AF.Exp, scale=-1.0)
            mk.I("act", "activation", out=ee.v, in_=ee.v, func=AF.Ln, bias=C.oneb[:, 0:1], scale=1.0)
            mk.I("dve", "tensor_scalar", out=rr_.v, in0=xs.v, scalar1=0.0, scalar2=None, op0=ALU.max)
            mk.I("dve", "tensor_tensor", out=ee.v, in0=ee.v, in1=rr_.v, op=ALU.add)
            mk.I("dve", "tensor_tensor", out=G.Gt[:, c, :], in0=ee.v, in1=nA.v, op=ALU.mult)
            mk.I("act", "activation", out=G.Bt[:, c, :], in_=b[:, 2 * NH:4 * NH], func=AF.Sigmoid)
            mk.I("dve", "tensor_scalar", out=G.NBt[:, c, :], in0=G.Bt[:, c, :], scalar1=-1.0, scalar2=None, op0=ALU.mult)
    for step in range(nch if GSTOP['v'] >= 2 else 0):
        for h in range(NH):
            for d in range(2):
                gdn_unit(mk, C, G, h, d, step if d == 0 else nch - 1 - step)
    for h in range(NH):
        if state_out is not None:
            for d in range(2):
                mk.dma("sp", state_out[d, h], G.S[h][d].v)
        if win is not None:
            sel, dst = win
            for g in range(2):
                for hf in range(2):
                    ow = C.ct[0]
                    zw = C.ct[1]
                    for jj in range(4):
                        c0 = 1024 * jj + 512 * g + 257 * hf
                        if jj == 0:
                            mk.I("dve", "tensor_scalar", out=ow[:, 0:257], in0=G.OT[h][:, c0:c0 + 257], scalar1=sel[:, 0:1], scalar2=None, op0=ALU.mult)
                            mk.I("dve", "tensor_scalar", out=zw[:, 0:257], in0=G.ZT[h][:, c0:c0 + 257], scalar1=sel[:, 0:1], scalar2=None, op0=ALU.mult)
                        else:
                            mk.I("dve", "scalar_tensor_tensor", out=ow[:, 0:257], in0=G.OT[h][:, c0:c0 + 257], scalar=sel[:, jj:jj + 1],
                                 in1=ow[:, 0:257], op0=ALU.mult, op1=ALU.add)
                            mk.I("dve", "scalar_tensor_tensor", out=zw[:, 0:257], in0=G.ZT[h][:, c0:c0 + 257], scalar=sel[:, jj:jj + 1],
                                 in1=zw[:, 0:257], op0=ALU.mult, op1=ALU.add)
                    bcast_sum_rstd(mk, C, [(ow[:, 0:257], 128)], 257, 128.0, C.rstd)
                    mk.I("dve", "scalar_tensor_tensor", out=ow[:, 0:257], in0=ow[:, 0:257], scalar=gng[:, 0:1],
                         in1=C.rstd[:, 0:257], op0=ALU.mult, op1=ALU.mult)
                    mk.I("dve", "tensor_tensor", out=ow[:, 0:257], in0=ow[:, 0:257], in1=zw[:, 0:257], op=ALU.mult)
                    mk.dma("sp", dst[g, :, 257 * hf:257 * hf + 257], ow[:, 0:257])
            continue
        for it in range(ntile):
            t0 = it * TTs
            bcast_sum_rstd(mk, C, [(G.OT[h][:, t0:t0 + TTs], 128)], TTs, 128.0, C.rstd)
            o = C.ct[C.ct_rr % 2]
            C.ct_rr += 1
            mk.I("dve", "scalar_tensor_tensor", out=o[:, 0:TTs], in0=G.OT[h][:, t0:t0 + TTs], scalar=gng[:, 0:1],
                 in1=C.rstd[:, 0:TTs], op0=ALU.mult, op1=ALU.mult)
            mk.I("dve", "tensor_tensor", out=o[:, 0:TTs], in0=o[:, 0:TTs], in1=G.ZT[h][:, t0:t0 + TTs], op=ALU.mult)
            mk.dma("sp", mix_out[h, :, t0:t0 + TTs], o[:, 0:TTs])
    mk.release(mark)


def mla_phase(mk, C, x_rows, T, NH, ci, gm1, Wd, nctx, rope, mix_out, ckv_out, kpe_out):
    mark0 = mk.mark()
    TTs = min(256, T)
    ntile = T // TTs
    NK = T + nctx
    nkt = NK // 128
    scale = 192.0 ** -0.5
    gq = mk.alloc([4], F32, "gq")
    gkv = mk.alloc([4], F32, "gkv")
    wq = mk.alloc([4, NH * 192], BF16, "wq")
    wkv = mk.alloc([4, NH * 256], BF16, "wkv")
    rot = mk.alloc([64], F32, "rot")
    onesb = mk.alloc([128], BF16, "onesb")
    kmax2 = mk.alloc([NH], F32, "kmax2")
    KPET = mk.alloc([NK], BF16, "KPET")
    KN = [mk.alloc([NK], BF16, f"KN{h}") for h in range(NH)]
    V = [mk.alloc([nkt, 128], BF16, f"V{h}") for h in range(NH)]
    mk.dma("sp", gq.v, Wd["gq"])
    mk.dma("sp", gkv.v, Wd["gkv"])
    mk.dma("pool", wq.v, Wd["wq"].rearrange("(kc p) n -> p kc n", p=128))
    mk.dma("pool", wkv.v, Wd["wkv"].rearrange("(kc p) n -> p kc n", p=128))
    mk.dma("sp", rot[0:64, :], Wd["rot"])
    mk.I("dve", "memset", onesb.v.ap, 1.0, writes=[onesb])
    mark1 = mk.mark()
    CKVT = mk.alloc([4, NK], BF16, "CKVT")
    mark2 = mk.mark()

    def work_tiles():
        W_ = Ctx()
        W_.hT = mk.alloc([16, TTs], BF16, "hTm")
        W_.xT = mk.alloc([16, TTs], F32, "xTm")
        W_.raw = mk.alloc([4, TTs], F32, "rawm")
        W_.cs = [mk.alloc([TTs], F32, f"cs{i}") for i in range(2)]
        W_.rp = [mk.alloc([TTs], F32, f"rp{i}") for i in range(3)]
        return W_

    def proj_norm(W_, wcol0, gvec, dst_fn, f32_out=None):
        wt = C.wt[C.wt_rr % len(C.wt)]
        C.wt_rr += 1
        load_weight_tile(mk, wt, Wd["w_m"][:, wcol0:wcol0 + 512], 16, 512)
        for q in range(4):
            b = mk.bank()
            for kc in range(16):
                mk.I("pe", "matmul", out=b[:, 0:TTs], lhsT=wt[:, kc, q * 128:(q + 1) * 128], rhs=W_.hT[:, kc, 0:TTs],
                     start=(kc == 0), stop=(kc == 15))
            mk.I("act", "activation", out=W_.raw[:, q, :], in_=b[:, 0:TTs], func=AF.Copy)
        bcast_sum_rstd(mk, C, [(W_.raw[:, q, :], 128) for q in range(4)], TTs, 512.0, C.rstd)
        for q in range(4):
            if f32_out is not None:
                mk.I("dve", "scalar_tensor_tensor", out=W_.raw[:, q, :], in0=W_.raw[:, q, :], scalar=gvec[:, q:q + 1],
                     in1=C.rstd[:, 0:TTs], op0=ALU.mult, op1=ALU.mult)
                mk.I("act", "activation", out=dst_fn(q), in_=W_.raw[:, q, :], func=AF.Copy)
            else:
                mk.I("dve", "scalar_tensor_tensor", out=dst_fn(q), in0=W_.raw[:, q, :], scalar=gvec[:, q:q + 1],
                     in1=C.rstd[:, 0:TTs], op0=ALU.mult, op1=ALU.mult)

    def do_rope(W_, src_bank_view, n, t0, dst):
        x = W_.rp[0]
        mk.I("act", "activation", out=x[0:64, 0:n], in_=src_bank_view, func=AF.Copy)
        if not rope:
            mk.I("dve", "tensor_copy", out=dst, in_=x[0:64, 0:n])
            return
        mk.dma("sp", W_.cs[0][0:64, 0:n], Wd["cos"][:, t0:t0 + n])
        mk.dma("sp", W_.cs[1][0:64, 0:n], Wd["sin"][:, t0:t0 + n])
        b = mk.bank()
        mk.I("pe", "matmul", out=b[0:64, 0:n], lhsT=rot[0:64, :], rhs=x[0:64, 0:n], start=True, stop=True)
        mk.I("dve", "tensor_tensor", out=W_.rp[1][0:64, 0:n], in0=x[0:64, 0:n], in1=W_.cs[0][0:64, 0:n], op=ALU.mult)
        mk.I("dve", "tensor_tensor", out=W_.rp[2][0:64, 0:n], in0=b[0:64, 0:n], in1=W_.cs[1][0:64, 0:n], op=ALU.mult)
        mk.I("dve", "tensor_tensor", out=dst, in0=W_.rp[1][0:64, 0:n], in1=W_.rp[2][0:64, 0:n], op=ALU.add)

    W_ = work_tiles()
    for it in range(ntile):
        t0 = it * TTs
        norm_mod_tile(mk, C, x_rows[t0:t0 + TTs, :], TTs, W_.xT, W_.hT, C.rstd, gm1, ci)
        proj_norm(W_, 512, gkv, lambda q: CKVT[:, q, t0:t0 + TTs], f32_out=(ckv_out is not None) or True)
        if ckv_out is not None:
            for s in range(TTs // 128):
                ys = C.xstage[s % 2]
                b = mk.bank()
                for q in range(4):
                    mk.I("pe", "transpose", out=b[:, q * 128:(q + 1) * 128], in_=W_.raw[:, q, s * 128:(s + 1) * 128], identity=C.ident.v)
                mk.I("dve", "tensor_copy", out=ys[:, 0:512], in_=b[:, 0:512])
                mk.dma("sp", ckv_out[t0 + s * 128:t0 + (s + 1) * 128, :], ys[:, 0:512])
        wt = C.wt[C.wt_rr % len(C.wt)]
        C.wt_rr += 1
        load_weight_tile(mk, wt, Wd["w_m"][:, 1024:1088], 16, 64)
        b = mk.bank()
        for kc in range(16):
            mk.I("pe", "matmul", out=b[0:64, 0:TTs], lhsT=wt[:, kc, 0:64], rhs=W_.hT[:, kc, 0:TTs], start=(kc == 0), stop=(kc == 15))
        do_rope(W_, b[0:64, 0:TTs], TTs, t0, KPET[0:64, t0:t0 + TTs])
        if kpe_out is not None:
            for s in range(TTs // 128):
                ys = C.xstage[s % 2]
                b2 = mk.bank()
                mk.I("pe", "transpose", out=b2[:, 0:64], in_=W_.rp[0][0:64, s * 128:(s + 1) * 128], identity=C.ident[0:64, 0:64])
                mk.I("dve", "tensor_copy", out=ys[:, 0:64], in_=b2[:, 0:64])
                mk.dma("sp", kpe_out[t0 + s * 128:t0 + (s + 1) * 128, :], ys[:, 0:64])
    for s in range(nctx // 128):
        xs = C.xstage[s % 2]
        mk.dma("sp", xs[:, 0:512], Wd["cache_ckv"][s * 128:(s + 1) * 128, :])
        mk.dma("sp", xs[:, 512:576], Wd["cache_kpe"][s * 128:(s + 1) * 128, :])
        b = mk.bank()
        for q in range(4):
            mk.I("pe", "transpose", out=b[:, q * 128:(q + 1) * 128], in_=xs[:, q * 128:(q + 1) * 128], identity=C.ident.v)
        mk.I("dve", "tensor_copy", out=CKVT[:, :, T + s * 128:T + (s + 1) * 128], in_=b[:, 0:512].rearrange("p (q t) -> p q t", q=4))
        b2 = mk.bank()
        mk.I("pe", "transpose", out=b2[0:64, 0:128], in_=xs[:, 512:576], identity=C.ident.v)
        mk.I("act", "activation", out=KPET[0:64, T + s * 128:T + (s + 1) * 128], in_=b2[0:64, 0:128], func=AF.Copy)
    mk.release(mark2)
    sqk = mk.alloc([512], F32, "sqk")
    kss = mk.alloc([512], F32, "kss")
    for h in range(NH):
        for k0 in range(0, NK, 512):
            n = min(512, NK - k0)
            b = mk.bank()
            for kc in range(4):
                mk.I("pe", "matmul", out=b[:, 0:n], lhsT=wkv[:, kc, h * 256:h * 256 + 128], rhs=CKVT[:, kc, k0:k0 + n],
                     start=(kc == 0), stop=(kc == 3))
            mk.I("act", "activation", out=KN[h][:, k0:k0 + n], in_=b[:, 0:n], func=AF.Copy)
            bs = mk.bank()
            mk.I("act", "activation", out=sqk[:, 0:n], in_=b[:, 0:n], func=AF.Square)
            mk.I("pe", "matmul", out=bs[0:1, 0:n], lhsT=C.ones[:, 0:1], rhs=sqk[:, 0:n], start=True, stop=False)
            mk.I("act", "activation", out=kss[0:64, 0:n], in_=KPET[0:64, k0:k0 + n], func=AF.Square)
            mk.I("pe", "matmul", out=bs[0:1, 0:n], lhsT=C.ones[0:64, 0:1], rhs=kss[0:64, 0:n], start=False, stop=True)
            if k0 == 0:
                mk.I("dve", "tensor_reduce", out=kmax2[0:1, h:h + 1], in_=bs[0:1, 0:n], axis=AX.X, op=ALU.max)
            else:
                mk.I("dve", "tensor_reduce", out=kss[0:1, 0:1], in_=bs[0:1, 0:n], axis=AX.X, op=ALU.max)
                mk.I("dve", "tensor_tensor", out=kmax2[0:1, h:h + 1], in0=kmax2[0:1, h:h + 1], in1=kss[0:1, 0:1], op=ALU.max)
        for kt in range(nkt):
            b = mk.bank()
            for kc in range(4):
                mk.I("pe", "matmul", out=b[:, 0:128], lhsT=CKVT[:, kc, kt * 128:(kt + 1) * 128], rhs=wkv[:, kc, h * 256 + 128:h * 256 + 256],
                     start=(kc == 0), stop=(kc == 3))
            mk.I("dve", "tensor_copy", out=V[h][:, kt, :], in_=b[:, 0:128])
    mk.release(mark1)
    W_ = work_tiles()
    QN = mk.alloc([4, TTs], BF16, "QN")
    qn = mk.alloc([TTs], BF16, "qn")
    qr = mk.alloc([TTs], BF16, "qr")
    negm = mk.alloc([TTs], BF16, "negm")
    mrow = mk.alloc([TTs], F32, "mrow")
    PTb = [mk.alloc([TTs], BF16, f"PT{i}") for i in range(3)]
    rs = mk.alloc([TTs], F32, "rs")
    oo = mk.alloc([TTs], F32, "oo")
    for it in range(ntile):
        t0 = it * TTs
        norm_mod_tile(mk, C, x_rows[t0:t0 + TTs, :], TTs, W_.xT, W_.hT, C.rstd, gm1, ci)
        proj_norm(W_, 0, gq, lambda q: QN[:, q, :])
        for h in range(NH):
            b = mk.bank()
            for kc in range(4):
                mk.I("pe", "matmul", out=b[:, 0:TTs], lhsT=wq[:, kc, h * 192:h * 192 + 128], rhs=QN[:, kc, :], start=(kc == 0), stop=(kc == 3))
            mk.I("act", "activation", out=qn.v, in_=b[:, 0:TTs], func=AF.Copy)
            mk.I("act", "activation", out=C.sq[0][:, 0:TTs], in_=b[:, 0:TTs], func=AF.Square)
            b2 = mk.bank()
            for kc in range(4):
                mk.I("pe", "matmul", out=b2[0:64, 0:TTs], lhsT=wq[:, kc, h * 192 + 128:h * 192 + 192], rhs=QN[:, kc, :], start=(kc == 0), stop=(kc == 3))
            mk.I("act", "activation", out=C.sq[1][0:64, 0:TTs], in_=b2[0:64, 0:TTs], func=AF.Square)
            do_rope(W_, b2[0:64, 0:TTs], TTs, t0, qr[0:64, :])
            bm = mk.bank()
            mk.I("pe", "matmul", out=bm[0:1, 0:TTs], lhsT=C.ones[:, 0:1], rhs=C.sq[0][:, 0:TTs], start=True, stop=False)
            mk.I("pe", "matmul", out=bm[0:1, 0:TTs], lhsT=C.ones[0:64, 0:1], rhs=C.sq[1][0:64, 0:TTs], start=False, stop=True)
            mk.I("act", "activation", out=mrow[0:1, :], in_=bm[0:1, 0:TTs], func=AF.Sqrt, scale=kmax2[0:1, h:h + 1])
            mk.I("dve", "tensor_scalar", out=negm[0:1, :], in0=mrow[0:1, :], scalar1=-1.0, scalar2=None, op0=ALU.mult)
            bo = mk.reserve()
            bsum = mk.reserve()
            for kt in range(nkt):
                ks = slice(kt * 128, (kt + 1) * 128)
                bs = mk.bank()
                mk.I("pe", "matmul", out=bs[:, 0:TTs], lhsT=KN[h][:, ks], rhs=qn.v, start=True, stop=False)
                mk.I("pe", "matmul", out=bs[:, 0:TTs], lhsT=KPET[0:64, ks], rhs=qr[0:64, :], start=False, stop=False)
                mk.I("pe", "matmul", out=bs[:, 0:TTs], lhsT=onesb[0:1, :], rhs=negm[0:1, :], start=False, stop=True)
                PT = PTb[kt % 3]
                mk.I("act", "activation", out=PT.v, in_=bs[:, 0:TTs], func=AF.Exp, scale=scale)
                mk.I("pe", "matmul", out=bo[:, 0:TTs], lhsT=V[h][:, kt, :], rhs=PT.v, start=(kt == 0), stop=(kt == nkt - 1))
                mk.I("pe", "matmul", out=bsum[:, 0:TTs], lhsT=onesb.v, rhs=PT.v, start=(kt == 0), stop=(kt == nkt - 1))
            mk.I("dve", "reciprocal", out=rs.v, in_=bsum[:, 0:TTs])
            mk.I("dve", "tensor_tensor", out=oo.v, in0=bo[:, 0:TTs], in1=rs.v, op=ALU.mult)
            mk.dma("sp", mix_out[h, :, t0:t0 + TTs], oo.v)
            mk.unreserve(bo)
            mk.unreserve(bsum)
    mk.release(mark0)


def alloc_common(mk, C, D):
    C.ident = mk.alloc([128], F32, "ident")
    C.identb = mk.alloc([128], BF16, "identb")
    C.ones = mk.alloc([128], F32, "ones")
    C.epsb = mk.alloc([1], F32, "epsb")
    C.oneb = mk.alloc([1], F32, "oneb")
    C.tri = [mk.alloc([128], F32, f"tri{d}") for d in range(2)]
    C.ms = [mk.alloc([128], F32, f"ms{d}") for d in range(2)]
    C.scond = mk.alloc([16, 2], BF16, "scond")
    C.condf = mk.alloc([16, 2], F32, "condf")
    C.b_ada = mk.alloc([96], F32, "b_ada")
    C.mod = mk.alloc([96, 2], F32, "mod")
    C.xstage = [mk.alloc([2048], F32, f"xs{i}") for i in range(2)]
    C.sq = [mk.alloc([514], F32, f"sq{i}") for i in range(2)]
    C.tmpn = mk.alloc([514], F32, "tmpn")
    C.rstd = mk.alloc([514], F32, "rstd")
    C.wt = [mk.alloc([16, 512], BF16, f"wt{i}") for i in range(2)]
    C.wt_rr = 0
    C.tmpx = [mk.alloc([514], F32, f"tmpx{i}") for i in range(2)]
    C.ct = [mk.alloc([512], F32, f"ct{i}") for i in range(2)]
    C.ct_rr = 0
    C.R = [mk.alloc([516], F32, f"R{i}") for i in range(2)]
    C.R_rr = 0
    mk.dma("sp", C.ident.v, D["ident"])
    mk.dma("sp", C.tri[0].v, D["tri0"])
    mk.dma("sp", C.tri[1].v, D["tri1"])
    mk.dma("sp", C.ms[0].v, D["ms0"])
    mk.dma("sp", C.ms[1].v, D["ms1"])
    mk.I("dve", "tensor_copy", out=C.identb.v, in_=C.ident.v)
    mk.I("dve", "memset", C.ones.v.ap, 1.0, writes=[C.ones])
    mk.I("dve", "memset", C.epsb.v.ap, EPS, writes=[C.epsb])
    mk.I("dve", "memset", C.oneb.v.ap, 1.0, writes=[C.oneb])
    mk.dma("sp", C.condf.v, D["condT"])
    mk.dma("sp", C.b_ada.v, D["b_adaT"])
    mk.I("act", "activation", out=C.scond.v, in_=C.condf.v, func=AF.Silu)


def build_phase1(parts=('pg', 'pm', 'sg', 'sm')):
    nc = bass.Bass("TRN2", target_bir_lowering=False)
    DEBUG["nc"] = nc
    DEBUG["done"] = set()
    D = {}

    def din(name, shape):
        D[name] = nc.dram_tensor(name, list(shape), F32, kind="ExternalInput").ap()

    def dout(name, shape):
        D[name] = nc.dram_tensor(name, list(shape), F32, kind="ExternalOutput").ap()

    for name, shape in (("xp", [2, 256, 2048]), ("xs", [4096, 2048]), ("condT", [128, 16, 2]), ("w_ada", [2048, 12288]),
                        ("b_adaT", [128, 96]), ("norm1T", [128, 16]), ("ident", [128, 128]), ("tri0", [128, 128]),
                        ("tri1", [128, 128]), ("ms0", [128, 128]), ("ms1", [128, 128]), ("s0p", [2, 8, 128, 128]),
                        ("s0s", [2, 2, 1, 128, 128]), ("w_g_p", [2048, 4096]), ("w_ab_p", [2048, 32]), ("cw_p", [128, 24, 3]),
                        ("dtb_p", [128, 16]), ("alog_p", [128, 16]), ("gng", [128, 1]), ("w_g_s", [2, 2048, 512]),
                        ("w_ab_s", [2, 2048, 4]), ("cw_s", [2, 128, 3, 3]), ("dtb_s", [2, 128, 2]), ("alog_s", [2, 128, 2]),
                        ("w_m", [2048, 1088]), ("gq", [128, 4]), ("gkv", [128, 4]), ("wq_p", [512, 1536]), ("wkv_p", [512, 2048]),
                        ("wq_s", [512, 384]), ("wkv_s", [512, 512]), ("cos", [64, 4096]), ("sin", [64, 4096]), ("rot", [64, 64]),
                        ("cache_ckv", [256, 512]), ("cache_kpe", [256, 64])):
        din(name, shape)
    for name, shape in (("mixp", [2, 16, 128, 256]), ("mixs", [4, 128, 4096]), ("new_state", [2, 2, 8, 128, 128]),
                        ("new_ckv", [2, 256, 512]), ("new_kpe", [2, 256, 64])):
        dout(name, shape)
    with ExitStack() as st:
        mk = MK(nc, st)
        C = Ctx()
        alloc_common(mk, C, D)
        n1g = mk.alloc([16], F32, "n1g")
        gm1 = mk.alloc([16, 2], F32, "gm1")
        mk.dma("sp", n1g.v, D["norm1T"])
        adaln(mk, C, D["w_ada"], [0, 1], 2)
        mk.I("dve", "tensor_scalar", out=gm1.v, in0=C.mod[:, 16:32, :], scalar1=1.0, scalar2=None, op0=ALU.add)
        mk.I("dve", "tensor_tensor", out=gm1.v, in0=gm1.v,
             in1=n1g.v.rearrange("p (c o) -> p c o", o=1).bc([128, 16, 2]), op=ALU.mult)
        Wp = dict(w_g=D["w_g_p"], w_ab=D["w_ab_p"], cw=D["cw_p"], dtb=D["dtb_p"], alog=D["alog_p"], gng=D["gng"])
        Wmp = dict(w_m=D["w_m"], gq=D["gq"], gkv=D["gkv"], wq=D["wq_p"], wkv=D["wkv_p"], rot=D["rot"])
        for s in range(2):
            if 'pg' in parts:
                gdn_phase(mk, C, D["xp"][s], 256, 8, 0, gm1, Wp, D["s0p"], D["mixp"][s, 0:8], D["new_state"][s])
            if 'pm' in parts:
                mla_phase(mk, C, D["xp"][s], 256, 8, 0, gm1, Wmp, 0, False, D["mixp"][s, 8:16], D["new_ckv"][s], D["new_kpe"][s])
        for lh in range(2):
            Ws = dict(w_g=D["w_g_s"][lh], w_ab=D["w_ab_s"][lh], cw=D["cw_s"][lh], dtb=D["dtb_s"][lh], alog=D["alog_s"][lh], gng=D["gng"])
            if 'sg' in parts:
                gdn_phase(mk, C, D["xs"], 4096, 1, 1, gm1, Ws, D["s0s"][lh], D["mixs"][lh:lh + 1], None)
        Wms = dict(w_m=D["w_m"], gq=D["gq"], gkv=D["gkv"], wq=D["wq_s"], wkv=D["wkv_s"], rot=D["rot"], cos=D["cos"], sin=D["sin"],
                   cache_ckv=D["cache_ckv"], cache_kpe=D["cache_kpe"])
        if 'sm' in parts:
            mla_phase(mk, C, D["xs"], 4096, 2, 1, gm1, Wms, 256, True, D["mixs"][2:4], None, None)
        mk.finalize()
        print("phase1 instructions:", mk.n_inst, {e: len(mk.ops[e]) for e in ENGS})
    return nc


def rope_tables():
    rows = 4096 // 64
    row = np.repeat(np.arange(rows, dtype=np.float32), 64)
    col = np.tile(np.arange(64, dtype=np.float32), rows)
    inv = (np.float32(10000.0) ** (-np.arange(16, dtype=np.float32) / np.float32(16))).astype(np.float32)
    ang = np.concatenate([row[:, None] * inv, col[:, None] * inv], axis=-1).astype(np.float32)
    cos, sin = np.cos(ang).astype(np.float32), np.sin(ang).astype(np.float32)
    cos2 = np.ascontiguousarray(np.concatenate([cos, cos], axis=1).T)
    sin2 = np.ascontiguousarray(np.concatenate([sin, sin], axis=1).T)
    rot = np.zeros((64, 64), np.float32)
    for m in range(32):
        rot[m + 32, m] = -1.0
        rot[m, m + 32] = 1.0
    return cos2, sin2, rot


def phase1_inputs(core, I):
    b, j = core // 4, core % 4
    w_in = I["w_in"][0]
    cw = I["gdn_conv_w"][0]
    dtb = I["gdn_dt_bias"][0]
    alog = I["gdn_a_log"][0]

    def wg(h):
        return np.concatenate([w_in[:, c0 + h * 128:c0 + (h + 1) * 128] for c0 in (0, 1024, 2048, 3072)], axis=1)

    def cwh(h):
        return np.stack([cw[:, comp * 1024 + h * 128:comp * 1024 + (h + 1) * 128].T for comp in range(3)], axis=1)

    cos2, sin2, rot = rope_tables()
    tri0 = np.triu(np.ones((128, 128), np.float32))
    tri1 = np.tril(np.ones((128, 128), np.float32))
    ms0 = np.tril(np.ones((128, 128), np.float32), -1)
    ms1 = np.triu(np.ones((128, 128), np.float32), 1)
    cond = np.stack([I["c_ctx"], I["c"][b]], axis=0)
    hg = [2 * j, 2 * j + 1]
    wq = I["mla_w_q_b"][0]
    wkv = I["mla_w_kv_b"][0]
    d = dict(
        xp=np.ascontiguousarray(I["x_prompt"][2 * core:2 * core + 2]), xs=np.ascontiguousarray(I["x_sample"][b]),
        condT=np.ascontiguousarray(cond.reshape(2, 16, 128).transpose(2, 1, 0)), w_ada=I["w_ada"][0],
        b_adaT=colT(I["b_ada"][0], 96), norm1T=colT(I["norm1_g"][0], 16), ident=np.eye(128, dtype=np.float32),
        tri0=tri0, tri1=tri1, ms0=ms0, ms1=ms1, s0p=np.zeros((2, 8, 128, 128), np.float32),
        s0s=np.ascontiguousarray(np.stack([I["state_gdn"][b, 0, :, h:h + 1] for h in hg], axis=0)),
        w_g_p=np.concatenate([wg(h) for h in range(8)], axis=1), w_ab_p=np.ascontiguousarray(w_in[:, 4096:4128]),
        cw_p=np.ascontiguousarray(np.concatenate([cwh(h) for h in range(8)], axis=1)),
        dtb_p=np.ascontiguousarray(np.broadcast_to(dtb.reshape(1, 16), (128, 16))),
        alog_p=np.ascontiguousarray(np.broadcast_to(alog.reshape(1, 16), (128, 16))),
        gng=np.ascontiguousarray(I["gdn_norm_g"][0].reshape(128, 1)),
        w_g_s=np.stack([wg(h) for h in hg], axis=0),
        w_ab_s=np.stack([w_in[:, [4096 + h, 4104 + h, 4112 + h, 4120 + h]] for h in hg], axis=0),
        cw_s=np.stack([cwh(h) for h in hg], axis=0),
        dtb_s=np.stack([np.broadcast_to(dtb[:, h].reshape(1, 2), (128, 2)) for h in hg], axis=0),
        alog_s=np.stack([np.broadcast_to(alog[:, h].reshape(1, 2), (128, 2)) for h in hg], axis=0),
        w_m=np.ascontiguousarray(w_in[:, 4128:5216]), gq=colT(I["mla_q_norm_g"][0], 4), gkv=colT(I["mla_kv_norm_g"][0], 4),
        wq_p=wq, wkv_p=wkv, wq_s=np.ascontiguousarray(wq[:, hg[0] * 192:(hg[1] + 1) * 192]),
        wkv_s=np.ascontiguousarray(wkv[:, hg[0] * 256:(hg[1] + 1) * 256]), cos=cos2, sin=sin2, rot=rot,
        cache_ckv=np.ascontiguousarray(I["cache_mla_ckv"][b, 0]), cache_kpe=np.ascontiguousarray(I["cache_mla_kpe"][b, 0]))
    return {k: np.ascontiguousarray(np.asarray(v, np.float32)) for k, v in d.items()}


_NC = {}


def kernel_twolaunch(**I):
    I = {k: np.asarray(v) for k, v in I.items()}
    if "p1" not in _NC:
        _NC["p1"] = build_phase1()
        _NC["p2"] = build_phase2()
    r1 = run_bass_kernel_spmd(_NC["p1"], [phase1_inputs(c, I) for c in range(NCORES)], core_ids=list(range(NCORES))).results
    mix_p = np.zeros((16, 256, 2048), np.float32)
    mix_s = np.zeros((2, 4096, 2048), np.float32)
    new_state = np.zeros((16, 1, 2, 8, 128, 128), np.float32)
    new_ckv = np.zeros((16, 1, 256, 512), np.float32)
    new_kpe = np.zeros((16, 1, 256, 64), np.float32)
    for c in range(NCORES):
        b, j = c // 4, c % 4
        r = r1[c]
        for s in range(2):
            mix_p[2 * c + s] = r["mixp"][s].transpose(2, 0, 1).reshape(256, 2048)
            new_state[2 * c + s, 0] = r["new_state"][s]
            new_ckv[2 * c + s, 0] = r["new_ckv"][s]
            new_kpe[2 * c + s, 0] = r["new_kpe"][s]
        for lh in range(2):
            hgl = 2 * j + lh
            mix_s[b, :, hgl * 128:(hgl + 1) * 128] = r["mixs"][lh].T
            mix_s[b, :, 1024 + hgl * 128:1024 + (hgl + 1) * 128] = r["mixs"][2 + lh].T
    r2 = run_bass_kernel_spmd(_NC["p2"], [phase2_inputs(c, I["x_prompt"], I["x_sample"], mix_p, mix_s, I["c"], I["c_ctx"],
                                                        I["w_ada"][0], I["b_ada"][0], I["w_out"][0], I["norm2_g"][0],
                                                        I["w_up"][0], I["ffn_conv_w"][0], I["ffn_conv_b"][0], I["w_down"][0],
                                                        I["final_norm_g"]) for c in range(NCORES)],
                              core_ids=list(range(NCORES))).results
    yp = np.zeros((16, 256, 2048), np.float32)
    ys = np.zeros((2, 4096, 2048), np.float32)
    for c in range(NCORES):
        b, j = c // 4, c % 4
        y = r2[c]["y"]
        yp[2 * c:2 * c + 2] = y[0].reshape(2, 256, 2048)
        ys[b, 1024 * j:1024 * j + 512] = y[1]
        ys[b, 1024 * j + 512:1024 * j + 1024] = y[2]
    return (yp, ys, new_state, new_ckv, new_kpe)


def mla_fused(mk, C, x_rows, T, ci, gm1, Wd, nctx, xq, dst):
    NH = 8
    mark0 = mk.mark()
    TTs = 256
    TQ = 257
    ntile = T // TTs
    NK = T + nctx
    nkt = NK // 128
    scale = 192.0 ** -0.5
    gq = mk.alloc([4], F32, "gq")
    gkv = mk.alloc([4], F32, "gkv")
    wq = mk.alloc([4, NH * 192], BF16, "wq")
    wkv = mk.alloc([4, NH * 256], BF16, "wkv")
    rot = mk.alloc([64], F32, "rot")
    onesb = mk.alloc([128], BF16, "onesb")
    kmax2 = mk.alloc([NH], F32, "kmax2")
    KPET = mk.alloc([NK], BF16, "KPET")
    QN = mk.alloc([4, 4 * TQ], BF16, "QNall")
    mk.dma("sp", gq.v, Wd["gq"])
    mk.dma("sp", gkv.v, Wd["gkv"])
    mk.dma("pool", wq.v, Wd["wq"].rearrange("(kc p) n -> p kc n", p=128))
    mk.dma("pool", wkv.v, Wd["wkv"].rearrange("(kc p) n -> p kc n", p=128))
    mk.dma("sp", rot[0:64, :], Wd["rot"])
    mk.I("dve", "memset", onesb.v.ap, 1.0, writes=[onesb])
    CKVT = mk.alloc([4, NK], BF16, "CKVT")
    mark2 = mk.mark()
    hT = mk.alloc([16, TQ], BF16, "hTm")
    xT = mk.alloc([16, TQ], F32, "xTm")
    raw = mk.alloc([4, TQ], F32, "rawm")
    cs = [mk.alloc([TQ], F32, f"cs{i}") for i in range(2)]
    rp = [mk.alloc([TQ], F32, f"rp{i}") for i in range(3)]

    def proj_norm(W, wcol0, gvec, dst_fn):
        wt = C.wt[C.wt_rr % len(C.wt)]
        C.wt_rr += 1
        load_weight_tile(mk, wt, Wd["w_m"][:, wcol0:wcol0 + 512], 16, 512)
        for q in range(4):
            b = mk.bank()
            for kc in range(16):
                mk.I("pe", "matmul", out=b[:, 0:W], lhsT=wt[:, kc, q * 128:(q + 1) * 128], rhs=hT[:, kc, 0:W], start=(kc == 0), stop=(kc == 15))
            mk.I("act", "activation", out=raw[:, q, 0:W], in_=b[:, 0:W], func=AF.Copy)
        bcast_sum_rstd(mk, C, [(raw[:, q, 0:W], 128) for q in range(4)], W, 512.0, C.rstd)
        for q in range(4):
            mk.I("dve", "scalar_tensor_tensor", out=dst_fn(q), in0=raw[:, q, 0:W], scalar=gvec[:, q:q + 1], in1=C.rstd[:, 0:W], op0=ALU.mult, op1=ALU.mult)

    def do_rope(src, n, cos_ap, sin_ap, dstv):
        x = rp[0]
        mk.I("act", "activation", out=x[0:64, 0:n], in_=src, func=AF.Copy)
        mk.dma("sp", cs[0][0:64, 0:n], cos_ap)
        mk.dma("sp", cs[1][0:64, 0:n], sin_ap)
        b = mk.bank()
        mk.I("pe", "matmul", out=b[0:64, 0:n], lhsT=rot[0:64, :], rhs=x[0:64, 0:n], start=True, stop=True)
        mk.I("dve", "tensor_tensor", out=rp[1][0:64, 0:n], in0=x[0:64, 0:n], in1=cs[0][0:64, 0:n], op=ALU.mult)
        mk.I("dve", "tensor_tensor", out=rp[2][0:64, 0:n], in0=b[0:64, 0:n], in1=cs[1][0:64, 0:n], op=ALU.mult)
        mk.I("dve", "tensor_tensor", out=dstv, in0=rp[1][0:64, 0:n], in1=rp[2][0:64, 0:n], op=ALU.add)

    for it in range(ntile):
        t0 = it * TTs
        norm_mod_tile(mk, C, x_rows[t0:t0 + TTs, :], TTs, xT, hT, C.rstd, gm1, ci)
        proj_norm(TTs, 512, gkv, lambda q: CKVT[:, q, t0:t0 + TTs])
        wt = C.wt[C.wt_rr % len(C.wt)]
        C.wt_rr += 1
        load_weight_tile(mk, wt, Wd["w_m"][:, 1024:1088], 16, 64)
        b = mk.bank()
        for kc in range(16):
            mk.I("pe", "matmul", out=b[0:64, 0:TTs], lhsT=wt[:, kc, 0:64], rhs=hT[:, kc, 0:TTs], start=(kc == 0), stop=(kc == 15))
        do_rope(b[0:64, 0:TTs], TTs, Wd["cos"][:, t0:t0 + TTs], Wd["sin"][:, t0:t0 + TTs], KPET[0:64, t0:t0 + TTs])
    for s in range(nctx // 128):
        xs = C.xstage[s % 2]
        mk.dma("sp", xs[:, 0:512], Wd["cache_ckv"][s * 128:(s + 1) * 128, :])
        mk.dma("sp", xs[:, 512:576], Wd["cache_kpe"][s * 128:(s + 1) * 128, :])
        b = mk.bank()
        for q in range(4):
            mk.I("pe", "transpose", out=b[:, q * 128:(q + 1) * 128], in_=xs[:, q * 128:(q + 1) * 128], identity=C.ident.v)
        mk.I("dve", "tensor_copy", out=CKVT[:, :, T + s * 128:T + (s + 1) * 128], in_=b[:, 0:512].rearrange("p (q t) -> p q t", q=4))
        b2 = mk.bank()
        mk.I("pe", "transpose", out=b2[0:64, 0:128], in_=xs[:, 512:576], identity=C.ident.v)
        mk.I("act", "activation", out=KPET[0:64, T + s * 128:T + (s + 1) * 128], in_=b2[0:64, 0:128], func=AF.Copy)
    for slot in range(4):
        g, hf = slot // 2, slot % 2
        norm_mod_tile(mk, C, xq[g, hf * TQ:(hf + 1) * TQ, :], TQ, xT, hT, C.rstd, gm1, ci)
        proj_norm(TQ, 0, gq, lambda q: QN[:, q, slot * TQ:(slot + 1) * TQ])
    mk.release(mark2)
    for h in range(NH):
        markh = mk.mark()
        KN = mk.alloc([NK], BF16, "KNh")
        V = mk.alloc([nkt, 128], BF16, "Vh")
        sqk = mk.alloc([512], F32, "sqk")
        kss = mk.alloc([512], F32, "kss")
        qn = mk.alloc([TQ], BF16, "qn")
        qr = mk.alloc([TQ], BF16, "qr")
        negm = mk.alloc([TQ], BF16, "negm")
        mrow = mk.alloc([TQ], F32, "mrow")
        PTb = [mk.alloc([TQ], BF16, f"PT{i}") for i in range(3)]
        rs = mk.alloc([TQ], F32, "rs")
        oo = mk.alloc([TQ], F32, "oo")
        cs = [mk.alloc([TQ], F32, f"csq{i}") for i in range(2)]
        rp = [mk.alloc([TQ], F32, f"rpq{i}") for i in range(3)]
        for k0 in range(0, NK, 512):
            n = min(512, NK - k0)
            b = mk.bank()
            for kc in range(4):
                mk.I("pe", "matmul", out=b[:, 0:n], lhsT=wkv[:, kc, h * 256:h * 256 + 128], rhs=CKVT[:, kc, k0:k0 + n], start=(kc == 0), stop=(kc == 3))
            mk.I("act", "activation", out=KN[:, k0:k0 + n], in_=b[:, 0:n], func=AF.Copy)
            bs = mk.bank()
            mk.I("act", "activation", out=sqk[:, 0:n], in_=b[:, 0:n], func=AF.Square)
            mk.I("pe", "matmul", out=bs[0:1, 0:n], lhsT=C.ones[:, 0:1], rhs=sqk[:, 0:n], start=True, stop=False)
            mk.I("act", "activation", out=kss[0:64, 0:n], in_=KPET[0:64, k0:k0 + n], func=AF.Square)
            mk.I("pe", "matmul", out=bs[0:1, 0:n], lhsT=C.ones[0:64, 0:1], rhs=kss[0:64, 0:n], start=False, stop=True)
            if k0 == 0:
                mk.I("dve", "tensor_reduce", out=kmax2[0:1, h:h + 1], in_=bs[0:1, 0:n], axis=AX.X, op=ALU.max)
            else:
                mk.I("dve", "tensor_reduce", out=kss[0:1, 0:1], in_=bs[0:1, 0:n], axis=AX.X, op=ALU.max)
                mk.I("dve", "tensor_tensor", out=kmax2[0:1, h:h + 1], in0=kmax2[0:1, h:h + 1], in1=kss[0:1, 0:1], op=ALU.max)
        for kt in range(nkt):
            b = mk.bank()
            for kc in range(4):
                mk.I("pe", "matmul", out=b[:, 0:128], lhsT=CKVT[:, kc, kt * 128:(kt + 1) * 128], rhs=wkv[:, kc, h * 256 + 128:h * 256 + 256], start=(kc == 0), stop=(kc == 3))
            mk.I("dve", "tensor_copy", out=V[:, kt, :], in_=b[:, 0:128])
        for slot in range(4):
            g, hf = slot // 2, slot % 2
            qs = slice(slot * TQ, (slot + 1) * TQ)
            b = mk.bank()
            for kc in range(4):
                mk.I("pe", "matmul", out=b[:, 0:TQ], lhsT=wq[:, kc, h * 192:h * 192 + 128], rhs=QN[:, kc, qs], start=(kc == 0), stop=(kc == 3))
            mk.I("act", "activation", out=qn.v, in_=b[:, 0:TQ], func=AF.Copy)
            mk.I("act", "activation", out=C.sq[0][:, 0:TQ], in_=b[:, 0:TQ], func=AF.Square)
            b2 = mk.bank()
            for kc in range(4):
                mk.I("pe", "matmul", out=b2[0:64, 0:TQ], lhsT=wq[:, kc, h * 192 + 128:h * 192 + 192], rhs=QN[:, kc, qs], start=(kc == 0), stop=(kc == 3))
            mk.I("act", "activation", out=C.sq[1][0:64, 0:TQ], in_=b2[0:64, 0:TQ], func=AF.Square)
            do_rope(b2[0:64, 0:TQ], TQ, Wd["cosq"][:, qs], Wd["sinq"][:, qs], qr[0:64, :])
            bm = mk.bank()
            mk.I("pe", "matmul", out=bm[0:1, 0:TQ], lhsT=C.ones[:, 0:1], rhs=C.sq[0][:, 0:TQ], start=True, stop=False)
            mk.I("pe", "matmul", out=bm[0:1, 0:TQ], lhsT=C.ones[0:64, 0:1], rhs=C.sq[1][0:64, 0:TQ], start=False, stop=True)
            mk.I("act", "activation", out=mrow[0:1, :], in_=bm[0:1, 0:TQ], func=AF.Sqrt, scale=kmax2[0:1, h:h + 1])
            mk.I("dve", "tensor_scalar", out=negm[0:1, :], in0=mrow[0:1, :], scalar1=-1.0, scalar2=None, op0=ALU.mult)
            bo = mk.reserve()
            bsum = mk.reserve()
            for kt in range(nkt):
                ks = slice(kt * 128, (kt + 1) * 128)
                bs = mk.bank()
                mk.I("pe", "matmul", out=bs[:, 0:TQ], lhsT=KN[:, ks], rhs=qn.v, start=True, stop=False)
                mk.I("pe", "matmul", out=bs[:, 0:TQ], lhsT=KPET[0:64, ks], rhs=qr[0:64, :], start=False, stop=False)
                mk.I("pe", "matmul", out=bs[:, 0:TQ], lhsT=onesb[0:1, :], rhs=negm[0:1, :], start=False, stop=True)
                PT = PTb[kt % 3]
                mk.I("act", "activation", out=PT.v, in_=bs[:, 0:TQ], func=AF.Exp, scale=scale)
                mk.I("pe", "matmul", out=bo[:, 0:TQ], lhsT=V[:, kt, :], rhs=PT.v, start=(kt == 0), stop=(kt == nkt - 1))
                mk.I("pe", "matmul", out=bsum[:, 0:TQ], lhsT=onesb.v, rhs=PT.v, start=(kt == 0), stop=(kt == nkt - 1))
            mk.I("dve", "reciprocal", out=rs.v, in_=bsum[:, 0:TQ])
            mk.I("dve", "tensor_tensor", out=oo.v, in0=bo[:, 0:TQ], in1=rs.v, op=ALU.mult)
            mk.dma("sp", dst[g, h, :, hf * TQ:(hf + 1) * TQ], oo.v)
            mk.unreserve(bo)
            mk.unreserve(bsum)
        mk.release(markh)
    mk.release(mark0)


def build_fused():
    nc = bass.Bass("TRN2", target_bir_lowering=False)
    DEBUG["nc"] = nc
    DEBUG["done"] = set()
    D = {}

    def din(name, shape):
        D[name] = nc.dram_tensor(name, list(shape), F32, kind="ExternalInput").ap()

    def dout(name, shape):
        D[name] = nc.dram_tensor(name, list(shape), F32, kind="ExternalOutput").ap()

    for name, shape in (("xp", [2, 256, 2048]), ("xs", [4096, 2048]), ("x2", [3, 514, 2048]), ("condT", [128, 16, 2]),
                        ("w_ada", [2048, 12288]), ("b_adaT", [128, 96]), ("norm1T", [128, 16]), ("ident", [128, 128]),
                        ("tri0", [128, 128]), ("tri1", [128, 128]), ("ms0", [128, 128]), ("ms1", [128, 128]),
                        ("s0p", [2, 8, 128, 128]), ("s0h", [8, 2, 1, 128, 128]), ("w_g_p", [2048, 4096]), ("w_ab_p", [2048, 32]),
                        ("cw_p", [128, 24, 3]), ("dtb_p", [128, 16]), ("alog_p", [128, 16]), ("gng", [128, 1]),
                        ("w_ab_h", [8, 2048, 4]), ("dtb_h", [8, 128, 2]), ("alog_h", [8, 128, 2]),
                        ("w_m", [2048, 1088]), ("gq", [128, 4]), ("gkv", [128, 4]), ("wq_p", [512, 1536]), ("wkv_p", [512, 2048]),
                        ("cos", [64, 4096]), ("sin", [64, 4096]), ("cosq", [64, 1028]), ("sinq", [64, 1028]), ("rot", [64, 64]),
                        ("cache_ckv", [256, 512]), ("cache_kpe", [256, 64]), ("sel", [128, 4]), ("hmask", [128, 4]),
                        ("w_out", [2048, 2048]), ("norm2T", [128, 16]), ("w_up", [2048, 11264]), ("fcwT", [128, 88, 3]),
                        ("fcbT", [128, 88]), ("w_down", [5632, 2048]), ("fnormT", [128, 16])):
        din(name, shape)
    for name, shape in (("y", [3, 512, 2048]), ("new_state", [2, 2, 8, 128, 128]), ("new_ckv", [2, 256, 512]), ("new_kpe", [2, 256, 64])):
        dout(name, shape)
    mixp_d = nc.dram_tensor("mixp_d", [2, 16, 128, 256], F32).ap()
    mixs_d = nc.dram_tensor("mixs_d", [2, 16, 128, 514], F32).ap()
    with ExitStack() as st:
        mk = MK(nc, st)
        C = Ctx()
        alloc_common(mk, C, D)
        n1g = mk.alloc([16], F32, "n1g")
        gm1 = mk.alloc([16, 2], F32, "gm1")
        gm2 = mk.alloc([16, 2], F32, "gm2")
        n2g = mk.alloc([16], F32, "n2g")
        fng = mk.alloc([16], F32, "fng")
        fcw = mk.alloc([88, 3], F32, "fcw")
        fcb = mk.alloc([88], F32, "fcb")
        hm = mk.alloc([4], F32, "hm")
        sel = mk.alloc([4], F32, "sel")
        for t, nm in ((n1g, "norm1T"), (n2g, "norm2T"), (fng, "fnormT"), (fcw, "fcwT"), (fcb, "fcbT"), (hm, "hmask"), (sel, "sel")):
            mk.dma("sp", t.v, D[nm])
        adaln(mk, C, D["w_ada"], [0, 1, 2, 3, 4, 5], 2)
        for (gm, ng, lo) in ((gm1, n1g, 16), (gm2, n2g, 64)):
            mk.I("dve", "tensor_scalar", out=gm.v, in0=C.mod[:, lo:lo + 16, :], scalar1=1.0, scalar2=None, op0=ALU.add)
            mk.I("dve", "tensor_tensor", out=gm.v, in0=gm.v, in1=ng.v.rearrange("p (c o) -> p c o", o=1).bc([128, 16, 2]), op=ALU.mult)
        Wp = dict(w_g=D["w_g_p"], w_ab=D["w_ab_p"], cw=D["cw_p"], dtb=D["dtb_p"], alog=D["alog_p"], gng=D["gng"])
        Wmp = dict(w_m=D["w_m"], gq=D["gq"], gkv=D["gkv"], wq=D["wq_p"], wkv=D["wkv_p"], rot=D["rot"])
        for s in range(2):
            gdn_phase(mk, C, D["xp"][s], 256, 8, 0, gm1, Wp, D["s0p"], mixp_d[s, 0:8], D["new_state"][s])
            mla_phase(mk, C, D["xp"][s], 256, 8, 0, gm1, Wmp, 0, False, mixp_d[s, 8:16], D["new_ckv"][s], D["new_kpe"][s])
        for h in range(8):
            Ws = dict(w_g=D["w_g_p"][:, h * 512:(h + 1) * 512], w_ab=D["w_ab_h"][h], cw=D["cw_p"][:, 3 * h:3 * h + 3, :],
                      dtb=D["dtb_h"][h], alog=D["alog_h"][h], gng=D["gng"])
            gdn_phase(mk, C, D["xs"], 4096, 1, 1, gm1, Ws, D["s0h"][h], None, None, win=(sel, mixs_d[:, h]))
        Wms = dict(w_m=D["w_m"], gq=D["gq"], gkv=D["gkv"], wq=D["wq_p"], wkv=D["wkv_p"], rot=D["rot"], cos=D["cos"], sin=D["sin"],
                   cosq=D["cosq"], sinq=D["sinq"], cache_ckv=D["cache_ckv"], cache_kpe=D["cache_kpe"])
        mla_fused(mk, C, D["xs"], 4096, 1, gm1, Wms, 256, D["x2"][1:3], mixs_d[:, 8:16])
        mk.barrier()
        xT = mk.alloc([16, 514], F32, "xT")
        actin = mk.alloc([16, 514], BF16, "actin")
        actT = mk.alloc([44, 512], BF16, "actT")
        Ra = [mk.alloc([514], F32, f"Ra{i}") for i in range(2)]
        Rg = [mk.alloc([514], F32, f"Rg{i}") for i in range(2)]
        ta = [mk.alloc([512], F32, f"ta{i}") for i in range(2)]
        tg = [mk.alloc([512], F32, f"tg{i}") for i in range(2)]

        def load_mix(ti, actin_, W):
            if ti == 0:
                for s in range(2):
                    mk.dma("pool", actin_[:, :, s * 256:(s + 1) * 256], mixp_d[s].rearrange("c p w -> p c w"))
            else:
                mk.dma("pool", actin_[:, :, 0:W], mixs_d[ti - 1].rearrange("c p w -> p c w"))

        phase2_tiles(mk, C, D["x2"], D["y"], load_mix, n2g, fng, fcw, fcb, hm, gm2, xT, actin, actT, Ra, Rg, ta, tg, C.tmpx, C.rstd,
                     D["w_out"], D["w_up"], D["w_down"])
        mk.finalize()
        print("fused instructions:", mk.n_inst, {e: len(mk.ops[e]) for e in ENGS}, "sbuf words", mk.top)
    return nc


def fused_inputs(core, I):
    b, j = core // 4, core % 4
    d = phase1_inputs(core, I)
    for k in ("s0s", "w_g_s", "w_ab_s", "cw_s", "dtb_s", "alog_s", "wq_s", "wkv_s"):
        d.pop(k)
    p2 = phase2_inputs(core, I["x_prompt"], I["x_sample"], np.zeros((16, 256, 2048), np.float32), np.zeros((2, 4096, 2048), np.float32),
                       I["c"], I["c_ctx"], I["w_ada"][0], I["b_ada"][0], I["w_out"][0], I["norm2_g"][0], I["w_up"][0],
                       I["ffn_conv_w"][0], I["ffn_conv_b"][0], I["w_down"][0], I["final_norm_g"])
    for k in ("x2", "hmask", "w_out", "norm2T", "w_up", "fcwT", "fcbT", "w_down", "fnormT"):
        d[k] = p2[k]
    w_in = I["w_in"][0]
    dtb = I["gdn_dt_bias"][0]
    alog = I["gdn_a_log"][0]
    d["s0h"] = np.stack([I["state_gdn"][b, 0, :, h:h + 1] for h in range(8)], axis=0)
    d["w_ab_h"] = np.stack([w_in[:, [4096 + h, 4104 + h, 4112 + h, 4120 + h]] for h in range(8)], axis=0)
    d["dtb_h"] = np.stack([np.broadcast_to(dtb[:, h].reshape(1, 2), (128, 2)) for h in range(8)], axis=0)
    d["alog_h"] = np.stack([np.broadcast_to(alog[:, h].reshape(1, 2), (128, 2)) for h in range(8)], axis=0)
    cos2, sin2 = d["cos"], d["sin"]
    cosq = np.zeros((64, 1028), np.float32)
    sinq = np.zeros((64, 1028), np.float32)
    for g in range(2):
        lo = 1024 * j + 512 * g - 1
        a, e = max(lo, 0), min(lo + 514, 4096)
        cosq[:, g * 514 + a - lo:g * 514 + e - lo] = cos2[:, a:e]
        sinq[:, g * 514 + a - lo:g * 514 + e - lo] = sin2[:, a:e]
    d["cosq"], d["sinq"] = cosq, sinq
    sel = np.zeros((128, 4), np.float32)
    sel[:, j] = 1.0
    d["sel"] = sel
    return {k: np.ascontiguousarray(np.asarray(v, np.float32)) for k, v in d.items()}


def kernel_fused(**I):
    I = {k: np.asarray(v) for k, v in I.items()}
    if "f" not in _NC:
        _NC["f"] = build_fused()
    r = run_bass_kernel_spmd(_NC["f"], [fused_inputs(c, I) for c in range(NCORES)], core_ids=list(range(NCORES))).results
    yp = np.zeros((16, 256, 2048), np.float32)
    ys = np.zeros((2, 4096, 2048), np.float32)
    new_state = np.zeros((16, 1, 2, 8, 128, 128), np.float32)
    new_ckv = np.zeros((16, 1, 256, 512), np.float32)
    new_kpe = np.zeros((16, 1, 256, 64), np.float32)
    for c in range(NCORES):
        b, j = c // 4, c % 4
        y = r[c]["y"]
        yp[2 * c:2 * c + 2] = y[0].reshape(2, 256, 2048)
        ys[b, 1024 * j:1024 * j + 512] = y[1]
        ys[b, 1024 * j + 512:1024 * j + 1024] = y[2]
        for s in range(2):
            new_state[2 * c + s, 0] = r[c]["new_state"][s]
            new_ckv[2 * c + s, 0] = r[c]["new_ckv"][s]
            new_kpe[2 * c + s, 0] = r[c]["new_kpe"][s]
    return (yp, ys, new_state, new_ckv, new_kpe)


def kernel(**inputs):
    return kernel_fused(**inputs)
```

```python
import numpy as np
from contextlib import ExitStack
import concourse.bass as bass
import concourse.mybir as mybir
from concourse.bass_utils import run_bass_kernel_spmd

F32 = mybir.dt.float32
BF16 = mybir.dt.bfloat16
ALU = mybir.AluOpType
AF = mybir.ActivationFunctionType
AX = mybir.AxisListType
ENGS = ("pe", "act", "dve", "pool", "sp")
NCORES = 8
PENG = "dve"
GSTOP = {"v": 99}
NPAR = 4
EPS = 1e-6


class View:
    __slots__ = ("tile", "ap", "gen")

    def __init__(self, tile, ap, gen=None):
        self.tile = tile
        self.ap = ap
        self.gen = gen

    def __getitem__(self, idx):
        return View(self.tile, self.ap[idx], self.gen)

    def bc(self, shape):
        return View(self.tile, self.ap.to_broadcast(list(shape)), self.gen)

    def rearrange(self, s, **kw):
        return View(self.tile, self.ap.rearrange(s, **kw), self.gen)


class BankRef:
    def __init__(self, tile, gen):
        self.tile = tile
        self.gen = gen

    def __getitem__(self, idx):
        return View(self.tile, self.tile.h[idx], self.gen)


class Tile:
    def __init__(self, ap, name):
        self.h = ap
        self.name = name
        self.last_write = None
        self.reads = []
        self.dma_sem = None
        self.dma_count = 0

    def __getitem__(self, idx):
        return View(self, self.h[idx])

    @property
    def v(self):
        return View(self, self.h)


class MK:
    ARENA_F32 = 50688
    N_DMA_SEMS = 16

    def __init__(self, nc, stack):
        self.nc = nc
        self.stack = stack
        self.ops = {e: [] for e in ENGS}
        self.seq = {e: 0 for e in ENGS}
        self.sem = {e: stack.enter_context(nc.semaphore("sem_" + e)) for e in ("pe", "act", "dve", "pool")}
        self.waited = {e: {} for e in ENGS}
        self.dma_tiles = []
        self.n_inst = 0
        self.arena = stack.enter_context(nc.sbuf_tensor("arena", [128, self.ARENA_F32], F32))
        self.top = 0
        self.banks = [Tile(stack.enter_context(nc.psum_tensor(f"bank{i}", [128, 512], F32)), f"bank{i}")
                      for i in range(8)]
        for b in self.banks:
            b.is_bank = True
        self.bank_rr = 0
        self.reserved = []
        self.dma_sem_pool = {}
        self.dma_sem_rr = {}
        self.tcount = 0

    def alloc(self, free_shape, dtype=F32, name=None, parts=128):
        n = int(np.prod(free_shape))
        words = n if dtype == F32 else (n + 1) // 2
        words = (words + 7) // 8 * 8
        assert self.top + words <= self.ARENA_F32, f"SBUF arena overflow allocating {name} {free_shape}"
        ap = self.arena[0:parts, self.top:self.top + words]
        self.top += words
        if dtype != F32:
            ap = ap.bitcast(dtype)
        ap = ap[:, 0:n]
        if len(free_shape) == 2:
            ap = ap.rearrange("p (a b) -> p a b", a=free_shape[0])
        elif len(free_shape) == 3:
            ap = ap.rearrange("p (a b c) -> p a b c", a=free_shape[0], b=free_shape[1])
        self.tcount += 1
        return Tile(ap, name or f"t{self.tcount}")

    def mark(self):
        return self.top

    def release(self, mark):
        self.barrier()
        self.top = mark

    def bank(self):
        while True:
            b = self.banks[self.bank_rr % 8]
            self.bank_rr += 1
            if b not in self.reserved:
                b.gen = getattr(b, "gen", 0) + 1
                return BankRef(b, b.gen)

    def reserve(self):
        b = self.bank()
        self.reserved.append(b.tile)
        return b

    def unreserve(self, b):
        self.reserved.remove(b.tile)

    def _resolve(self, ev):
        if ev[0] == "dma":
            t = ev[1]
            return (t.dma_sem, t.dma_count, None)
        return (self.sem[ev[0]], ev[1], ev[0])

    def _wait(self, eng, ev):
        sem, val, src = self._resolve(ev)
        if src == eng and eng == "pe":
            return
        key = id(sem)
        if self.waited[eng].get(key, 0) >= val:
            return
        self.waited[eng][key] = val
        self.ops[eng].append(("w", sem, val))

    def _deps(self, eng, reads, writes):
        for t in reads:
            if t.last_write is not None:
                self._wait(eng, t.last_write)
            if getattr(t, "is_bank", False):
                for ev in t.reads:
                    if ev[0] != eng:
                        self._wait(eng, ev)
        for t in writes:
            if t.last_write is not None:
                self._wait(eng, t.last_write)
            for ev in t.reads:
                self._wait(eng, ev)

    def _commit(self, ev, reads, writes):
        for t in writes:
            t.last_write = ev
            t.reads = []
        for t in reads:
            if t in writes:
                continue
            t.reads.append(ev)
            if len(t.reads) > 24:
                best = {}
                for e in t.reads:
                    k = e[0] if e[0] != "dma" else ("dma", id(e[1]))
                    if k not in best or (e[0] != "dma" and e[1] > best[k][1]):
                        best[k] = e
                t.reads = list(best.values())

    def I(self, eng, meth, *args, reads=(), writes=(), **kw):
        rd, wr = list(reads), list(writes)
        real = {}
        for k, v in kw.items():
            if isinstance(v, View):
                assert v.gen is None or v.gen == v.tile.gen, f"stale PSUM bank handle used by {meth} ({k})"
                (wr if k in ("out", "accum_out") else rd).append(v.tile)
                real[k] = v.ap
            else:
                real[k] = v
        rargs = []
        for v in args:
            if isinstance(v, View):
                rd.append(v.tile)
                rargs.append(v.ap)
            else:
                rargs.append(v)
        self._deps(eng, rd, wr)
        self.seq[eng] += 1
        ev = (eng, self.seq[eng])
        self.ops[eng].append(("i", meth, rargs, real))
        self._commit(ev, rd, wr)
        self.n_inst += 1
        return ev

    def dma(self, q, out, in_, **kw):
        rd, wr = [], []
        st = None
        if isinstance(out, View):
            wr.append(out.tile)
            o = out.ap
            st = out.tile
        else:
            o = out
        if isinstance(in_, View):
            rd.append(in_.tile)
            i = in_.ap
            if st is None:
                st = in_.tile
        else:
            i = in_
        if st.dma_sem is None:
            st.dma_sem = {}
        if q not in st.dma_sem:
            pool = self.dma_sem_pool.setdefault(q, [])
            if len(pool) < self.N_DMA_SEMS:
                ds = Tile(None, "dsem_%s%d" % (q, len(pool)))
                ds.dma_sem = self.stack.enter_context(self.nc.semaphore("ds_%s%d" % (q, len(pool))))
                pool.append(ds)
                self.dma_tiles.append(ds)
            rr = self.dma_sem_rr.get(q, 0)
            self.dma_sem_rr[q] = rr + 1
            st.dma_sem[q] = pool[rr % self.N_DMA_SEMS]
        dsem = st.dma_sem[q]
        self._deps(q, rd, wr)
        if dsem.dma_count:
            self._wait(q, ("dma", dsem))
        dsem.dma_count += 16
        self.ops[q].append(("d", o, i, kw, dsem.dma_sem))
        ev = ("dma", dsem)
        self._commit(ev, rd, wr)
        self.n_inst += 1
        return ev

    def barrier(self):
        for e in ENGS:
            for src in ("pe", "act", "dve", "pool"):
                if src != e and self.seq[src] > 0:
                    self._wait(e, (src, self.seq[src]))
            for t in self.dma_tiles:
                if t.dma_count:
                    self._wait(e, ("dma", t))

    def finalize(self):
        for t in self.dma_tiles:
            self.ops["sp"].append(("w", t.dma_sem, t.dma_count))
        sem = self.sem

        def run(eng_name):
            def f(e):
                for op in self.ops[eng_name]:
                    if op[0] == "w":
                        e.wait_ge(op[1], op[2])
                    elif op[0] == "i":
                        ins = getattr(e, op[1])(*op[2], **op[3])
                        if eng_name in sem:
                            ins.then_inc(sem[eng_name], 1)
                    else:
                        e.dma_start(out=op[1], in_=op[2], **op[3]).then_inc(op[4], 16)
            return f

        with self.nc.Block() as block:
            block.tensor(run("pe"))
            block.scalar(run("act"))
            block.vector(run("dve"))
            block.gpsimd(run("pool"))
            block.sync(run("sp"))


class Ctx:
    pass


DEBUG = {"on": False, "nc": None, "done": set()}


def dump(mk, name, view, shape):
    if not DEBUG["on"] or name in DEBUG["done"]:
        return
    DEBUG["done"].add(name)
    ap = DEBUG["nc"].dram_tensor("dbg_" + name, list(shape), F32, kind="ExternalOutput").ap()
    stg = mk.alloc(list(shape[1:]), F32, "dbgs_" + name, parts=shape[0]) if False else None
    mk.dma("sp", ap, view)


def halves(W):
    if W <= 512:
        return [(0, W)]
    h = (W + 1) // 2
    return [(0, h), (h, W - h)]


def mm_group(mk, W, steps):
    outs = []
    for (c0, n) in halves(W):
        b = mk.bank()
        for i, (lhsT, rhs_fn) in enumerate(steps):
            mk.I("pe", "matmul", out=b[:, 0:n], lhsT=lhsT, rhs=rhs_fn(c0, n),
                 start=(i == 0), stop=(i == len(steps) - 1))
        outs.append((b, c0, n))
    return outs


def load_weight_tile(mk, wt, src_ap, KC, ncols):
    mk.dma("pool", wt[:, 0:KC, 0:ncols], src_ap.rearrange("(kc p) n -> p kc n", p=128))


def load_xT(mk, C, x_rows, W, xT, eng_rr):
    nsub = (W + 127) // 128
    for s in range(nsub):
        r0 = s * 128
        n = min(128, W - r0)
        xs = C.xstage[s % 2]
        mk.dma("sp", xs[0:n, :], x_rows[r0:r0 + n, :])
        for g in range(4):
            b = mk.bank()
            for q in range(4):
                c = g * 4 + q
                mk.I("pe", "transpose", out=b[:, q * 128:q * 128 + n], in_=xs[0:n, c * 128:(c + 1) * 128],
                     identity=C.ident[0:n, 0:n])
            src = b[:, 0:512].rearrange("p (q t) -> p q t", q=4)[:, :, 0:n]
            dst = xT[:, g * 4:(g + 1) * 4, r0:r0 + n]
            if (s * 4 + g) % 2 == 0:
                mk.I("dve", "tensor_copy", out=dst, in_=src)
            else:
                mk.I("act", "activation", out=dst, in_=src, func=AF.Copy)


def rms_rstd(mk, C, XT, nch, W, col0, dim, rstd):
    pieces = halves(W)
    banks = [mk.bank() for _ in pieces]
    for c in range(nch):
        sq = C.sq[c % 2]
        mk.I("act", "activation", out=sq[:, 0:W], in_=XT[:, c, col0:col0 + W], func=AF.Square)
        for (b, (c0, n)) in zip(banks, pieces):
            mk.I("pe", "matmul", out=b[:, 0:n], lhsT=C.ones.v, rhs=sq[:, c0:c0 + n], start=(c == 0), stop=(c == nch - 1))
    for (b, (c0, n)) in zip(banks, pieces):
        mk.I("act", "activation", out=C.tmpn[:, c0:c0 + n], in_=b[:, 0:n], func=AF.Sqrt, scale=1.0 / dim, bias=C.epsb[:, 0:1])
        mk.I("dve", "reciprocal", out=rstd[:, c0:c0 + n], in_=C.tmpn[:, c0:c0 + n])


def adaln(mk, C, w_ada, which_list, ncond):
    b = mk.bank()
    for wh in which_list:
        for blk in range(4):
            col0 = wh * 2048 + blk * 512
            wt = C.wt[C.wt_rr % len(C.wt)]
            C.wt_rr += 1
            load_weight_tile(mk, wt, w_ada[:, col0:col0 + 512], 16, 512)
            for q in range(4):
                cc = wh * 16 + blk * 4 + q
                for kc in range(16):
                    mk.I("pe", "matmul", out=b[:, cc * ncond:(cc + 1) * ncond], lhsT=wt[:, kc, q * 128:(q + 1) * 128],
                         rhs=C.scond[:, kc, 0:ncond], start=(kc == 0), stop=(kc == 15))
    for wh in which_list:
        sl = slice(wh * 16, (wh + 1) * 16)
        mk.I("dve", "tensor_tensor", out=C.mod[:, sl, :],
             in0=b[:, wh * 16 * ncond:(wh + 1) * 16 * ncond].rearrange("p (c n) -> p c n", n=ncond),
             in1=C.b_ada[:, sl].rearrange("p (c o) -> p c o", o=1).bc([128, 16, ncond]), op=ALU.add)


P2_TILES = [dict(W=512, cond=0, segs=[(0, 256, 0), (256, 256, 0)], out0=0),
            dict(W=514, cond=1, segs=[(0, 514, 1)], out0=1),
            dict(W=514, cond=1, segs=[(0, 514, 1)], out0=1)]


def phase2_tiles(mk, C, x2, y, load_mix, n2g, fng, fcw, fcb, hm, gm2, xT, actin, actT, Ra, Rg, ta, tg, tmpx, rstd,
                 w_out, w_up, w_down):
    for ti, T in enumerate(P2_TILES):
        W, ci, out0 = T["W"], T["cond"], T["out0"]
        load_xT(mk, C, x2[ti], W, xT, 0)
        load_mix(ti, actin, W)
        for blk in range(4):
            wt = C.wt[C.wt_rr % len(C.wt)]
            C.wt_rr += 1
            load_weight_tile(mk, wt, w_out[:, blk * 512:(blk + 1) * 512], 16, 512)
            for q in range(4):
                cc = blk * 4 + q
                outs = mm_group(mk, W, [(wt[:, kc, q * 128:(q + 1) * 128],
                                        (lambda c0, n, kc=kc: actin[:, kc, c0:c0 + n])) for kc in range(16)])
                for (b, c0, n) in outs:
                    mk.I("dve", "scalar_tensor_tensor", out=xT[:, cc, c0:c0 + n], in0=b[:, 0:n],
                         scalar=C.mod[:, 32 + cc, ci:ci + 1], in1=xT[:, cc, c0:c0 + n], op0=ALU.mult, op1=ALU.add)
        rms_rstd(mk, C, xT, 16, W, 0, 2048.0, rstd)
        for cc in range(16):
            tx = tmpx[cc % 2]
            mk.I("dve", "scalar_tensor_tensor", out=tx[:, 0:W], in0=xT[:, cc, 0:W], scalar=gm2[:, cc, ci:ci + 1],
                 in1=rstd[:, 0:W], op0=ALU.mult, op1=ALU.mult)
            mk.I("act", "activation", out=actin[:, cc, 0:W], in_=tx[:, 0:W], func=AF.Identity,
                 bias=C.mod[:, 48 + cc, ci:ci + 1], scale=1.0)
        for jb in range(11):
            wa = C.wt[C.wt_rr % len(C.wt)]
            C.wt_rr += 1
            load_weight_tile(mk, wa, w_up[:, jb * 512:(jb + 1) * 512], 16, 512)
            wg = C.wt[C.wt_rr % len(C.wt)]
            C.wt_rr += 1
            load_weight_tile(mk, wg, w_up[:, 5632 + jb * 512:5632 + (jb + 1) * 512], 16, 512)
            for q in range(4):
                j = jb * 4 + q
                res = []
                for (wtile, R, chunk) in ((wa, Ra[j % 2], j), (wg, Rg[j % 2], 44 + j)):
                    outs = mm_group(mk, W, [(wtile[:, kc, q * 128:(q + 1) * 128],
                                            (lambda c0, n, kc=kc: actin[:, kc, c0:c0 + n])) for kc in range(16)])
                    for (b, c0, n) in outs:
                        mk.I("act", "activation", out=R[:, c0:c0 + n], in_=b[:, 0:n], func=AF.Copy)
                    res.append((R, chunk))
                for (R, chunk), tt in ((res[0], ta[j % 2]), (res[1], tg[j % 2])):
                    for (s0, L, halo) in T["segs"]:
                        if halo:
                            mk.I("dve", "tensor_scalar", out=R[:, s0:s0 + 1], in0=R[:, s0:s0 + 1],
                                 scalar1=hm[:, 2 * (ti - 1):2 * (ti - 1) + 1], scalar2=None, op0=ALU.mult)
                            mk.I("dve", "tensor_scalar", out=R[:, s0 + L - 1:s0 + L], in0=R[:, s0 + L - 1:s0 + L],
                                 scalar1=hm[:, 2 * (ti - 1) + 1:2 * (ti - 1) + 2], scalar2=None, op0=ALU.mult)
                            o0, n = s0 + 1, L - 2
                            mk.I("act", "activation", out=tt[:, 0:n], in_=R[:, o0:o0 + n], func=AF.Identity,
                                 scale=fcw[:, chunk, 1:2], bias=fcb[:, chunk:chunk + 1])
                            mk.I("dve", "scalar_tensor_tensor", out=tt[:, 0:n], in0=R[:, o0 - 1:o0 - 1 + n],
                                 scalar=fcw[:, chunk, 0:1], in1=tt[:, 0:n], op0=ALU.mult, op1=ALU.add)
                            mk.I("dve", "scalar_tensor_tensor", out=tt[:, 0:n], in0=R[:, o0 + 1:o0 + 1 + n],
                                 scalar=fcw[:, chunk, 2:3], in1=tt[:, 0:n], op0=ALU.mult, op1=ALU.add)
                        else:
                            mk.I("act", "activation", out=tt[:, s0:s0 + L], in_=R[:, s0:s0 + L], func=AF.Identity,
                                 scale=fcw[:, chunk, 1:2], bias=fcb[:, chunk:chunk + 1])
                            mk.I("dve", "scalar_tensor_tensor", out=tt[:, s0 + 1:s0 + L], in0=R[:, s0:s0 + L - 1],
                                 scalar=fcw[:, chunk, 0:1], in1=tt[:, s0 + 1:s0 + L], op0=ALU.mult, op1=ALU.add)
                            mk.I("dve", "scalar_tensor_tensor", out=tt[:, s0:s0 + L - 1], in0=R[:, s0 + 1:s0 + L],
                                 scalar=fcw[:, chunk, 2:3], in1=tt[:, s0:s0 + L - 1], op0=ALU.mult, op1=ALU.add)
                mk.I("act", "activation", out=ta[j % 2].v, in_=ta[j % 2].v, func=AF.Silu)
                mk.I("dve", "tensor_tensor", out=actT[:, j, :], in0=ta[j % 2].v, in1=tg[j % 2].v, op=ALU.mult)
        for cc in range(16):
            wt = C.wt[C.wt_rr % len(C.wt)]
            C.wt_rr += 1
            wv = wt.v.rearrange("p a b -> p (a b)")[:, 0:44 * 128].rearrange("p (k n) -> p k n", k=44)
            mk.dma("pool", wv, w_down[:, cc * 128:(cc + 1) * 128].rearrange("(kc p) n -> p kc n", p=128))
            b = mk.bank()
            for j in range(44):
                mk.I("pe", "matmul", out=b[:, 0:512], lhsT=wv[:, j, :], rhs=actT[:, j, :], start=(j == 0), stop=(j == 43))
            mk.I("dve", "scalar_tensor_tensor", out=xT[:, cc, out0:out0 + 512], in0=b[:, 0:512],
                 scalar=C.mod[:, 80 + cc, ci:ci + 1], in1=xT[:, cc, out0:out0 + 512], op0=ALU.mult, op1=ALU.add)
        rms_rstd(mk, C, xT, 16, 512, out0, 2048.0, rstd)
        for cc in range(16):
            mk.I("dve", "scalar_tensor_tensor", out=xT[:, cc, out0:out0 + 512], in0=xT[:, cc, out0:out0 + 512],
                 scalar=fng[:, cc:cc + 1], in1=rstd[:, 0:512], op0=ALU.mult, op1=ALU.mult)
        for s in range(4):
            ys = C.xstage[s % 2]
            for g in range(4):
                b = mk.bank()
                for q in range(4):
                    c = g * 4 + q
                    mk.I("pe", "transpose", out=b[:, q * 128:(q + 1) * 128],
                         in_=xT[:, c, out0 + s * 128:out0 + (s + 1) * 128], identity=C.ident.v)
                if g % 2 == 0:
                    mk.I("dve", "tensor_copy", out=ys[:, g * 512:(g + 1) * 512], in_=b[:, 0:512])
                else:
                    mk.I("act", "activation", out=ys[:, g * 512:(g + 1) * 512], in_=b[:, 0:512], func=AF.Copy)
            mk.dma("sp", y[ti, s * 128:(s + 1) * 128, :], ys.v)


def build_phase2():
    nc = bass.Bass("TRN2", target_bir_lowering=False)
    D = {}

    def din(name, shape, dt=F32):
        D[name] = nc.dram_tensor(name, list(shape), dt, kind="ExternalInput").ap()
        return D[name]

    x2 = din("x2", [3, 514, 2048])
    mix = din("mix", [3, 16, 128, 514])
    condT = din("condT", [128, 16, 2])
    hmask = din("hmask", [128, 4])
    w_ada = din("w_ada", [2048, 12288])
    b_adaT = din("b_adaT", [128, 96])
    w_out = din("w_out", [2048, 2048])
    norm2T = din("norm2T", [128, 16])
    w_up = din("w_up", [2048, 11264])
    fcwT = din("fcwT", [128, 88, 3])
    fcbT = din("fcbT", [128, 88])
    w_down = din("w_down", [5632, 2048])
    fnormT = din("fnormT", [128, 16])
    identD = din("ident", [128, 128])
    y = nc.dram_tensor("y", [3, 512, 2048], F32, kind="ExternalOutput").ap()

    with ExitStack() as st:
        mk = MK(nc, st)
        C = Ctx()
        C.ident = mk.alloc([128], F32, "ident")
        C.ones = mk.alloc([128], F32, "ones")
        C.epsb = mk.alloc([1], F32, "epsb")
        C.scond = mk.alloc([16, 2], BF16, "scond")
        condf = mk.alloc([16, 2], F32, "condf")
        C.b_ada = mk.alloc([96], F32, "b_ada")
        C.mod = mk.alloc([96, 2], F32, "mod")
        n2g = mk.alloc([16], F32, "n2g")
        fng = mk.alloc([16], F32, "fng")
        fcw = mk.alloc([88, 3], F32, "fcw")
        fcb = mk.alloc([88], F32, "fcb")
        hm = mk.alloc([4], F32, "hm")
        gm2 = mk.alloc([16, 2], F32, "gm2")
        C.xstage = [mk.alloc([2048], F32, f"xs{i}") for i in range(2)]
        C.sq = [mk.alloc([514], F32, f"sq{i}") for i in range(2)]
        C.tmpn = mk.alloc([514], F32, "tmpn")
        rstd = mk.alloc([514], F32, "rstd")
        C.wt = [mk.alloc([16, 512], BF16, f"wt{i}") for i in range(3)]
        C.wt_rr = 0
        xT = mk.alloc([16, 514], F32, "xT")
        actin = mk.alloc([16, 514], BF16, "actin")
        actT = mk.alloc([44, 512], BF16, "actT")
        Ra = [mk.alloc([514], F32, f"Ra{i}") for i in range(2)]
        Rg = [mk.alloc([514], F32, f"Rg{i}") for i in range(2)]
        ta = [mk.alloc([512], F32, f"ta{i}") for i in range(2)]
        tg = [mk.alloc([512], F32, f"tg{i}") for i in range(2)]
        tmpx = [mk.alloc([514], F32, f"tmpx{i}") for i in range(2)]

        mk.dma("sp", C.ident.v, identD)
        mk.I("dve", "memset", C.ones.v.ap, 1.0, writes=[C.ones])
        mk.I("dve", "memset", C.epsb.v.ap, EPS, writes=[C.epsb])
        mk.dma("sp", condf.v, condT)
        mk.dma("sp", C.b_ada.v, b_adaT)
        mk.dma("sp", n2g.v, norm2T)
        mk.dma("sp", fng.v, fnormT)
        mk.dma("sp", fcw.v, fcwT)
        mk.dma("sp", fcb.v, fcbT)
        mk.dma("sp", hm.v, hmask)
        mk.I("act", "activation", out=C.scond.v, in_=condf.v, func=AF.Silu)
        adaln(mk, C, w_ada, [2, 3, 4, 5], 2)
        mk.I("dve", "tensor_scalar", out=gm2.v, in0=C.mod[:, 64:80, :], scalar1=1.0, scalar2=None, op0=ALU.add)
        mk.I("dve", "tensor_tensor", out=gm2.v, in0=gm2.v,
             in1=n2g.v.rearrange("p (c o) -> p c o", o=1).bc([128, 16, 2]), op=ALU.mult)

        phase2_tiles(mk, C, x2, y, lambda ti, actin, W: mk.dma("pool", actin[:, :, 0:W], mix[ti, :, :, 0:W].rearrange("c p w -> p c w")),
                     n2g, fng, fcw, fcb, hm, gm2, xT, actin, actT, Ra, Rg, ta, tg, tmpx, rstd, w_out, w_up, w_down)
        mk.finalize()
        print("phase2 instructions:", mk.n_inst, {e: len(mk.ops[e]) for e in ENGS}, "sbuf words", mk.top)
    return nc


def colT(v, n):
    return np.ascontiguousarray(np.asarray(v, np.float32).reshape(n, 128).T)


def phase2_inputs(core, x_prompt, x_sample, mix_p, mix_s, c, c_ctx, w_ada, b_ada, w_out, norm2_g, w_up,
                  ffn_conv_w, ffn_conv_b, w_down, final_norm_g):
    b, j = core // 4, core % 4
    x2 = np.zeros((3, 514, 2048), np.float32)
    mix = np.zeros((3, 514, 2048), np.float32)
    x2[0, :512] = x_prompt[2 * core:2 * core + 2].reshape(512, 2048)
    mix[0, :512] = mix_p[2 * core:2 * core + 2].reshape(512, 2048)
    hmask = np.zeros((128, 4), np.float32)
    for g in range(2):
        lo = 1024 * j + 512 * g - 1
        hi = lo + 514
        a, e = max(lo, 0), min(hi, 4096)
        x2[1 + g, a - lo:e - lo] = x_sample[b, a:e]
        mix[1 + g, a - lo:e - lo] = mix_s[b, a:e]
        hmask[:, 2 * g] = 1.0 if lo >= 0 else 0.0
        hmask[:, 2 * g + 1] = 1.0 if hi <= 4096 else 0.0
    mixT = np.ascontiguousarray(mix.reshape(3, 514, 16, 128).transpose(0, 2, 3, 1))
    cond = np.stack([c_ctx, c[b]], axis=0)
    condT = np.ascontiguousarray(cond.reshape(2, 16, 128).transpose(2, 1, 0))
    return dict(x2=x2, mix=mixT, condT=condT, hmask=hmask, w_ada=w_ada, b_adaT=colT(b_ada, 96), w_out=w_out,
                norm2T=colT(norm2_g, 16), w_up=w_up,
                fcwT=np.ascontiguousarray(ffn_conv_w.reshape(3, 88, 128).transpose(2, 1, 0)),
                fcbT=colT(ffn_conv_b, 88), w_down=w_down, fnormT=colT(final_norm_g, 16),
                ident=np.eye(128, dtype=np.float32))


def norm_mod_tile(mk, C, x_rows, W, xT, hT, rstd, gm1, ci):
    load_xT(mk, C, x_rows, W, xT, 0)
    rms_rstd(mk, C, xT, 16, W, 0, 2048.0, rstd)
    for cc in range(16):
        tx = C.tmpx[cc % 2]
        mk.I("dve", "scalar_tensor_tensor", out=tx[:, 0:W], in0=xT[:, cc, 0:W], scalar=gm1[:, cc, ci:ci + 1],
             in1=rstd[:, 0:W], op0=ALU.mult, op1=ALU.mult)
        mk.I("act", "activation", out=hT[:, cc, 0:W], in_=tx[:, 0:W], func=AF.Identity,
             bias=C.mod[:, cc, ci:ci + 1], scale=1.0)


def bcast_sum_rstd(mk, C, srcs, W, dim, rstd, eps=EPS):
    b = mk.bank()
    for i, (s, P) in enumerate(srcs):
        sq = C.sq[i % 2]
        mk.I("act", "activation", out=sq[0:P, 0:W], in_=s, func=AF.Square)
        mk.I("pe", "matmul", out=b[:, 0:W], lhsT=C.ones[0:P, :], rhs=sq[0:P, 0:W], start=(i == 0), stop=(i == len(srcs) - 1))
    mk.I("act", "activation", out=C.tmpn[:, 0:W], in_=b[:, 0:W], func=AF.Sqrt, scale=1.0 / dim, bias=C.epsb[:, 0:1] if eps == EPS else C.zerob[:, 0:1])
    mk.I("dve", "reciprocal", out=rstd[:, 0:W], in_=C.tmpn[:, 0:W])


def gdn_unit(mk, C, G, h, d, c, k):
    NH = G.NH
    cols = slice(c * 128, (c + 1) * 128)
    gi = d * NH + h
    g_col = G.Gt[:, c, gi:gi + 1]
    beta_col = G.Bt[:, c, gi:gi + 1]
    nbeta_col = G.NBt[:, c, gi:gi + 1]
    TRI = C.tri[d]
    MS = C.ms[d]
    S = G.S[h][d]
    u = C.gu
    kT = G.KT[h][:, cols]
    qT = G.QT[h][:, cols]
    bk = mk.bank()
    mk.I("pe", "matmul", out=bk[:, 0:128], lhsT=kT, rhs=C.identb.v, start=True, stop=True)
    bv = mk.bank()
    mk.I("pe", "matmul", out=bv[:, 0:128], lhsT=G.VT[h][:, cols], rhs=C.identb.v, start=True, stop=True)
    ktm = u.ktm[k]
    vb = u.vb[k]
    mk.I("act", "activation", out=ktm.v, in_=bk[:, 0:128], func=AF.Copy)
    mk.I("dve", "tensor_scalar", out=vb.v, in0=bv[:, 0:128], scalar1=beta_col, scalar2=None, op0=ALU.mult)
    yield
    bg = mk.bank()
    mk.I("pe", "matmul", out=bg[:, 0:1], lhsT=TRI.v, rhs=g_col, start=True, stop=True)
    mk.I("pe", "matmul", out=bg[:, 128:256], lhsT=g_col.bc([128, 128]), rhs=TRI.v, start=True, stop=True)
    Gc = u.Gc[k]
    Gb = u.Gb[k]
    mk.I("dve", "tensor_copy", out=Gc[:, 0:1], in_=bg[:, 0:1])
    mk.I("act", "activation", out=Gb.v, in_=bg[:, 128:256], func=AF.Copy)
    gtot = Gb[:, 127:128] if d == 0 else Gb[:, 0:1]
    yield
    Dm, DTm = u.Dm[k], u.DTm[k]
    mk.I("dve", "tensor_scalar", out=Dm.v, in0=Gb.v, scalar1=Gc[:, 0:1], scalar2=0.0, op0=ALU.subtract, op1=ALU.max)
    mk.I("act", "activation", out=Dm.v, in_=Dm.v, func=AF.Exp, scale=-1.0)
    mk.I(PENG, "tensor_tensor", out=Dm.v, in0=Dm.v, in1=MS.v, op=ALU.mult)
    mk.I("dve", "tensor_scalar", out=DTm.v, in0=Gb.v, scalar1=Gc[:, 0:1], scalar2=0.0, op0=ALU.subtract, op1=ALU.min)
    mk.I("act", "activation", out=DTm.v, in_=DTm.v, func=AF.Exp)
    mk.I(PENG, "tensor_tensor", out=DTm.v, in0=DTm.v, in1=TRI.v, op=ALU.mult)
    yield
    bkk = mk.bank()
    kTc = u.kTc[k]
    mk.I("dve", "tensor_copy", out=kTc.v, in_=kT)
    mk.I("pe", "matmul", out=bkk[:, 0:128], lhsT=kT, rhs=kTc.v, start=True, stop=True)
    P, PT = u.P[k], u.PT[k]
    mk.I("dve", "scalar_tensor_tensor", out=P[0].v, in0=bkk[:, 0:128], scalar=nbeta_col, in1=Dm.v, op0=ALU.mult, op1=ALU.mult)
    bt = mk.bank()
    mk.I("pe", "transpose", out=bt[:, 0:128], in_=P[0].v, identity=C.ident.v)
    mk.I("act", "activation", out=PT[0].v, in_=bt[:, 0:128], func=AF.Copy)
    TT = u.TT[k]
    mk.I("dve", "tensor_tensor", out=TT.v, in0=bt[:, 0:128], in1=C.ident.v, op=ALU.add)
    yield
    cur = 0
    for lev in range(1, 7):
        nxt = 1 - cur
        b1 = mk.bank()
        mk.I("pe", "matmul", out=b1[:, 0:128], lhsT=PT[cur].v, rhs=P[cur].v, start=True, stop=True)
        if lev < 6:
            mk.I("pe", "matmul", out=b1[:, 128:256], lhsT=P[cur].v, rhs=PT[cur].v, start=True, stop=True)
        mk.I("act", "activation", out=P[nxt].v, in_=b1[:, 0:128], func=AF.Copy)
        if lev < 6:
            mk.I("dve", "tensor_copy", out=PT[nxt].v, in_=b1[:, 128:256])
        b2 = mk.bank()
        mk.I("pe", "matmul", out=b2[:, 0:128], lhsT=P[nxt].v, rhs=TT.v, start=True, stop=True)
        mk.I("dve", "tensor_tensor", out=TT.v, in0=b2[:, 0:128], in1=TT.v, op=ALU.add)
        cur = nxt
        yield
    yield
    sc = u.sc[k]
    mk.I("act", "activation", out=sc[:, 0:1], in_=Gc[:, 0:1], func=AF.Exp)
    mk.I("dve", "tensor_tensor", out=sc[:, 1:2], in0=sc[:, 0:1], in1=beta_col, op=ALU.mult)
    mk.I("act", "activation", out=sc[:, 2:3], in_=Gc[:, 0:1], func=AF.Exp, scale=-1.0, bias=gtot)
    mk.I("act", "activation", out=sc[:, 3:4], in_=gtot, func=AF.Exp)
    kbg, kdec = u.kbg[k], u.kdec[k]
    mk.I("act", "activation", out=kbg.v, in_=ktm.v, func=AF.Identity, scale=sc[:, 1:2])
    mk.I("dve", "tensor_scalar", out=kdec.v, in0=ktm.v, scalar1=sc[:, 2:3], scalar2=None, op0=ALU.mult)
    bu = mk.bank()
    mk.I("pe", "matmul", out=bu[:, 0:128], lhsT=TT.v, rhs=vb.v, start=True, stop=True)
    mk.I("pe", "matmul", out=bu[:, 128:256], lhsT=kbg.v, rhs=TT.v, start=True, stop=True)
    uu, wT = u.uu[k], u.wT[k]
    mk.I("act", "activation", out=uu.v, in_=bu[:, 0:128], func=AF.Copy)
    mk.I("dve", "tensor_copy", out=wT.v, in_=bu[:, 128:256])
    yield
    bq = mk.bank()
    mk.I("pe", "matmul", out=bq[:, 0:128], lhsT=kT, rhs=qT, start=True, stop=True)
    intraT, qgT, eGb = u.intraT[k], u.qgT[k], u.eGb[k]
    mk.I("dve", "tensor_tensor", out=intraT.v, in0=bq[:, 0:128], in1=DTm.v, op=ALU.mult)
    mk.I("act", "activation", out=eGb.v, in_=Gb.v, func=AF.Exp)
    mk.I(PENG, "tensor_tensor", out=qgT.v, in0=qT, in1=eGb.v, op=ALU.mult)
    yield
    b3 = mk.bank()
    mk.I("pe", "matmul", out=b3[:, 0:128], lhsT=wT.v, rhs=S.v, start=True, stop=True)
    vnew = u.vnew[k]
    mk.I("dve", "tensor_tensor", out=vnew.v, in0=uu.v, in1=b3[:, 0:128], op=ALU.subtract)
    b4 = mk.bank()
    mk.I("pe", "matmul", out=b4[:, 0:128], lhsT=S.v, rhs=qgT.v, start=True, stop=False)
    mk.I("pe", "matmul", out=b4[:, 0:128], lhsT=vnew.v, rhs=intraT.v, start=False, stop=True)
    mk.I("pe", "matmul", out=b4[:, 128:256], lhsT=kdec.v, rhs=vnew.v, start=True, stop=True)
    ocols = slice(c * 128 + G.pad, (c + 1) * 128 + G.pad)
    mk.I("dve", "tensor_tensor", out=G.OT[h][:, ocols], in0=G.OT[h][:, ocols], in1=b4[:, 0:128], op=ALU.add)
    mk.I("dve", "scalar_tensor_tensor", out=S.v, in0=S.v, scalar=sc[:, 3:4], in1=b4[:, 128:256], op0=ALU.mult, op1=ALU.add)
    if h == 0 and c == 0 and d == 0:
        for nm, vv in (("Gb", Gb), ("Dm", Dm), ("DTm", DTm), ("X", P[0] if False else None), ("TT", TT), ("vb", vb), ("kbg", kbg), ("kdec", kdec),
                       ("uu", uu), ("wT", wT), ("intraT", intraT), ("qgT", qgT), ("vnew", vnew), ("S", S)):
            if vv is not None:
                dump(mk, nm, vv.v, [128, 128])
        dump(mk, "sc", sc[:, 0:4], [128, 4])
        dump(mk, "Gc", Gc.v, [128, 1])


def conv_out(mk, C, G, h, comp, R, n, tok0):
    t = C.ct[C.ct_rr % 2]
    C.ct_rr += 1
    ch = h * 3 + comp
    mk.I("act", "activation", out=t[:, 0:n], in_=R[:, 1:1 + n], func=AF.Identity, scale=G.cw[:, ch, 1:2])
    mk.I("dve", "scalar_tensor_tensor", out=t[:, 0:n], in0=R[:, 0:n], scalar=G.cw[:, ch, 0:1], in1=t[:, 0:n], op0=ALU.mult, op1=ALU.add)
    mk.I("dve", "scalar_tensor_tensor", out=t[:, 0:n], in0=R[:, 2:2 + n], scalar=G.cw[:, ch, 2:3], in1=t[:, 0:n], op0=ALU.mult, op1=ALU.add)
    j0 = 1 if tok0 < 0 else 0
    if n - j0 <= 0:
        return
    dsl = slice(tok0 + j0, tok0 + n)
    if comp == 2:
        mk.I("act", "activation", out=G.VT[h][:, dsl], in_=t[:, j0:n], func=AF.Silu)
        return
    mk.I("act", "activation", out=t[:, 0:n], in_=t[:, 0:n], func=AF.Silu)
    bcast_sum_rstd(mk, C, [(t[:, 0:n], 128)], n, 1.0, C.rstd)
    dst = (G.QT if comp == 0 else G.KT)[h]
    mk.I("dve", "scalar_tensor_tensor", out=dst[:, dsl], in0=t[:, j0:n], scalar=(128.0 ** -0.5 if comp == 0 else 1.0),
         in1=C.rstd[:, j0:n], op0=ALU.mult, op1=ALU.mult)


def gdn_phase(mk, C, x_rows, T, NH, ci, gm1, Wd, s0_ap, mix_out, state_out, win=None):
    mark = mk.mark()
    G = Ctx()
    G.NH = NH
    pad = G.pad = 1 if win is not None else 0
    TTs = min(512, T)
    ntile = T // TTs
    nch = T // 128
    G.QT = [mk.alloc([T], BF16, f"QT{h}") for h in range(NH)]
    G.KT = [mk.alloc([T], BF16, f"KT{h}") for h in range(NH)]
    G.VT = [mk.alloc([T], BF16, f"VT{h}") for h in range(NH)]
    G.ZT = [mk.alloc([T + 2 * pad], BF16, f"ZT{h}") for h in range(NH)]
    G.OT = [mk.alloc([T + 2 * pad], F32, f"OT{h}") for h in range(NH)]
    G.Gt = mk.alloc([nch, 2 * NH], F32, "Gt")
    G.Bt = mk.alloc([nch, 2 * NH], F32, "Bt")
    G.NBt = mk.alloc([nch, 2 * NH], F32, "NBt")
    G.cw = mk.alloc([NH * 3, 3], F32, "cw")
    G.halo = [[mk.alloc([2], F32, f"halo{h}_{c}") for c in range(3)] for h in range(NH)]
    G.S = [[mk.alloc([128], F32, f"S{h}_{d}") for d in range(2)] for h in range(NH)]
    wab = mk.alloc([16, 4 * NH], BF16, "wab")
    dtb = mk.alloc([2 * NH], F32, "dtb")
    nA = mk.alloc([2 * NH], F32, "nA")
    gng = mk.alloc([1], F32, "gng")
    sm = [mk.alloc([2 * NH], F32, f"sm{i}") for i in range(4)]
    mk.dma("sp", G.cw.v, Wd["cw"])
    mk.dma("sp", dtb.v, Wd["dtb"])
    mk.dma("sp", nA.v, Wd["alog"])
    mk.dma("sp", gng.v, Wd["gng"])
    mk.dma("pool", wab.v, Wd["w_ab"].rearrange("(kc p) n -> p kc n", p=128))
    mk.I("act", "activation", out=nA.v, in_=nA.v, func=AF.Exp)
    mk.I("dve", "tensor_scalar", out=nA.v, in0=nA.v, scalar1=-1.0, scalar2=None, op0=ALU.mult)
    for h in range(NH):
        mk.I(PENG, "memset", G.OT[h].v.ap, 0.0, writes=[G.OT[h]])
        if pad:
            mk.I(PENG, "memset", G.ZT[h].v.ap, 0.0, writes=[G.ZT[h]])
        for c in range(3):
            mk.I(PENG, "memset", G.halo[h][c].v.ap, 0.0, writes=[G.halo[h][c]])
        for d in range(2):
            mk.dma("sp", G.S[h][d].v, s0_ap[d, h])
    mark_w = mk.mark()
    hT = mk.alloc([16, TTs], BF16, "hT")
    xT = mk.alloc([16, TTs], F32, "xTg")
    for it in range(ntile):
        t0 = it * TTs
        norm_mod_tile(mk, C, x_rows[t0:t0 + TTs, :], TTs, xT, hT, C.rstd, gm1, ci)
        for h in range(NH):
            wt = C.wt[C.wt_rr % len(C.wt)]
            C.wt_rr += 1
            load_weight_tile(mk, wt, Wd["w_g"][:, h * 512:(h + 1) * 512], 16, 512)
            for comp in range(4):
                b = mk.bank()
                for kc in range(16):
                    mk.I("pe", "matmul", out=b[:, 0:TTs], lhsT=wt[:, kc, comp * 128:(comp + 1) * 128], rhs=hT[:, kc, 0:TTs],
                         start=(kc == 0), stop=(kc == 15))
                if comp == 3:
                    mk.I("act", "activation", out=G.ZT[h][:, pad + t0:pad + t0 + TTs], in_=b[:, 0:TTs], func=AF.Silu)
                    continue
                R = C.R[C.R_rr % 2]
                C.R_rr += 1
                halo = G.halo[h][comp]
                mk.I("dve", "tensor_copy", out=R[:, 0:2], in_=halo.v)
                mk.I("act", "activation", out=R[:, 2:2 + TTs], in_=b[:, 0:TTs], func=AF.Copy)
                mk.I("dve", "tensor_copy", out=halo.v, in_=R[:, TTs:TTs + 2])
                conv_out(mk, C, G, h, comp, R, TTs, t0 - 1)
                if it == ntile - 1:
                    R2 = C.R[C.R_rr % 2]
                    C.R_rr += 1
                    mk.I("dve", "tensor_copy", out=R2[:, 0:2], in_=halo.v)
                    mk.I("dve", "memset", R2[:, 2:3].ap, 0.0, writes=[R2])
                    conv_out(mk, C, G, h, comp, R2, 1, T - 1)
        for s in range(TTs // 128 if GSTOP['v'] >= 1 else 0):
            c = (t0 + s * 128) // 128
            b = mk.bank()
            for kc in range(16):
                mk.I("pe", "matmul", out=b[:, 0:4 * NH], lhsT=hT[:, kc, s * 128:(s + 1) * 128], rhs=wab[:, kc, :],
                     start=(kc == 0), stop=(kc == 15))
            xs, ax, ee, rr_ = sm
            mk.I("dve", "tensor_tensor", out=xs.v, in0=b[:, 0:2 * NH], in1=dtb.v, op=ALU.add)
            mk.I("dve", "tensor_scalar", out=ax.v, in0=xs.v, scalar1=-1.0, scalar2=None, op0=ALU.mult)
            mk.I("dve", "tensor_tensor", out=ax.v, in0=ax.v, in1=xs.v, op=ALU.max)
            mk.I("act", "activation", out=ee.v, in_=ax.v, func=AF.Exp, scale=-1.0)
            mk.I("act", "activation", out=ee.v, in_=ee.v, func=AF.Ln, bias=C.oneb[:, 0:1], scale=1.0)
            mk.I("dve", "tensor_scalar", out=rr_.v, in0=xs.v, scalar1=0.0, scalar2=None, op0=ALU.max)
            mk.I("dve", "tensor_tensor", out=ee.v, in0=ee.v, in1=rr_.v, op=ALU.add)
            mk.I("dve", "tensor_tensor", out=G.Gt[:, c, :], in0=ee.v, in1=nA.v, op=ALU.mult)
            mk.I("act", "activation", out=G.Bt[:, c, :], in_=b[:, 2 * NH:4 * NH], func=AF.Sigmoid)
            mk.I("dve", "tensor_scalar", out=G.NBt[:, c, :], in0=G.Bt[:, c, :], scalar1=-1.0, scalar2=None, op0=ALU.mult)
    mk.release(mark_w)
    u = C.gu = Ctx()
    for nm in ("ktm", "Gb", "Dm", "DTm", "TT", "vb", "kbg", "kdec", "uu", "wT", "intraT", "qgT", "eGb", "vnew"):
        setattr(u, nm, [mk.alloc([128], F32, f"{nm}{k}") for k in range(NPAR)])
    u.P = [[mk.alloc([128], F32, f"P{k}{i}") for i in range(2)] for k in range(NPAR)]
    u.PT = [[mk.alloc([128], F32, f"PT{k}{i}") for i in range(2)] for k in range(NPAR)]
    u.Gc = [mk.alloc([1], F32, f"Gc{k}") for k in range(NPAR)]
    u.kTc = [mk.alloc([128], BF16, f"kTc{k}") for k in range(NPAR)]
    u.sc = [mk.alloc([8], F32, f"sc{k}") for k in range(NPAR)]
    pending = [(h, d, (step if d == 0 else nch - 1 - step)) for step in range(nch) for h in range(NH) for d in range(2)]
    active = []
    free = list(range(NPAR))
    while pending or active:
        while pending and free:
            h_, d_, c_ = pending.pop(0)
            k_ = free.pop(0)
            active.append((gdn_unit(mk, C, G, h_, d_, c_, k_), k_))
        for item in list(active):
            try:
                next(item[0])
            except StopIteration:
                active.remove(item)
                free.append(item[1])
    for h in range(NH):
        if state_out is not None:
            for d in range(2):
                mk.dma("sp", state_out[d, h], G.S[h][d].v)
        if win is not None:
            sel, dst = win
            for g in range(2):
                for hf in range(2):
                    ow = C.ct[0]
                    zw = C.ct[1]
                    for jj in range(4):
                        c0 = 1024 * jj + 512 * g + 257 * hf
                        if jj == 0:
                            mk.I("dve", "tensor_scalar", out=ow[:, 0:257], in0=G.OT[h][:, c0:c0 + 257], scalar1=sel[:, 0:1], scalar2=None, op0=ALU.mult)
                            mk.I("dve", "tensor_scalar", out=zw[:, 0:257], in0=G.ZT[h][:, c0:c0 + 257], scalar1=sel[:, 0:1], scalar2=None, op0=ALU.mult)
                        else:
                            mk.I("dve", "scalar_tensor_tensor", out=ow[:, 0:257], in0=G.OT[h][:, c0:c0 + 257], scalar=sel[:, jj:jj + 1],
                                 in1=ow[:, 0:257], op0=ALU.mult, op1=ALU.add)
                            mk.I("dve", "scalar_tensor_tensor", out=zw[:, 0:257], in0=G.ZT[h][:, c0:c0 + 257], scalar=sel[:, jj:jj + 1],
                                 in1=zw[:, 0:257], op0=ALU.mult, op1=ALU.add)
                    bcast_sum_rstd(mk, C, [(ow[:, 0:257], 128)], 257, 128.0, C.rstd)
                    mk.I("dve", "scalar_tensor_tensor", out=ow[:, 0:257], in0=ow[:, 0:257], scalar=gng[:, 0:1],
                         in1=C.rstd[:, 0:257], op0=ALU.mult, op1=ALU.mult)
                    mk.I("dve", "tensor_tensor", out=ow[:, 0:257], in0=ow[:, 0:257], in1=zw[:, 0:257], op=ALU.mult)
                    mk.dma("sp", dst[g, :, 257 * hf:257 * hf + 257], ow[:, 0:257])
            continue
        for it in range(ntile):
            t0 = it * TTs
            bcast_sum_rstd(mk, C, [(G.OT[h][:, t0:t0 + TTs], 128)], TTs, 128.0, C.rstd)
            o = C.ct[C.ct_rr % 2]
            C.ct_rr += 1
            mk.I("dve", "scalar_tensor_tensor", out=o[:, 0:TTs], in0=G.OT[h][:, t0:t0 + TTs], scalar=gng[:, 0:1],
                 in1=C.rstd[:, 0:TTs], op0=ALU.mult, op1=ALU.mult)
            mk.I("dve", "tensor_tensor", out=o[:, 0:TTs], in0=o[:, 0:TTs], in1=G.ZT[h][:, t0:t0 + TTs], op=ALU.mult)
            mk.dma("sp", mix_out[h, :, t0:t0 + TTs], o[:, 0:TTs])
    mk.release(mark)


def mla_phase(mk, C, x_rows, T, NH, ci, gm1, Wd, nctx, rope, mix_out, ckv_out, kpe_out):
    mark0 = mk.mark()
    TTs = min(256, T)
    ntile = T // TTs
    NK = T + nctx
    nkt = NK // 128
    scale = 192.0 ** -0.5
    gq = mk.alloc([4], F32, "gq")
    gkv = mk.alloc([4], F32, "gkv")
    wq = mk.alloc([4, NH * 192], BF16, "wq")
    wkv = mk.alloc([4, NH * 256], BF16, "wkv")
    rot = mk.alloc([64], F32, "rot")
    onesb = mk.alloc([128], BF16, "onesb")
    kmax2 = mk.alloc([NH], F32, "kmax2")
    KPET = mk.alloc([NK], BF16, "KPET")
    KN = [mk.alloc([NK], BF16, f"KN{h}") for h in range(NH)]
    V = [mk.alloc([nkt, 128], BF16, f"V{h}") for h in range(NH)]
    mk.dma("sp", gq.v, Wd["gq"])
    mk.dma("sp", gkv.v, Wd["gkv"])
    mk.dma("pool", wq.v, Wd["wq"].rearrange("(kc p) n -> p kc n", p=128))
    mk.dma("pool", wkv.v, Wd["wkv"].rearrange("(kc p) n -> p kc n", p=128))
    mk.dma("sp", rot[0:64, :], Wd["rot"])
    mk.I("dve", "memset", onesb.v.ap, 1.0, writes=[onesb])
    mark1 = mk.mark()
    CKVT = mk.alloc([4, NK], BF16, "CKVT")
    mark2 = mk.mark()

    def work_tiles():
        W_ = Ctx()
        W_.hT = mk.alloc([16, TTs], BF16, "hTm")
        W_.xT = mk.alloc([16, TTs], F32, "xTm")
        W_.raw = mk.alloc([4, TTs], F32, "rawm")
        W_.cs = [mk.alloc([TTs], F32, f"cs{i}") for i in range(2)]
        W_.rp = [mk.alloc([TTs], F32, f"rp{i}") for i in range(3)]
        return W_

    def proj_norm(W_, wcol0, gvec, dst_fn, f32_out=None):
        wt = C.wt[C.wt_rr % len(C.wt)]
        C.wt_rr += 1
        load_weight_tile(mk, wt, Wd["w_m"][:, wcol0:wcol0 + 512], 16, 512)
        for q in range(4):
            b = mk.bank()
            for kc in range(16):
                mk.I("pe", "matmul", out=b[:, 0:TTs], lhsT=wt[:, kc, q * 128:(q + 1) * 128], rhs=W_.hT[:, kc, 0:TTs],
                     start=(kc == 0), stop=(kc == 15))
            mk.I("act", "activation", out=W_.raw[:, q, :], in_=b[:, 0:TTs], func=AF.Copy)
        bcast_sum_rstd(mk, C, [(W_.raw[:, q, :], 128) for q in range(4)], TTs, 512.0, C.rstd)
        for q in range(4):
            if f32_out is not None:
                mk.I("dve", "scalar_tensor_tensor", out=W_.raw[:, q, :], in0=W_.raw[:, q, :], scalar=gvec[:, q:q + 1],
                     in1=C.rstd[:, 0:TTs], op0=ALU.mult, op1=ALU.mult)
                mk.I("act", "activation", out=dst_fn(q), in_=W_.raw[:, q, :], func=AF.Copy)
            else:
                mk.I("dve", "scalar_tensor_tensor", out=dst_fn(q), in0=W_.raw[:, q, :], scalar=gvec[:, q:q + 1],
                     in1=C.rstd[:, 0:TTs], op0=ALU.mult, op1=ALU.mult)

    def do_rope(W_, src_bank_view, n, t0, dst):
        x = W_.rp[0]
        mk.I("act", "activation", out=x[0:64, 0:n], in_=src_bank_view, func=AF.Copy)
        if not rope:
            mk.I("dve", "tensor_copy", out=dst, in_=x[0:64, 0:n])
            return
        mk.dma("sp", W_.cs[0][0:64, 0:n], Wd["cos"][:, t0:t0 + n])
        mk.dma("sp", W_.cs[1][0:64, 0:n], Wd["sin"][:, t0:t0 + n])
        b = mk.bank()
        mk.I("pe", "matmul", out=b[0:64, 0:n], lhsT=rot[0:64, :], rhs=x[0:64, 0:n], start=True, stop=True)
        mk.I("dve", "tensor_tensor", out=W_.rp[1][0:64, 0:n], in0=x[0:64, 0:n], in1=W_.cs[0][0:64, 0:n], op=ALU.mult)
        mk.I("dve", "tensor_tensor", out=W_.rp[2][0:64, 0:n], in0=b[0:64, 0:n], in1=W_.cs[1][0:64, 0:n], op=ALU.mult)
        mk.I("dve", "tensor_tensor", out=dst, in0=W_.rp[1][0:64, 0:n], in1=W_.rp[2][0:64, 0:n], op=ALU.add)

    W_ = work_tiles()
    for it in range(ntile):
        t0 = it * TTs
        norm_mod_tile(mk, C, x_rows[t0:t0 + TTs, :], TTs, W_.xT, W_.hT, C.rstd, gm1, ci)
        proj_norm(W_, 512, gkv, lambda q: CKVT[:, q, t0:t0 + TTs], f32_out=(ckv_out is not None) or True)
        if ckv_out is not None:
            for s in range(TTs // 128):
                ys = C.xstage[s % 2]
                b = mk.bank()
                for q in range(4):
                    mk.I("pe", "transpose", out=b[:, q * 128:(q + 1) * 128], in_=W_.raw[:, q, s * 128:(s + 1) * 128], identity=C.ident.v)
                mk.I("dve", "tensor_copy", out=ys[:, 0:512], in_=b[:, 0:512])
                mk.dma("sp", ckv_out[t0 + s * 128:t0 + (s + 1) * 128, :], ys[:, 0:512])
        wt = C.wt[C.wt_rr % len(C.wt)]
        C.wt_rr += 1
        load_weight_tile(mk, wt, Wd["w_m"][:, 1024:1088], 16, 64)
        b = mk.bank()
        for kc in range(16):
            mk.I("pe", "matmul", out=b[0:64, 0:TTs], lhsT=wt[:, kc, 0:64], rhs=W_.hT[:, kc, 0:TTs], start=(kc == 0), stop=(kc == 15))
        do_rope(W_, b[0:64, 0:TTs], TTs, t0, KPET[0:64, t0:t0 + TTs])
        if kpe_out is not None:
            for s in range(TTs // 128):
                ys = C.xstage[s % 2]
                b2 = mk.bank()
                mk.I("pe", "transpose", out=b2[:, 0:64], in_=W_.rp[0][0:64, s * 128:(s + 1) * 128], identity=C.ident[0:64, 0:64])
                mk.I("dve", "tensor_copy", out=ys[:, 0:64], in_=b2[:, 0:64])
                mk.dma("sp", kpe_out[t0 + s * 128:t0 + (s + 1) * 128, :], ys[:, 0:64])
    for s in range(nctx // 128):
        xs = C.xstage[s % 2]
        mk.dma("sp", xs[:, 0:512], Wd["cache_ckv"][s * 128:(s + 1) * 128, :])
        mk.dma("sp", xs[:, 512:576], Wd["cache_kpe"][s * 128:(s + 1) * 128, :])
        b = mk.bank()
        for q in range(4):
            mk.I("pe", "transpose", out=b[:, q * 128:(q + 1) * 128], in_=xs[:, q * 128:(q + 1) * 128], identity=C.ident.v)
        mk.I("dve", "tensor_copy", out=CKVT[:, :, T + s * 128:T + (s + 1) * 128], in_=b[:, 0:512].rearrange("p (q t) -> p q t", q=4))
        b2 = mk.bank()
        mk.I("pe", "transpose", out=b2[0:64, 0:128], in_=xs[:, 512:576], identity=C.ident.v)
        mk.I("act", "activation", out=KPET[0:64, T + s * 128:T + (s + 1) * 128], in_=b2[0:64, 0:128], func=AF.Copy)
    mk.release(mark2)
    sqk = mk.alloc([512], F32, "sqk")
    kss = mk.alloc([512], F32, "kss")
    for h in range(NH):
        for k0 in range(0, NK, 512):
            n = min(512, NK - k0)
            b = mk.bank()
            for kc in range(4):
                mk.I("pe", "matmul", out=b[:, 0:n], lhsT=wkv[:, kc, h * 256:h * 256 + 128], rhs=CKVT[:, kc, k0:k0 + n],
                     start=(kc == 0), stop=(kc == 3))
            mk.I("act", "activation", out=KN[h][:, k0:k0 + n], in_=b[:, 0:n], func=AF.Copy)
            bs = mk.bank()
            mk.I("act", "activation", out=sqk[:, 0:n], in_=b[:, 0:n], func=AF.Square)
            mk.I("pe", "matmul", out=bs[0:1, 0:n], lhsT=C.ones[:, 0:1], rhs=sqk[:, 0:n], start=True, stop=False)
            mk.I("act", "activation", out=kss[0:64, 0:n], in_=KPET[0:64, k0:k0 + n], func=AF.Square)
            mk.I("pe", "matmul", out=bs[0:1, 0:n], lhsT=C.ones[0:64, 0:1], rhs=kss[0:64, 0:n], start=False, stop=True)
            if k0 == 0:
                mk.I("dve", "tensor_reduce", out=kmax2[0:1, h:h + 1], in_=bs[0:1, 0:n], axis=AX.X, op=ALU.max)
            else:
                mk.I("dve", "tensor_reduce", out=kss[0:1, 0:1], in_=bs[0:1, 0:n], axis=AX.X, op=ALU.max)
                mk.I("dve", "tensor_tensor", out=kmax2[0:1, h:h + 1], in0=kmax2[0:1, h:h + 1], in1=kss[0:1, 0:1], op=ALU.max)
        for kt in range(nkt):
            b = mk.bank()
            for kc in range(4):
                mk.I("pe", "matmul", out=b[:, 0:128], lhsT=CKVT[:, kc, kt * 128:(kt + 1) * 128], rhs=wkv[:, kc, h * 256 + 128:h * 256 + 256],
                     start=(kc == 0), stop=(kc == 3))
            mk.I("dve", "tensor_copy", out=V[h][:, kt, :], in_=b[:, 0:128])
    mk.release(mark1)
    W_ = work_tiles()
    QN = mk.alloc([4, TTs], BF16, "QN")
    qn = mk.alloc([TTs], BF16, "qn")
    qr = mk.alloc([TTs], BF16, "qr")
    negm = mk.alloc([TTs], BF16, "negm")
    mrow = mk.alloc([TTs], F32, "mrow")
    PTb = [mk.alloc([TTs], BF16, f"PT{i}") for i in range(3)]
    rs = mk.alloc([TTs], F32, "rs")
    oo = mk.alloc([TTs], F32, "oo")
    for it in range(ntile):
        t0 = it * TTs
        norm_mod_tile(mk, C, x_rows[t0:t0 + TTs, :], TTs, W_.xT, W_.hT, C.rstd, gm1, ci)
        proj_norm(W_, 0, gq, lambda q: QN[:, q, :])
        for h in range(NH):
            b = mk.bank()
            for kc in range(4):
                mk.I("pe", "matmul", out=b[:, 0:TTs], lhsT=wq[:, kc, h * 192:h * 192 + 128], rhs=QN[:, kc, :], start=(kc == 0), stop=(kc == 3))
            mk.I("act", "activation", out=qn.v, in_=b[:, 0:TTs], func=AF.Copy)
            mk.I("act", "activation", out=C.sq[0][:, 0:TTs], in_=b[:, 0:TTs], func=AF.Square)
            b2 = mk.bank()
            for kc in range(4):
                mk.I("pe", "matmul", out=b2[0:64, 0:TTs], lhsT=wq[:, kc, h * 192 + 128:h * 192 + 192], rhs=QN[:, kc, :], start=(kc == 0), stop=(kc == 3))
            mk.I("act", "activation", out=C.sq[1][0:64, 0:TTs], in_=b2[0:64, 0:TTs], func=AF.Square)
            do_rope(W_, b2[0:64, 0:TTs], TTs, t0, qr[0:64, :])
            bm = mk.bank()
            mk.I("pe", "matmul", out=bm[0:1, 0:TTs], lhsT=C.ones[:, 0:1], rhs=C.sq[0][:, 0:TTs], start=True, stop=False)
            mk.I("pe", "matmul", out=bm[0:1, 0:TTs], lhsT=C.ones[0:64, 0:1], rhs=C.sq[1][0:64, 0:TTs], start=False, stop=True)
            mk.I("act", "activation", out=mrow[0:1, :], in_=bm[0:1, 0:TTs], func=AF.Sqrt, scale=kmax2[0:1, h:h + 1])
            mk.I("dve", "tensor_scalar", out=negm[0:1, :], in0=mrow[0:1, :], scalar1=-1.0, scalar2=None, op0=ALU.mult)
            bo = mk.reserve()
            bsum = mk.reserve()
            for kt in range(nkt):
                ks = slice(kt * 128, (kt + 1) * 128)
                bs = mk.bank()
                mk.I("pe", "matmul", out=bs[:, 0:TTs], lhsT=KN[h][:, ks], rhs=qn.v, start=True, stop=False)
                mk.I("pe", "matmul", out=bs[:, 0:TTs], lhsT=KPET[0:64, ks], rhs=qr[0:64, :], start=False, stop=False)
                mk.I("pe", "matmul", out=bs[:, 0:TTs], lhsT=onesb[0:1, :], rhs=negm[0:1, :], start=False, stop=True)
                PT = PTb[kt % 3]
                mk.I("act", "activation", out=PT.v, in_=bs[:, 0:TTs], func=AF.Exp, scale=scale)
                mk.I("pe", "matmul", out=bo[:, 0:TTs], lhsT=V[h][:, kt, :], rhs=PT.v, start=(kt == 0), stop=(kt == nkt - 1))
                mk.I("pe", "matmul", out=bsum[:, 0:TTs], lhsT=onesb.v, rhs=PT.v, start=(kt == 0), stop=(kt == nkt - 1))
            mk.I("dve", "reciprocal", out=rs.v, in_=bsum[:, 0:TTs])
            mk.I("dve", "tensor_tensor", out=oo.v, in0=bo[:, 0:TTs], in1=rs.v, op=ALU.mult)
            mk.dma("sp", mix_out[h, :, t0:t0 + TTs], oo.v)
            mk.unreserve(bo)
            mk.unreserve(bsum)
    mk.release(mark0)


def alloc_common(mk, C, D):
    C.ident = mk.alloc([128], F32, "ident")
    C.identb = mk.alloc([128], BF16, "identb")
    C.ones = mk.alloc([128], F32, "ones")
    C.epsb = mk.alloc([1], F32, "epsb")
    C.oneb = mk.alloc([1], F32, "oneb")
    C.tri = [mk.alloc([128], F32, f"tri{d}") for d in range(2)]
    C.ms = [mk.alloc([128], F32, f"ms{d}") for d in range(2)]
    C.scond = mk.alloc([16, 2], BF16, "scond")
    C.condf = mk.alloc([16, 2], F32, "condf")
    C.b_ada = mk.alloc([96], F32, "b_ada")
    C.mod = mk.alloc([96, 2], F32, "mod")
    C.xstage = [mk.alloc([2048], F32, f"xs{i}") for i in range(2)]
    C.sq = [mk.alloc([514], F32, f"sq{i}") for i in range(2)]
    C.tmpn = mk.alloc([514], F32, "tmpn")
    C.rstd = mk.alloc([514], F32, "rstd")
    C.wt = [mk.alloc([16, 512], BF16, f"wt{i}") for i in range(2)]
    C.wt_rr = 0
    C.tmpx = [mk.alloc([514], F32, f"tmpx{i}") for i in range(2)]
    C.ct = [mk.alloc([512], F32, f"ct{i}") for i in range(2)]
    C.ct_rr = 0
    C.R = [mk.alloc([516], F32, f"R{i}") for i in range(2)]
    C.R_rr = 0
    mk.dma("sp", C.ident.v, D["ident"])
    mk.dma("sp", C.tri[0].v, D["tri0"])
    mk.dma("sp", C.tri[1].v, D["tri1"])
    mk.dma("sp", C.ms[0].v, D["ms0"])
    mk.dma("sp", C.ms[1].v, D["ms1"])
    mk.I("dve", "tensor_copy", out=C.identb.v, in_=C.ident.v)
    mk.I("dve", "memset", C.ones.v.ap, 1.0, writes=[C.ones])
    mk.I("dve", "memset", C.epsb.v.ap, EPS, writes=[C.epsb])
    mk.I("dve", "memset", C.oneb.v.ap, 1.0, writes=[C.oneb])
    mk.dma("sp", C.condf.v, D["condT"])
    mk.dma("sp", C.b_ada.v, D["b_adaT"])
    mk.I("act", "activation", out=C.scond.v, in_=C.condf.v, func=AF.Silu)


def build_phase1(parts=('pg', 'pm', 'sg', 'sm')):
    nc = bass.Bass("TRN2", target_bir_lowering=False)
    DEBUG["nc"] = nc
    DEBUG["done"] = set()
    D = {}

    def din(name, shape):
        D[name] = nc.dram_tensor(name, list(shape), F32, kind="ExternalInput").ap()

    def dout(name, shape):
        D[name] = nc.dram_tensor(name, list(shape), F32, kind="ExternalOutput").ap()

    for name, shape in (("xp", [2, 256, 2048]), ("xs", [4096, 2048]), ("condT", [128, 16, 2]), ("w_ada", [2048, 12288]),
                        ("b_adaT", [128, 96]), ("norm1T", [128, 16]), ("ident", [128, 128]), ("tri0", [128, 128]),
                        ("tri1", [128, 128]), ("ms0", [128, 128]), ("ms1", [128, 128]), ("s0p", [2, 8, 128, 128]),
                        ("s0s", [2, 2, 1, 128, 128]), ("w_g_p", [2048, 4096]), ("w_ab_p", [2048, 32]), ("cw_p", [128, 24, 3]),
                        ("dtb_p", [128, 16]), ("alog_p", [128, 16]), ("gng", [128, 1]), ("w_g_s", [2, 2048, 512]),
                        ("w_ab_s", [2, 2048, 4]), ("cw_s", [2, 128, 3, 3]), ("dtb_s", [2, 128, 2]), ("alog_s", [2, 128, 2]),
                        ("w_m", [2048, 1088]), ("gq", [128, 4]), ("gkv", [128, 4]), ("wq_p", [512, 1536]), ("wkv_p", [512, 2048]),
                        ("wq_s", [512, 384]), ("wkv_s", [512, 512]), ("cos", [64, 4096]), ("sin", [64, 4096]), ("rot", [64, 64]),
                        ("cache_ckv", [256, 512]), ("cache_kpe", [256, 64])):
        din(name, shape)
    for name, shape in (("mixp", [2, 16, 128, 256]), ("mixs", [4, 128, 4096]), ("new_state", [2, 2, 8, 128, 128]),
                        ("new_ckv", [2, 256, 512]), ("new_kpe", [2, 256, 64])):
        dout(name, shape)
    with ExitStack() as st:
        mk = MK(nc, st)
        C = Ctx()
        alloc_common(mk, C, D)
        n1g = mk.alloc([16], F32, "n1g")
        gm1 = mk.alloc([16, 2], F32, "gm1")
        mk.dma("sp", n1g.v, D["norm1T"])
        adaln(mk, C, D["w_ada"], [0, 1], 2)
        mk.I("dve", "tensor_scalar", out=gm1.v, in0=C.mod[:, 16:32, :], scalar1=1.0, scalar2=None, op0=ALU.add)
        mk.I("dve", "tensor_tensor", out=gm1.v, in0=gm1.v,
             in1=n1g.v.rearrange("p (c o) -> p c o", o=1).bc([128, 16, 2]), op=ALU.mult)
        Wp = dict(w_g=D["w_g_p"], w_ab=D["w_ab_p"], cw=D["cw_p"], dtb=D["dtb_p"], alog=D["alog_p"], gng=D["gng"])
        Wmp = dict(w_m=D["w_m"], gq=D["gq"], gkv=D["gkv"], wq=D["wq_p"], wkv=D["wkv_p"], rot=D["rot"])
        for s in range(2):
            if 'pg' in parts:
                gdn_phase(mk, C, D["xp"][s], 256, 8, 0, gm1, Wp, D["s0p"], D["mixp"][s, 0:8], D["new_state"][s])
            if 'pm' in parts:
                mla_phase(mk, C, D["xp"][s], 256, 8, 0, gm1, Wmp, 0, False, D["mixp"][s, 8:16], D["new_ckv"][s], D["new_kpe"][s])
        for lh in range(2):
            Ws = dict(w_g=D["w_g_s"][lh], w_ab=D["w_ab_s"][lh], cw=D["cw_s"][lh], dtb=D["dtb_s"][lh], alog=D["alog_s"][lh], gng=D["gng"])
            if 'sg' in parts:
                gdn_phase(mk, C, D["xs"], 4096, 1, 1, gm1, Ws, D["s0s"][lh], D["mixs"][lh:lh + 1], None)
        Wms = dict(w_m=D["w_m"], gq=D["gq"], gkv=D["gkv"], wq=D["wq_s"], wkv=D["wkv_s"], rot=D["rot"], cos=D["cos"], sin=D["sin"],
                   cache_ckv=D["cache_ckv"], cache_kpe=D["cache_kpe"])
        if 'sm' in parts:
            mla_phase(mk, C, D["xs"], 4096, 2, 1, gm1, Wms, 256, True, D["mixs"][2:4], None, None)
        mk.finalize()
        print("phase1 instructions:", mk.n_inst, {e: len(mk.ops[e]) for e in ENGS})
    return nc


def rope_tables():
    rows = 4096 // 64
    row = np.repeat(np.arange(rows, dtype=np.float32), 64)
    col = np.tile(np.arange(64, dtype=np.float32), rows)
    inv = (np.float32(10000.0) ** (-np.arange(16, dtype=np.float32) / np.float32(16))).astype(np.float32)
    ang = np.concatenate([row[:, None] * inv, col[:, None] * inv], axis=-1).astype(np.float32)
    cos, sin = np.cos(ang).astype(np.float32), np.sin(ang).astype(np.float32)
    cos2 = np.ascontiguousarray(np.concatenate([cos, cos], axis=1).T)
    sin2 = np.ascontiguousarray(np.concatenate([sin, sin], axis=1).T)
    rot = np.zeros((64, 64), np.float32)
    for m in range(32):
        rot[m + 32, m] = -1.0
        rot[m, m + 32] = 1.0
    return cos2, sin2, rot


def phase1_inputs(core, I):
    b, j = core // 4, core % 4
    w_in = I["w_in"][0]
    cw = I["gdn_conv_w"][0]
    dtb = I["gdn_dt_bias"][0]
    alog = I["gdn_a_log"][0]

    def wg(h):
        return np.concatenate([w_in[:, c0 + h * 128:c0 + (h + 1) * 128] for c0 in (0, 1024, 2048, 3072)], axis=1)

    def cwh(h):
        return np.stack([cw[:, comp * 1024 + h * 128:comp * 1024 + (h + 1) * 128].T for comp in range(3)], axis=1)

    cos2, sin2, rot = rope_tables()
    tri0 = np.triu(np.ones((128, 128), np.float32))
    tri1 = np.tril(np.ones((128, 128), np.float32))
    ms0 = np.tril(np.ones((128, 128), np.float32), -1)
    ms1 = np.triu(np.ones((128, 128), np.float32), 1)
    cond = np.stack([I["c_ctx"], I["c"][b]], axis=0)
    hg = [2 * j, 2 * j + 1]
    wq = I["mla_w_q_b"][0]
    wkv = I["mla_w_kv_b"][0]
    d = dict(
        xp=np.ascontiguousarray(I["x_prompt"][2 * core:2 * core + 2]), xs=np.ascontiguousarray(I["x_sample"][b]),
        condT=np.ascontiguousarray(cond.reshape(2, 16, 128).transpose(2, 1, 0)), w_ada=I["w_ada"][0],
        b_adaT=colT(I["b_ada"][0], 96), norm1T=colT(I["norm1_g"][0], 16), ident=np.eye(128, dtype=np.float32),
        tri0=tri0, tri1=tri1, ms0=ms0, ms1=ms1, s0p=np.zeros((2, 8, 128, 128), np.float32),
        s0s=np.ascontiguousarray(np.stack([I["state_gdn"][b, 0, :, h:h + 1] for h in hg], axis=0)),
        w_g_p=np.concatenate([wg(h) for h in range(8)], axis=1), w_ab_p=np.ascontiguousarray(w_in[:, 4096:4128]),
        cw_p=np.ascontiguousarray(np.concatenate([cwh(h) for h in range(8)], axis=1)),
        dtb_p=np.ascontiguousarray(np.broadcast_to(dtb.reshape(1, 16), (128, 16))),
        alog_p=np.ascontiguousarray(np.broadcast_to(alog.reshape(1, 16), (128, 16))),
        gng=np.ascontiguousarray(I["gdn_norm_g"][0].reshape(128, 1)),
        w_g_s=np.stack([wg(h) for h in hg], axis=0),
        w_ab_s=np.stack([w_in[:, [4096 + h, 4104 + h, 4112 + h, 4120 + h]] for h in hg], axis=0),
        cw_s=np.stack([cwh(h) for h in hg], axis=0),
        dtb_s=np.stack([np.broadcast_to(dtb[:, h].reshape(1, 2), (128, 2)) for h in hg], axis=0),
        alog_s=np.stack([np.broadcast_to(alog[:, h].reshape(1, 2), (128, 2)) for h in hg], axis=0),
        w_m=np.ascontiguousarray(w_in[:, 4128:5216]), gq=colT(I["mla_q_norm_g"][0], 4), gkv=colT(I["mla_kv_norm_g"][0], 4),
        wq_p=wq, wkv_p=wkv, wq_s=np.ascontiguousarray(wq[:, hg[0] * 192:(hg[1] + 1) * 192]),
        wkv_s=np.ascontiguousarray(wkv[:, hg[0] * 256:(hg[1] + 1) * 256]), cos=cos2, sin=sin2, rot=rot,
        cache_ckv=np.ascontiguousarray(I["cache_mla_ckv"][b, 0]), cache_kpe=np.ascontiguousarray(I["cache_mla_kpe"][b, 0]))
    return {k: np.ascontiguousarray(np.asarray(v, np.float32)) for k, v in d.items()}


_NC = {}


def kernel_twolaunch(**I):
    I = {k: np.asarray(v) for k, v in I.items()}
    if "p1" not in _NC:
        _NC["p1"] = build_phase1()
        _NC["p2"] = build_phase2()
    r1 = run_bass_kernel_spmd(_NC["p1"], [phase1_inputs(c, I) for c in range(NCORES)], core_ids=list(range(NCORES))).results
    mix_p = np.zeros((16, 256, 2048), np.float32)
    mix_s = np.zeros((2, 4096, 2048), np.float32)
    new_state = np.zeros((16, 1, 2, 8, 128, 128), np.float32)
    new_ckv = np.zeros((16, 1, 256, 512), np.float32)
    new_kpe = np.zeros((16, 1, 256, 64), np.float32)
    for c in range(NCORES):
        b, j = c // 4, c % 4
        r = r1[c]
        for s in range(2):
            mix_p[2 * c + s] = r["mixp"][s].transpose(2, 0, 1).reshape(256, 2048)
            new_state[2 * c + s, 0] = r["new_state"][s]
            new_ckv[2 * c + s, 0] = r["new_ckv"][s]
            new_kpe[2 * c + s, 0] = r["new_kpe"][s]
        for lh in range(2):
            hgl = 2 * j + lh
            mix_s[b, :, hgl * 128:(hgl + 1) * 128] = r["mixs"][lh].T
            mix_s[b, :, 1024 + hgl * 128:1024 + (hgl + 1) * 128] = r["mixs"][2 + lh].T
    r2 = run_bass_kernel_spmd(_NC["p2"], [phase2_inputs(c, I["x_prompt"], I["x_sample"], mix_p, mix_s, I["c"], I["c_ctx"],
                                                        I["w_ada"][0], I["b_ada"][0], I["w_out"][0], I["norm2_g"][0],
                                                        I["w_up"][0], I["ffn_conv_w"][0], I["ffn_conv_b"][0], I["w_down"][0],
                                                        I["final_norm_g"]) for c in range(NCORES)],
                              core_ids=list(range(NCORES))).results
    yp = np.zeros((16, 256, 2048), np.float32)
    ys = np.zeros((2, 4096, 2048), np.float32)
    for c in range(NCORES):
        b, j = c // 4, c % 4
        y = r2[c]["y"]
        yp[2 * c:2 * c + 2] = y[0].reshape(2, 256, 2048)
        ys[b, 1024 * j:1024 * j + 512] = y[1]
        ys[b, 1024 * j + 512:1024 * j + 1024] = y[2]
    return (yp, ys, new_state, new_ckv, new_kpe)


def mla_fused(mk, C, x_rows, T, ci, gm1, Wd, nctx, xq, dst):
    NH = 8
    mark0 = mk.mark()
    TTs = 256
    TQ = 257
    ntile = T // TTs
    NK = T + nctx
    nkt = NK // 128
    scale = 192.0 ** -0.5
    gq = mk.alloc([4], F32, "gq")
    gkv = mk.alloc([4], F32, "gkv")
    wq = mk.alloc([4, NH * 192], BF16, "wq")
    wkv = mk.alloc([4, NH * 256], BF16, "wkv")
    rot = mk.alloc([64], F32, "rot")
    onesb = mk.alloc([128], BF16, "onesb")
    kmax2 = mk.alloc([NH], F32, "kmax2")
    KPET = mk.alloc([NK], BF16, "KPET")
    QN = mk.alloc([4, 4 * TQ], BF16, "QNall")
    mk.dma("sp", gq.v, Wd["gq"])
    mk.dma("sp", gkv.v, Wd["gkv"])
    mk.dma("pool", wq.v, Wd["wq"].rearrange("(kc p) n -> p kc n", p=128))
    mk.dma("pool", wkv.v, Wd["wkv"].rearrange("(kc p) n -> p kc n", p=128))
    mk.dma("sp", rot[0:64, :], Wd["rot"])
    mk.I("dve", "memset", onesb.v.ap, 1.0, writes=[onesb])
    CKVT = mk.alloc([4, NK], BF16, "CKVT")
    mark2 = mk.mark()
    hT = mk.alloc([16, TQ], BF16, "hTm")
    xT = mk.alloc([16, TQ], F32, "xTm")
    raw = mk.alloc([4, TQ], F32, "rawm")
    cs = [mk.alloc([TQ], F32, f"cs{i}") for i in range(2)]
    rp = [mk.alloc([TQ], F32, f"rp{i}") for i in range(3)]

    def proj_norm(W, wcol0, gvec, dst_fn):
        wt = C.wt[C.wt_rr % len(C.wt)]
        C.wt_rr += 1
        load_weight_tile(mk, wt, Wd["w_m"][:, wcol0:wcol0 + 512], 16, 512)
        for q in range(4):
            b = mk.bank()
            for kc in range(16):
                mk.I("pe", "matmul", out=b[:, 0:W], lhsT=wt[:, kc, q * 128:(q + 1) * 128], rhs=hT[:, kc, 0:W], start=(kc == 0), stop=(kc == 15))
            mk.I("act", "activation", out=raw[:, q, 0:W], in_=b[:, 0:W], func=AF.Copy)
        bcast_sum_rstd(mk, C, [(raw[:, q, 0:W], 128) for q in range(4)], W, 512.0, C.rstd)
        for q in range(4):
            mk.I("dve", "scalar_tensor_tensor", out=dst_fn(q), in0=raw[:, q, 0:W], scalar=gvec[:, q:q + 1], in1=C.rstd[:, 0:W], op0=ALU.mult, op1=ALU.mult)

    def do_rope(src, n, cos_ap, sin_ap, dstv):
        x = rp[0]
        mk.I("act", "activation", out=x[0:64, 0:n], in_=src, func=AF.Copy)
        mk.dma("sp", cs[0][0:64, 0:n], cos_ap)
        mk.dma("sp", cs[1][0:64, 0:n], sin_ap)
        b = mk.bank()
        mk.I("pe", "matmul", out=b[0:64, 0:n], lhsT=rot[0:64, :], rhs=x[0:64, 0:n], start=True, stop=True)
        mk.I("dve", "tensor_tensor", out=rp[1][0:64, 0:n], in0=x[0:64, 0:n], in1=cs[0][0:64, 0:n], op=ALU.mult)
        mk.I("dve", "tensor_tensor", out=rp[2][0:64, 0:n], in0=b[0:64, 0:n], in1=cs[1][0:64, 0:n], op=ALU.mult)
        mk.I("dve", "tensor_tensor", out=dstv, in0=rp[1][0:64, 0:n], in1=rp[2][0:64, 0:n], op=ALU.add)

    for it in range(ntile):
        t0 = it * TTs
        norm_mod_tile(mk, C, x_rows[t0:t0 + TTs, :], TTs, xT, hT, C.rstd, gm1, ci)
        proj_norm(TTs, 512, gkv, lambda q: CKVT[:, q, t0:t0 + TTs])
        wt = C.wt[C.wt_rr % len(C.wt)]
        C.wt_rr += 1
        load_weight_tile(mk, wt, Wd["w_m"][:, 1024:1088], 16, 64)
        b = mk.bank()
        for kc in range(16):
            mk.I("pe", "matmul", out=b[0:64, 0:TTs], lhsT=wt[:, kc, 0:64], rhs=hT[:, kc, 0:TTs], start=(kc == 0), stop=(kc == 15))
        do_rope(b[0:64, 0:TTs], TTs, Wd["cos"][:, t0:t0 + TTs], Wd["sin"][:, t0:t0 + TTs], KPET[0:64, t0:t0 + TTs])
    for s in range(nctx // 128):
        xs = C.xstage[s % 2]
        mk.dma("sp", xs[:, 0:512], Wd["cache_ckv"][s * 128:(s + 1) * 128, :])
        mk.dma("sp", xs[:, 512:576], Wd["cache_kpe"][s * 128:(s + 1) * 128, :])
        b = mk.bank()
        for q in range(4):
            mk.I("pe", "transpose", out=b[:, q * 128:(q + 1) * 128], in_=xs[:, q * 128:(q + 1) * 128], identity=C.ident.v)
        mk.I("dve", "tensor_copy", out=CKVT[:, :, T + s * 128:T + (s + 1) * 128], in_=b[:, 0:512].rearrange("p (q t) -> p q t", q=4))
        b2 = mk.bank()
        mk.I("pe", "transpose", out=b2[0:64, 0:128], in_=xs[:, 512:576], identity=C.ident.v)
        mk.I("act", "activation", out=KPET[0:64, T + s * 128:T + (s + 1) * 128], in_=b2[0:64, 0:128], func=AF.Copy)
    for slot in range(4):
        g, hf = slot // 2, slot % 2
        norm_mod_tile(mk, C, xq[g, hf * TQ:(hf + 1) * TQ, :], TQ, xT, hT, C.rstd, gm1, ci)
        proj_norm(TQ, 0, gq, lambda q: QN[:, q, slot * TQ:(slot + 1) * TQ])
    mk.release(mark2)
    for h in range(NH):
        markh = mk.mark()
        KN = mk.alloc([NK], BF16, "KNh")
        V = mk.alloc([nkt, 128], BF16, "Vh")
        sqk = mk.alloc([512], F32, "sqk")
        kss = mk.alloc([512], F32, "kss")
        qn = mk.alloc([TQ], BF16, "qn")
        qr = mk.alloc([TQ], BF16, "qr")
        negm = mk.alloc([TQ], BF16, "negm")
        mrow = mk.alloc([TQ], F32, "mrow")
        PTb = [mk.alloc([TQ], BF16, f"PT{i}") for i in range(3)]
        rs = mk.alloc([TQ], F32, "rs")
        oo = mk.alloc([TQ], F32, "oo")
        cs = [mk.alloc([TQ], F32, f"csq{i}") for i in range(2)]
        rp = [mk.alloc([TQ], F32, f"rpq{i}") for i in range(3)]
        for k0 in range(0, NK, 512):
            n = min(512, NK - k0)
            b = mk.bank()
            for kc in range(4):
                mk.I("pe", "matmul", out=b[:, 0:n], lhsT=wkv[:, kc, h * 256:h * 256 + 128], rhs=CKVT[:, kc, k0:k0 + n], start=(kc == 0), stop=(kc == 3))
            mk.I("act", "activation", out=KN[:, k0:k0 + n], in_=b[:, 0:n], func=AF.Copy)
            bs = mk.bank()
            mk.I("act", "activation", out=sqk[:, 0:n], in_=b[:, 0:n], func=AF.Square)
            mk.I("pe", "matmul", out=bs[0:1, 0:n], lhsT=C.ones[:, 0:1], rhs=sqk[:, 0:n], start=True, stop=False)
            mk.I("act", "activation", out=kss[0:64, 0:n], in_=KPET[0:64, k0:k0 + n], func=AF.Square)
            mk.I("pe", "matmul", out=bs[0:1, 0:n], lhsT=C.ones[0:64, 0:1], rhs=kss[0:64, 0:n], start=False, stop=True)
            if k0 == 0:
                mk.I("dve", "tensor_reduce", out=kmax2[0:1, h:h + 1], in_=bs[0:1, 0:n], axis=AX.X, op=ALU.max)
            else:
                mk.I("dve", "tensor_reduce", out=kss[0:1, 0:1], in_=bs[0:1, 0:n], axis=AX.X, op=ALU.max)
                mk.I("dve", "tensor_tensor", out=kmax2[0:1, h:h + 1], in0=kmax2[0:1, h:h + 1], in1=kss[0:1, 0:1], op=ALU.max)
        for kt in range(nkt):
            b = mk.bank()
            for kc in range(4):
                mk.I("pe", "matmul", out=b[:, 0:128], lhsT=CKVT[:, kc, kt * 128:(kt + 1) * 128], rhs=wkv[:, kc, h * 256 + 128:h * 256 + 256], start=(kc == 0), stop=(kc == 3))
            mk.I("dve", "tensor_copy", out=V[:, kt, :], in_=b[:, 0:128])
        for slot in range(4):
            g, hf = slot // 2, slot % 2
            qs = slice(slot * TQ, (slot + 1) * TQ)
            b = mk.bank()
            for kc in range(4):
                mk.I("pe", "matmul", out=b[:, 0:TQ], lhsT=wq[:, kc, h * 192:h * 192 + 128], rhs=QN[:, kc, qs], start=(kc == 0), stop=(kc == 3))
            mk.I("act", "activation", out=qn.v, in_=b[:, 0:TQ], func=AF.Copy)
            mk.I("act", "activation", out=C.sq[0][:, 0:TQ], in_=b[:, 0:TQ], func=AF.Square)
            b2 = mk.bank()
            for kc in range(4):
                mk.I("pe", "matmul", out=b2[0:64, 0:TQ], lhsT=wq[:, kc, h * 192 + 128:h * 192 + 192], rhs=QN[:, kc, qs], start=(kc == 0), stop=(kc == 3))
            mk.I("act", "activation", out=C.sq[1][0:64, 0:TQ], in_=b2[0:64, 0:TQ], func=AF.Square)
            do_rope(b2[0:64, 0:TQ], TQ, Wd["cosq"][:, qs], Wd["sinq"][:, qs], qr[0:64, :])
            bm = mk.bank()
            mk.I("pe", "matmul", out=bm[0:1, 0:TQ], lhsT=C.ones[:, 0:1], rhs=C.sq[0][:, 0:TQ], start=True, stop=False)
            mk.I("pe", "matmul", out=bm[0:1, 0:TQ], lhsT=C.ones[0:64, 0:1], rhs=C.sq[1][0:64, 0:TQ], start=False, stop=True)
            mk.I("act", "activation", out=mrow[0:1, :], in_=bm[0:1, 0:TQ], func=AF.Sqrt, scale=kmax2[0:1, h:h + 1])
            mk.I("dve", "tensor_scalar", out=negm[0:1, :], in0=mrow[0:1, :], scalar1=-1.0, scalar2=None, op0=ALU.mult)
            bo = mk.reserve()
            bsum = mk.reserve()
            for kt in range(nkt):
                ks = slice(kt * 128, (kt + 1) * 128)
                bs = mk.bank()
                mk.I("pe", "matmul", out=bs[:, 0:TQ], lhsT=KN[:, ks], rhs=qn.v, start=True, stop=False)
                mk.I("pe", "matmul", out=bs[:, 0:TQ], lhsT=KPET[0:64, ks], rhs=qr[0:64, :], start=False, stop=False)
                mk.I("pe", "matmul", out=bs[:, 0:TQ], lhsT=onesb[0:1, :], rhs=negm[0:1, :], start=False, stop=True)
                PT = PTb[kt % 3]
                mk.I("act", "activation", out=PT.v, in_=bs[:, 0:TQ], func=AF.Exp, scale=scale)
                mk.I("pe", "matmul", out=bo[:, 0:TQ], lhsT=V[:, kt, :], rhs=PT.v, start=(kt == 0), stop=(kt == nkt - 1))
                mk.I("pe", "matmul", out=bsum[:, 0:TQ], lhsT=onesb.v, rhs=PT.v, start=(kt == 0), stop=(kt == nkt - 1))
            mk.I("dve", "reciprocal", out=rs.v, in_=bsum[:, 0:TQ])
            mk.I("dve", "tensor_tensor", out=oo.v, in0=bo[:, 0:TQ], in1=rs.v, op=ALU.mult)
            mk.dma("sp", dst[g, h, :, hf * TQ:(hf + 1) * TQ], oo.v)
            mk.unreserve(bo)
            mk.unreserve(bsum)
        mk.release(markh)
    mk.release(mark0)


def build_fused():
    nc = bass.Bass("TRN2", target_bir_lowering=False)
    DEBUG["nc"] = nc
    DEBUG["done"] = set()
    D = {}

    def din(name, shape):
        D[name] = nc.dram_tensor(name, list(shape), F32, kind="ExternalInput").ap()

    def dout(name, shape):
        D[name] = nc.dram_tensor(name, list(shape), F32, kind="ExternalOutput").ap()

    for name, shape in (("xp", [2, 256, 2048]), ("xs", [4096, 2048]), ("x2", [3, 514, 2048]), ("condT", [128, 16, 2]),
                        ("w_ada", [2048, 12288]), ("b_adaT", [128, 96]), ("norm1T", [128, 16]), ("ident", [128, 128]),
                        ("tri0", [128, 128]), ("tri1", [128, 128]), ("ms0", [128, 128]), ("ms1", [128, 128]),
                        ("s0p", [2, 8, 128, 128]), ("s0h", [8, 2, 1, 128, 128]), ("w_g_p", [2048, 4096]), ("w_ab_p", [2048, 32]),
                        ("cw_p", [128, 24, 3]), ("dtb_p", [128, 16]), ("alog_p", [128, 16]), ("gng", [128, 1]),
                        ("w_ab_h", [8, 2048, 4]), ("dtb_h", [8, 128, 2]), ("alog_h", [8, 128, 2]),
                        ("w_m", [2048, 1088]), ("gq", [128, 4]), ("gkv", [128, 4]), ("wq_p", [512, 1536]), ("wkv_p", [512, 2048]),
                        ("cos", [64, 4096]), ("sin", [64, 4096]), ("cosq", [64, 1028]), ("sinq", [64, 1028]), ("rot", [64, 64]),
                        ("cache_ckv", [256, 512]), ("cache_kpe", [256, 64]), ("sel", [128, 4]), ("hmask", [128, 4]),
                        ("w_out", [2048, 2048]), ("norm2T", [128, 16]), ("w_up", [2048, 11264]), ("fcwT", [128, 88, 3]),
                        ("fcbT", [128, 88]), ("w_down", [5632, 2048]), ("fnormT", [128, 16])):
        din(name, shape)
    for name, shape in (("y", [3, 512, 2048]), ("new_state", [2, 2, 8, 128, 128]), ("new_ckv", [2, 256, 512]), ("new_kpe", [2, 256, 64])):
        dout(name, shape)
    mixp_d = nc.dram_tensor("mixp_d", [2, 16, 128, 256], F32).ap()
    mixs_d = nc.dram_tensor("mixs_d", [2, 16, 128, 514], F32).ap()
    with ExitStack() as st:
        mk = MK(nc, st)
        C = Ctx()
        alloc_common(mk, C, D)
        n1g = mk.alloc([16], F32, "n1g")
        gm1 = mk.alloc([16, 2], F32, "gm1")
        gm2 = mk.alloc([16, 2], F32, "gm2")
        n2g = mk.alloc([16], F32, "n2g")
        fng = mk.alloc([16], F32, "fng")
        fcw = mk.alloc([88, 3], F32, "fcw")
        fcb = mk.alloc([88], F32, "fcb")
        hm = mk.alloc([4], F32, "hm")
        sel = mk.alloc([4], F32, "sel")
        for t, nm in ((n1g, "norm1T"), (n2g, "norm2T"), (fng, "fnormT"), (fcw, "fcwT"), (fcb, "fcbT"), (hm, "hmask"), (sel, "sel")):
            mk.dma("sp", t.v, D[nm])
        adaln(mk, C, D["w_ada"], [0, 1, 2, 3, 4, 5], 2)
        for (gm, ng, lo) in ((gm1, n1g, 16), (gm2, n2g, 64)):
            mk.I("dve", "tensor_scalar", out=gm.v, in0=C.mod[:, lo:lo + 16, :], scalar1=1.0, scalar2=None, op0=ALU.add)
            mk.I("dve", "tensor_tensor", out=gm.v, in0=gm.v, in1=ng.v.rearrange("p (c o) -> p c o", o=1).bc([128, 16, 2]), op=ALU.mult)
        Wp = dict(w_g=D["w_g_p"], w_ab=D["w_ab_p"], cw=D["cw_p"], dtb=D["dtb_p"], alog=D["alog_p"], gng=D["gng"])
        Wmp = dict(w_m=D["w_m"], gq=D["gq"], gkv=D["gkv"], wq=D["wq_p"], wkv=D["wkv_p"], rot=D["rot"])
        for s in range(2):
            gdn_phase(mk, C, D["xp"][s], 256, 8, 0, gm1, Wp, D["s0p"], mixp_d[s, 0:8], D["new_state"][s])
            mla_phase(mk, C, D["xp"][s], 256, 8, 0, gm1, Wmp, 0, False, mixp_d[s, 8:16], D["new_ckv"][s], D["new_kpe"][s])
        for h in range(8):
            Ws = dict(w_g=D["w_g_p"][:, h * 512:(h + 1) * 512], w_ab=D["w_ab_h"][h], cw=D["cw_p"][:, 3 * h:3 * h + 3, :],
                      dtb=D["dtb_h"][h], alog=D["alog_h"][h], gng=D["gng"])
            gdn_phase(mk, C, D["xs"], 4096, 1, 1, gm1, Ws, D["s0h"][h], None, None, win=(sel, mixs_d[:, h]))
        Wms = dict(w_m=D["w_m"], gq=D["gq"], gkv=D["gkv"], wq=D["wq_p"], wkv=D["wkv_p"], rot=D["rot"], cos=D["cos"], sin=D["sin"],
                   cosq=D["cosq"], sinq=D["sinq"], cache_ckv=D["cache_ckv"], cache_kpe=D["cache_kpe"])
        mla_fused(mk, C, D["xs"], 4096, 1, gm1, Wms, 256, D["x2"][1:3], mixs_d[:, 8:16])
        mk.barrier()
        xT = mk.alloc([16, 514], F32, "xT")
        actin = mk.alloc([16, 514], BF16, "actin")
        actT = mk.alloc([44, 512], BF16, "actT")
        Ra = [mk.alloc([514], F32, f"Ra{i}") for i in range(2)]
        Rg = [mk.alloc([514], F32, f"Rg{i}") for i in range(2)]
        ta = [mk.alloc([512], F32, f"ta{i}") for i in range(2)]
        tg = [mk.alloc([512], F32, f"tg{i}") for i in range(2)]

        def load_mix(ti, actin_, W):
            if ti == 0:
                for s in range(2):
                    mk.dma("pool", actin_[:, :, s * 256:(s + 1) * 256], mixp_d[s].rearrange("c p w -> p c w"))
            else:
                mk.dma("pool", actin_[:, :, 0:W], mixs_d[ti - 1].rearrange("c p w -> p c w"))

        phase2_tiles(mk, C, D["x2"], D["y"], load_mix, n2g, fng, fcw, fcb, hm, gm2, xT, actin, actT, Ra, Rg, ta, tg, C.tmpx, C.rstd,
                     D["w_out"], D["w_up"], D["w_down"])
        mk.finalize()
        print("fused instructions:", mk.n_inst, {e: len(mk.ops[e]) for e in ENGS}, "sbuf words", mk.top)
    return nc


def fused_inputs(core, I):
    b, j = core // 4, core % 4
    d = phase1_inputs(core, I)
    for k in ("s0s", "w_g_s", "w_ab_s", "cw_s", "dtb_s", "alog_s", "wq_s", "wkv_s"):
        d.pop(k)
    p2 = phase2_inputs(core, I["x_prompt"], I["x_sample"], np.zeros((16, 256, 2048), np.float32), np.zeros((2, 4096, 2048), np.float32),
                       I["c"], I["c_ctx"], I["w_ada"][0], I["b_ada"][0], I["w_out"][0], I["norm2_g"][0], I["w_up"][0],
                       I["ffn_conv_w"][0], I["ffn_conv_b"][0], I["w_down"][0], I["final_norm_g"])
    for k in ("x2", "hmask", "w_out", "norm2T", "w_up", "fcwT", "fcbT", "w_down", "fnormT"):
        d[k] = p2[k]
    w_in = I["w_in"][0]
    dtb = I["gdn_dt_bias"][0]
    alog = I["gdn_a_log"][0]
    d["s0h"] = np.stack([I["state_gdn"][b, 0, :, h:h + 1] for h in range(8)], axis=0)
    d["w_ab_h"] = np.stack([w_in[:, [4096 + h, 4104 + h, 4112 + h, 4120 + h]] for h in range(8)], axis=0)
    d["dtb_h"] = np.stack([np.broadcast_to(dtb[:, h].reshape(1, 2), (128, 2)) for h in range(8)], axis=0)
    d["alog_h"] = np.stack([np.broadcast_to(alog[:, h].reshape(1, 2), (128, 2)) for h in range(8)], axis=0)
    cos2, sin2 = d["cos"], d["sin"]
    cosq = np.zeros((64, 1028), np.float32)
    sinq = np.zeros((64, 1028), np.float32)
    for g in range(2):
        lo = 1024 * j + 512 * g - 1
        a, e = max(lo, 0), min(lo + 514, 4096)
        cosq[:, g * 514 + a - lo:g * 514 + e - lo] = cos2[:, a:e]
        sinq[:, g * 514 + a - lo:g * 514 + e - lo] = sin2[:, a:e]
    d["cosq"], d["sinq"] = cosq, sinq
    sel = np.zeros((128, 4), np.float32)
    sel[:, j] = 1.0
    d["sel"] = sel
    return {k: np.ascontiguousarray(np.asarray(v, np.float32)) for k, v in d.items()}


def kernel_fused(**I):
    I = {k: np.asarray(v) for k, v in I.items()}
    if "f" not in _NC:
        _NC["f"] = build_fused()
    r = run_bass_kernel_spmd(_NC["f"], [fused_inputs(c, I) for c in range(NCORES)], core_ids=list(range(NCORES))).results
    yp = np.zeros((16, 256, 2048), np.float32)
    ys = np.zeros((2, 4096, 2048), np.float32)
    new_state = np.zeros((16, 1, 2, 8, 128, 128), np.float32)
    new_ckv = np.zeros((16, 1, 256, 512), np.float32)
    new_kpe = np.zeros((16, 1, 256, 64), np.float32)
    for c in range(NCORES):
        b, j = c // 4, c % 4
        y = r[c]["y"]
        yp[2 * c:2 * c + 2] = y[0].reshape(2, 256, 2048)
        ys[b, 1024 * j:1024 * j + 512] = y[1]
        ys[b, 1024 * j + 512:1024 * j + 1024] = y[2]
        for s in range(2):
            new_state[2 * c + s, 0] = r[c]["new_state"][s]
            new_ckv[2 * c + s, 0] = r[c]["new_ckv"][s]
            new_kpe[2 * c + s, 0] = r[c]["new_kpe"][s]
    return (yp, ys, new_state, new_ckv, new_kpe)


def kernel(**inputs):
    return kernel_fused(**inputs)
```

```python
import numpy as np
from contextlib import ExitStack
import concourse.bass as bass
import concourse.mybir as mybir
from concourse.bass_utils import run_bass_kernel_spmd

F32 = mybir.dt.float32
BF16 = mybir.dt.bfloat16
ALU = mybir.AluOpType
AF = mybir.ActivationFunctionType
AX = mybir.AxisListType
ENGS = ("pe", "act", "dve", "pool", "sp")
NCORES = 8
PENG = "dve"
GSTOP = {"v": 99}
NPAR = 4
EPS = 1e-6


class View:
    __slots__ = ("tile", "ap", "gen")

    def __init__(self, tile, ap, gen=None):
        self.tile = tile
        self.ap = ap
        self.gen = gen

    def __getitem__(self, idx):
        return View(self.tile, self.ap[idx], self.gen)

    def bc(self, shape):
        return View(self.tile, self.ap.to_broadcast(list(shape)), self.gen)

    def rearrange(self, s, **kw):
        return View(self.tile, self.ap.rearrange(s, **kw), self.gen)


class BankRef:
    def __init__(self, tile, gen):
        self.tile = tile
        self.gen = gen

    def __getitem__(self, idx):
        return View(self.tile, self.tile.h[idx], self.gen)


class Tile:
    def __init__(self, ap, name):
        self.h = ap
        self.name = name
        self.last_write = None
        self.reads = []
        self.dma_sem = None
        self.dma_count = 0

    def __getitem__(self, idx):
        return View(self, self.h[idx])

    @property
    def v(self):
        return View(self, self.h)


class MK:
    ARENA_F32 = 50688
    N_DMA_SEMS = 16

    def __init__(self, nc, stack):
        self.nc = nc
        self.stack = stack
        self.ops = {e: [] for e in ENGS}
        self.seq = {e: 0 for e in ENGS}
        self.sem = {e: stack.enter_context(nc.semaphore("sem_" + e)) for e in ("pe", "act", "dve", "pool")}
        self.waited = {e: {} for e in ENGS}
        self.dma_tiles = []
        self.n_inst = 0
        self.arena = stack.enter_context(nc.sbuf_tensor("arena", [128, self.ARENA_F32], F32))
        self.top = 0
        self.banks = [Tile(stack.enter_context(nc.psum_tensor(f"bank{i}", [128, 512], F32)), f"bank{i}")
                      for i in range(8)]
        for b in self.banks:
            b.is_bank = True
        self.bank_rr = 0
        self.reserved = []
        self.dma_sem_pool = {}
        self.dma_sem_rr = {}
        self.tcount = 0

    def alloc(self, free_shape, dtype=F32, name=None, parts=128):
        n = int(np.prod(free_shape))
        words = n if dtype == F32 else (n + 1) // 2
        words = (words + 7) // 8 * 8
        assert self.top + words <= self.ARENA_F32, f"SBUF arena overflow allocating {name} {free_shape}"
        ap = self.arena[0:parts, self.top:self.top + words]
        self.top += words
        if dtype != F32:
            ap = ap.bitcast(dtype)
        ap = ap[:, 0:n]
        if len(free_shape) == 2:
            ap = ap.rearrange("p (a b) -> p a b", a=free_shape[0])
        elif len(free_shape) == 3:
            ap = ap.rearrange("p (a b c) -> p a b c", a=free_shape[0], b=free_shape[1])
        self.tcount += 1
        return Tile(ap, name or f"t{self.tcount}")

    def mark(self):
        return self.top

    def release(self, mark):
        self.barrier()
        self.top = mark

    def bank(self):
        while True:
            b = self.banks[self.bank_rr % 8]
            self.bank_rr += 1
            if b not in self.reserved:
                b.gen = getattr(b, "gen", 0) + 1
                return BankRef(b, b.gen)

    def reserve(self):
        b = self.bank()
        self.reserved.append(b.tile)
        return b

    def unreserve(self, b):
        self.reserved.remove(b.tile)

    def _resolve(self, ev):
        if ev[0] == "dma":
            t = ev[1]
            return (t.dma_sem, t.dma_count, None)
        return (self.sem[ev[0]], ev[1], ev[0])

    def _wait(self, eng, ev):
        sem, val, src = self._resolve(ev)
        if src == eng and eng == "pe":
            return
        key = id(sem)
        if self.waited[eng].get(key, 0) >= val:
            return
        self.waited[eng][key] = val
        self.ops[eng].append(("w", sem, val))

    def _deps(self, eng, reads, writes):
        for t in reads:
            if t.last_write is not None:
                self._wait(eng, t.last_write)
            if getattr(t, "is_bank", False):
                for ev in t.reads:
                    if ev[0] != eng:
                        self._wait(eng, ev)
        for t in writes:
            if t.last_write is not None:
                self._wait(eng, t.last_write)
            for ev in t.reads:
                self._wait(eng, ev)

    def _commit(self, ev, reads, writes):
        for t in writes:
            t.last_write = ev
            t.reads = []
        for t in reads:
            if t in writes:
                continue
            t.reads.append(ev)
            if len(t.reads) > 24:
                best = {}
                for e in t.reads:
                    k = e[0] if e[0] != "dma" else ("dma", id(e[1]))
                    if k not in best or (e[0] != "dma" and e[1] > best[k][1]):
                        best[k] = e
                t.reads = list(best.values())

    def I(self, eng, meth, *args, reads=(), writes=(), **kw):
        rd, wr = list(reads), list(writes)
        real = {}
        for k, v in kw.items():
            if isinstance(v, View):
                assert v.gen is None or v.gen == v.tile.gen, f"stale PSUM bank handle used by {meth} ({k})"
                (wr if k in ("out", "accum_out") else rd).append(v.tile)
                real[k] = v.ap
            else:
                real[k] = v
        rargs = []
        for v in args:
            if isinstance(v, View):
                rd.append(v.tile)
                rargs.append(v.ap)
            else:
                rargs.append(v)
        self._deps(eng, rd, wr)
        self.seq[eng] += 1
        ev = (eng, self.seq[eng])
        self.ops[eng].append(("i", meth, rargs, real))
        self._commit(ev, rd, wr)
        self.n_inst += 1
        return ev

    def dma(self, q, out, in_, **kw):
        rd, wr = [], []
        st = None
        if isinstance(out, View):
            wr.append(out.tile)
            o = out.ap
            st = out.tile
        else:
            o = out
        if isinstance(in_, View):
            rd.append(in_.tile)
            i = in_.ap
            if st is None:
                st = in_.tile
        else:
            i = in_
        if st.dma_sem is None:
            st.dma_sem = {}
        if q not in st.dma_sem:
            pool = self.dma_sem_pool.setdefault(q, [])
            if len(pool) < self.N_DMA_SEMS:
                ds = Tile(None, "dsem_%s%d" % (q, len(pool)))
                ds.dma_sem = self.stack.enter_context(self.nc.semaphore("ds_%s%d" % (q, len(pool))))
                pool.append(ds)
                self.dma_tiles.append(ds)
            rr = self.dma_sem_rr.get(q, 0)
            self.dma_sem_rr[q] = rr + 1
            st.dma_sem[q] = pool[rr % self.N_DMA_SEMS]
        dsem = st.dma_sem[q]
        self._deps(q, rd, wr)
        if dsem.dma_count:
            self._wait(q, ("dma", dsem))
        dsem.dma_count += 16
        self.ops[q].append(("d", o, i, kw, dsem.dma_sem))
        ev = ("dma", dsem)
        self._commit(ev, rd, wr)
        self.n_inst += 1
        return ev

    def barrier(self):
        for e in ENGS:
            for src in ("pe", "act", "dve", "pool"):
                if src != e and self.seq[src] > 0:
                    self._wait(e, (src, self.seq[src]))
            for t in self.dma_tiles:
                if t.dma_count:
                    self._wait(e, ("dma", t))

    def finalize(self):
        for t in self.dma_tiles:
            self.ops["sp"].append(("w", t.dma_sem, t.dma_count))
        sem = self.sem

        def run(eng_name):
            def f(e):
                for op in self.ops[eng_name]:
                    if op[0] == "w":
                        e.wait_ge(op[1], op[2])
                    elif op[0] == "i":
                        ins = getattr(e, op[1])(*op[2], **op[3])
                        if eng_name in sem:
                            ins.then_inc(sem[eng_name], 1)
                    else:
                        e.dma_start(out=op[1], in_=op[2], **op[3]).then_inc(op[4], 16)
            return f

        with self.nc.Block() as block:
            block.tensor(run("pe"))
            block.scalar(run("act"))
            block.vector(run("dve"))
            block.gpsimd(run("pool"))
            block.sync(run("sp"))


class Ctx:
    pass


DEBUG = {"on": False, "nc": None, "done": set()}


def dump(mk, name, view, shape):
    if not DEBUG["on"] or name in DEBUG["done"]:
        return
    DEBUG["done"].add(name)
    ap = DEBUG["nc"].dram_tensor("dbg_" + name, list(shape), F32, kind="ExternalOutput").ap()
    stg = mk.alloc(list(shape[1:]), F32, "dbgs_" + name, parts=shape[0]) if False else None
    mk.dma("sp", ap, view)


def halves(W):
    if W <= 512:
        return [(0, W)]
    h = (W + 1) // 2
    return [(0, h), (h, W - h)]


def mm_group(mk, W, steps):
    outs = []
    for (c0, n) in halves(W):
        b = mk.bank()
        for i, (lhsT, rhs_fn) in enumerate(steps):
            mk.I("pe", "matmul", out=b[:, 0:n], lhsT=lhsT, rhs=rhs_fn(c0, n),
                 start=(i == 0), stop=(i == len(steps) - 1))
        outs.append((b, c0, n))
    return outs


def load_weight_tile(mk, wt, src_ap, KC, ncols):
    mk.dma("pool", wt[:, 0:KC, 0:ncols], src_ap.rearrange("(kc p) n -> p kc n", p=128))


def load_xT(mk, C, x_rows, W, xT, eng_rr):
    nsub = (W + 127) // 128
    for s in range(nsub):
        r0 = s * 128
        n = min(128, W - r0)
        xs = C.xstage[s % 2]
        mk.dma("sp", xs[0:n, :], x_rows[r0:r0 + n, :])
        for g in range(4):
            b = mk.bank()
            for q in range(4):
                c = g * 4 + q
                mk.I("pe", "transpose", out=b[:, q * 128:q * 128 + n], in_=xs[0:n, c * 128:(c + 1) * 128],
                     identity=C.ident[0:n, 0:n])
            src = b[:, 0:512].rearrange("p (q t) -> p q t", q=4)[:, :, 0:n]
            dst = xT[:, g * 4:(g + 1) * 4, r0:r0 + n]
            if (s * 4 + g) % 2 == 0:
                mk.I("dve", "tensor_copy", out=dst, in_=src)
            else:
                mk.I("act", "activation", out=dst, in_=src, func=AF.Copy)


def rms_rstd(mk, C, XT, nch, W, col0, dim, rstd):
    pieces = halves(W)
    banks = [mk.bank() for _ in pieces]
    use_b = hasattr(C, "sqb")
    for c in range(nch):
        if use_b:
            sq = C.sqb[C.sqb_rr % 4]
            C.sqb_rr += 1
            ones = C.onesb16
        else:
            sq = C.sq[c % 2]
            ones = C.ones
        mk.I("act", "activation", out=sq[:, 0:W], in_=XT[:, c, col0:col0 + W], func=AF.Square)
        for (b, (c0, n)) in zip(banks, pieces):
            mk.I("pe", "matmul", out=b[:, 0:n], lhsT=ones.v, rhs=sq[:, c0:c0 + n], start=(c == 0), stop=(c == nch - 1))
    for (b, (c0, n)) in zip(banks, pieces):
        mk.I("act", "activation", out=C.tmpn[:, c0:c0 + n], in_=b[:, 0:n], func=AF.Sqrt, scale=1.0 / dim, bias=C.epsb[:, 0:1])
        mk.I("dve", "reciprocal", out=rstd[:, c0:c0 + n], in_=C.tmpn[:, c0:c0 + n])


def adaln(mk, C, w_ada, which_list, ncond):
    b = mk.bank()
    for wh in which_list:
        for blk in range(4):
            col0 = wh * 2048 + blk * 512
            wt = C.wt[C.wt_rr % len(C.wt)]
            C.wt_rr += 1
            load_weight_tile(mk, wt, w_ada[:, col0:col0 + 512], 16, 512)
            for q in range(4):
                cc = wh * 16 + blk * 4 + q
                for kc in range(16):
                    mk.I("pe", "matmul", out=b[:, cc * ncond:(cc + 1) * ncond], lhsT=wt[:, kc, q * 128:(q + 1) * 128],
                         rhs=C.scond[:, kc, 0:ncond], start=(kc == 0), stop=(kc == 15))
    for wh in which_list:
        sl = slice(wh * 16, (wh + 1) * 16)
        mk.I("dve", "tensor_tensor", out=C.mod[:, sl, :],
             in0=b[:, wh * 16 * ncond:(wh + 1) * 16 * ncond].rearrange("p (c n) -> p c n", n=ncond),
             in1=C.b_ada[:, sl].rearrange("p (c o) -> p c o", o=1).bc([128, 16, ncond]), op=ALU.add)


P2_TILES = [dict(W=512, cond=0, segs=[(0, 256, 0), (256, 256, 0)], out0=0),
            dict(W=514, cond=1, segs=[(0, 514, 1)], out0=1),
            dict(W=514, cond=1, segs=[(0, 514, 1)], out0=1)]


def phase2_tiles(mk, C, x2, y, load_mix, n2g, fng, fcw, fcb, hm, gm2, xT, actin, actT, Ra, Rg, ta, tg, tmpx, rstd,
                 w_out, w_up, w_down):
    for ti, T in enumerate(P2_TILES):
        W, ci, out0 = T["W"], T["cond"], T["out0"]
        load_xT(mk, C, x2[ti], W, xT, 0)
        load_mix(ti, actin, W)
        for blk in range(4):
            wt = C.wt[C.wt_rr % len(C.wt)]
            C.wt_rr += 1
            load_weight_tile(mk, wt, w_out[:, blk * 512:(blk + 1) * 512], 16, 512)
            for q in range(4):
                cc = blk * 4 + q
                outs = mm_group(mk, W, [(wt[:, kc, q * 128:(q + 1) * 128],
                                        (lambda c0, n, kc=kc: actin[:, kc, c0:c0 + n])) for kc in range(16)])
                for (b, c0, n) in outs:
                    mk.I("dve", "scalar_tensor_tensor", out=xT[:, cc, c0:c0 + n], in0=b[:, 0:n],
                         scalar=C.mod[:, 32 + cc, ci:ci + 1], in1=xT[:, cc, c0:c0 + n], op0=ALU.mult, op1=ALU.add)
        rms_rstd(mk, C, xT, 16, W, 0, 2048.0, rstd)
        for cc in range(16):
            tx = tmpx[cc % 2]
            mk.I("dve", "scalar_tensor_tensor", out=tx[:, 0:W], in0=xT[:, cc, 0:W], scalar=gm2[:, cc, ci:ci + 1],
                 in1=rstd[:, 0:W], op0=ALU.mult, op1=ALU.mult)
            mk.I("act", "activation", out=actin[:, cc, 0:W], in_=tx[:, 0:W], func=AF.Identity,
                 bias=C.mod[:, 48 + cc, ci:ci + 1], scale=1.0)
        for jb in range(11):
            wa = C.wt[C.wt_rr % len(C.wt)]
            C.wt_rr += 1
            load_weight_tile(mk, wa, w_up[:, jb * 512:(jb + 1) * 512], 16, 512)
            wg = C.wt[C.wt_rr % len(C.wt)]
            C.wt_rr += 1
            load_weight_tile(mk, wg, w_up[:, 5632 + jb * 512:5632 + (jb + 1) * 512], 16, 512)
            for q in range(4):
                j = jb * 4 + q
                res = []
                for (wtile, R, chunk) in ((wa, Ra[j % 2], j), (wg, Rg[j % 2], 44 + j)):
                    outs = mm_group(mk, W, [(wtile[:, kc, q * 128:(q + 1) * 128],
                                            (lambda c0, n, kc=kc: actin[:, kc, c0:c0 + n])) for kc in range(16)])
                    for (b, c0, n) in outs:
                        mk.I("act", "activation", out=R[:, c0:c0 + n], in_=b[:, 0:n], func=AF.Copy)
                    res.append((R, chunk))
                for (R, chunk), tt in ((res[0], ta[j % 2]), (res[1], tg[j % 2])):
                    for (s0, L, halo) in T["segs"]:
                        if halo:
                            mk.I("dve", "tensor_scalar", out=R[:, s0:s0 + 1], in0=R[:, s0:s0 + 1],
                                 scalar1=hm[:, 2 * (ti - 1):2 * (ti - 1) + 1], scalar2=None, op0=ALU.mult)
                            mk.I("dve", "tensor_scalar", out=R[:, s0 + L - 1:s0 + L], in0=R[:, s0 + L - 1:s0 + L],
                                 scalar1=hm[:, 2 * (ti - 1) + 1:2 * (ti - 1) + 2], scalar2=None, op0=ALU.mult)
                            o0, n = s0 + 1, L - 2
                            mk.I("act", "activation", out=tt[:, 0:n], in_=R[:, o0:o0 + n], func=AF.Identity,
                                 scale=fcw[:, chunk, 1:2], bias=fcb[:, chunk:chunk + 1])
                            mk.I("dve", "scalar_tensor_tensor", out=tt[:, 0:n], in0=R[:, o0 - 1:o0 - 1 + n],
                                 scalar=fcw[:, chunk, 0:1], in1=tt[:, 0:n], op0=ALU.mult, op1=ALU.add)
                            mk.I("dve", "scalar_tensor_tensor", out=tt[:, 0:n], in0=R[:, o0 + 1:o0 + 1 + n],
                                 scalar=fcw[:, chunk, 2:3], in1=tt[:, 0:n], op0=ALU.mult, op1=ALU.add)
                        else:
                            mk.I("act", "activation", out=tt[:, s0:s0 + L], in_=R[:, s0:s0 + L], func=AF.Identity,
                                 scale=fcw[:, chunk, 1:2], bias=fcb[:, chunk:chunk + 1])
                            mk.I("dve", "scalar_tensor_tensor", out=tt[:, s0 + 1:s0 + L], in0=R[:, s0:s0 + L - 1],
                                 scalar=fcw[:, chunk, 0:1], in1=tt[:, s0 + 1:s0 + L], op0=ALU.mult, op1=ALU.add)
                            mk.I("dve", "scalar_tensor_tensor", out=tt[:, s0:s0 + L - 1], in0=R[:, s0 + 1:s0 + L],
                                 scalar=fcw[:, chunk, 2:3], in1=tt[:, s0:s0 + L - 1], op0=ALU.mult, op1=ALU.add)
                mk.I("act", "activation", out=ta[j % 2].v, in_=ta[j % 2].v, func=AF.Silu)
                mk.I("dve", "tensor_tensor", out=actT[:, j, :], in0=ta[j % 2].v, in1=tg[j % 2].v, op=ALU.mult)
        for cc in range(16):
            wt = C.wt[C.wt_rr % len(C.wt)]
            C.wt_rr += 1
            wv = wt.v.rearrange("p a b -> p (a b)")[:, 0:44 * 128].rearrange("p (k n) -> p k n", k=44)
            mk.dma("pool", wv, w_down[:, cc * 128:(cc + 1) * 128].rearrange("(kc p) n -> p kc n", p=128))
            b = mk.bank()
            for j in range(44):
                mk.I("pe", "matmul", out=b[:, 0:512], lhsT=wv[:, j, :], rhs=actT[:, j, :], start=(j == 0), stop=(j == 43))
            mk.I("dve", "scalar_tensor_tensor", out=xT[:, cc, out0:out0 + 512], in0=b[:, 0:512],
                 scalar=C.mod[:, 80 + cc, ci:ci + 1], in1=xT[:, cc, out0:out0 + 512], op0=ALU.mult, op1=ALU.add)
        rms_rstd(mk, C, xT, 16, 512, out0, 2048.0, rstd)
        for cc in range(16):
            mk.I("dve", "scalar_tensor_tensor", out=xT[:, cc, out0:out0 + 512], in0=xT[:, cc, out0:out0 + 512],
                 scalar=fng[:, cc:cc + 1], in1=rstd[:, 0:512], op0=ALU.mult, op1=ALU.mult)
        for s in range(4):
            ys = C.xstage[s % 2]
            for g in range(4):
                b = mk.bank()
                for q in range(4):
                    c = g * 4 + q
                    mk.I("pe", "transpose", out=b[:, q * 128:(q + 1) * 128],
                         in_=xT[:, c, out0 + s * 128:out0 + (s + 1) * 128], identity=C.ident.v)
                if g % 2 == 0:
                    mk.I("dve", "tensor_copy", out=ys[:, g * 512:(g + 1) * 512], in_=b[:, 0:512])
                else:
                    mk.I("act", "activation", out=ys[:, g * 512:(g + 1) * 512], in_=b[:, 0:512], func=AF.Copy)
            mk.dma("sp", y[ti, s * 128:(s + 1) * 128, :], ys.v)


def build_phase2():
    nc = bass.Bass("TRN2", target_bir_lowering=False)
    D = {}

    def din(name, shape, dt=F32):
        D[name] = nc.dram_tensor(name, list(shape), dt, kind="ExternalInput").ap()
        return D[name]

    x2 = din("x2", [3, 514, 2048])
    mix = din("mix", [3, 16, 128, 514])
    condT = din("condT", [128, 16, 2])
    hmask = din("hmask", [128, 4])
    w_ada = din("w_ada", [2048, 12288])
    b_adaT = din("b_adaT", [128, 96])
    w_out = din("w_out", [2048, 2048])
    norm2T = din("norm2T", [128, 16])
    w_up = din("w_up", [2048, 11264])
    fcwT = din("fcwT", [128, 88, 3])
    fcbT = din("fcbT", [128, 88])
    w_down = din("w_down", [5632, 2048])
    fnormT = din("fnormT", [128, 16])
    identD = din("ident", [128, 128])
    y = nc.dram_tensor("y", [3, 512, 2048], F32, kind="ExternalOutput").ap()

    with ExitStack() as st:
        mk = MK(nc, st)
        C = Ctx()
        C.ident = mk.alloc([128], F32, "ident")
        C.ones = mk.alloc([128], F32, "ones")
        C.epsb = mk.alloc([1], F32, "epsb")
        C.scond = mk.alloc([16, 2], BF16, "scond")
        condf = mk.alloc([16, 2], F32, "condf")
        C.b_ada = mk.alloc([96], F32, "b_ada")
        C.mod = mk.alloc([96, 2], F32, "mod")
        n2g = mk.alloc([16], F32, "n2g")
        fng = mk.alloc([16], F32, "fng")
        fcw = mk.alloc([88, 3], F32, "fcw")
        fcb = mk.alloc([88], F32, "fcb")
        hm = mk.alloc([4], F32, "hm")
        gm2 = mk.alloc([16, 2], F32, "gm2")
        C.xstage = [mk.alloc([2048], F32, f"xs{i}") for i in range(2)]
        C.sq = [mk.alloc([514], F32, f"sq{i}") for i in range(2)]
        C.tmpn = mk.alloc([514], F32, "tmpn")
        rstd = mk.alloc([514], F32, "rstd")
        C.wt = [mk.alloc([16, 512], BF16, f"wt{i}") for i in range(3)]
        C.wt_rr = 0
        xT = mk.alloc([16, 514], F32, "xT")
        actin = mk.alloc([16, 514], BF16, "actin")
        actT = mk.alloc([44, 512], BF16, "actT")
        Ra = [mk.alloc([514], F32, f"Ra{i}") for i in range(2)]
        Rg = [mk.alloc([514], F32, f"Rg{i}") for i in range(2)]
        ta = [mk.alloc([512], F32, f"ta{i}") for i in range(2)]
        tg = [mk.alloc([512], F32, f"tg{i}") for i in range(2)]
        tmpx = [mk.alloc([514], F32, f"tmpx{i}") for i in range(2)]

        mk.dma("sp", C.ident.v, identD)
        mk.I("dve", "memset", C.ones.v.ap, 1.0, writes=[C.ones])
        mk.I("dve", "memset", C.epsb.v.ap, EPS, writes=[C.epsb])
        mk.dma("sp", condf.v, condT)
        mk.dma("sp", C.b_ada.v, b_adaT)
        mk.dma("sp", n2g.v, norm2T)
        mk.dma("sp", fng.v, fnormT)
        mk.dma("sp", fcw.v, fcwT)
        mk.dma("sp", fcb.v, fcbT)
        mk.dma("sp", hm.v, hmask)
        mk.I("act", "activation", out=C.scond.v, in_=condf.v, func=AF.Silu)
        adaln(mk, C, w_ada, [2, 3, 4, 5], 2)
        mk.I("dve", "tensor_scalar", out=gm2.v, in0=C.mod[:, 64:80, :], scalar1=1.0, scalar2=None, op0=ALU.add)
        mk.I("dve", "tensor_tensor", out=gm2.v, in0=gm2.v,
             in1=n2g.v.rearrange("p (c o) -> p c o", o=1).bc([128, 16, 2]), op=ALU.mult)

        phase2_tiles(mk, C, x2, y, lambda ti, actin, W: mk.dma("pool", actin[:, :, 0:W], mix[ti, :, :, 0:W].rearrange("c p w -> p c w")),
                     n2g, fng, fcw, fcb, hm, gm2, xT, actin, actT, Ra, Rg, ta, tg, tmpx, rstd, w_out, w_up, w_down)
        mk.finalize()
        print("phase2 instructions:", mk.n_inst, {e: len(mk.ops[e]) for e in ENGS}, "sbuf words", mk.top)
    return nc


def colT(v, n):
    return np.ascontiguousarray(np.asarray(v, np.float32).reshape(n, 128).T)


def phase2_inputs(core, x_prompt, x_sample, mix_p, mix_s, c, c_ctx, w_ada, b_ada, w_out, norm2_g, w_up,
                  ffn_conv_w, ffn_conv_b, w_down, final_norm_g):
    b, j = core // 4, core % 4
    x2 = np.zeros((3, 514, 2048), np.float32)
    mix = np.zeros((3, 514, 2048), np.float32)
    x2[0, :512] = x_prompt[2 * core:2 * core + 2].reshape(512, 2048)
    mix[0, :512] = mix_p[2 * core:2 * core + 2].reshape(512, 2048)
    hmask = np.zeros((128, 4), np.float32)
    for g in range(2):
        lo = 1024 * j + 512 * g - 1
        hi = lo + 514
        a, e = max(lo, 0), min(hi, 4096)
        x2[1 + g, a - lo:e - lo] = x_sample[b, a:e]
        mix[1 + g, a - lo:e - lo] = mix_s[b, a:e]
        hmask[:, 2 * g] = 1.0 if lo >= 0 else 0.0
        hmask[:, 2 * g + 1] = 1.0 if hi <= 4096 else 0.0
    mixT = np.ascontiguousarray(mix.reshape(3, 514, 16, 128).transpose(0, 2, 3, 1))
    cond = np.stack([c_ctx, c[b]], axis=0)
    condT = np.ascontiguousarray(cond.reshape(2, 16, 128).transpose(2, 1, 0))
    return dict(x2=x2, mix=mixT, condT=condT, hmask=hmask, w_ada=w_ada, b_adaT=colT(b_ada, 96), w_out=w_out,
                norm2T=colT(norm2_g, 16), w_up=w_up,
                fcwT=np.ascontiguousarray(ffn_conv_w.reshape(3, 88, 128).transpose(2, 1, 0)),
                fcbT=colT(ffn_conv_b, 88), w_down=w_down, fnormT=colT(final_norm_g, 16),
                ident=np.eye(128, dtype=np.float32))


def norm_mod_tile(mk, C, x_rows, W, xT, hT, rstd, gm1, ci):
    load_xT(mk, C, x_rows, W, xT, 0)
    rms_rstd(mk, C, xT, 16, W, 0, 2048.0, rstd)
    for cc in range(16):
        tx = C.tmpx[cc % 2]
        mk.I("dve", "scalar_tensor_tensor", out=tx[:, 0:W], in0=xT[:, cc, 0:W], scalar=gm1[:, cc, ci:ci + 1],
             in1=rstd[:, 0:W], op0=ALU.mult, op1=ALU.mult)
        mk.I("act", "activation", out=hT[:, cc, 0:W], in_=tx[:, 0:W], func=AF.Identity,
             bias=C.mod[:, cc, ci:ci + 1], scale=1.0)


def bcast_sum_rstd(mk, C, srcs, W, dim, rstd, eps=EPS):
    b = mk.bank()
    for i, (s, P) in enumerate(srcs):
        sq = C.sq[i % 2]
        mk.I("act", "activation", out=sq[0:P, 0:W], in_=s, func=AF.Square)
        mk.I("pe", "matmul", out=b[:, 0:W], lhsT=C.ones[0:P, :], rhs=sq[0:P, 0:W], start=(i == 0), stop=(i == len(srcs) - 1))
    mk.I("act", "activation", out=C.tmpn[:, 0:W], in_=b[:, 0:W], func=AF.Sqrt, scale=1.0 / dim, bias=C.epsb[:, 0:1] if eps == EPS else C.zerob[:, 0:1])
    mk.I("dve", "reciprocal", out=rstd[:, 0:W], in_=C.tmpn[:, 0:W])


def gdn_unit(mk, C, G, h, d, c, k):
    NH = G.NH
    cols = slice(c * 128, (c + 1) * 128)
    gi = d * NH + h
    g_col = G.Gt[:, c, gi:gi + 1]
    beta_col = G.Bt[:, c, gi:gi + 1]
    nbeta_col = G.NBt[:, c, gi:gi + 1]
    TRI = C.tri[d]
    MS = C.ms[d]
    S = G.S[h][d]
    u = C.gu
    kT = G.KT[h][:, cols]
    qT = G.QT[h][:, cols]
    bk = mk.bank()
    mk.I("pe", "matmul", out=bk[:, 0:128], lhsT=kT, rhs=C.identb.v, start=True, stop=True)
    bv = mk.bank()
    mk.I("pe", "matmul", out=bv[:, 0:128], lhsT=G.VT[h][:, cols], rhs=C.identb.v, start=True, stop=True)
    ktm = u.ktm[k]
    vb = u.vb[k]
    mk.I("act", "activation", out=ktm.v, in_=bk[:, 0:128], func=AF.Copy)
    mk.I("dve", "tensor_scalar", out=vb.v, in0=bv[:, 0:128], scalar1=beta_col, scalar2=None, op0=ALU.mult)
    yield
    bg = mk.bank()
    mk.I("pe", "matmul", out=bg[:, 0:1], lhsT=TRI.v, rhs=g_col, start=True, stop=True)
    mk.I("pe", "matmul", out=bg[:, 128:256], lhsT=g_col.bc([128, 128]), rhs=TRI.v, start=True, stop=True)
    Gc = u.Gc[k]
    Gb = u.Gb[k]
    mk.I("dve", "tensor_copy", out=Gc[:, 0:1], in_=bg[:, 0:1])
    mk.I("act", "activation", out=Gb.v, in_=bg[:, 128:256], func=AF.Copy)
    gtot = Gb[:, 127:128] if d == 0 else Gb[:, 0:1]
    yield
    Dm, DTm = u.Dm[k], u.DTm[k]
    mk.I("dve", "tensor_scalar", out=Dm.v, in0=Gb.v, scalar1=Gc[:, 0:1], scalar2=0.0, op0=ALU.subtract, op1=ALU.max)
    mk.I("act", "activation", out=Dm.v, in_=Dm.v, func=AF.Exp, scale=-1.0)
    mk.I(PENG, "tensor_tensor", out=Dm.v, in0=Dm.v, in1=MS.v, op=ALU.mult)
    mk.I("dve", "tensor_scalar", out=DTm.v, in0=Gb.v, scalar1=Gc[:, 0:1], scalar2=0.0, op0=ALU.subtract, op1=ALU.min)
    mk.I("act", "activation", out=DTm.v, in_=DTm.v, func=AF.Exp)
    mk.I(PENG, "tensor_tensor", out=DTm.v, in0=DTm.v, in1=TRI.v, op=ALU.mult)
    yield
    bkk = mk.bank()
    kTc = u.kTc[k]
    mk.I("dve", "tensor_copy", out=kTc.v, in_=kT)
    mk.I("pe", "matmul", out=bkk[:, 0:128], lhsT=kT, rhs=kTc.v, start=True, stop=True)
    P, PT = u.P[k], u.PT[k]
    mk.I("dve", "scalar_tensor_tensor", out=P[0].v, in0=bkk[:, 0:128], scalar=nbeta_col, in1=Dm.v, op0=ALU.mult, op1=ALU.mult)
    bt = mk.bank()
    mk.I("pe", "transpose", out=bt[:, 0:128], in_=P[0].v, identity=C.ident.v)
    mk.I("act", "activation", out=PT[0].v, in_=bt[:, 0:128], func=AF.Copy)
    TT = u.TT[k]
    mk.I("dve", "tensor_tensor", out=TT.v, in0=bt[:, 0:128], in1=C.ident.v, op=ALU.add)
    yield
    cur = 0
    for lev in range(1, 7):
        nxt = 1 - cur
        b1 = mk.bank()
        mk.I("pe", "matmul", out=b1[:, 0:128], lhsT=PT[cur].v, rhs=P[cur].v, start=True, stop=True)
        if lev < 6:
            mk.I("pe", "matmul", out=b1[:, 128:256], lhsT=P[cur].v, rhs=PT[cur].v, start=True, stop=True)
        mk.I("act", "activation", out=P[nxt].v, in_=b1[:, 0:128], func=AF.Copy)
        if lev < 6:
            mk.I("dve", "tensor_copy", out=PT[nxt].v, in_=b1[:, 128:256])
        b2 = mk.bank()
        mk.I("pe", "matmul", out=b2[:, 0:128], lhsT=P[nxt].v, rhs=TT.v, start=True, stop=True)
        mk.I("dve", "tensor_tensor", out=TT.v, in0=b2[:, 0:128], in1=TT.v, op=ALU.add)
        cur = nxt
        yield
    yield
    sc = u.sc[k]
    mk.I("act", "activation", out=sc[:, 0:1], in_=Gc[:, 0:1], func=AF.Exp)
    mk.I("dve", "tensor_tensor", out=sc[:, 1:2], in0=sc[:, 0:1], in1=beta_col, op=ALU.mult)
    mk.I("act", "activation", out=sc[:, 2:3], in_=Gc[:, 0:1], func=AF.Exp, scale=-1.0, bias=gtot)
    mk.I("act", "activation", out=sc[:, 3:4], in_=gtot, func=AF.Exp)
    kbg, kdec = u.kbg[k], u.kdec[k]
    mk.I("act", "activation", out=kbg.v, in_=ktm.v, func=AF.Identity, scale=sc[:, 1:2])
    mk.I("dve", "tensor_scalar", out=kdec.v, in0=ktm.v, scalar1=sc[:, 2:3], scalar2=None, op0=ALU.mult)
    bu = mk.bank()
    mk.I("pe", "matmul", out=bu[:, 0:128], lhsT=TT.v, rhs=vb.v, start=True, stop=True)
    mk.I("pe", "matmul", out=bu[:, 128:256], lhsT=kbg.v, rhs=TT.v, start=True, stop=True)
    uu, wT = u.uu[k], u.wT[k]
    mk.I("act", "activation", out=uu.v, in_=bu[:, 0:128], func=AF.Copy)
    mk.I("dve", "tensor_copy", out=wT.v, in_=bu[:, 128:256])
    yield
    bq = mk.bank()
    mk.I("pe", "matmul", out=bq[:, 0:128], lhsT=kT, rhs=qT, start=True, stop=True)
    intraT, qgT, eGb = u.intraT[k], u.qgT[k], u.eGb[k]
    mk.I("dve", "tensor_tensor", out=intraT.v, in0=bq[:, 0:128], in1=DTm.v, op=ALU.mult)
    mk.I("act", "activation", out=eGb.v, in_=Gb.v, func=AF.Exp)
    mk.I(PENG, "tensor_tensor", out=qgT.v, in0=qT, in1=eGb.v, op=ALU.mult)
    yield
    b3 = mk.bank()
    mk.I("pe", "matmul", out=b3[:, 0:128], lhsT=wT.v, rhs=S.v, start=True, stop=True)
    vnew = u.vnew[k]
    mk.I("dve", "tensor_tensor", out=vnew.v, in0=uu.v, in1=b3[:, 0:128], op=ALU.subtract)
    b4 = mk.bank()
    mk.I("pe", "matmul", out=b4[:, 0:128], lhsT=S.v, rhs=qgT.v, start=True, stop=False)
    mk.I("pe", "matmul", out=b4[:, 0:128], lhsT=vnew.v, rhs=intraT.v, start=False, stop=True)
    mk.I("pe", "matmul", out=b4[:, 128:256], lhsT=kdec.v, rhs=vnew.v, start=True, stop=True)
    ocols = slice(c * 128 + G.pad, (c + 1) * 128 + G.pad)
    mk.I("dve", "tensor_tensor", out=G.OT[h][:, ocols], in0=G.OT[h][:, ocols], in1=b4[:, 0:128], op=ALU.add)
    mk.I("dve", "scalar_tensor_tensor", out=S.v, in0=S.v, scalar=sc[:, 3:4], in1=b4[:, 128:256], op0=ALU.mult, op1=ALU.add)
    if h == 0 and c == 0 and d == 0:
        for nm, vv in (("Gb", Gb), ("Dm", Dm), ("DTm", DTm), ("X", P[0] if False else None), ("TT", TT), ("vb", vb), ("kbg", kbg), ("kdec", kdec),
                       ("uu", uu), ("wT", wT), ("intraT", intraT), ("qgT", qgT), ("vnew", vnew), ("S", S)):
            if vv is not None:
                dump(mk, nm, vv.v, [128, 128])
        dump(mk, "sc", sc[:, 0:4], [128, 4])
        dump(mk, "Gc", Gc.v, [128, 1])


def conv_out(mk, C, G, h, comp, R, n, tok0):
    t = C.ct[C.ct_rr % 2]
    C.ct_rr += 1
    ch = h * 3 + comp
    mk.I("act", "activation", out=t[:, 0:n], in_=R[:, 1:1 + n], func=AF.Identity, scale=G.cw[:, ch, 1:2])
    mk.I("dve", "scalar_tensor_tensor", out=t[:, 0:n], in0=R[:, 0:n], scalar=G.cw[:, ch, 0:1], in1=t[:, 0:n], op0=ALU.mult, op1=ALU.add)
    mk.I("dve", "scalar_tensor_tensor", out=t[:, 0:n], in0=R[:, 2:2 + n], scalar=G.cw[:, ch, 2:3], in1=t[:, 0:n], op0=ALU.mult, op1=ALU.add)
    j0 = 1 if tok0 < 0 else 0
    if n - j0 <= 0:
        return
    dsl = slice(tok0 + j0, tok0 + n)
    if comp == 2:
        mk.I("act", "activation", out=G.VT[h][:, dsl], in_=t[:, j0:n], func=AF.Silu)
        return
    mk.I("act", "activation", out=t[:, 0:n], in_=t[:, 0:n], func=AF.Silu)
    bcast_sum_rstd(mk, C, [(t[:, 0:n], 128)], n, 1.0, C.rstd)
    dst = (G.QT if comp == 0 else G.KT)[h]
    mk.I("dve", "scalar_tensor_tensor", out=dst[:, dsl], in0=t[:, j0:n], scalar=(128.0 ** -0.5 if comp == 0 else 1.0),
         in1=C.rstd[:, j0:n], op0=ALU.mult, op1=ALU.mult)


def gdn_phase(mk, C, x_rows, T, NH, ci, gm1, Wd, s0_ap, mix_out, state_out, win=None, hcache=None):
    mark = mk.mark()
    G = Ctx()
    G.NH = NH
    pad = G.pad = 1 if win is not None else 0
    TTs = min(512, T)
    ntile = T // TTs
    nch = T // 128
    G.QT = [mk.alloc([T], BF16, f"QT{h}") for h in range(NH)]
    G.KT = [mk.alloc([T], BF16, f"KT{h}") for h in range(NH)]
    G.VT = [mk.alloc([T], BF16, f"VT{h}") for h in range(NH)]
    G.ZT = [mk.alloc([T + 2 * pad], BF16, f"ZT{h}") for h in range(NH)]
    G.OT = [mk.alloc([T + 2 * pad], F32, f"OT{h}") for h in range(NH)]
    G.Gt = mk.alloc([nch, 2 * NH], F32, "Gt")
    G.Bt = mk.alloc([nch, 2 * NH], F32, "Bt")
    G.NBt = mk.alloc([nch, 2 * NH], F32, "NBt")
    G.cw = mk.alloc([NH * 3, 3], F32, "cw")
    G.halo = [[mk.alloc([2], F32, f"halo{h}_{c}") for c in range(3)] for h in range(NH)]
    G.S = [[mk.alloc([128], F32, f"S{h}_{d}") for d in range(2)] for h in range(NH)]
    wab = mk.alloc([16, 4 * NH], BF16, "wab")
    dtb = mk.alloc([2 * NH], F32, "dtb")
    nA = mk.alloc([2 * NH], F32, "nA")
    gng = mk.alloc([1], F32, "gng")
    sm = [mk.alloc([2 * NH], F32, f"sm{i}") for i in range(4)]
    mk.dma("sp", G.cw.v, Wd["cw"])
    mk.dma("sp", dtb.v, Wd["dtb"])
    mk.dma("sp", nA.v, Wd["alog"])
    mk.dma("sp", gng.v, Wd["gng"])
    mk.dma("pool", wab.v, Wd["w_ab"].rearrange("(kc p) n -> p kc n", p=128))
    mk.I("act", "activation", out=nA.v, in_=nA.v, func=AF.Exp)
    mk.I("dve", "tensor_scalar", out=nA.v, in0=nA.v, scalar1=-1.0, scalar2=None, op0=ALU.mult)
    for h in range(NH):
        mk.I(PENG, "memset", G.OT[h].v.ap, 0.0, writes=[G.OT[h]])
        if pad:
            mk.I(PENG, "memset", G.ZT[h].v.ap, 0.0, writes=[G.ZT[h]])
        for c in range(3):
            mk.I(PENG, "memset", G.halo[h][c].v.ap, 0.0, writes=[G.halo[h][c]])
        for d in range(2):
            mk.dma("sp", G.S[h][d].v, s0_ap[d, h])
    mark_w = mk.mark()
    hT = mk.alloc([16, TTs], BF16, "hT")
    xT = mk.alloc([16, TTs], F32, "xTg")
    for it in range(ntile):
        t0 = it * TTs
        if hcache is not None and hcache[1] == "read":
            mk.dma("sp", hT.v, hcache[0][it])
        else:
            norm_mod_tile(mk, C, x_rows[t0:t0 + TTs, :], TTs, xT, hT, C.rstd, gm1, ci)
            if hcache is not None:
                mk.dma("sp", hcache[0][it], hT.v)
        for h in range(NH):
            wt = C.wt[C.wt_rr % len(C.wt)]
            C.wt_rr += 1
            load_weight_tile(mk, wt, Wd["w_g"][:, h * 512:(h + 1) * 512], 16, 512)
            for comp in range(4):
                b = mk.bank()
                for kc in range(16):
                    mk.I("pe", "matmul", out=b[:, 0:TTs], lhsT=wt[:, kc, comp * 128:(comp + 1) * 128], rhs=hT[:, kc, 0:TTs],
                         start=(kc == 0), stop=(kc == 15))
                if comp == 3:
                    mk.I("act", "activation", out=G.ZT[h][:, pad + t0:pad + t0 + TTs], in_=b[:, 0:TTs], func=AF.Silu)
                    continue
                R = C.R[C.R_rr % 2]
                C.R_rr += 1
                halo = G.halo[h][comp]
                mk.I("dve", "tensor_copy", out=R[:, 0:2], in_=halo.v)
                mk.I("act", "activation", out=R[:, 2:2 + TTs], in_=b[:, 0:TTs], func=AF.Copy)
                mk.I("dve", "tensor_copy", out=halo.v, in_=R[:, TTs:TTs + 2])
                conv_out(mk, C, G, h, comp, R, TTs, t0 - 1)
                if it == ntile - 1:
                    R2 = C.R[C.R_rr % 2]
                    C.R_rr += 1
                    mk.I("dve", "tensor_copy", out=R2[:, 0:2], in_=halo.v)
                    mk.I("dve", "memset", R2[:, 2:3].ap, 0.0, writes=[R2])
                    conv_out(mk, C, G, h, comp, R2, 1, T - 1)
        for s in range(TTs // 128 if GSTOP['v'] >= 1 else 0):
            c = (t0 + s * 128) // 128
            b = mk.bank()
            for kc in range(16):
                mk.I("pe", "matmul", out=b[:, 0:4 * NH], lhsT=hT[:, kc, s * 128:(s + 1) * 128], rhs=wab[:, kc, :],
                     start=(kc == 0), stop=(kc == 15))
            xs, ax, ee, rr_ = sm
            mk.I("dve", "tensor_tensor", out=xs.v, in0=b[:, 0:2 * NH], in1=dtb.v, op=ALU.add)
            mk.I("dve", "tensor_scalar", out=ax.v, in0=xs.v, scalar1=-1.0, scalar2=None, op0=ALU.mult)
            mk.I("dve", "tensor_tensor", out=ax.v, in0=ax.v, in1=xs.v, op=ALU.max)
            mk.I("act", "activation", out=ee.v, in_=ax.v, func=AF.Exp, scale=-1.0)
            mk.I("act", "activation", out=ee.v, in_=ee.v, func=AF.Ln, bias=C.oneb[:, 0:1], scale=1.0)
            mk.I("dve", "tensor_scalar", out=rr_.v, in0=xs.v, scalar1=0.0, scalar2=None, op0=ALU.max)
            mk.I("dve", "tensor_tensor", out=ee.v, in0=ee.v, in1=rr_.v, op=ALU.add)
            mk.I("dve", "tensor_tensor", out=G.Gt[:, c, :], in0=ee.v, in1=nA.v, op=ALU.mult)
            mk.I("act", "activation", out=G.Bt[:, c, :], in_=b[:, 2 * NH:4 * NH], func=AF.Sigmoid)
            mk.I("dve", "tensor_scalar", out=G.NBt[:, c, :], in0=G.Bt[:, c, :], scalar1=-1.0, scalar2=None, op0=ALU.mult)
    mk.release(mark_w)
    u = C.gu = Ctx()
    for nm in ("ktm", "Gb", "Dm", "DTm", "TT", "vb", "kbg", "kdec", "uu", "wT", "intraT", "qgT", "eGb", "vnew"):
        setattr(u, nm, [mk.alloc([128], F32, f"{nm}{k}") for k in range(NPAR)])
    u.P = [[mk.alloc([128], F32, f"P{k}{i}") for i in range(2)] for k in range(NPAR)]
    u.PT = [[mk.alloc([128], F32, f"PT{k}{i}") for i in range(2)] for k in range(NPAR)]
    u.Gc = [mk.alloc([1], F32, f"Gc{k}") for k in range(NPAR)]
    u.kTc = [mk.alloc([128], BF16, f"kTc{k}") for k in range(NPAR)]
    u.sc = [mk.alloc([8], F32, f"sc{k}") for k in range(NPAR)]
    pending = [(h, d, (step if d == 0 else nch - 1 - step)) for step in range(nch) for h in range(NH) for d in range(2)]
    active = []
    free = list(range(NPAR))
    while pending or active:
        while pending and free:
            h_, d_, c_ = pending.pop(0)
            k_ = free.pop(0)
            active.append((gdn_unit(mk, C, G, h_, d_, c_, k_), k_))
        for item in list(active):
            try:
                next(item[0])
            except StopIteration:
                active.remove(item)
                free.append(item[1])
    for h in range(NH):
        if state_out is not None:
            for d in range(2):
                mk.dma("sp", state_out[d, h], G.S[h][d].v)
        if win is not None:
            sel, dst = win
            for g in range(2):
                for hf in range(2):
                    ow = C.ct[0]
                    zw = C.ct[1]
                    for jj in range(4):
                        c0 = 1024 * jj + 512 * g + 257 * hf
                        if jj == 0:
                            mk.I("dve", "tensor_scalar", out=ow[:, 0:257], in0=G.OT[h][:, c0:c0 + 257], scalar1=sel[:, 0:1], scalar2=None, op0=ALU.mult)
                            mk.I("dve", "tensor_scalar", out=zw[:, 0:257], in0=G.ZT[h][:, c0:c0 + 257], scalar1=sel[:, 0:1], scalar2=None, op0=ALU.mult)
                        else:
                            mk.I("dve", "scalar_tensor_tensor", out=ow[:, 0:257], in0=G.OT[h][:, c0:c0 + 257], scalar=sel[:, jj:jj + 1],
                                 in1=ow[:, 0:257], op0=ALU.mult, op1=ALU.add)
                            mk.I("dve", "scalar_tensor_tensor", out=zw[:, 0:257], in0=G.ZT[h][:, c0:c0 + 257], scalar=sel[:, jj:jj + 1],
                                 in1=zw[:, 0:257], op0=ALU.mult, op1=ALU.add)
                    bcast_sum_rstd(mk, C, [(ow[:, 0:257], 128)], 257, 128.0, C.rstd)
                    mk.I("dve", "scalar_tensor_tensor", out=ow[:, 0:257], in0=ow[:, 0:257], scalar=gng[:, 0:1],
                         in1=C.rstd[:, 0:257], op0=ALU.mult, op1=ALU.mult)
                    mk.I("dve", "tensor_tensor", out=ow[:, 0:257], in0=ow[:, 0:257], in1=zw[:, 0:257], op=ALU.mult)
                    mk.dma("sp", dst[g, :, 257 * hf:257 * hf + 257], ow[:, 0:257])
            continue
        for it in range(ntile):
            t0 = it * TTs
            bcast_sum_rstd(mk, C, [(G.OT[h][:, t0:t0 + TTs], 128)], TTs, 128.0, C.rstd)
            o = C.ct[C.ct_rr % 2]
            C.ct_rr += 1
            mk.I("dve", "scalar_tensor_tensor", out=o[:, 0:TTs], in0=G.OT[h][:, t0:t0 + TTs], scalar=gng[:, 0:1],
                 in1=C.rstd[:, 0:TTs], op0=ALU.mult, op1=ALU.mult)
            mk.I("dve", "tensor_tensor", out=o[:, 0:TTs], in0=o[:, 0:TTs], in1=G.ZT[h][:, t0:t0 + TTs], op=ALU.mult)
            mk.dma("sp", mix_out[h, :, t0:t0 + TTs], o[:, 0:TTs])
    mk.release(mark)


def mla_phase(mk, C, x_rows, T, NH, ci, gm1, Wd, nctx, rope, mix_out, ckv_out, kpe_out):
    mark0 = mk.mark()
    TTs = min(256, T)
    ntile = T // TTs
    NK = T + nctx
    nkt = NK // 128
    scale = 192.0 ** -0.5
    gq = mk.alloc([4], F32, "gq")
    gkv = mk.alloc([4], F32, "gkv")
    wq = mk.alloc([4, NH * 192], BF16, "wq")
    wkv = mk.alloc([4, NH * 256], BF16, "wkv")
    rot = mk.alloc([64], F32, "rot")
    onesb = mk.alloc([128], BF16, "onesb")
    kmax2 = mk.alloc([NH], F32, "kmax2")
    KPET = mk.alloc([NK], BF16, "KPET")
    KN = [mk.alloc([NK], BF16, f"KN{h}") for h in range(NH)]
    V = [mk.alloc([nkt, 128], BF16, f"V{h}") for h in range(NH)]
    mk.dma("sp", gq.v, Wd["gq"])
    mk.dma("sp", gkv.v, Wd["gkv"])
    mk.dma("pool", wq.v, Wd["wq"].rearrange("(kc p) n -> p kc n", p=128))
    mk.dma("pool", wkv.v, Wd["wkv"].rearrange("(kc p) n -> p kc n", p=128))
    mk.dma("sp", rot[0:64, :], Wd["rot"])
    mk.I("dve", "memset", onesb.v.ap, 1.0, writes=[onesb])
    mark1 = mk.mark()
    CKVT = mk.alloc([4, NK], BF16, "CKVT")
    mark2 = mk.mark()

    def work_tiles():
        W_ = Ctx()
        W_.hT = mk.alloc([16, TTs], BF16, "hTm")
        W_.xT = mk.alloc([16, TTs], F32, "xTm")
        W_.raw = mk.alloc([4, TTs], F32, "rawm")
        W_.cs = [mk.alloc([TTs], F32, f"cs{i}") for i in range(2)]
        W_.rp = [mk.alloc([TTs], F32, f"rp{i}") for i in range(3)]
        return W_

    def proj_norm(W_, wcol0, gvec, dst_fn, f32_out=None):
        wt = C.wt[C.wt_rr % len(C.wt)]
        C.wt_rr += 1
        load_weight_tile(mk, wt, Wd["w_m"][:, wcol0:wcol0 + 512], 16, 512)
        for q in range(4):
            b = mk.bank()
            for kc in range(16):
                mk.I("pe", "matmul", out=b[:, 0:TTs], lhsT=wt[:, kc, q * 128:(q + 1) * 128], rhs=W_.hT[:, kc, 0:TTs],
                     start=(kc == 0), stop=(kc == 15))
            mk.I("act", "activation", out=W_.raw[:, q, :], in_=b[:, 0:TTs], func=AF.Copy)
        bcast_sum_rstd(mk, C, [(W_.raw[:, q, :], 128) for q in range(4)], TTs, 512.0, C.rstd)
        for q in range(4):
            if f32_out is not None:
                mk.I("dve", "scalar_tensor_tensor", out=W_.raw[:, q, :], in0=W_.raw[:, q, :], scalar=gvec[:, q:q + 1],
                     in1=C.rstd[:, 0:TTs], op0=ALU.mult, op1=ALU.mult)
                mk.I("act", "activation", out=dst_fn(q), in_=W_.raw[:, q, :], func=AF.Copy)
            else:
                mk.I("dve", "scalar_tensor_tensor", out=dst_fn(q), in0=W_.raw[:, q, :], scalar=gvec[:, q:q + 1],
                     in1=C.rstd[:, 0:TTs], op0=ALU.mult, op1=ALU.mult)

    def do_rope(W_, src_bank_view, n, t0, dst):
        x = W_.rp[0]
        mk.I("act", "activation", out=x[0:64, 0:n], in_=src_bank_view, func=AF.Copy)
        if not rope:
            mk.I("dve", "tensor_copy", out=dst, in_=x[0:64, 0:n])
            return
        mk.dma("sp", W_.cs[0][0:64, 0:n], Wd["cos"][:, t0:t0 + n])
        mk.dma("sp", W_.cs[1][0:64, 0:n], Wd["sin"][:, t0:t0 + n])
        b = mk.bank()
        mk.I("pe", "matmul", out=b[0:64, 0:n], lhsT=rot[0:64, :], rhs=x[0:64, 0:n], start=True, stop=True)
        mk.I("dve", "tensor_tensor", out=W_.rp[1][0:64, 0:n], in0=x[0:64, 0:n], in1=W_.cs[0][0:64, 0:n], op=ALU.mult)
        mk.I("dve", "tensor_tensor", out=W_.rp[2][0:64, 0:n], in0=b[0:64, 0:n], in1=W_.cs[1][0:64, 0:n], op=ALU.mult)
        mk.I("dve", "tensor_tensor", out=dst, in0=W_.rp[1][0:64, 0:n], in1=W_.rp[2][0:64, 0:n], op=ALU.add)

    W_ = work_tiles()
    for it in range(ntile):
        t0 = it * TTs
        norm_mod_tile(mk, C, x_rows[t0:t0 + TTs, :], TTs, W_.xT, W_.hT, C.rstd, gm1, ci)
        proj_norm(W_, 512, gkv, lambda q: CKVT[:, q, t0:t0 + TTs], f32_out=(ckv_out is not None) or True)
        if ckv_out is not None:
            for s in range(TTs // 128):
                ys = C.xstage[s % 2]
                b = mk.bank()
                for q in range(4):
                    mk.I("pe", "transpose", out=b[:, q * 128:(q + 1) * 128], in_=W_.raw[:, q, s * 128:(s + 1) * 128], identity=C.ident.v)
                mk.I("dve", "tensor_copy", out=ys[:, 0:512], in_=b[:, 0:512])
                mk.dma("sp", ckv_out[t0 + s * 128:t0 + (s + 1) * 128, :], ys[:, 0:512])
        wt = C.wt[C.wt_rr % len(C.wt)]
        C.wt_rr += 1
        load_weight_tile(mk, wt, Wd["w_m"][:, 1024:1088], 16, 64)
        b = mk.bank()
        for kc in range(16):
            mk.I("pe", "matmul", out=b[0:64, 0:TTs], lhsT=wt[:, kc, 0:64], rhs=W_.hT[:, kc, 0:TTs], start=(kc == 0), stop=(kc == 15))
        do_rope(W_, b[0:64, 0:TTs], TTs, t0, KPET[0:64, t0:t0 + TTs])
        if kpe_out is not None:
            for s in range(TTs // 128):
                ys = C.xstage[s % 2]
                b2 = mk.bank()
                mk.I("pe", "transpose", out=b2[:, 0:64], in_=W_.rp[0][0:64, s * 128:(s + 1) * 128], identity=C.ident[0:64, 0:64])
                mk.I("dve", "tensor_copy", out=ys[:, 0:64], in_=b2[:, 0:64])
                mk.dma("sp", kpe_out[t0 + s * 128:t0 + (s + 1) * 128, :], ys[:, 0:64])
    for s in range(nctx // 128):
        xs = C.xstage[s % 2]
        mk.dma("sp", xs[:, 0:512], Wd["cache_ckv"][s * 128:(s + 1) * 128, :])
        mk.dma("sp", xs[:, 512:576], Wd["cache_kpe"][s * 128:(s + 1) * 128, :])
        b = mk.bank()
        for q in range(4):
            mk.I("pe", "transpose", out=b[:, q * 128:(q + 1) * 128], in_=xs[:, q * 128:(q + 1) * 128], identity=C.ident.v)
        mk.I("dve", "tensor_copy", out=CKVT[:, :, T + s * 128:T + (s + 1) * 128], in_=b[:, 0:512].rearrange("p (q t) -> p q t", q=4))
        b2 = mk.bank()
        mk.I("pe", "transpose", out=b2[0:64, 0:128], in_=xs[:, 512:576], identity=C.ident.v)
        mk.I("act", "activation", out=KPET[0:64, T + s * 128:T + (s + 1) * 128], in_=b2[0:64, 0:128], func=AF.Copy)
    mk.release(mark2)
    sqk = mk.alloc([512], F32, "sqk")
    kss = mk.alloc([512], F32, "kss")
    for h in range(NH):
        for k0 in range(0, NK, 512):
            n = min(512, NK - k0)
            b = mk.bank()
            for kc in range(4):
                mk.I("pe", "matmul", out=b[:, 0:n], lhsT=wkv[:, kc, h * 256:h * 256 + 128], rhs=CKVT[:, kc, k0:k0 + n],
                     start=(kc == 0), stop=(kc == 3))
            mk.I("act", "activation", out=KN[h][:, k0:k0 + n], in_=b[:, 0:n], func=AF.Copy)
            bs = mk.bank()
            mk.I("act", "activation", out=sqk[:, 0:n], in_=b[:, 0:n], func=AF.Square)
            mk.I("pe", "matmul", out=bs[0:1, 0:n], lhsT=C.ones[:, 0:1], rhs=sqk[:, 0:n], start=True, stop=False)
            mk.I("act", "activation", out=kss[0:64, 0:n], in_=KPET[0:64, k0:k0 + n], func=AF.Square)
            mk.I("pe", "matmul", out=bs[0:1, 0:n], lhsT=C.ones[0:64, 0:1], rhs=kss[0:64, 0:n], start=False, stop=True)
            if k0 == 0:
                mk.I("dve", "tensor_reduce", out=kmax2[0:1, h:h + 1], in_=bs[0:1, 0:n], axis=AX.X, op=ALU.max)
            else:
                mk.I("dve", "tensor_reduce", out=kss[0:1, 0:1], in_=bs[0:1, 0:n], axis=AX.X, op=ALU.max)
                mk.I("dve", "tensor_tensor", out=kmax2[0:1, h:h + 1], in0=kmax2[0:1, h:h + 1], in1=kss[0:1, 0:1], op=ALU.max)
        for kt in range(nkt):
            b = mk.bank()
            for kc in range(4):
                mk.I("pe", "matmul", out=b[:, 0:128], lhsT=CKVT[:, kc, kt * 128:(kt + 1) * 128], rhs=wkv[:, kc, h * 256 + 128:h * 256 + 256],
                     start=(kc == 0), stop=(kc == 3))
            mk.I("dve", "tensor_copy", out=V[h][:, kt, :], in_=b[:, 0:128])
    mk.release(mark1)
    W_ = work_tiles()
    QN = mk.alloc([4, TTs], BF16, "QN")
    qn = mk.alloc([TTs], BF16, "qn")
    qr = mk.alloc([TTs], BF16, "qr")
    negm = mk.alloc([TTs], BF16, "negm")
    mrow = mk.alloc([TTs], F32, "mrow")
    PTb = [mk.alloc([TTs], BF16, f"PT{i}") for i in range(3)]
    rs = mk.alloc([TTs], F32, "rs")
    oo = mk.alloc([TTs], F32, "oo")
    for it in range(ntile):
        t0 = it * TTs
        norm_mod_tile(mk, C, x_rows[t0:t0 + TTs, :], TTs, W_.xT, W_.hT, C.rstd, gm1, ci)
        proj_norm(W_, 0, gq, lambda q: QN[:, q, :])
        for h in range(NH):
            b = mk.bank()
            for kc in range(4):
                mk.I("pe", "matmul", out=b[:, 0:TTs], lhsT=wq[:, kc, h * 192:h * 192 + 128], rhs=QN[:, kc, :], start=(kc == 0), stop=(kc == 3))
            mk.I("act", "activation", out=qn.v, in_=b[:, 0:TTs], func=AF.Copy)
            mk.I("act", "activation", out=C.sq[0][:, 0:TTs], in_=b[:, 0:TTs], func=AF.Square)
            b2 = mk.bank()
            for kc in range(4):
                mk.I("pe", "matmul", out=b2[0:64, 0:TTs], lhsT=wq[:, kc, h * 192 + 128:h * 192 + 192], rhs=QN[:, kc, :], start=(kc == 0), stop=(kc == 3))
            mk.I("act", "activation", out=C.sq[1][0:64, 0:TTs], in_=b2[0:64, 0:TTs], func=AF.Square)
            do_rope(W_, b2[0:64, 0:TTs], TTs, t0, qr[0:64, :])
            bm = mk.bank()
            mk.I("pe", "matmul", out=bm[0:1, 0:TTs], lhsT=C.ones[:, 0:1], rhs=C.sq[0][:, 0:TTs], start=True, stop=False)
            mk.I("pe", "matmul", out=bm[0:1, 0:TTs], lhsT=C.ones[0:64, 0:1], rhs=C.sq[1][0:64, 0:TTs], start=False, stop=True)
            mk.I("act", "activation", out=mrow[0:1, :], in_=bm[0:1, 0:TTs], func=AF.Sqrt, scale=kmax2[0:1, h:h + 1])
            mk.I("dve", "tensor_scalar", out=negm[0:1, :], in0=mrow[0:1, :], scalar1=-1.0, scalar2=None, op0=ALU.mult)
            bo = mk.reserve()
            bsum = mk.reserve()
            for kt in range(nkt):
                ks = slice(kt * 128, (kt + 1) * 128)
                bs = mk.bank()
                mk.I("pe", "matmul", out=bs[:, 0:TTs], lhsT=KN[h][:, ks], rhs=qn.v, start=True, stop=False)
                mk.I("pe", "matmul", out=bs[:, 0:TTs], lhsT=KPET[0:64, ks], rhs=qr[0:64, :], start=False, stop=False)
                mk.I("pe", "matmul", out=bs[:, 0:TTs], lhsT=onesb[0:1, :], rhs=negm[0:1, :], start=False, stop=True)
                PT = PTb[kt % 3]
                mk.I("act", "activation", out=PT.v, in_=bs[:, 0:TTs], func=AF.Exp, scale=scale)
                mk.I("pe", "matmul", out=bo[:, 0:TTs], lhsT=V[h][:, kt, :], rhs=PT.v, start=(kt == 0), stop=(kt == nkt - 1))
                mk.I("pe", "matmul", out=bsum[:, 0:TTs], lhsT=onesb.v, rhs=PT.v, start=(kt == 0), stop=(kt == nkt - 1))
            mk.I("dve", "reciprocal", out=rs.v, in_=bsum[:, 0:TTs])
            mk.I("dve", "tensor_tensor", out=oo.v, in0=bo[:, 0:TTs], in1=rs.v, op=ALU.mult)
            mk.dma("sp", mix_out[h, :, t0:t0 + TTs], oo.v)
            mk.unreserve(bo)
            mk.unreserve(bsum)
    mk.release(mark0)


def alloc_common(mk, C, D):
    C.ident = mk.alloc([128], F32, "ident")
    C.identb = mk.alloc([128], BF16, "identb")
    C.ones = mk.alloc([128], F32, "ones")
    C.epsb = mk.alloc([1], F32, "epsb")
    C.oneb = mk.alloc([1], F32, "oneb")
    C.tri = [mk.alloc([128], F32, f"tri{d}") for d in range(2)]
    C.ms = [mk.alloc([128], F32, f"ms{d}") for d in range(2)]
    C.scond = mk.alloc([16, 2], BF16, "scond")
    C.condf = mk.alloc([16, 2], F32, "condf")
    C.b_ada = mk.alloc([96], F32, "b_ada")
    C.mod = mk.alloc([96, 2], F32, "mod")
    C.xstage = [mk.alloc([2048], F32, f"xs{i}") for i in range(2)]
    C.sq = [mk.alloc([514], F32, f"sq{i}") for i in range(2)]
    C.sqb = [mk.alloc([514], BF16, f"sqb{i}") for i in range(4)]
    C.sqb_rr = 0
    C.onesb16 = mk.alloc([128], BF16, "onesb16")
    C.tmpn = mk.alloc([514], F32, "tmpn")
    C.rstd = mk.alloc([514], F32, "rstd")
    C.wt = [mk.alloc([16, 512], BF16, f"wt{i}") for i in range(2)]
    C.wt_rr = 0
    C.tmpx = [mk.alloc([514], F32, f"tmpx{i}") for i in range(2)]
    C.ct = [mk.alloc([512], F32, f"ct{i}") for i in range(2)]
    C.ct_rr = 0
    C.R = [mk.alloc([516], F32, f"R{i}") for i in range(2)]
    C.R_rr = 0
    mk.dma("sp", C.ident.v, D["ident"])
    mk.dma("sp", C.tri[0].v, D["tri0"])
    mk.dma("sp", C.tri[1].v, D["tri1"])
    mk.dma("sp", C.ms[0].v, D["ms0"])
    mk.dma("sp", C.ms[1].v, D["ms1"])
    mk.I("dve", "tensor_copy", out=C.identb.v, in_=C.ident.v)
    mk.I("dve", "memset", C.ones.v.ap, 1.0, writes=[C.ones])
    mk.I("dve", "memset", C.onesb16.v.ap, 1.0, writes=[C.onesb16])
    mk.I("dve", "memset", C.epsb.v.ap, EPS, writes=[C.epsb])
    mk.I("dve", "memset", C.oneb.v.ap, 1.0, writes=[C.oneb])
    mk.dma("sp", C.condf.v, D["condT"])
    mk.dma("sp", C.b_ada.v, D["b_adaT"])
    mk.I("act", "activation", out=C.scond.v, in_=C.condf.v, func=AF.Silu)


def build_phase1(parts=('pg', 'pm', 'sg', 'sm')):
    nc = bass.Bass("TRN2", target_bir_lowering=False)
    DEBUG["nc"] = nc
    DEBUG["done"] = set()
    D = {}

    def din(name, shape):
        D[name] = nc.dram_tensor(name, list(shape), F32, kind="ExternalInput").ap()

    def dout(name, shape):
        D[name] = nc.dram_tensor(name, list(shape), F32, kind="ExternalOutput").ap()

    for name, shape in (("xp", [2, 256, 2048]), ("xs", [4096, 2048]), ("condT", [128, 16, 2]), ("w_ada", [2048, 12288]),
                        ("b_adaT", [128, 96]), ("norm1T", [128, 16]), ("ident", [128, 128]), ("tri0", [128, 128]),
                        ("tri1", [128, 128]), ("ms0", [128, 128]), ("ms1", [128, 128]), ("s0p", [2, 8, 128, 128]),
                        ("s0s", [2, 2, 1, 128, 128]), ("w_g_p", [2048, 4096]), ("w_ab_p", [2048, 32]), ("cw_p", [128, 24, 3]),
                        ("dtb_p", [128, 16]), ("alog_p", [128, 16]), ("gng", [128, 1]), ("w_g_s", [2, 2048, 512]),
                        ("w_ab_s", [2, 2048, 4]), ("cw_s", [2, 128, 3, 3]), ("dtb_s", [2, 128, 2]), ("alog_s", [2, 128, 2]),
                        ("w_m", [2048, 1088]), ("gq", [128, 4]), ("gkv", [128, 4]), ("wq_p", [512, 1536]), ("wkv_p", [512, 2048]),
                        ("wq_s", [512, 384]), ("wkv_s", [512, 512]), ("cos", [64, 4096]), ("sin", [64, 4096]), ("rot", [64, 64]),
                        ("cache_ckv", [256, 512]), ("cache_kpe", [256, 64])):
        din(name, shape)
    for name, shape in (("mixp", [2, 16, 128, 256]), ("mixs", [4, 128, 4096]), ("new_state", [2, 2, 8, 128, 128]),
                        ("new_ckv", [2, 256, 512]), ("new_kpe", [2, 256, 64])):
        dout(name, shape)
    with ExitStack() as st:
        mk = MK(nc, st)
        C = Ctx()
        alloc_common(mk, C, D)
        n1g = mk.alloc([16], F32, "n1g")
        gm1 = mk.alloc([16, 2], F32, "gm1")
        mk.dma("sp", n1g.v, D["norm1T"])
        adaln(mk, C, D["w_ada"], [0, 1], 2)
        mk.I("dve", "tensor_scalar", out=gm1.v, in0=C.mod[:, 16:32, :], scalar1=1.0, scalar2=None, op0=ALU.add)
        mk.I("dve", "tensor_tensor", out=gm1.v, in0=gm1.v,
             in1=n1g.v.rearrange("p (c o) -> p c o", o=1).bc([128, 16, 2]), op=ALU.mult)
        Wp = dict(w_g=D["w_g_p"], w_ab=D["w_ab_p"], cw=D["cw_p"], dtb=D["dtb_p"], alog=D["alog_p"], gng=D["gng"])
        Wmp = dict(w_m=D["w_m"], gq=D["gq"], gkv=D["gkv"], wq=D["wq_p"], wkv=D["wkv_p"], rot=D["rot"])
        for s in range(2):
            if 'pg' in parts:
                gdn_phase(mk, C, D["xp"][s], 256, 8, 0, gm1, Wp, D["s0p"], D["mixp"][s, 0:8], D["new_state"][s])
            if 'pm' in parts:
                mla_phase(mk, C, D["xp"][s], 256, 8, 0, gm1, Wmp, 0, False, D["mixp"][s, 8:16], D["new_ckv"][s], D["new_kpe"][s])
        for lh in range(2):
            Ws = dict(w_g=D["w_g_s"][lh], w_ab=D["w_ab_s"][lh], cw=D["cw_s"][lh], dtb=D["dtb_s"][lh], alog=D["alog_s"][lh], gng=D["gng"])
            if 'sg' in parts:
                gdn_phase(mk, C, D["xs"], 4096, 1, 1, gm1, Ws, D["s0s"][lh], D["mixs"][lh:lh + 1], None)
        Wms = dict(w_m=D["w_m"], gq=D["gq"], gkv=D["gkv"], wq=D["wq_s"], wkv=D["wkv_s"], rot=D["rot"], cos=D["cos"], sin=D["sin"],
                   cache_ckv=D["cache_ckv"], cache_kpe=D["cache_kpe"])
        if 'sm' in parts:
            mla_phase(mk, C, D["xs"], 4096, 2, 1, gm1, Wms, 256, True, D["mixs"][2:4], None, None)
        mk.finalize()
        print("phase1 instructions:", mk.n_inst, {e: len(mk.ops[e]) for e in ENGS})
    return nc


def rope_tables():
    rows = 4096 // 64
    row = np.repeat(np.arange(rows, dtype=np.float32), 64)
    col = np.tile(np.arange(64, dtype=np.float32), rows)
    inv = (np.float32(10000.0) ** (-np.arange(16, dtype=np.float32) / np.float32(16))).astype(np.float32)
    ang = np.concatenate([row[:, None] * inv, col[:, None] * inv], axis=-1).astype(np.float32)
    cos, sin = np.cos(ang).astype(np.float32), np.sin(ang).astype(np.float32)
    cos2 = np.ascontiguousarray(np.concatenate([cos, cos], axis=1).T)
    sin2 = np.ascontiguousarray(np.concatenate([sin, sin], axis=1).T)
    rot = np.zeros((64, 64), np.float32)
    for m in range(32):
        rot[m + 32, m] = -1.0
        rot[m, m + 32] = 1.0
    return cos2, sin2, rot


def phase1_inputs(core, I):
    b, j = core // 4, core % 4
    w_in = I["w_in"][0]
    cw = I["gdn_conv_w"][0]
    dtb = I["gdn_dt_bias"][0]
    alog = I["gdn_a_log"][0]

    def wg(h):
        return np.concatenate([w_in[:, c0 + h * 128:c0 + (h + 1) * 128] for c0 in (0, 1024, 2048, 3072)], axis=1)

    def cwh(h):
        return np.stack([cw[:, comp * 1024 + h * 128:comp * 1024 + (h + 1) * 128].T for comp in range(3)], axis=1)

    cos2, sin2, rot = rope_tables()
    tri0 = np.triu(np.ones((128, 128), np.float32))
    tri1 = np.tril(np.ones((128, 128), np.float32))
    ms0 = np.tril(np.ones((128, 128), np.float32), -1)
    ms1 = np.triu(np.ones((128, 128), np.float32), 1)
    cond = np.stack([I["c_ctx"], I["c"][b]], axis=0)
    hg = [2 * j, 2 * j + 1]
    wq = I["mla_w_q_b"][0]
    wkv = I["mla_w_kv_b"][0]
    d = dict(
        xp=np.ascontiguousarray(I["x_prompt"][2 * core:2 * core + 2]), xs=np.ascontiguousarray(I["x_sample"][b]),
        condT=np.ascontiguousarray(cond.reshape(2, 16, 128).transpose(2, 1, 0)), w_ada=I["w_ada"][0],
        b_adaT=colT(I["b_ada"][0], 96), norm1T=colT(I["norm1_g"][0], 16), ident=np.eye(128, dtype=np.float32),
        tri0=tri0, tri1=tri1, ms0=ms0, ms1=ms1, s0p=np.zeros((2, 8, 128, 128), np.float32),
        s0s=np.ascontiguousarray(np.stack([I["state_gdn"][b, 0, :, h:h + 1] for h in hg], axis=0)),
        w_g_p=np.concatenate([wg(h) for h in range(8)], axis=1), w_ab_p=np.ascontiguousarray(w_in[:, 4096:4128]),
        cw_p=np.ascontiguousarray(np.concatenate([cwh(h) for h in range(8)], axis=1)),
        dtb_p=np.ascontiguousarray(np.broadcast_to(dtb.reshape(1, 16), (128, 16))),
        alog_p=np.ascontiguousarray(np.broadcast_to(alog.reshape(1, 16), (128, 16))),
        gng=np.ascontiguousarray(I["gdn_norm_g"][0].reshape(128, 1)),
        w_g_s=np.stack([wg(h) for h in hg], axis=0),
        w_ab_s=np.stack([w_in[:, [4096 + h, 4104 + h, 4112 + h, 4120 + h]] for h in hg], axis=0),
        cw_s=np.stack([cwh(h) for h in hg], axis=0),
        dtb_s=np.stack([np.broadcast_to(dtb[:, h].reshape(1, 2), (128, 2)) for h in hg], axis=0),
        alog_s=np.stack([np.broadcast_to(alog[:, h].reshape(1, 2), (128, 2)) for h in hg], axis=0),
        w_m=np.ascontiguousarray(w_in[:, 4128:5216]), gq=colT(I["mla_q_norm_g"][0], 4), gkv=colT(I["mla_kv_norm_g"][0], 4),
        wq_p=wq, wkv_p=wkv, wq_s=np.ascontiguousarray(wq[:, hg[0] * 192:(hg[1] + 1) * 192]),
        wkv_s=np.ascontiguousarray(wkv[:, hg[0] * 256:(hg[1] + 1) * 256]), cos=cos2, sin=sin2, rot=rot,
        cache_ckv=np.ascontiguousarray(I["cache_mla_ckv"][b, 0]), cache_kpe=np.ascontiguousarray(I["cache_mla_kpe"][b, 0]))
    return {k: np.ascontiguousarray(np.asarray(v, np.float32)) for k, v in d.items()}


_NC = {}


def kernel_twolaunch(**I):
    I = {k: np.asarray(v) for k, v in I.items()}
    if "p1" not in _NC:
        _NC["p1"] = build_phase1()
        _NC["p2"] = build_phase2()
    r1 = run_bass_kernel_spmd(_NC["p1"], [phase1_inputs(c, I) for c in range(NCORES)], core_ids=list(range(NCORES))).results
    mix_p = np.zeros((16, 256, 2048), np.float32)
    mix_s = np.zeros((2, 4096, 2048), np.float32)
    new_state = np.zeros((16, 1, 2, 8, 128, 128), np.float32)
    new_ckv = np.zeros((16, 1, 256, 512), np.float32)
    new_kpe = np.zeros((16, 1, 256, 64), np.float32)
    for c in range(NCORES):
        b, j = c // 4, c % 4
        r = r1[c]
        for s in range(2):
            mix_p[2 * c + s] = r["mixp"][s].transpose(2, 0, 1).reshape(256, 2048)
            new_state[2 * c + s, 0] = r["new_state"][s]
            new_ckv[2 * c + s, 0] = r["new_ckv"][s]
            new_kpe[2 * c + s, 0] = r["new_kpe"][s]
        for lh in range(2):
            hgl = 2 * j + lh
            mix_s[b, :, hgl * 128:(hgl + 1) * 128] = r["mixs"][lh].T
            mix_s[b, :, 1024 + hgl * 128:1024 + (hgl + 1) * 128] = r["mixs"][2 + lh].T
    r2 = run_bass_kernel_spmd(_NC["p2"], [phase2_inputs(c, I["x_prompt"], I["x_sample"], mix_p, mix_s, I["c"], I["c_ctx"],
                                                        I["w_ada"][0], I["b_ada"][0], I["w_out"][0], I["norm2_g"][0],
                                                        I["w_up"][0], I["ffn_conv_w"][0], I["ffn_conv_b"][0], I["w_down"][0],
                                                        I["final_norm_g"]) for c in range(NCORES)],
                              core_ids=list(range(NCORES))).results
    yp = np.zeros((16, 256, 2048), np.float32)
    ys = np.zeros((2, 4096, 2048), np.float32)
    for c in range(NCORES):
        b, j = c // 4, c % 4
        y = r2[c]["y"]
        yp[2 * c:2 * c + 2] = y[0].reshape(2, 256, 2048)
        ys[b, 1024 * j:1024 * j + 512] = y[1]
        ys[b, 1024 * j + 512:1024 * j + 1024] = y[2]
    return (yp, ys, new_state, new_ckv, new_kpe)


def mla_fused(mk, C, x_rows, T, ci, gm1, Wd, nctx, xq, dst, hT_d=None):
    NH = 8
    mark0 = mk.mark()
    TTs = 256
    TQ = 257
    ntile = T // TTs
    NK = T + nctx
    nkt = NK // 128
    scale = 192.0 ** -0.5
    gq = mk.alloc([4], F32, "gq")
    gkv = mk.alloc([4], F32, "gkv")
    wq = mk.alloc([4, NH * 192], BF16, "wq")
    wkv = mk.alloc([4, NH * 256], BF16, "wkv")
    rot = mk.alloc([64], F32, "rot")
    onesb = mk.alloc([128], BF16, "onesb")
    kmax2 = mk.alloc([NH], F32, "kmax2")
    KPET = mk.alloc([NK], BF16, "KPET")
    QN = mk.alloc([4, 4 * TQ], BF16, "QNall")
    mk.dma("sp", gq.v, Wd["gq"])
    mk.dma("sp", gkv.v, Wd["gkv"])
    mk.dma("pool", wq.v, Wd["wq"].rearrange("(kc p) n -> p kc n", p=128))
    mk.dma("pool", wkv.v, Wd["wkv"].rearrange("(kc p) n -> p kc n", p=128))
    mk.dma("sp", rot[0:64, :], Wd["rot"])
    mk.I("dve", "memset", onesb.v.ap, 1.0, writes=[onesb])
    CKVT = mk.alloc([4, NK], BF16, "CKVT")
    mark2 = mk.mark()
    hT = mk.alloc([16, TQ], BF16, "hTm")
    xT = mk.alloc([16, TQ], F32, "xTm")
    raw = mk.alloc([4, TQ], F32, "rawm")
    cs = [mk.alloc([TQ], F32, f"cs{i}") for i in range(2)]
    rp = [mk.alloc([TQ], F32, f"rp{i}") for i in range(3)]

    def proj_norm(W, wcol0, gvec, dst_fn):
        wt = C.wt[C.wt_rr % len(C.wt)]
        C.wt_rr += 1
        load_weight_tile(mk, wt, Wd["w_m"][:, wcol0:wcol0 + 512], 16, 512)
        for q in range(4):
            b = mk.bank()
            for kc in range(16):
                mk.I("pe", "matmul", out=b[:, 0:W], lhsT=wt[:, kc, q * 128:(q + 1) * 128], rhs=hT[:, kc, 0:W], start=(kc == 0), stop=(kc == 15))
            mk.I("act", "activation", out=raw[:, q, 0:W], in_=b[:, 0:W], func=AF.Copy)
        bcast_sum_rstd(mk, C, [(raw[:, q, 0:W], 128) for q in range(4)], W, 512.0, C.rstd)
        for q in range(4):
            mk.I("dve", "scalar_tensor_tensor", out=dst_fn(q), in0=raw[:, q, 0:W], scalar=gvec[:, q:q + 1], in1=C.rstd[:, 0:W], op0=ALU.mult, op1=ALU.mult)

    def do_rope(src, n, cos_ap, sin_ap, dstv):
        x = rp[0]
        mk.I("act", "activation", out=x[0:64, 0:n], in_=src, func=AF.Copy)
        mk.dma("sp", cs[0][0:64, 0:n], cos_ap)
        mk.dma("sp", cs[1][0:64, 0:n], sin_ap)
        b = mk.bank()
        mk.I("pe", "matmul", out=b[0:64, 0:n], lhsT=rot[0:64, :], rhs=x[0:64, 0:n], start=True, stop=True)
        mk.I("dve", "tensor_tensor", out=rp[1][0:64, 0:n], in0=x[0:64, 0:n], in1=cs[0][0:64, 0:n], op=ALU.mult)
        mk.I("dve", "tensor_tensor", out=rp[2][0:64, 0:n], in0=b[0:64, 0:n], in1=cs[1][0:64, 0:n], op=ALU.mult)
        mk.I("dve", "tensor_tensor", out=dstv, in0=rp[1][0:64, 0:n], in1=rp[2][0:64, 0:n], op=ALU.add)

    for it in range(ntile):
        t0 = it * TTs
        if hT_d is not None:
            mk.dma("sp", hT[:, :, 0:TTs], hT_d[it // 2][:, :, (it % 2) * TTs:(it % 2 + 1) * TTs])
        else:
            norm_mod_tile(mk, C, x_rows[t0:t0 + TTs, :], TTs, xT, hT, C.rstd, gm1, ci)
        proj_norm(TTs, 512, gkv, lambda q: CKVT[:, q, t0:t0 + TTs])
        wt = C.wt[C.wt_rr % len(C.wt)]
        C.wt_rr += 1
        load_weight_tile(mk, wt, Wd["w_m"][:, 1024:1088], 16, 64)
        b = mk.bank()
        for kc in range(16):
            mk.I("pe", "matmul", out=b[0:64, 0:TTs], lhsT=wt[:, kc, 0:64], rhs=hT[:, kc, 0:TTs], start=(kc == 0), stop=(kc == 15))
        do_rope(b[0:64, 0:TTs], TTs, Wd["cos"][:, t0:t0 + TTs], Wd["sin"][:, t0:t0 + TTs], KPET[0:64, t0:t0 + TTs])
    for s in range(nctx // 128):
        xs = C.xstage[s % 2]
        mk.dma("sp", xs[:, 0:512], Wd["cache_ckv"][s * 128:(s + 1) * 128, :])
        mk.dma("sp", xs[:, 512:576], Wd["cache_kpe"][s * 128:(s + 1) * 128, :])
        b = mk.bank()
        for q in range(4):
            mk.I("pe", "transpose", out=b[:, q * 128:(q + 1) * 128], in_=xs[:, q * 128:(q + 1) * 128], identity=C.ident.v)
        mk.I("dve", "tensor_copy", out=CKVT[:, :, T + s * 128:T + (s + 1) * 128], in_=b[:, 0:512].rearrange("p (q t) -> p q t", q=4))
        b2 = mk.bank()
        mk.I("pe", "transpose", out=b2[0:64, 0:128], in_=xs[:, 512:576], identity=C.ident.v)
        mk.I("act", "activation", out=KPET[0:64, T + s * 128:T + (s + 1) * 128], in_=b2[0:64, 0:128], func=AF.Copy)
    for slot in range(4):
        g, hf = slot // 2, slot % 2
        norm_mod_tile(mk, C, xq[g, hf * TQ:(hf + 1) * TQ, :], TQ, xT, hT, C.rstd, gm1, ci)
        proj_norm(TQ, 0, gq, lambda q: QN[:, q, slot * TQ:(slot + 1) * TQ])
    mk.release(mark2)
    for h in range(NH):
        markh = mk.mark()
        KN = mk.alloc([NK], BF16, "KNh")
        V = mk.alloc([nkt, 128], BF16, "Vh")
        sqk = mk.alloc([512], F32, "sqk")
        kss = mk.alloc([512], F32, "kss")
        qn = mk.alloc([TQ], BF16, "qn")
        qr = mk.alloc([TQ], BF16, "qr")
        negm = mk.alloc([TQ], BF16, "negm")
        mrow = mk.alloc([TQ], F32, "mrow")
        PTb = [mk.alloc([TQ], BF16, f"PT{i}") for i in range(3)]
        rs = mk.alloc([TQ], F32, "rs")
        oo = mk.alloc([TQ], F32, "oo")
        cs = [mk.alloc([TQ], F32, f"csq{i}") for i in range(2)]
        rp = [mk.alloc([TQ], F32, f"rpq{i}") for i in range(3)]
        for k0 in range(0, NK, 512):
            n = min(512, NK - k0)
            b = mk.bank()
            for kc in range(4):
                mk.I("pe", "matmul", out=b[:, 0:n], lhsT=wkv[:, kc, h * 256:h * 256 + 128], rhs=CKVT[:, kc, k0:k0 + n], start=(kc == 0), stop=(kc == 3))
            mk.I("act", "activation", out=KN[:, k0:k0 + n], in_=b[:, 0:n], func=AF.Copy)
            bs = mk.bank()
            mk.I("act", "activation", out=sqk[:, 0:n], in_=b[:, 0:n], func=AF.Square)
            mk.I("pe", "matmul", out=bs[0:1, 0:n], lhsT=C.ones[:, 0:1], rhs=sqk[:, 0:n], start=True, stop=False)
            mk.I("act", "activation", out=kss[0:64, 0:n], in_=KPET[0:64, k0:k0 + n], func=AF.Square)
            mk.I("pe", "matmul", out=bs[0:1, 0:n], lhsT=C.ones[0:64, 0:1], rhs=kss[0:64, 0:n], start=False, stop=True)
            if k0 == 0:
                mk.I("dve", "tensor_reduce", out=kmax2[0:1, h:h + 1], in_=bs[0:1, 0:n], axis=AX.X, op=ALU.max)
            else:
                mk.I("dve", "tensor_reduce", out=kss[0:1, 0:1], in_=bs[0:1, 0:n], axis=AX.X, op=ALU.max)
                mk.I("dve", "tensor_tensor", out=kmax2[0:1, h:h + 1], in0=kmax2[0:1, h:h + 1], in1=kss[0:1, 0:1], op=ALU.max)
        for kt in range(nkt):
            b = mk.bank()
            for kc in range(4):
                mk.I("pe", "matmul", out=b[:, 0:128], lhsT=CKVT[:, kc, kt * 128:(kt + 1) * 128], rhs=wkv[:, kc, h * 256 + 128:h * 256 + 256], start=(kc == 0), stop=(kc == 3))
            mk.I("dve", "tensor_copy", out=V[:, kt, :], in_=b[:, 0:128])
        for slot in range(4):
            g, hf = slot // 2, slot % 2
            qs = slice(slot * TQ, (slot + 1) * TQ)
            b = mk.bank()
            for kc in range(4):
                mk.I("pe", "matmul", out=b[:, 0:TQ], lhsT=wq[:, kc, h * 192:h * 192 + 128], rhs=QN[:, kc, qs], start=(kc == 0), stop=(kc == 3))
            mk.I("act", "activation", out=qn.v, in_=b[:, 0:TQ], func=AF.Copy)
            mk.I("act", "activation", out=C.sq[0][:, 0:TQ], in_=b[:, 0:TQ], func=AF.Square)
            b2 = mk.bank()
            for kc in range(4):
                mk.I("pe", "matmul", out=b2[0:64, 0:TQ], lhsT=wq[:, kc, h * 192 + 128:h * 192 + 192], rhs=QN[:, kc, qs], start=(kc == 0), stop=(kc == 3))
            mk.I("act", "activation", out=C.sq[1][0:64, 0:TQ], in_=b2[0:64, 0:TQ], func=AF.Square)
            do_rope(b2[0:64, 0:TQ], TQ, Wd["cosq"][:, qs], Wd["sinq"][:, qs], qr[0:64, :])
            bm = mk.bank()
            mk.I("pe", "matmul", out=bm[0:1, 0:TQ], lhsT=C.ones[:, 0:1], rhs=C.sq[0][:, 0:TQ], start=True, stop=False)
            mk.I("pe", "matmul", out=bm[0:1, 0:TQ], lhsT=C.ones[0:64, 0:1], rhs=C.sq[1][0:64, 0:TQ], start=False, stop=True)
            mk.I("act", "activation", out=mrow[0:1, :], in_=bm[0:1, 0:TQ], func=AF.Sqrt, scale=kmax2[0:1, h:h + 1])
            mk.I("dve", "tensor_scalar", out=negm[0:1, :], in0=mrow[0:1, :], scalar1=-1.0, scalar2=None, op0=ALU.mult)
            bo = mk.reserve()
            bsum = mk.reserve()
            for kt in range(nkt):
                ks = slice(kt * 128, (kt + 1) * 128)
                bs = mk.bank()
                mk.I("pe", "matmul", out=bs[:, 0:TQ], lhsT=KN[:, ks], rhs=qn.v, start=True, stop=False)
                mk.I("pe", "matmul", out=bs[:, 0:TQ], lhsT=KPET[0:64, ks], rhs=qr[0:64, :], start=False, stop=False)
                mk.I("pe", "matmul", out=bs[:, 0:TQ], lhsT=onesb[0:1, :], rhs=negm[0:1, :], start=False, stop=True)
                PT = PTb[kt % 3]
                mk.I("act", "activation", out=PT.v, in_=bs[:, 0:TQ], func=AF.Exp, scale=scale)
                mk.I("pe", "matmul", out=bo[:, 0:TQ], lhsT=V[:, kt, :], rhs=PT.v, start=(kt == 0), stop=(kt == nkt - 1))
                mk.I("pe", "matmul", out=bsum[:, 0:TQ], lhsT=onesb.v, rhs=PT.v, start=(kt == 0), stop=(kt == nkt - 1))
            mk.I("dve", "reciprocal", out=rs.v, in_=bsum[:, 0:TQ])
            mk.I("dve", "tensor_tensor", out=oo.v, in0=bo[:, 0:TQ], in1=rs.v, op=ALU.mult)
            mk.dma("sp", dst[g, h, :, hf * TQ:(hf + 1) * TQ], oo.v)
            mk.unreserve(bo)
            mk.unreserve(bsum)
        mk.release(markh)
    mk.release(mark0)


def build_fused():
    nc = bass.Bass("TRN2", target_bir_lowering=False)
    DEBUG["nc"] = nc
    DEBUG["done"] = set()
    D = {}

    def din(name, shape):
        D[name] = nc.dram_tensor(name, list(shape), F32, kind="ExternalInput").ap()

    def dout(name, shape):
        D[name] = nc.dram_tensor(name, list(shape), F32, kind="ExternalOutput").ap()

    for name, shape in (("xp", [2, 256, 2048]), ("xs", [4096, 2048]), ("x2", [3, 514, 2048]), ("condT", [128, 16, 2]),
                        ("w_ada", [2048, 12288]), ("b_adaT", [128, 96]), ("norm1T", [128, 16]), ("ident", [128, 128]),
                        ("tri0", [128, 128]), ("tri1", [128, 128]), ("ms0", [128, 128]), ("ms1", [128, 128]),
                        ("s0p", [2, 8, 128, 128]), ("s0h", [8, 2, 1, 128, 128]), ("w_g_p", [2048, 4096]), ("w_ab_p", [2048, 32]),
                        ("cw_p", [128, 24, 3]), ("dtb_p", [128, 16]), ("alog_p", [128, 16]), ("gng", [128, 1]),
                        ("w_ab_h", [8, 2048, 4]), ("dtb_h", [8, 128, 2]), ("alog_h", [8, 128, 2]),
                        ("w_m", [2048, 1088]), ("gq", [128, 4]), ("gkv", [128, 4]), ("wq_p", [512, 1536]), ("wkv_p", [512, 2048]),
                        ("cos", [64, 4096]), ("sin", [64, 4096]), ("cosq", [64, 1028]), ("sinq", [64, 1028]), ("rot", [64, 64]),
                        ("cache_ckv", [256, 512]), ("cache_kpe", [256, 64]), ("sel", [128, 4]), ("hmask", [128, 4]),
                        ("w_out", [2048, 2048]), ("norm2T", [128, 16]), ("w_up", [2048, 11264]), ("fcwT", [128, 88, 3]),
                        ("fcbT", [128, 88]), ("w_down", [5632, 2048]), ("fnormT", [128, 16])):
        din(name, shape)
    for name, shape in (("y", [3, 512, 2048]), ("new_state", [2, 2, 8, 128, 128]), ("new_ckv", [2, 256, 512]), ("new_kpe", [2, 256, 64])):
        dout(name, shape)
    mixp_d = nc.dram_tensor("mixp_d", [2, 16, 128, 256], F32).ap()
    mixs_d = nc.dram_tensor("mixs_d", [2, 16, 128, 514], F32).ap()
    hT_d = nc.dram_tensor("hT_d", [8, 128, 16, 512], BF16).ap()
    with ExitStack() as st:
        mk = MK(nc, st)
        C = Ctx()
        alloc_common(mk, C, D)
        n1g = mk.alloc([16], F32, "n1g")
        gm1 = mk.alloc([16, 2], F32, "gm1")
        gm2 = mk.alloc([16, 2], F32, "gm2")
        n2g = mk.alloc([16], F32, "n2g")
        fng = mk.alloc([16], F32, "fng")
        fcw = mk.alloc([88, 3], F32, "fcw")
        fcb = mk.alloc([88], F32, "fcb")
        hm = mk.alloc([4], F32, "hm")
        sel = mk.alloc([4], F32, "sel")
        for t, nm in ((n1g, "norm1T"), (n2g, "norm2T"), (fng, "fnormT"), (fcw, "fcwT"), (fcb, "fcbT"), (hm, "hmask"), (sel, "sel")):
            mk.dma("sp", t.v, D[nm])
        adaln(mk, C, D["w_ada"], [0, 1, 2, 3, 4, 5], 2)
        for (gm, ng, lo) in ((gm1, n1g, 16), (gm2, n2g, 64)):
            mk.I("dve", "tensor_scalar", out=gm.v, in0=C.mod[:, lo:lo + 16, :], scalar1=1.0, scalar2=None, op0=ALU.add)
            mk.I("dve", "tensor_tensor", out=gm.v, in0=gm.v, in1=ng.v.rearrange("p (c o) -> p c o", o=1).bc([128, 16, 2]), op=ALU.mult)
        Wp = dict(w_g=D["w_g_p"], w_ab=D["w_ab_p"], cw=D["cw_p"], dtb=D["dtb_p"], alog=D["alog_p"], gng=D["gng"])
        Wmp = dict(w_m=D["w_m"], gq=D["gq"], gkv=D["gkv"], wq=D["wq_p"], wkv=D["wkv_p"], rot=D["rot"])
        for s in range(2):
            gdn_phase(mk, C, D["xp"][s], 256, 8, 0, gm1, Wp, D["s0p"], mixp_d[s, 0:8], D["new_state"][s])
            mla_phase(mk, C, D["xp"][s], 256, 8, 0, gm1, Wmp, 0, False, mixp_d[s, 8:16], D["new_ckv"][s], D["new_kpe"][s])
        for h in range(8):
            Ws = dict(w_g=D["w_g_p"][:, h * 512:(h + 1) * 512], w_ab=D["w_ab_h"][h], cw=D["cw_p"][:, 3 * h:3 * h + 3, :],
                      dtb=D["dtb_h"][h], alog=D["alog_h"][h], gng=D["gng"])
            gdn_phase(mk, C, D["xs"], 4096, 1, 1, gm1, Ws, D["s0h"][h], None, None, win=(sel, mixs_d[:, h]),
                      hcache=(hT_d, "write" if h == 0 else "read"))
        Wms = dict(w_m=D["w_m"], gq=D["gq"], gkv=D["gkv"], wq=D["wq_p"], wkv=D["wkv_p"], rot=D["rot"], cos=D["cos"], sin=D["sin"],
                   cosq=D["cosq"], sinq=D["sinq"], cache_ckv=D["cache_ckv"], cache_kpe=D["cache_kpe"])
        mla_fused(mk, C, D["xs"], 4096, 1, gm1, Wms, 256, D["x2"][1:3], mixs_d[:, 8:16], hT_d)
        mk.barrier()
        xT = mk.alloc([16, 514], F32, "xT")
        actin = mk.alloc([16, 514], BF16, "actin")
        actT = mk.alloc([44, 512], BF16, "actT")
        Ra = [mk.alloc([514], F32, f"Ra{i}") for i in range(2)]
        Rg = [mk.alloc([514], F32, f"Rg{i}") for i in range(2)]
        ta = [mk.alloc([512], F32, f"ta{i}") for i in range(2)]
        tg = [mk.alloc([512], F32, f"tg{i}") for i in range(2)]

        def load_mix(ti, actin_, W):
            if ti == 0:
                for s in range(2):
                    mk.dma("pool", actin_[:, :, s * 256:(s + 1) * 256], mixp_d[s].rearrange("c p w -> p c w"))
            else:
                mk.dma("pool", actin_[:, :, 0:W], mixs_d[ti - 1].rearrange("c p w -> p c w"))

        phase2_tiles(mk, C, D["x2"], D["y"], load_mix, n2g, fng, fcw, fcb, hm, gm2, xT, actin, actT, Ra, Rg, ta, tg, C.tmpx, C.rstd,
                     D["w_out"], D["w_up"], D["w_down"])
        mk.finalize()
        print("fused instructions:", mk.n_inst, {e: len(mk.ops[e]) for e in ENGS}, "sbuf words", mk.top)
    return nc


def fused_inputs(core, I):
    b, j = core // 4, core % 4
    d = phase1_inputs(core, I)
    for k in ("s0s", "w_g_s", "w_ab_s", "cw_s", "dtb_s", "alog_s", "wq_s", "wkv_s"):
        d.pop(k)
    p2 = phase2_inputs(core, I["x_prompt"], I["x_sample"], np.zeros((16, 256, 2048), np.float32), np.zeros((2, 4096, 2048), np.float32),
                       I["c"], I["c_ctx"], I["w_ada"][0], I["b_ada"][0], I["w_out"][0], I["norm2_g"][0], I["w_up"][0],
                       I["ffn_conv_w"][0], I["ffn_conv_b"][0], I["w_down"][0], I["final_norm_g"])
    for k in ("x2", "hmask", "w_out", "norm2T", "w_up", "fcwT", "fcbT", "w_down", "fnormT"):
        d[k] = p2[k]
    w_in = I["w_in"][0]
    dtb = I["gdn_dt_bias"][0]
    alog = I["gdn_a_log"][0]
    d["s0h"] = np.stack([I["state_gdn"][b, 0, :, h:h + 1] for h in range(8)], axis=0)
    d["w_ab_h"] = np.stack([w_in[:, [4096 + h, 4104 + h, 4112 + h, 4120 + h]] for h in range(8)], axis=0)
    d["dtb_h"] = np.stack([np.broadcast_to(dtb[:, h].reshape(1, 2), (128, 2)) for h in range(8)], axis=0)
    d["alog_h"] = np.stack([np.broadcast_to(alog[:, h].reshape(1, 2), (128, 2)) for h in range(8)], axis=0)
    cos2, sin2 = d["cos"], d["sin"]
    cosq = np.zeros((64, 1028), np.float32)
    sinq = np.zeros((64, 1028), np.float32)
    for g in range(2):
        lo = 1024 * j + 512 * g - 1
        a, e = max(lo, 0), min(lo + 514, 4096)
        cosq[:, g * 514 + a - lo:g * 514 + e - lo] = cos2[:, a:e]
        sinq[:, g * 514 + a - lo:g * 514 + e - lo] = sin2[:, a:e]
    d["cosq"], d["sinq"] = cosq, sinq
    sel = np.zeros((128, 4), np.float32)
    sel[:, j] = 1.0
    d["sel"] = sel
    return {k: np.ascontiguousarray(np.asarray(v, np.float32)) for k, v in d.items()}


def kernel_fused(**I):
    I = {k: np.asarray(v) for k, v in I.items()}
    if "f" not in _NC:
        _NC["f"] = build_fused()
    r = run_bass_kernel_spmd(_NC["f"], [fused_inputs(c, I) for c in range(NCORES)], core_ids=list(range(NCORES))).results
    yp = np.zeros((16, 256, 2048), np.float32)
    ys = np.zeros((2, 4096, 2048), np.float32)
    new_state = np.zeros((16, 1, 2, 8, 128, 128), np.float32)
    new_ckv = np.zeros((16, 1, 256, 512), np.float32)
    new_kpe = np.zeros((16, 1, 256, 64), np.float32)
    for c in range(NCORES):
        b, j = c // 4, c % 4
        y = r[c]["y"]
        yp[2 * c:2 * c + 2] = y[0].reshape(2, 256, 2048)
        ys[b, 1024 * j:1024 * j + 512] = y[1]
        ys[b, 1024 * j + 512:1024 * j + 1024] = y[2]
        for s in range(2):
            new_state[2 * c + s, 0] = r[c]["new_state"][s]
            new_ckv[2 * c + s, 0] = r[c]["new_ckv"][s]
            new_kpe[2 * c + s, 0] = r[c]["new_kpe"][s]
    return (yp, ys, new_state, new_ckv, new_kpe)


def kernel(**inputs):
    return kernel_fused(**inputs)
```

```python
import numpy as np
from contextlib import ExitStack
import concourse.bass as bass
import concourse.mybir as mybir
from concourse.bass_utils import run_bass_kernel_spmd

F32 = mybir.dt.float32
BF16 = mybir.dt.bfloat16
ALU = mybir.AluOpType
AF = mybir.ActivationFunctionType
AX = mybir.AxisListType
ENGS = ("pe", "act", "dve", "pool", "sp")
NCORES = 8
PENG = "dve"
GSTOP = {"v": 99}
NPAR = 6
EPS = 1e-6


class View:
    __slots__ = ("tile", "ap", "gen")

    def __init__(self, tile, ap, gen=None):
        self.tile = tile
        self.ap = ap
        self.gen = gen

    def __getitem__(self, idx):
        return View(self.tile, self.ap[idx], self.gen)

    def bc(self, shape):
        return View(self.tile, self.ap.to_broadcast(list(shape)), self.gen)

    def rearrange(self, s, **kw):
        return View(self.tile, self.ap.rearrange(s, **kw), self.gen)


class BankRef:
    def __init__(self, tile, gen):
        self.tile = tile
        self.gen = gen

    def __getitem__(self, idx):
        return View(self.tile, self.tile.h[idx], self.gen)


class Tile:
    def __init__(self, ap, name):
        self.h = ap
        self.name = name
        self.last_write = None
        self.reads = []
        self.dma_sem = None
        self.dma_count = 0

    def __getitem__(self, idx):
        return View(self, self.h[idx])

    @property
    def v(self):
        return View(self, self.h)


class MK:
    ARENA_F32 = 50688
    N_DMA_SEMS = 16

    def __init__(self, nc, stack):
        self.nc = nc
        self.stack = stack
        self.ops = {e: [] for e in ENGS}
        self.seq = {e: 0 for e in ENGS}
        self.sem = {e: stack.enter_context(nc.semaphore("sem_" + e)) for e in ("pe", "act", "dve", "pool")}
        self.waited = {e: {} for e in ENGS}
        self.dma_tiles = []
        self.n_inst = 0
        self.arena = stack.enter_context(nc.sbuf_tensor("arena", [128, self.ARENA_F32], F32))
        self.top = 0
        self.banks = [Tile(stack.enter_context(nc.psum_tensor(f"bank{i}", [128, 512], F32)), f"bank{i}")
                      for i in range(8)]
        for b in self.banks:
            b.is_bank = True
        self.bank_rr = 0
        self.reserved = []
        self.dma_sem_pool = {}
        self.dma_sem_rr = {}
        self.tcount = 0

    def alloc(self, free_shape, dtype=F32, name=None, parts=128):
        n = int(np.prod(free_shape))
        words = n if dtype == F32 else (n + 1) // 2
        words = (words + 7) // 8 * 8
        assert self.top + words <= self.ARENA_F32, f"SBUF arena overflow allocating {name} {free_shape}"
        ap = self.arena[0:parts, self.top:self.top + words]
        self.top += words
        if dtype != F32:
            ap = ap.bitcast(dtype)
        ap = ap[:, 0:n]
        if len(free_shape) == 2:
            ap = ap.rearrange("p (a b) -> p a b", a=free_shape[0])
        elif len(free_shape) == 3:
            ap = ap.rearrange("p (a b c) -> p a b c", a=free_shape[0], b=free_shape[1])
        self.tcount += 1
        return Tile(ap, name or f"t{self.tcount}")

    def mark(self):
        return self.top

    def release(self, mark):
        self.barrier()
        self.top = mark

    def bank(self):
        while True:
            b = self.banks[self.bank_rr % 8]
            self.bank_rr += 1
            if b not in self.reserved:
                b.gen = getattr(b, "gen", 0) + 1
                return BankRef(b, b.gen)

    def reserve(self):
        b = self.bank()
        self.reserved.append(b.tile)
        return b

    def unreserve(self, b):
        self.reserved.remove(b.tile)

    def _resolve(self, ev):
        if ev[0] == "dma":
            t = ev[1]
            return (t.dma_sem, t.dma_count, None)
        return (self.sem[ev[0]], ev[1], ev[0])

    def _wait(self, eng, ev):
        sem, val, src = self._resolve(ev)
        if src == eng and eng == "pe":
            return
        key = id(sem)
        if self.waited[eng].get(key, 0) >= val:
            return
        self.waited[eng][key] = val
        self.ops[eng].append(("w", sem, val))

    def _deps(self, eng, reads, writes):
        for t in reads:
            if t.last_write is not None:
                self._wait(eng, t.last_write)
            if getattr(t, "is_bank", False):
                for ev in t.reads:
                    if ev[0] != eng:
                        self._wait(eng, ev)
        for t in writes:
            if t.last_write is not None:
                self._wait(eng, t.last_write)
            for ev in t.reads:
                self._wait(eng, ev)

    def _commit(self, ev, reads, writes):
        for t in writes:
            t.last_write = ev
            t.reads = []
        for t in reads:
            if t in writes:
                continue
            t.reads.append(ev)
            if len(t.reads) > 24:
                best = {}
                for e in t.reads:
                    k = e[0] if e[0] != "dma" else ("dma", id(e[1]))
                    if k not in best or (e[0] != "dma" and e[1] > best[k][1]):
                        best[k] = e
                t.reads = list(best.values())

    def I(self, eng, meth, *args, reads=(), writes=(), **kw):
        rd, wr = list(reads), list(writes)
        real = {}
        for k, v in kw.items():
            if isinstance(v, View):
                assert v.gen is None or v.gen == v.tile.gen, f"stale PSUM bank handle used by {meth} ({k})"
                (wr if k in ("out", "accum_out") else rd).append(v.tile)
                real[k] = v.ap
            else:
                real[k] = v
        rargs = []
        for v in args:
            if isinstance(v, View):
                rd.append(v.tile)
                rargs.append(v.ap)
            else:
                rargs.append(v)
        self._deps(eng, rd, wr)
        self.seq[eng] += 1
        ev = (eng, self.seq[eng])
        self.ops[eng].append(("i", meth, rargs, real))
        self._commit(ev, rd, wr)
        self.n_inst += 1
        return ev

    def dma(self, q, out, in_, **kw):
        rd, wr = [], []
        st = None
        if isinstance(out, View):
            wr.append(out.tile)
            o = out.ap
            st = out.tile
        else:
            o = out
        if isinstance(in_, View):
            rd.append(in_.tile)
            i = in_.ap
            if st is None:
                st = in_.tile
        else:
            i = in_
        if st.dma_sem is None:
            st.dma_sem = {}
        if q not in st.dma_sem:
            pool = self.dma_sem_pool.setdefault(q, [])
            if len(pool) < self.N_DMA_SEMS:
                ds = Tile(None, "dsem_%s%d" % (q, len(pool)))
                ds.dma_sem = self.stack.enter_context(self.nc.semaphore("ds_%s%d" % (q, len(pool))))
                pool.append(ds)
                self.dma_tiles.append(ds)
            rr = self.dma_sem_rr.get(q, 0)
            self.dma_sem_rr[q] = rr + 1
            st.dma_sem[q] = pool[rr % self.N_DMA_SEMS]
        dsem = st.dma_sem[q]
        self._deps(q, rd, wr)
        if dsem.dma_count:
            self._wait(q, ("dma", dsem))
        dsem.dma_count += 16
        self.ops[q].append(("d", o, i, kw, dsem.dma_sem))
        ev = ("dma", dsem)
        self._commit(ev, rd, wr)
        self.n_inst += 1
        return ev

    def barrier(self):
        for e in ENGS:
            for src in ("pe", "act", "dve", "pool"):
                if src != e and self.seq[src] > 0:
                    self._wait(e, (src, self.seq[src]))
            for t in self.dma_tiles:
                if t.dma_count:
                    self._wait(e, ("dma", t))

    def finalize(self):
        for t in self.dma_tiles:
            self.ops["sp"].append(("w", t.dma_sem, t.dma_count))
        sem = self.sem

        def run(eng_name):
            def f(e):
                for op in self.ops[eng_name]:
                    if op[0] == "w":
                        e.wait_ge(op[1], op[2])
                    elif op[0] == "i":
                        ins = getattr(e, op[1])(*op[2], **op[3])
                        if eng_name in sem:
                            ins.then_inc(sem[eng_name], 1)
                    else:
                        e.dma_start(out=op[1], in_=op[2], **op[3]).then_inc(op[4], 16)
            return f

        with self.nc.Block() as block:
            block.tensor(run("pe"))
            block.scalar(run("act"))
            block.vector(run("dve"))
            block.gpsimd(run("pool"))
            block.sync(run("sp"))


class Ctx:
    pass


DEBUG = {"on": False, "nc": None, "done": set()}


def dump(mk, name, view, shape):
    if not DEBUG["on"] or name in DEBUG["done"]:
        return
    DEBUG["done"].add(name)
    ap = DEBUG["nc"].dram_tensor("dbg_" + name, list(shape), F32, kind="ExternalOutput").ap()
    stg = mk.alloc(list(shape[1:]), F32, "dbgs_" + name, parts=shape[0]) if False else None
    mk.dma("sp", ap, view)


def halves(W):
    if W <= 512:
        return [(0, W)]
    h = (W + 1) // 2
    return [(0, h), (h, W - h)]


def mm_group(mk, W, steps):
    outs = []
    for (c0, n) in halves(W):
        b = mk.bank()
        for i, (lhsT, rhs_fn) in enumerate(steps):
            mk.I("pe", "matmul", out=b[:, 0:n], lhsT=lhsT, rhs=rhs_fn(c0, n),
                 start=(i == 0), stop=(i == len(steps) - 1))
        outs.append((b, c0, n))
    return outs


def load_weight_tile(mk, wt, src_ap, KC, ncols):
    mk.dma("pool", wt[:, 0:KC, 0:ncols], src_ap.rearrange("(kc p) n -> p kc n", p=128))


def load_xT(mk, C, x_rows, W, xT, eng_rr):
    nsub = (W + 127) // 128
    for s in range(nsub):
        r0 = s * 128
        n = min(128, W - r0)
        xs = C.xstage[s % 2]
        mk.dma("sp", xs[0:n, :], x_rows[r0:r0 + n, :])
        for g in range(4):
            b = mk.bank()
            for q in range(4):
                c = g * 4 + q
                mk.I("pe", "transpose", out=b[:, q * 128:q * 128 + n], in_=xs[0:n, c * 128:(c + 1) * 128],
                     identity=C.ident[0:n, 0:n])
            src = b[:, 0:512].rearrange("p (q t) -> p q t", q=4)[:, :, 0:n]
            dst = xT[:, g * 4:(g + 1) * 4, r0:r0 + n]
            if (s * 4 + g) % 2 == 0:
                mk.I("dve", "tensor_copy", out=dst, in_=src)
            else:
                mk.I("act", "activation", out=dst, in_=src, func=AF.Copy)


def rms_rstd(mk, C, XT, nch, W, col0, dim, rstd):
    pieces = halves(W)
    banks = [mk.bank() for _ in pieces]
    use_b = hasattr(C, "sqb")
    for c in range(nch):
        if use_b:
            sq = C.sqb[C.sqb_rr % 4]
            C.sqb_rr += 1
            ones = C.onesb16
        else:
            sq = C.sq[c % 2]
            ones = C.ones
        mk.I("act", "activation", out=sq[:, 0:W], in_=XT[:, c, col0:col0 + W], func=AF.Square)
        for (b, (c0, n)) in zip(banks, pieces):
            mk.I("pe", "matmul", out=b[:, 0:n], lhsT=ones.v, rhs=sq[:, c0:c0 + n], start=(c == 0), stop=(c == nch - 1))
    for (b, (c0, n)) in zip(banks, pieces):
        mk.I("act", "activation", out=C.tmpn[:, c0:c0 + n], in_=b[:, 0:n], func=AF.Sqrt, scale=1.0 / dim, bias=C.epsb[:, 0:1])
        mk.I("dve", "reciprocal", out=rstd[:, c0:c0 + n], in_=C.tmpn[:, c0:c0 + n])


def adaln(mk, C, w_ada, which_list, ncond):
    b = mk.bank()
    for wh in which_list:
        for blk in range(4):
            col0 = wh * 2048 + blk * 512
            wt = C.wt[C.wt_rr % len(C.wt)]
            C.wt_rr += 1
            load_weight_tile(mk, wt, w_ada[:, col0:col0 + 512], 16, 512)
            for q in range(4):
                cc = wh * 16 + blk * 4 + q
                for kc in range(16):
                    mk.I("pe", "matmul", out=b[:, cc * ncond:(cc + 1) * ncond], lhsT=wt[:, kc, q * 128:(q + 1) * 128],
                         rhs=C.scond[:, kc, 0:ncond], start=(kc == 0), stop=(kc == 15))
    for wh in which_list:
        sl = slice(wh * 16, (wh + 1) * 16)
        mk.I("dve", "tensor_tensor", out=C.mod[:, sl, :],
             in0=b[:, wh * 16 * ncond:(wh + 1) * 16 * ncond].rearrange("p (c n) -> p c n", n=ncond),
             in1=C.b_ada[:, sl].rearrange("p (c o) -> p c o", o=1).bc([128, 16, ncond]), op=ALU.add)


P2_TILES = [dict(W=512, cond=0, segs=[(0, 256, 0), (256, 256, 0)], out0=0),
            dict(W=514, cond=1, segs=[(0, 514, 1)], out0=1),
            dict(W=514, cond=1, segs=[(0, 514, 1)], out0=1)]


def phase2_tiles(mk, C, x2, y, load_mix, n2g, fng, fcw, fcb, hm, gm2, xT, actin, actT, Ra, Rg, ta, tg, tmpx, rstd,
                 w_out, w_up, w_down):
    for ti, T in enumerate(P2_TILES):
        W, ci, out0 = T["W"], T["cond"], T["out0"]
        load_xT(mk, C, x2[ti], W, xT, 0)
        load_mix(ti, actin, W)
        for blk in range(4):
            wt = C.wt[C.wt_rr % len(C.wt)]
            C.wt_rr += 1
            load_weight_tile(mk, wt, w_out[:, blk * 512:(blk + 1) * 512], 16, 512)
            for q in range(4):
                cc = blk * 4 + q
                outs = mm_group(mk, W, [(wt[:, kc, q * 128:(q + 1) * 128],
                                        (lambda c0, n, kc=kc: actin[:, kc, c0:c0 + n])) for kc in range(16)])
                for (b, c0, n) in outs:
                    mk.I("dve", "scalar_tensor_tensor", out=xT[:, cc, c0:c0 + n], in0=b[:, 0:n],
                         scalar=C.mod[:, 32 + cc, ci:ci + 1], in1=xT[:, cc, c0:c0 + n], op0=ALU.mult, op1=ALU.add)
        rms_rstd(mk, C, xT, 16, W, 0, 2048.0, rstd)
        for cc in range(16):
            tx = tmpx[cc % 2]
            mk.I("dve", "scalar_tensor_tensor", out=tx[:, 0:W], in0=xT[:, cc, 0:W], scalar=gm2[:, cc, ci:ci + 1],
                 in1=rstd[:, 0:W], op0=ALU.mult, op1=ALU.mult)
            mk.I("act", "activation", out=actin[:, cc, 0:W], in_=tx[:, 0:W], func=AF.Identity,
                 bias=C.mod[:, 48 + cc, ci:ci + 1], scale=1.0)
        for jb in range(11):
            wa = C.wt[C.wt_rr % len(C.wt)]
            C.wt_rr += 1
            load_weight_tile(mk, wa, w_up[:, jb * 512:(jb + 1) * 512], 16, 512)
            wg = C.wt[C.wt_rr % len(C.wt)]
            C.wt_rr += 1
            load_weight_tile(mk, wg, w_up[:, 5632 + jb * 512:5632 + (jb + 1) * 512], 16, 512)
            for q in range(4):
                j = jb * 4 + q
                res = []
                for (wtile, R, chunk) in ((wa, Ra[j % 2], j), (wg, Rg[j % 2], 44 + j)):
                    outs = mm_group(mk, W, [(wtile[:, kc, q * 128:(q + 1) * 128],
                                            (lambda c0, n, kc=kc: actin[:, kc, c0:c0 + n])) for kc in range(16)])
                    for (b, c0, n) in outs:
                        mk.I("act", "activation", out=R[:, c0:c0 + n], in_=b[:, 0:n], func=AF.Copy)
                    res.append((R, chunk))
                for (R, chunk), tt in ((res[0], ta[j % 2]), (res[1], tg[j % 2])):
                    for (s0, L, halo) in T["segs"]:
                        if halo:
                            mk.I("dve", "tensor_scalar", out=R[:, s0:s0 + 1], in0=R[:, s0:s0 + 1],
                                 scalar1=hm[:, 2 * (ti - 1):2 * (ti - 1) + 1], scalar2=None, op0=ALU.mult)
                            mk.I("dve", "tensor_scalar", out=R[:, s0 + L - 1:s0 + L], in0=R[:, s0 + L - 1:s0 + L],
                                 scalar1=hm[:, 2 * (ti - 1) + 1:2 * (ti - 1) + 2], scalar2=None, op0=ALU.mult)
                            o0, n = s0 + 1, L - 2
                            mk.I("act", "activation", out=tt[:, 0:n], in_=R[:, o0:o0 + n], func=AF.Identity,
                                 scale=fcw[:, chunk, 1:2], bias=fcb[:, chunk:chunk + 1])
                            mk.I("dve", "scalar_tensor_tensor", out=tt[:, 0:n], in0=R[:, o0 - 1:o0 - 1 + n],
                                 scalar=fcw[:, chunk, 0:1], in1=tt[:, 0:n], op0=ALU.mult, op1=ALU.add)
                            mk.I("dve", "scalar_tensor_tensor", out=tt[:, 0:n], in0=R[:, o0 + 1:o0 + 1 + n],
                                 scalar=fcw[:, chunk, 2:3], in1=tt[:, 0:n], op0=ALU.mult, op1=ALU.add)
                        else:
                            mk.I("act", "activation", out=tt[:, s0:s0 + L], in_=R[:, s0:s0 + L], func=AF.Identity,
                                 scale=fcw[:, chunk, 1:2], bias=fcb[:, chunk:chunk + 1])
                            mk.I("dve", "scalar_tensor_tensor", out=tt[:, s0 + 1:s0 + L], in0=R[:, s0:s0 + L - 1],
                                 scalar=fcw[:, chunk, 0:1], in1=tt[:, s0 + 1:s0 + L], op0=ALU.mult, op1=ALU.add)
                            mk.I("dve", "scalar_tensor_tensor", out=tt[:, s0:s0 + L - 1], in0=R[:, s0 + 1:s0 + L],
                                 scalar=fcw[:, chunk, 2:3], in1=tt[:, s0:s0 + L - 1], op0=ALU.mult, op1=ALU.add)
                mk.I("act", "activation", out=ta[j % 2].v, in_=ta[j % 2].v, func=AF.Silu)
                mk.I("dve", "tensor_tensor", out=actT[:, j, :], in0=ta[j % 2].v, in1=tg[j % 2].v, op=ALU.mult)
        for cc in range(16):
            wt = C.wt[C.wt_rr % len(C.wt)]
            C.wt_rr += 1
            wv = wt.v.rearrange("p a b -> p (a b)")[:, 0:44 * 128].rearrange("p (k n) -> p k n", k=44)
            mk.dma("pool", wv, w_down[:, cc * 128:(cc + 1) * 128].rearrange("(kc p) n -> p kc n", p=128))
            b = mk.bank()
            for j in range(44):
                mk.I("pe", "matmul", out=b[:, 0:512], lhsT=wv[:, j, :], rhs=actT[:, j, :], start=(j == 0), stop=(j == 43))
            mk.I("dve", "scalar_tensor_tensor", out=xT[:, cc, out0:out0 + 512], in0=b[:, 0:512],
                 scalar=C.mod[:, 80 + cc, ci:ci + 1], in1=xT[:, cc, out0:out0 + 512], op0=ALU.mult, op1=ALU.add)
        rms_rstd(mk, C, xT, 16, 512, out0, 2048.0, rstd)
        for cc in range(16):
            mk.I("dve", "scalar_tensor_tensor", out=xT[:, cc, out0:out0 + 512], in0=xT[:, cc, out0:out0 + 512],
                 scalar=fng[:, cc:cc + 1], in1=rstd[:, 0:512], op0=ALU.mult, op1=ALU.mult)
        for s in range(4):
            ys = C.xstage[s % 2]
            for g in range(4):
                b = mk.bank()
                for q in range(4):
                    c = g * 4 + q
                    mk.I("pe", "transpose", out=b[:, q * 128:(q + 1) * 128],
                         in_=xT[:, c, out0 + s * 128:out0 + (s + 1) * 128], identity=C.ident.v)
                if g % 2 == 0:
                    mk.I("dve", "tensor_copy", out=ys[:, g * 512:(g + 1) * 512], in_=b[:, 0:512])
                else:
                    mk.I("act", "activation", out=ys[:, g * 512:(g + 1) * 512], in_=b[:, 0:512], func=AF.Copy)
            mk.dma("sp", y[ti, s * 128:(s + 1) * 128, :], ys.v)


def build_phase2():
    nc = bass.Bass("TRN2", target_bir_lowering=False)
    D = {}

    def din(name, shape, dt=F32):
        D[name] = nc.dram_tensor(name, list(shape), dt, kind="ExternalInput").ap()
        return D[name]

    x2 = din("x2", [3, 514, 2048])
    mix = din("mix", [3, 16, 128, 514])
    condT = din("condT", [128, 16, 2])
    hmask = din("hmask", [128, 4])
    w_ada = din("w_ada", [2048, 12288])
    b_adaT = din("b_adaT", [128, 96])
    w_out = din("w_out", [2048, 2048])
    norm2T = din("norm2T", [128, 16])
    w_up = din("w_up", [2048, 11264])
    fcwT = din("fcwT", [128, 88, 3])
    fcbT = din("fcbT", [128, 88])
    w_down = din("w_down", [5632, 2048])
    fnormT = din("fnormT", [128, 16])
    identD = din("ident", [128, 128])
    y = nc.dram_tensor("y", [3, 512, 2048], F32, kind="ExternalOutput").ap()

    with ExitStack() as st:
        mk = MK(nc, st)
        C = Ctx()
        C.ident = mk.alloc([128], F32, "ident")
        C.ones = mk.alloc([128], F32, "ones")
        C.epsb = mk.alloc([1], F32, "epsb")
        C.scond = mk.alloc([16, 2], BF16, "scond")
        condf = mk.alloc([16, 2], F32, "condf")
        C.b_ada = mk.alloc([96], F32, "b_ada")
        C.mod = mk.alloc([96, 2], F32, "mod")
        n2g = mk.alloc([16], F32, "n2g")
        fng = mk.alloc([16], F32, "fng")
        fcw = mk.alloc([88, 3], F32, "fcw")
        fcb = mk.alloc([88], F32, "fcb")
        hm = mk.alloc([4], F32, "hm")
        gm2 = mk.alloc([16, 2], F32, "gm2")
        C.xstage = [mk.alloc([2048], F32, f"xs{i}") for i in range(2)]
        C.sq = [mk.alloc([514], F32, f"sq{i}") for i in range(2)]
        C.tmpn = mk.alloc([514], F32, "tmpn")
        rstd = mk.alloc([514], F32, "rstd")
        C.wt = [mk.alloc([16, 512], BF16, f"wt{i}") for i in range(3)]
        C.wt_rr = 0
        xT = mk.alloc([16, 514], F32, "xT")
        actin = mk.alloc([16, 514], BF16, "actin")
        actT = mk.alloc([44, 512], BF16, "actT")
        Ra = [mk.alloc([514], F32, f"Ra{i}") for i in range(2)]
        Rg = [mk.alloc([514], F32, f"Rg{i}") for i in range(2)]
        ta = [mk.alloc([512], F32, f"ta{i}") for i in range(2)]
        tg = [mk.alloc([512], F32, f"tg{i}") for i in range(2)]
        tmpx = [mk.alloc([514], F32, f"tmpx{i}") for i in range(2)]

        mk.dma("sp", C.ident.v, identD)
        mk.I("dve", "memset", C.ones.v.ap, 1.0, writes=[C.ones])
        mk.I("dve", "memset", C.epsb.v.ap, EPS, writes=[C.epsb])
        mk.dma("sp", condf.v, condT)
        mk.dma("sp", C.b_ada.v, b_adaT)
        mk.dma("sp", n2g.v, norm2T)
        mk.dma("sp", fng.v, fnormT)
        mk.dma("sp", fcw.v, fcwT)
        mk.dma("sp", fcb.v, fcbT)
        mk.dma("sp", hm.v, hmask)
        mk.I("act", "activation", out=C.scond.v, in_=condf.v, func=AF.Silu)
        adaln(mk, C, w_ada, [2, 3, 4, 5], 2)
        mk.I("dve", "tensor_scalar", out=gm2.v, in0=C.mod[:, 64:80, :], scalar1=1.0, scalar2=None, op0=ALU.add)
        mk.I("dve", "tensor_tensor", out=gm2.v, in0=gm2.v,
             in1=n2g.v.rearrange("p (c o) -> p c o", o=1).bc([128, 16, 2]), op=ALU.mult)

        phase2_tiles(mk, C, x2, y, lambda ti, actin, W: mk.dma("pool", actin[:, :, 0:W], mix[ti, :, :, 0:W].rearrange("c p w -> p c w")),
                     n2g, fng, fcw, fcb, hm, gm2, xT, actin, actT, Ra, Rg, ta, tg, tmpx, rstd, w_out, w_up, w_down)
        mk.finalize()
        print("phase2 instructions:", mk.n_inst, {e: len(mk.ops[e]) for e in ENGS}, "sbuf words", mk.top)
    return nc


def colT(v, n):
    return np.ascontiguousarray(np.asarray(v, np.float32).reshape(n, 128).T)


def phase2_inputs(core, x_prompt, x_sample, mix_p, mix_s, c, c_ctx, w_ada, b_ada, w_out, norm2_g, w_up,
                  ffn_conv_w, ffn_conv_b, w_down, final_norm_g):
    b, j = core // 4, core % 4
    x2 = np.zeros((3, 514, 2048), np.float32)
    mix = np.zeros((3, 514, 2048), np.float32)
    x2[0, :512] = x_prompt[2 * core:2 * core + 2].reshape(512, 2048)
    mix[0, :512] = mix_p[2 * core:2 * core + 2].reshape(512, 2048)
    hmask = np.zeros((128, 4), np.float32)
    for g in range(2):
        lo = 1024 * j + 512 * g - 1
        hi = lo + 514
        a, e = max(lo, 0), min(hi, 4096)
        x2[1 + g, a - lo:e - lo] = x_sample[b, a:e]
        mix[1 + g, a - lo:e - lo] = mix_s[b, a:e]
        hmask[:, 2 * g] = 1.0 if lo >= 0 else 0.0
        hmask[:, 2 * g + 1] = 1.0 if hi <= 4096 else 0.0
    mixT = np.ascontiguousarray(mix.reshape(3, 514, 16, 128).transpose(0, 2, 3, 1))
    cond = np.stack([c_ctx, c[b]], axis=0)
    condT = np.ascontiguousarray(cond.reshape(2, 16, 128).transpose(2, 1, 0))
    return dict(x2=x2, mix=mixT, condT=condT, hmask=hmask, w_ada=w_ada, b_adaT=colT(b_ada, 96), w_out=w_out,
                norm2T=colT(norm2_g, 16), w_up=w_up,
                fcwT=np.ascontiguousarray(ffn_conv_w.reshape(3, 88, 128).transpose(2, 1, 0)),
                fcbT=colT(ffn_conv_b, 88), w_down=w_down, fnormT=colT(final_norm_g, 16),
                ident=np.eye(128, dtype=np.float32))


def norm_mod_tile(mk, C, x_rows, W, xT, hT, rstd, gm1, ci):
    load_xT(mk, C, x_rows, W, xT, 0)
    rms_rstd(mk, C, xT, 16, W, 0, 2048.0, rstd)
    for cc in range(16):
        tx = C.tmpx[cc % 2]
        mk.I("dve", "scalar_tensor_tensor", out=tx[:, 0:W], in0=xT[:, cc, 0:W], scalar=gm1[:, cc, ci:ci + 1],
             in1=rstd[:, 0:W], op0=ALU.mult, op1=ALU.mult)
        mk.I("act", "activation", out=hT[:, cc, 0:W], in_=tx[:, 0:W], func=AF.Identity,
             bias=C.mod[:, cc, ci:ci + 1], scale=1.0)


def bcast_sum_rstd(mk, C, srcs, W, dim, rstd, eps=EPS):
    b = mk.bank()
    for i, (s, P) in enumerate(srcs):
        sq = C.sq[i % 2]
        mk.I("act", "activation", out=sq[0:P, 0:W], in_=s, func=AF.Square)
        mk.I("pe", "matmul", out=b[:, 0:W], lhsT=C.ones[0:P, :], rhs=sq[0:P, 0:W], start=(i == 0), stop=(i == len(srcs) - 1))
    mk.I("act", "activation", out=C.tmpn[:, 0:W], in_=b[:, 0:W], func=AF.Sqrt, scale=1.0 / dim, bias=C.epsb[:, 0:1] if eps == EPS else C.zerob[:, 0:1])
    mk.I("dve", "reciprocal", out=rstd[:, 0:W], in_=C.tmpn[:, 0:W])


def gdn_unit(mk, C, G, h, d, c, k):
    NH = G.NH
    cols = slice(c * 128, (c + 1) * 128)
    gi = d * NH + h
    g_col = G.Gt[:, c, gi:gi + 1]
    beta_col = G.Bt[:, c, gi:gi + 1]
    nbeta_col = G.NBt[:, c, gi:gi + 1]
    TRI = C.tri[d]
    MS = C.ms[d]
    S = G.S[h][d]
    u = C.gu
    kT = G.KT[h][:, cols]
    qT = G.QT[h][:, cols]
    bk = mk.bank()
    mk.I("pe", "matmul", out=bk[:, 0:128], lhsT=kT, rhs=C.identb.v, start=True, stop=True)
    bv = mk.bank()
    mk.I("pe", "matmul", out=bv[:, 0:128], lhsT=G.VT[h][:, cols], rhs=C.identb.v, start=True, stop=True)
    ktm = u.ktm[k]
    vb = u.vb[k]
    mk.I("act", "activation", out=ktm.v, in_=bk[:, 0:128], func=AF.Copy)
    mk.I("dve", "tensor_scalar", out=vb.v, in0=bv[:, 0:128], scalar1=beta_col, scalar2=None, op0=ALU.mult)
    yield
    bg = mk.bank()
    mk.I("pe", "matmul", out=bg[:, 0:1], lhsT=TRI.v, rhs=g_col, start=True, stop=True)
    mk.I("pe", "matmul", out=bg[:, 128:256], lhsT=g_col.bc([128, 128]), rhs=TRI.v, start=True, stop=True)
    Gc = u.Gc[k]
    Gb = u.Gb[k]
    mk.I("dve", "tensor_copy", out=Gc[:, 0:1], in_=bg[:, 0:1])
    mk.I("act", "activation", out=Gb.v, in_=bg[:, 128:256], func=AF.Copy)
    gtot = Gb[:, 127:128] if d == 0 else Gb[:, 0:1]
    yield
    Dm, DTm = u.Dm[k], u.DTm[k]
    mk.I("dve", "tensor_scalar", out=Dm.v, in0=Gb.v, scalar1=Gc[:, 0:1], scalar2=0.0, op0=ALU.subtract, op1=ALU.max)
    mk.I("act", "activation", out=Dm.v, in_=Dm.v, func=AF.Exp, scale=-1.0)
    mk.I(PENG, "tensor_tensor", out=Dm.v, in0=Dm.v, in1=MS.v, op=ALU.mult)
    mk.I("dve", "tensor_scalar", out=DTm.v, in0=Gb.v, scalar1=Gc[:, 0:1], scalar2=0.0, op0=ALU.subtract, op1=ALU.min)
    mk.I("act", "activation", out=DTm.v, in_=DTm.v, func=AF.Exp)
    mk.I(PENG, "tensor_tensor", out=DTm.v, in0=DTm.v, in1=TRI.v, op=ALU.mult)
    yield
    bkk = mk.bank()
    kTc = u.kTc[k]
    mk.I("dve", "tensor_copy", out=kTc.v, in_=kT)
    mk.I("pe", "matmul", out=bkk[:, 0:128], lhsT=kT, rhs=kTc.v, start=True, stop=True)
    P, PT = u.P[k], u.PT[k]
    mk.I("dve", "scalar_tensor_tensor", out=P[0].v, in0=bkk[:, 0:128], scalar=nbeta_col, in1=Dm.v, op0=ALU.mult, op1=ALU.mult)
    bt = mk.bank()
    mk.I("pe", "transpose", out=bt[:, 0:128], in_=P[0].v, identity=C.ident.v)
    mk.I("act", "activation", out=PT[0].v, in_=bt[:, 0:128], func=AF.Copy)
    TT = u.TT[k]
    mk.I("dve", "tensor_tensor", out=TT.v, in0=bt[:, 0:128], in1=C.ident.v, op=ALU.add)
    yield
    cur = 0
    for lev in range(1, 7):
        nxt = 1 - cur
        b1 = mk.bank()
        mk.I("pe", "matmul", out=b1[:, 0:128], lhsT=PT[cur].v, rhs=P[cur].v, start=True, stop=True)
        if lev < 6:
            mk.I("pe", "matmul", out=b1[:, 128:256], lhsT=P[cur].v, rhs=PT[cur].v, start=True, stop=True)
        mk.I("act", "activation", out=P[nxt].v, in_=b1[:, 0:128], func=AF.Copy)
        if lev < 6:
            mk.I("dve", "tensor_copy", out=PT[nxt].v, in_=b1[:, 128:256])
        b2 = mk.bank()
        mk.I("pe", "matmul", out=b2[:, 0:128], lhsT=P[nxt].v, rhs=TT.v, start=True, stop=True)
        mk.I("dve", "tensor_tensor", out=TT.v, in0=b2[:, 0:128], in1=TT.v, op=ALU.add)
        cur = nxt
        yield
    yield
    sc = u.sc[k]
    mk.I("act", "activation", out=sc[:, 0:1], in_=Gc[:, 0:1], func=AF.Exp)
    mk.I("dve", "tensor_tensor", out=sc[:, 1:2], in0=sc[:, 0:1], in1=beta_col, op=ALU.mult)
    mk.I("act", "activation", out=sc[:, 2:3], in_=Gc[:, 0:1], func=AF.Exp, scale=-1.0, bias=gtot)
    mk.I("act", "activation", out=sc[:, 3:4], in_=gtot, func=AF.Exp)
    kbg, kdec = u.kbg[k], u.kdec[k]
    mk.I("act", "activation", out=kbg.v, in_=ktm.v, func=AF.Identity, scale=sc[:, 1:2])
    mk.I("dve", "tensor_scalar", out=kdec.v, in0=ktm.v, scalar1=sc[:, 2:3], scalar2=None, op0=ALU.mult)
    bu = mk.bank()
    mk.I("pe", "matmul", out=bu[:, 0:128], lhsT=TT.v, rhs=vb.v, start=True, stop=True)
    mk.I("pe", "matmul", out=bu[:, 128:256], lhsT=kbg.v, rhs=TT.v, start=True, stop=True)
    uu, wT = u.uu[k], u.wT[k]
    mk.I("act", "activation", out=uu.v, in_=bu[:, 0:128], func=AF.Copy)
    mk.I("dve", "tensor_copy", out=wT.v, in_=bu[:, 128:256])
    yield
    bq = mk.bank()
    mk.I("pe", "matmul", out=bq[:, 0:128], lhsT=kT, rhs=qT, start=True, stop=True)
    intraT, qgT, eGb = u.intraT[k], u.qgT[k], u.eGb[k]
    mk.I("dve", "tensor_tensor", out=intraT.v, in0=bq[:, 0:128], in1=DTm.v, op=ALU.mult)
    mk.I("act", "activation", out=eGb.v, in_=Gb.v, func=AF.Exp)
    mk.I(PENG, "tensor_tensor", out=qgT.v, in0=qT, in1=eGb.v, op=ALU.mult)
    yield
    b3 = mk.bank()
    mk.I("pe", "matmul", out=b3[:, 0:128], lhsT=wT.v, rhs=S.v, start=True, stop=True)
    vnew = u.vnew[k]
    mk.I("dve", "tensor_tensor", out=vnew.v, in0=uu.v, in1=b3[:, 0:128], op=ALU.subtract)
    b4 = mk.bank()
    mk.I("pe", "matmul", out=b4[:, 0:128], lhsT=S.v, rhs=qgT.v, start=True, stop=False)
    mk.I("pe", "matmul", out=b4[:, 0:128], lhsT=vnew.v, rhs=intraT.v, start=False, stop=True)
    mk.I("pe", "matmul", out=b4[:, 128:256], lhsT=kdec.v, rhs=vnew.v, start=True, stop=True)
    ocols = slice(c * 128 + G.pad, (c + 1) * 128 + G.pad)
    mk.I("dve", "tensor_tensor", out=G.OT[h][:, ocols], in0=G.OT[h][:, ocols], in1=b4[:, 0:128], op=ALU.add)
    mk.I("dve", "scalar_tensor_tensor", out=S.v, in0=S.v, scalar=sc[:, 3:4], in1=b4[:, 128:256], op0=ALU.mult, op1=ALU.add)
    if h == 0 and c == 0 and d == 0:
        for nm, vv in (("Gb", Gb), ("Dm", Dm), ("DTm", DTm), ("X", P[0] if False else None), ("TT", TT), ("vb", vb), ("kbg", kbg), ("kdec", kdec),
                       ("uu", uu), ("wT", wT), ("intraT", intraT), ("qgT", qgT), ("vnew", vnew), ("S", S)):
            if vv is not None:
                dump(mk, nm, vv.v, [128, 128])
        dump(mk, "sc", sc[:, 0:4], [128, 4])
        dump(mk, "Gc", Gc.v, [128, 1])


def conv_out(mk, C, G, h, comp, R, n, tok0):
    t = C.ct[C.ct_rr % 2]
    C.ct_rr += 1
    ch = h * 3 + comp
    mk.I("act", "activation", out=t[:, 0:n], in_=R[:, 1:1 + n], func=AF.Identity, scale=G.cw[:, ch, 1:2])
    mk.I("dve", "scalar_tensor_tensor", out=t[:, 0:n], in0=R[:, 0:n], scalar=G.cw[:, ch, 0:1], in1=t[:, 0:n], op0=ALU.mult, op1=ALU.add)
    mk.I("dve", "scalar_tensor_tensor", out=t[:, 0:n], in0=R[:, 2:2 + n], scalar=G.cw[:, ch, 2:3], in1=t[:, 0:n], op0=ALU.mult, op1=ALU.add)
    j0 = 1 if tok0 < 0 else 0
    if n - j0 <= 0:
        return
    dsl = slice(tok0 + j0, tok0 + n)
    if comp == 2:
        mk.I("act", "activation", out=G.VT[h][:, dsl], in_=t[:, j0:n], func=AF.Silu)
        return
    mk.I("act", "activation", out=t[:, 0:n], in_=t[:, 0:n], func=AF.Silu)
    bcast_sum_rstd(mk, C, [(t[:, 0:n], 128)], n, 1.0, C.rstd)
    dst = (G.QT if comp == 0 else G.KT)[h]
    mk.I("dve", "scalar_tensor_tensor", out=dst[:, dsl], in0=t[:, j0:n], scalar=(128.0 ** -0.5 if comp == 0 else 1.0),
         in1=C.rstd[:, j0:n], op0=ALU.mult, op1=ALU.mult)


def gdn_phase(mk, C, x_rows, T, NH, ci, gm1, Wd, s0_ap, mix_out, state_out, win=None, hcache=None):
    mark = mk.mark()
    G = Ctx()
    G.NH = NH
    pad = G.pad = 1 if win is not None else 0
    TTs = min(512, T)
    ntile = T // TTs
    nch = T // 128
    G.QT = [mk.alloc([T], BF16, f"QT{h}") for h in range(NH)]
    G.KT = [mk.alloc([T], BF16, f"KT{h}") for h in range(NH)]
    G.VT = [mk.alloc([T], BF16, f"VT{h}") for h in range(NH)]
    G.ZT = [mk.alloc([T + 2 * pad], BF16, f"ZT{h}") for h in range(NH)]
    G.OT = [mk.alloc([T + 2 * pad], F32, f"OT{h}") for h in range(NH)]
    G.Gt = mk.alloc([nch, 2 * NH], F32, "Gt")
    G.Bt = mk.alloc([nch, 2 * NH], F32, "Bt")
    G.NBt = mk.alloc([nch, 2 * NH], F32, "NBt")
    G.cw = mk.alloc([NH * 3, 3], F32, "cw")
    G.halo = [[mk.alloc([2], F32, f"halo{h}_{c}") for c in range(3)] for h in range(NH)]
    G.S = [[mk.alloc([128], F32, f"S{h}_{d}") for d in range(2)] for h in range(NH)]
    wab = mk.alloc([16, 4 * NH], BF16, "wab")
    dtb = mk.alloc([2 * NH], F32, "dtb")
    nA = mk.alloc([2 * NH], F32, "nA")
    gng = mk.alloc([1], F32, "gng")
    sm = [mk.alloc([2 * NH], F32, f"sm{i}") for i in range(4)]
    mk.dma("sp", G.cw.v, Wd["cw"])
    mk.dma("sp", dtb.v, Wd["dtb"])
    mk.dma("sp", nA.v, Wd["alog"])
    mk.dma("sp", gng.v, Wd["gng"])
    mk.dma("pool", wab.v, Wd["w_ab"].rearrange("(kc p) n -> p kc n", p=128))
    mk.I("act", "activation", out=nA.v, in_=nA.v, func=AF.Exp)
    mk.I("dve", "tensor_scalar", out=nA.v, in0=nA.v, scalar1=-1.0, scalar2=None, op0=ALU.mult)
    for h in range(NH):
        mk.I(PENG, "memset", G.OT[h].v.ap, 0.0, writes=[G.OT[h]])
        if pad:
            mk.I(PENG, "memset", G.ZT[h].v.ap, 0.0, writes=[G.ZT[h]])
        for c in range(3):
            mk.I(PENG, "memset", G.halo[h][c].v.ap, 0.0, writes=[G.halo[h][c]])
        for d in range(2):
            mk.dma("sp", G.S[h][d].v, s0_ap[d, h])
    mark_w = mk.mark()
    hT = mk.alloc([16, TTs], BF16, "hT")
    xT = mk.alloc([16, TTs], F32, "xTg")
    for it in range(ntile):
        t0 = it * TTs
        if hcache is not None and hcache[1] == "read":
            mk.dma("sp", hT.v, hcache[0][it])
        else:
            norm_mod_tile(mk, C, x_rows[t0:t0 + TTs, :], TTs, xT, hT, C.rstd, gm1, ci)
            if hcache is not None:
                mk.dma("sp", hcache[0][it], hT.v)
        for h in range(NH):
            wt = C.wt[C.wt_rr % len(C.wt)]
            C.wt_rr += 1
            load_weight_tile(mk, wt, Wd["w_g"][:, h * 512:(h + 1) * 512], 16, 512)
            for comp in range(4):
                b = mk.bank()
                for kc in range(16):
                    mk.I("pe", "matmul", out=b[:, 0:TTs], lhsT=wt[:, kc, comp * 128:(comp + 1) * 128], rhs=hT[:, kc, 0:TTs],
                         start=(kc == 0), stop=(kc == 15))
                if comp == 3:
                    mk.I("act", "activation", out=G.ZT[h][:, pad + t0:pad + t0 + TTs], in_=b[:, 0:TTs], func=AF.Silu)
                    continue
                R = C.R[C.R_rr % 2]
                C.R_rr += 1
                halo = G.halo[h][comp]
                mk.I("dve", "tensor_copy", out=R[:, 0:2], in_=halo.v)
                mk.I("act", "activation", out=R[:, 2:2 + TTs], in_=b[:, 0:TTs], func=AF.Copy)
                mk.I("dve", "tensor_copy", out=halo.v, in_=R[:, TTs:TTs + 2])
                conv_out(mk, C, G, h, comp, R, TTs, t0 - 1)
                if it == ntile - 1:
                    R2 = C.R[C.R_rr % 2]
                    C.R_rr += 1
                    mk.I("dve", "tensor_copy", out=R2[:, 0:2], in_=halo.v)
                    mk.I("dve", "memset", R2[:, 2:3].ap, 0.0, writes=[R2])
                    conv_out(mk, C, G, h, comp, R2, 1, T - 1)
        for s in range(TTs // 128 if GSTOP['v'] >= 1 else 0):
            c = (t0 + s * 128) // 128
            b = mk.bank()
            for kc in range(16):
                mk.I("pe", "matmul", out=b[:, 0:4 * NH], lhsT=hT[:, kc, s * 128:(s + 1) * 128], rhs=wab[:, kc, :],
                     start=(kc == 0), stop=(kc == 15))
            xs, ax, ee, rr_ = sm
            mk.I("dve", "tensor_tensor", out=xs.v, in0=b[:, 0:2 * NH], in1=dtb.v, op=ALU.add)
            mk.I("dve", "tensor_scalar", out=ax.v, in0=xs.v, scalar1=-1.0, scalar2=None, op0=ALU.mult)
            mk.I("dve", "tensor_tensor", out=ax.v, in0=ax.v, in1=xs.v, op=ALU.max)
            mk.I("act", "activation", out=ee.v, in_=ax.v, func=AF.Exp, scale=-1.0)
            mk.I("act", "activation", out=ee.v, in_=ee.v, func=AF.Ln, bias=C.oneb[:, 0:1], scale=1.0)
            mk.I("dve", "tensor_scalar", out=rr_.v, in0=xs.v, scalar1=0.0, scalar2=None, op0=ALU.max)
            mk.I("dve", "tensor_tensor", out=ee.v, in0=ee.v, in1=rr_.v, op=ALU.add)
            mk.I("dve", "tensor_tensor", out=G.Gt[:, c, :], in0=ee.v, in1=nA.v, op=ALU.mult)
            mk.I("act", "activation", out=G.Bt[:, c, :], in_=b[:, 2 * NH:4 * NH], func=AF.Sigmoid)
            mk.I("dve", "tensor_scalar", out=G.NBt[:, c, :], in0=G.Bt[:, c, :], scalar1=-1.0, scalar2=None, op0=ALU.mult)
    mk.release(mark_w)
    u = C.gu = Ctx()
    for nm in ("ktm", "Gb", "Dm", "DTm", "TT", "vb", "kbg", "kdec", "uu", "wT", "intraT", "qgT", "eGb", "vnew"):
        setattr(u, nm, [mk.alloc([128], F32, f"{nm}{k}") for k in range(NPAR)])
    u.P = [[mk.alloc([128], F32, f"P{k}{i}") for i in range(2)] for k in range(NPAR)]
    u.PT = [[mk.alloc([128], F32, f"PT{k}{i}") for i in range(2)] for k in range(NPAR)]
    u.Gc = [mk.alloc([1], F32, f"Gc{k}") for k in range(NPAR)]
    u.kTc = [mk.alloc([128], BF16, f"kTc{k}") for k in range(NPAR)]
    u.sc = [mk.alloc([8], F32, f"sc{k}") for k in range(NPAR)]
    pending = [(h, d, (step if d == 0 else nch - 1 - step)) for step in range(nch) for h in range(NH) for d in range(2)]
    active = []
    free = list(range(NPAR))
    while pending or active:
        while pending and free:
            h_, d_, c_ = pending.pop(0)
            k_ = free.pop(0)
            active.append((gdn_unit(mk, C, G, h_, d_, c_, k_), k_))
        for item in list(active):
            try:
                next(item[0])
            except StopIteration:
                active.remove(item)
                free.append(item[1])
    for h in range(NH):
        if state_out is not None:
            for d in range(2):
                mk.dma("sp", state_out[d, h], G.S[h][d].v)
        if win is not None:
            sel, dst = win
            for g in range(2):
                for hf in range(2):
                    ow = C.ct[0]
                    zw = C.ct[1]
                    for jj in range(4):
                        c0 = 1024 * jj + 512 * g + 257 * hf
                        if jj == 0:
                            mk.I("dve", "tensor_scalar", out=ow[:, 0:257], in0=G.OT[h][:, c0:c0 + 257], scalar1=sel[:, 0:1], scalar2=None, op0=ALU.mult)
                            mk.I("dve", "tensor_scalar", out=zw[:, 0:257], in0=G.ZT[h][:, c0:c0 + 257], scalar1=sel[:, 0:1], scalar2=None, op0=ALU.mult)
                        else:
                            mk.I("dve", "scalar_tensor_tensor", out=ow[:, 0:257], in0=G.OT[h][:, c0:c0 + 257], scalar=sel[:, jj:jj + 1],
                                 in1=ow[:, 0:257], op0=ALU.mult, op1=ALU.add)
                            mk.I("dve", "scalar_tensor_tensor", out=zw[:, 0:257], in0=G.ZT[h][:, c0:c0 + 257], scalar=sel[:, jj:jj + 1],
                                 in1=zw[:, 0:257], op0=ALU.mult, op1=ALU.add)
                    bcast_sum_rstd(mk, C, [(ow[:, 0:257], 128)], 257, 128.0, C.rstd)
                    mk.I("dve", "scalar_tensor_tensor", out=ow[:, 0:257], in0=ow[:, 0:257], scalar=gng[:, 0:1],
                         in1=C.rstd[:, 0:257], op0=ALU.mult, op1=ALU.mult)
                    mk.I("dve", "tensor_tensor", out=ow[:, 0:257], in0=ow[:, 0:257], in1=zw[:, 0:257], op=ALU.mult)
                    mk.dma("sp", dst[g, :, 257 * hf:257 * hf + 257], ow[:, 0:257])
            continue
        for it in range(ntile):
            t0 = it * TTs
            bcast_sum_rstd(mk, C, [(G.OT[h][:, t0:t0 + TTs], 128)], TTs, 128.0, C.rstd)
            o = C.ct[C.ct_rr % 2]
            C.ct_rr += 1
            mk.I("dve", "scalar_tensor_tensor", out=o[:, 0:TTs], in0=G.OT[h][:, t0:t0 + TTs], scalar=gng[:, 0:1],
                 in1=C.rstd[:, 0:TTs], op0=ALU.mult, op1=ALU.mult)
            mk.I("dve", "tensor_tensor", out=o[:, 0:TTs], in0=o[:, 0:TTs], in1=G.ZT[h][:, t0:t0 + TTs], op=ALU.mult)
            mk.dma("sp", mix_out[h, :, t0:t0 + TTs], o[:, 0:TTs])
    mk.release(mark)


def mla_phase(mk, C, x_rows, T, NH, ci, gm1, Wd, nctx, rope, mix_out, ckv_out, kpe_out):
    mark0 = mk.mark()
    TTs = min(256, T)
    ntile = T // TTs
    NK = T + nctx
    nkt = NK // 128
    scale = 192.0 ** -0.5
    gq = mk.alloc([4], F32, "gq")
    gkv = mk.alloc([4], F32, "gkv")
    wq = mk.alloc([4, NH * 192], BF16, "wq")
    wkv = mk.alloc([4, NH * 256], BF16, "wkv")
    rot = mk.alloc([64], F32, "rot")
    onesb = mk.alloc([128], BF16, "onesb")
    kmax2 = mk.alloc([NH], F32, "kmax2")
    KPET = mk.alloc([NK], BF16, "KPET")
    KN = [mk.alloc([NK], BF16, f"KN{h}") for h in range(NH)]
    V = [mk.alloc([nkt, 128], BF16, f"V{h}") for h in range(NH)]
    mk.dma("sp", gq.v, Wd["gq"])
    mk.dma("sp", gkv.v, Wd["gkv"])
    mk.dma("pool", wq.v, Wd["wq"].rearrange("(kc p) n -> p kc n", p=128))
    mk.dma("pool", wkv.v, Wd["wkv"].rearrange("(kc p) n -> p kc n", p=128))
    mk.dma("sp", rot[0:64, :], Wd["rot"])
    mk.I("dve", "memset", onesb.v.ap, 1.0, writes=[onesb])
    mark1 = mk.mark()
    CKVT = mk.alloc([4, NK], BF16, "CKVT")
    mark2 = mk.mark()

    def work_tiles():
        W_ = Ctx()
        W_.hT = mk.alloc([16, TTs], BF16, "hTm")
        W_.xT = mk.alloc([16, TTs], F32, "xTm")
        W_.raw = mk.alloc([4, TTs], F32, "rawm")
        W_.cs = [mk.alloc([TTs], F32, f"cs{i}") for i in range(2)]
        W_.rp = [mk.alloc([TTs], F32, f"rp{i}") for i in range(3)]
        return W_

    def proj_norm(W_, wcol0, gvec, dst_fn, f32_out=None):
        wt = C.wt[C.wt_rr % len(C.wt)]
        C.wt_rr += 1
        load_weight_tile(mk, wt, Wd["w_m"][:, wcol0:wcol0 + 512], 16, 512)
        for q in range(4):
            b = mk.bank()
            for kc in range(16):
                mk.I("pe", "matmul", out=b[:, 0:TTs], lhsT=wt[:, kc, q * 128:(q + 1) * 128], rhs=W_.hT[:, kc, 0:TTs],
                     start=(kc == 0), stop=(kc == 15))
            mk.I("act", "activation", out=W_.raw[:, q, :], in_=b[:, 0:TTs], func=AF.Copy)
        bcast_sum_rstd(mk, C, [(W_.raw[:, q, :], 128) for q in range(4)], TTs, 512.0, C.rstd)
        for q in range(4):
            if f32_out is not None:
                mk.I("dve", "scalar_tensor_tensor", out=W_.raw[:, q, :], in0=W_.raw[:, q, :], scalar=gvec[:, q:q + 1],
                     in1=C.rstd[:, 0:TTs], op0=ALU.mult, op1=ALU.mult)
                mk.I("act", "activation", out=dst_fn(q), in_=W_.raw[:, q, :], func=AF.Copy)
            else:
                mk.I("dve", "scalar_tensor_tensor", out=dst_fn(q), in0=W_.raw[:, q, :], scalar=gvec[:, q:q + 1],
                     in1=C.rstd[:, 0:TTs], op0=ALU.mult, op1=ALU.mult)

    def do_rope(W_, src_bank_view, n, t0, dst):
        x = W_.rp[0]
        mk.I("act", "activation", out=x[0:64, 0:n], in_=src_bank_view, func=AF.Copy)
        if not rope:
            mk.I("dve", "tensor_copy", out=dst, in_=x[0:64, 0:n])
            return
        mk.dma("sp", W_.cs[0][0:64, 0:n], Wd["cos"][:, t0:t0 + n])
        mk.dma("sp", W_.cs[1][0:64, 0:n], Wd["sin"][:, t0:t0 + n])
        b = mk.bank()
        mk.I("pe", "matmul", out=b[0:64, 0:n], lhsT=rot[0:64, :], rhs=x[0:64, 0:n], start=True, stop=True)
        mk.I("dve", "tensor_tensor", out=W_.rp[1][0:64, 0:n], in0=x[0:64, 0:n], in1=W_.cs[0][0:64, 0:n], op=ALU.mult)
        mk.I("dve", "tensor_tensor", out=W_.rp[2][0:64, 0:n], in0=b[0:64, 0:n], in1=W_.cs[1][0:64, 0:n], op=ALU.mult)
        mk.I("dve", "tensor_tensor", out=dst, in0=W_.rp[1][0:64, 0:n], in1=W_.rp[2][0:64, 0:n], op=ALU.add)

    W_ = work_tiles()
    for it in range(ntile):
        t0 = it * TTs
        norm_mod_tile(mk, C, x_rows[t0:t0 + TTs, :], TTs, W_.xT, W_.hT, C.rstd, gm1, ci)
        proj_norm(W_, 512, gkv, lambda q: CKVT[:, q, t0:t0 + TTs], f32_out=(ckv_out is not None) or True)
        if ckv_out is not None:
            for s in range(TTs // 128):
                ys = C.xstage[s % 2]
                b = mk.bank()
                for q in range(4):
                    mk.I("pe", "transpose", out=b[:, q * 128:(q + 1) * 128], in_=W_.raw[:, q, s * 128:(s + 1) * 128], identity=C.ident.v)
                mk.I("dve", "tensor_copy", out=ys[:, 0:512], in_=b[:, 0:512])
                mk.dma("sp", ckv_out[t0 + s * 128:t0 + (s + 1) * 128, :], ys[:, 0:512])
        wt = C.wt[C.wt_rr % len(C.wt)]
        C.wt_rr += 1
        load_weight_tile(mk, wt, Wd["w_m"][:, 1024:1088], 16, 64)
        b = mk.bank()
        for kc in range(16):
            mk.I("pe", "matmul", out=b[0:64, 0:TTs], lhsT=wt[:, kc, 0:64], rhs=W_.hT[:, kc, 0:TTs], start=(kc == 0), stop=(kc == 15))
        do_rope(W_, b[0:64, 0:TTs], TTs, t0, KPET[0:64, t0:t0 + TTs])
        if kpe_out is not None:
            for s in range(TTs // 128):
                ys = C.xstage[s % 2]
                b2 = mk.bank()
                mk.I("pe", "transpose", out=b2[:, 0:64], in_=W_.rp[0][0:64, s * 128:(s + 1) * 128], identity=C.ident[0:64, 0:64])
                mk.I("dve", "tensor_copy", out=ys[:, 0:64], in_=b2[:, 0:64])
                mk.dma("sp", kpe_out[t0 + s * 128:t0 + (s + 1) * 128, :], ys[:, 0:64])
    for s in range(nctx // 128):
        xs = C.xstage[s % 2]
        mk.dma("sp", xs[:, 0:512], Wd["cache_ckv"][s * 128:(s + 1) * 128, :])
        mk.dma("sp", xs[:, 512:576], Wd["cache_kpe"][s * 128:(s + 1) * 128, :])
        b = mk.bank()
        for q in range(4):
            mk.I("pe", "transpose", out=b[:, q * 128:(q + 1) * 128], in_=xs[:, q * 128:(q + 1) * 128], identity=C.ident.v)
        mk.I("dve", "tensor_copy", out=CKVT[:, :, T + s * 128:T + (s + 1) * 128], in_=b[:, 0:512].rearrange("p (q t) -> p q t", q=4))
        b2 = mk.bank()
        mk.I("pe", "transpose", out=b2[0:64, 0:128], in_=xs[:, 512:576], identity=C.ident.v)
        mk.I("act", "activation", out=KPET[0:64, T + s * 128:T + (s + 1) * 128], in_=b2[0:64, 0:128], func=AF.Copy)
    mk.release(mark2)
    sqk = mk.alloc([512], F32, "sqk")
    kss = mk.alloc([512], F32, "kss")
    for h in range(NH):
        for k0 in range(0, NK, 512):
            n = min(512, NK - k0)
            b = mk.bank()
            for kc in range(4):
                mk.I("pe", "matmul", out=b[:, 0:n], lhsT=wkv[:, kc, h * 256:h * 256 + 128], rhs=CKVT[:, kc, k0:k0 + n],
                     start=(kc == 0), stop=(kc == 3))
            mk.I("act", "activation", out=KN[h][:, k0:k0 + n], in_=b[:, 0:n], func=AF.Copy)
            bs = mk.bank()
            mk.I("act", "activation", out=sqk[:, 0:n], in_=b[:, 0:n], func=AF.Square)
            mk.I("pe", "matmul", out=bs[0:1, 0:n], lhsT=C.ones[:, 0:1], rhs=sqk[:, 0:n], start=True, stop=False)
            mk.I("act", "activation", out=kss[0:64, 0:n], in_=KPET[0:64, k0:k0 + n], func=AF.Square)
            mk.I("pe", "matmul", out=bs[0:1, 0:n], lhsT=C.ones[0:64, 0:1], rhs=kss[0:64, 0:n], start=False, stop=True)
            if k0 == 0:
                mk.I("dve", "tensor_reduce", out=kmax2[0:1, h:h + 1], in_=bs[0:1, 0:n], axis=AX.X, op=ALU.max)
            else:
                mk.I("dve", "tensor_reduce", out=kss[0:1, 0:1], in_=bs[0:1, 0:n], axis=AX.X, op=ALU.max)
                mk.I("dve", "tensor_tensor", out=kmax2[0:1, h:h + 1], in0=kmax2[0:1, h:h + 1], in1=kss[0:1, 0:1], op=ALU.max)
        for kt in range(nkt):
            b = mk.bank()
            for kc in range(4):
                mk.I("pe", "matmul", out=b[:, 0:128], lhsT=CKVT[:, kc, kt * 128:(kt + 1) * 128], rhs=wkv[:, kc, h * 256 + 128:h * 256 + 256],
                     start=(kc == 0), stop=(kc == 3))
            mk.I("dve", "tensor_copy", out=V[h][:, kt, :], in_=b[:, 0:128])
    mk.release(mark1)
    W_ = work_tiles()
    QN = mk.alloc([4, TTs], BF16, "QN")
    qn = mk.alloc([TTs], BF16, "qn")
    qr = mk.alloc([TTs], BF16, "qr")
    negm = mk.alloc([TTs], BF16, "negm")
    mrow = mk.alloc([TTs], F32, "mrow")
    PTb = [mk.alloc([TTs], BF16, f"PT{i}") for i in range(3)]
    rs = mk.alloc([TTs], F32, "rs")
    oo = mk.alloc([TTs], F32, "oo")
    for it in range(ntile):
        t0 = it * TTs
        norm_mod_tile(mk, C, x_rows[t0:t0 + TTs, :], TTs, W_.xT, W_.hT, C.rstd, gm1, ci)
        proj_norm(W_, 0, gq, lambda q: QN[:, q, :])
        for h in range(NH):
            b = mk.bank()
            for kc in range(4):
                mk.I("pe", "matmul", out=b[:, 0:TTs], lhsT=wq[:, kc, h * 192:h * 192 + 128], rhs=QN[:, kc, :], start=(kc == 0), stop=(kc == 3))
            mk.I("act", "activation", out=qn.v, in_=b[:, 0:TTs], func=AF.Copy)
            mk.I("act", "activation", out=C.sq[0][:, 0:TTs], in_=b[:, 0:TTs], func=AF.Square)
            b2 = mk.bank()
            for kc in range(4):
                mk.I("pe", "matmul", out=b2[0:64, 0:TTs], lhsT=wq[:, kc, h * 192 + 128:h * 192 + 192], rhs=QN[:, kc, :], start=(kc == 0), stop=(kc == 3))
            mk.I("act", "activation", out=C.sq[1][0:64, 0:TTs], in_=b2[0:64, 0:TTs], func=AF.Square)
            do_rope(W_, b2[0:64, 0:TTs], TTs, t0, qr[0:64, :])
            bm = mk.bank()
            mk.I("pe", "matmul", out=bm[0:1, 0:TTs], lhsT=C.ones[:, 0:1], rhs=C.sq[0][:, 0:TTs], start=True, stop=False)
            mk.I("pe", "matmul", out=bm[0:1, 0:TTs], lhsT=C.ones[0:64, 0:1], rhs=C.sq[1][0:64, 0:TTs], start=False, stop=True)
            mk.I("act", "activation", out=mrow[0:1, :], in_=bm[0:1, 0:TTs], func=AF.Sqrt, scale=kmax2[0:1, h:h + 1])
            mk.I("dve", "tensor_scalar", out=negm[0:1, :], in0=mrow[0:1, :], scalar1=-1.0, scalar2=None, op0=ALU.mult)
            bo = mk.reserve()
            bsum = mk.reserve()
            for kt in range(nkt):
                ks = slice(kt * 128, (kt + 1) * 128)
                bs = mk.bank()
                mk.I("pe", "matmul", out=bs[:, 0:TTs], lhsT=KN[h][:, ks], rhs=qn.v, start=True, stop=False)
                mk.I("pe", "matmul", out=bs[:, 0:TTs], lhsT=KPET[0:64, ks], rhs=qr[0:64, :], start=False, stop=False)
                mk.I("pe", "matmul", out=bs[:, 0:TTs], lhsT=onesb[0:1, :], rhs=negm[0:1, :], start=False, stop=True)
                PT = PTb[kt % 3]
                mk.I("act", "activation", out=PT.v, in_=bs[:, 0:TTs], func=AF.Exp, scale=scale)
                mk.I("pe", "matmul", out=bo[:, 0:TTs], lhsT=V[h][:, kt, :], rhs=PT.v, start=(kt == 0), stop=(kt == nkt - 1))
                mk.I("pe", "matmul", out=bsum[:, 0:TTs], lhsT=onesb.v, rhs=PT.v, start=(kt == 0), stop=(kt == nkt - 1))
            mk.I("dve", "reciprocal", out=rs.v, in_=bsum[:, 0:TTs])
            mk.I("dve", "tensor_tensor", out=oo.v, in0=bo[:, 0:TTs], in1=rs.v, op=ALU.mult)
            mk.dma("sp", mix_out[h, :, t0:t0 + TTs], oo.v)
            mk.unreserve(bo)
            mk.unreserve(bsum)
    mk.release(mark0)


def alloc_common(mk, C, D):
    C.ident = mk.alloc([128], F32, "ident")
    C.identb = mk.alloc([128], BF16, "identb")
    C.ones = mk.alloc([128], F32, "ones")
    C.epsb = mk.alloc([1], F32, "epsb")
    C.oneb = mk.alloc([1], F32, "oneb")
    C.tri = [mk.alloc([128], F32, f"tri{d}") for d in range(2)]
    C.ms = [mk.alloc([128], F32, f"ms{d}") for d in range(2)]
    C.scond = mk.alloc([16, 2], BF16, "scond")
    C.condf = mk.alloc([16, 2], F32, "condf")
    C.b_ada = mk.alloc([96], F32, "b_ada")
    C.mod = mk.alloc([96, 2], F32, "mod")
    C.xstage = [mk.alloc([2048], F32, f"xs{i}") for i in range(2)]
    C.sq = [mk.alloc([514], F32, f"sq{i}") for i in range(2)]
    C.sqb = [mk.alloc([514], BF16, f"sqb{i}") for i in range(4)]
    C.sqb_rr = 0
    C.onesb16 = mk.alloc([128], BF16, "onesb16")
    C.tmpn = mk.alloc([514], F32, "tmpn")
    C.rstd = mk.alloc([514], F32, "rstd")
    C.wt = [mk.alloc([16, 512], BF16, f"wt{i}") for i in range(2)]
    C.wt_rr = 0
    C.tmpx = [mk.alloc([514], F32, f"tmpx{i}") for i in range(2)]
    C.ct = [mk.alloc([512], F32, f"ct{i}") for i in range(2)]
    C.ct_rr = 0
    C.R = [mk.alloc([516], F32, f"R{i}") for i in range(2)]
    C.R_rr = 0
    mk.dma("sp", C.ident.v, D["ident"])
    mk.dma("sp", C.tri[0].v, D["tri0"])
    mk.dma("sp", C.tri[1].v, D["tri1"])
    mk.dma("sp", C.ms[0].v, D["ms0"])
    mk.dma("sp", C.ms[1].v, D["ms1"])
    mk.I("dve", "tensor_copy", out=C.identb.v, in_=C.ident.v)
    mk.I("dve", "memset", C.ones.v.ap, 1.0, writes=[C.ones])
    mk.I("dve", "memset", C.onesb16.v.ap, 1.0, writes=[C.onesb16])
    mk.I("dve", "memset", C.epsb.v.ap, EPS, writes=[C.epsb])
    mk.I("dve", "memset", C.oneb.v.ap, 1.0, writes=[C.oneb])
    mk.dma("sp", C.condf.v, D["condT"])
    mk.dma("sp", C.b_ada.v, D["b_adaT"])
    mk.I("act", "activation", out=C.scond.v, in_=C.condf.v, func=AF.Silu)


def build_phase1(parts=('pg', 'pm', 'sg', 'sm')):
    nc = bass.Bass("TRN2", target_bir_lowering=False)
    DEBUG["nc"] = nc
    DEBUG["done"] = set()
    D = {}

    def din(name, shape):
        D[name] = nc.dram_tensor(name, list(shape), F32, kind="ExternalInput").ap()

    def dout(name, shape):
        D[name] = nc.dram_tensor(name, list(shape), F32, kind="ExternalOutput").ap()

    for name, shape in (("xp", [2, 256, 2048]), ("xs", [4096, 2048]), ("condT", [128, 16, 2]), ("w_ada", [2048, 12288]),
                        ("b_adaT", [128, 96]), ("norm1T", [128, 16]), ("ident", [128, 128]), ("tri0", [128, 128]),
                        ("tri1", [128, 128]), ("ms0", [128, 128]), ("ms1", [128, 128]), ("s0p", [2, 8, 128, 128]),
                        ("s0s", [2, 2, 1, 128, 128]), ("w_g_p", [2048, 4096]), ("w_ab_p", [2048, 32]), ("cw_p", [128, 24, 3]),
                        ("dtb_p", [128, 16]), ("alog_p", [128, 16]), ("gng", [128, 1]), ("w_g_s", [2, 2048, 512]),
                        ("w_ab_s", [2, 2048, 4]), ("cw_s", [2, 128, 3, 3]), ("dtb_s", [2, 128, 2]), ("alog_s", [2, 128, 2]),
                        ("w_m", [2048, 1088]), ("gq", [128, 4]), ("gkv", [128, 4]), ("wq_p", [512, 1536]), ("wkv_p", [512, 2048]),
                        ("wq_s", [512, 384]), ("wkv_s", [512, 512]), ("cos", [64, 4096]), ("sin", [64, 4096]), ("rot", [64, 64]),
                        ("cache_ckv", [256, 512]), ("cache_kpe", [256, 64])):
        din(name, shape)
    for name, shape in (("mixp", [2, 16, 128, 256]), ("mixs", [4, 128, 4096]), ("new_state", [2, 2, 8, 128, 128]),
                        ("new_ckv", [2, 256, 512]), ("new_kpe", [2, 256, 64])):
        dout(name, shape)
    with ExitStack() as st:
        mk = MK(nc, st)
        C = Ctx()
        alloc_common(mk, C, D)
        n1g = mk.alloc([16], F32, "n1g")
        gm1 = mk.alloc([16, 2], F32, "gm1")
        mk.dma("sp", n1g.v, D["norm1T"])
        adaln(mk, C, D["w_ada"], [0, 1], 2)
        mk.I("dve", "tensor_scalar", out=gm1.v, in0=C.mod[:, 16:32, :], scalar1=1.0, scalar2=None, op0=ALU.add)
        mk.I("dve", "tensor_tensor", out=gm1.v, in0=gm1.v,
             in1=n1g.v.rearrange("p (c o) -> p c o", o=1).bc([128, 16, 2]), op=ALU.mult)
        Wp = dict(w_g=D["w_g_p"], w_ab=D["w_ab_p"], cw=D["cw_p"], dtb=D["dtb_p"], alog=D["alog_p"], gng=D["gng"])
        Wmp = dict(w_m=D["w_m"], gq=D["gq"], gkv=D["gkv"], wq=D["wq_p"], wkv=D["wkv_p"], rot=D["rot"])
        for s in range(2):
            if 'pg' in parts:
                gdn_phase(mk, C, D["xp"][s], 256, 8, 0, gm1, Wp, D["s0p"], D["mixp"][s, 0:8], D["new_state"][s])
            if 'pm' in parts:
                mla_phase(mk, C, D["xp"][s], 256, 8, 0, gm1, Wmp, 0, False, D["mixp"][s, 8:16], D["new_ckv"][s], D["new_kpe"][s])
        for lh in range(2):
            Ws = dict(w_g=D["w_g_s"][lh], w_ab=D["w_ab_s"][lh], cw=D["cw_s"][lh], dtb=D["dtb_s"][lh], alog=D["alog_s"][lh], gng=D["gng"])
            if 'sg' in parts:
                gdn_phase(mk, C, D["xs"], 4096, 1, 1, gm1, Ws, D["s0s"][lh], D["mixs"][lh:lh + 1], None)
        Wms = dict(w_m=D["w_m"], gq=D["gq"], gkv=D["gkv"], wq=D["wq_s"], wkv=D["wkv_s"], rot=D["rot"], cos=D["cos"], sin=D["sin"],
                   cache_ckv=D["cache_ckv"], cache_kpe=D["cache_kpe"])
        if 'sm' in parts:
            mla_phase(mk, C, D["xs"], 4096, 2, 1, gm1, Wms, 256, True, D["mixs"][2:4], None, None)
        mk.finalize()
        print("phase1 instructions:", mk.n_inst, {e: len(mk.ops[e]) for e in ENGS})
    return nc


def rope_tables():
    rows = 4096 // 64
    row = np.repeat(np.arange(rows, dtype=np.float32), 64)
    col = np.tile(np.arange(64, dtype=np.float32), rows)
    inv = (np.float32(10000.0) ** (-np.arange(16, dtype=np.float32) / np.float32(16))).astype(np.float32)
    ang = np.concatenate([row[:, None] * inv, col[:, None] * inv], axis=-1).astype(np.float32)
    cos, sin = np.cos(ang).astype(np.float32), np.sin(ang).astype(np.float32)
    cos2 = np.ascontiguousarray(np.concatenate([cos, cos], axis=1).T)
    sin2 = np.ascontiguousarray(np.concatenate([sin, sin], axis=1).T)
    rot = np.zeros((64, 64), np.float32)
    for m in range(32):
        rot[m + 32, m] = -1.0
        rot[m, m + 32] = 1.0
    return cos2, sin2, rot


def phase1_inputs(core, I):
    b, j = core // 4, core % 4
    w_in = I["w_in"][0]
    cw = I["gdn_conv_w"][0]
    dtb = I["gdn_dt_bias"][0]
    alog = I["gdn_a_log"][0]

    def wg(h):
        return np.concatenate([w_in[:, c0 + h * 128:c0 + (h + 1) * 128] for c0 in (0, 1024, 2048, 3072)], axis=1)

    def cwh(h):
        return np.stack([cw[:, comp * 1024 + h * 128:comp * 1024 + (h + 1) * 128].T for comp in range(3)], axis=1)

    cos2, sin2, rot = rope_tables()
    tri0 = np.triu(np.ones((128, 128), np.float32))
    tri1 = np.tril(np.ones((128, 128), np.float32))
    ms0 = np.tril(np.ones((128, 128), np.float32), -1)
    ms1 = np.triu(np.ones((128, 128), np.float32), 1)
    cond = np.stack([I["c_ctx"], I["c"][b]], axis=0)
    hg = [2 * j, 2 * j + 1]
    wq = I["mla_w_q_b"][0]
    wkv = I["mla_w_kv_b"][0]
    d = dict(
        xp=np.ascontiguousarray(I["x_prompt"][2 * core:2 * core + 2]), xs=np.ascontiguousarray(I["x_sample"][b]),
        condT=np.ascontiguousarray(cond.reshape(2, 16, 128).transpose(2, 1, 0)), w_ada=I["w_ada"][0],
        b_adaT=colT(I["b_ada"][0], 96), norm1T=colT(I["norm1_g"][0], 16), ident=np.eye(128, dtype=np.float32),
        tri0=tri0, tri1=tri1, ms0=ms0, ms1=ms1, s0p=np.zeros((2, 8, 128, 128), np.float32),
        s0s=np.ascontiguousarray(np.stack([I["state_gdn"][b, 0, :, h:h + 1] for h in hg], axis=0)),
        w_g_p=np.concatenate([wg(h) for h in range(8)], axis=1), w_ab_p=np.ascontiguousarray(w_in[:, 4096:4128]),
        cw_p=np.ascontiguousarray(np.concatenate([cwh(h) for h in range(8)], axis=1)),
        dtb_p=np.ascontiguousarray(np.broadcast_to(dtb.reshape(1, 16), (128, 16))),
        alog_p=np.ascontiguousarray(np.broadcast_to(alog.reshape(1, 16), (128, 16))),
        gng=np.ascontiguousarray(I["gdn_norm_g"][0].reshape(128, 1)),
        w_g_s=np.stack([wg(h) for h in hg], axis=0),
        w_ab_s=np.stack([w_in[:, [4096 + h, 4104 + h, 4112 + h, 4120 + h]] for h in hg], axis=0),
        cw_s=np.stack([cwh(h) for h in hg], axis=0),
        dtb_s=np.stack([np.broadcast_to(dtb[:, h].reshape(1, 2), (128, 2)) for h in hg], axis=0),
        alog_s=np.stack([np.broadcast_to(alog[:, h].reshape(1, 2), (128, 2)) for h in hg], axis=0),
        w_m=np.ascontiguousarray(w_in[:, 4128:5216]), gq=colT(I["mla_q_norm_g"][0], 4), gkv=colT(I["mla_kv_norm_g"][0], 4),
        wq_p=wq, wkv_p=wkv, wq_s=np.ascontiguousarray(wq[:, hg[0] * 192:(hg[1] + 1) * 192]),
        wkv_s=np.ascontiguousarray(wkv[:, hg[0] * 256:(hg[1] + 1) * 256]), cos=cos2, sin=sin2, rot=rot,
        cache_ckv=np.ascontiguousarray(I["cache_mla_ckv"][b, 0]), cache_kpe=np.ascontiguousarray(I["cache_mla_kpe"][b, 0]))
    return {k: np.ascontiguousarray(np.asarray(v, np.float32)) for k, v in d.items()}


_NC = {}


def kernel_twolaunch(**I):
    I = {k: np.asarray(v) for k, v in I.items()}
    if "p1" not in _NC:
        _NC["p1"] = build_phase1()
        _NC["p2"] = build_phase2()
    r1 = run_bass_kernel_spmd(_NC["p1"], [phase1_inputs(c, I) for c in range(NCORES)], core_ids=list(range(NCORES))).results
    mix_p = np.zeros((16, 256, 2048), np.float32)
    mix_s = np.zeros((2, 4096, 2048), np.float32)
    new_state = np.zeros((16, 1, 2, 8, 128, 128), np.float32)
    new_ckv = np.zeros((16, 1, 256, 512), np.float32)
    new_kpe = np.zeros((16, 1, 256, 64), np.float32)
    for c in range(NCORES):
        b, j = c // 4, c % 4
        r = r1[c]
        for s in range(2):
            mix_p[2 * c + s] = r["mixp"][s].transpose(2, 0, 1).reshape(256, 2048)
            new_state[2 * c + s, 0] = r["new_state"][s]
            new_ckv[2 * c + s, 0] = r["new_ckv"][s]
            new_kpe[2 * c + s, 0] = r["new_kpe"][s]
        for lh in range(2):
            hgl = 2 * j + lh
            mix_s[b, :, hgl * 128:(hgl + 1) * 128] = r["mixs"][lh].T
            mix_s[b, :, 1024 + hgl * 128:1024 + (hgl + 1) * 128] = r["mixs"][2 + lh].T
    r2 = run_bass_kernel_spmd(_NC["p2"], [phase2_inputs(c, I["x_prompt"], I["x_sample"], mix_p, mix_s, I["c"], I["c_ctx"],
                                                        I["w_ada"][0], I["b_ada"][0], I["w_out"][0], I["norm2_g"][0],
                                                        I["w_up"][0], I["ffn_conv_w"][0], I["ffn_conv_b"][0], I["w_down"][0],
                                                        I["final_norm_g"]) for c in range(NCORES)],
                              core_ids=list(range(NCORES))).results
    yp = np.zeros((16, 256, 2048), np.float32)
    ys = np.zeros((2, 4096, 2048), np.float32)
    for c in range(NCORES):
        b, j = c // 4, c % 4
        y = r2[c]["y"]
        yp[2 * c:2 * c + 2] = y[0].reshape(2, 256, 2048)
        ys[b, 1024 * j:1024 * j + 512] = y[1]
        ys[b, 1024 * j + 512:1024 * j + 1024] = y[2]
    return (yp, ys, new_state, new_ckv, new_kpe)


def mla_fused(mk, C, x_rows, T, ci, gm1, Wd, nctx, xq, dst, hT_d=None):
    NH = 8
    mark0 = mk.mark()
    TTs = 256
    TQ = 257
    ntile = T // TTs
    NK = T + nctx
    nkt = NK // 128
    scale = 192.0 ** -0.5
    gq = mk.alloc([4], F32, "gq")
    gkv = mk.alloc([4], F32, "gkv")
    wq = mk.alloc([4, NH * 192], BF16, "wq")
    wkv = mk.alloc([4, NH * 256], BF16, "wkv")
    rot = mk.alloc([64], F32, "rot")
    onesb = mk.alloc([128], BF16, "onesb")
    kmax2 = mk.alloc([NH], F32, "kmax2")
    KPET = mk.alloc([NK], BF16, "KPET")
    QN = mk.alloc([4, 4 * TQ], BF16, "QNall")
    mk.dma("sp", gq.v, Wd["gq"])
    mk.dma("sp", gkv.v, Wd["gkv"])
    mk.dma("pool", wq.v, Wd["wq"].rearrange("(kc p) n -> p kc n", p=128))
    mk.dma("pool", wkv.v, Wd["wkv"].rearrange("(kc p) n -> p kc n", p=128))
    mk.dma("sp", rot[0:64, :], Wd["rot"])
    mk.I("dve", "memset", onesb.v.ap, 1.0, writes=[onesb])
    CKVT = mk.alloc([4, NK], BF16, "CKVT")
    mark2 = mk.mark()
    hT = mk.alloc([16, TQ], BF16, "hTm")
    xT = mk.alloc([16, TQ], F32, "xTm")
    raw = mk.alloc([4, TQ], F32, "rawm")
    cs = [mk.alloc([TQ], F32, f"cs{i}") for i in range(2)]
    rp = [mk.alloc([TQ], F32, f"rp{i}") for i in range(3)]

    def proj_norm(W, wcol0, gvec, dst_fn):
        wt = C.wt[C.wt_rr % len(C.wt)]
        C.wt_rr += 1
        load_weight_tile(mk, wt, Wd["w_m"][:, wcol0:wcol0 + 512], 16, 512)
        for q in range(4):
            b = mk.bank()
            for kc in range(16):
                mk.I("pe", "matmul", out=b[:, 0:W], lhsT=wt[:, kc, q * 128:(q + 1) * 128], rhs=hT[:, kc, 0:W], start=(kc == 0), stop=(kc == 15))
            mk.I("act", "activation", out=raw[:, q, 0:W], in_=b[:, 0:W], func=AF.Copy)
        bcast_sum_rstd(mk, C, [(raw[:, q, 0:W], 128) for q in range(4)], W, 512.0, C.rstd)
        for q in range(4):
            mk.I("dve", "scalar_tensor_tensor", out=dst_fn(q), in0=raw[:, q, 0:W], scalar=gvec[:, q:q + 1], in1=C.rstd[:, 0:W], op0=ALU.mult, op1=ALU.mult)

    def do_rope(src, n, cos_ap, sin_ap, dstv):
        x = rp[0]
        mk.I("act", "activation", out=x[0:64, 0:n], in_=src, func=AF.Copy)
        mk.dma("sp", cs[0][0:64, 0:n], cos_ap)
        mk.dma("sp", cs[1][0:64, 0:n], sin_ap)
        b = mk.bank()
        mk.I("pe", "matmul", out=b[0:64, 0:n], lhsT=rot[0:64, :], rhs=x[0:64, 0:n], start=True, stop=True)
        mk.I("dve", "tensor_tensor", out=rp[1][0:64, 0:n], in0=x[0:64, 0:n], in1=cs[0][0:64, 0:n], op=ALU.mult)
        mk.I("dve", "tensor_tensor", out=rp[2][0:64, 0:n], in0=b[0:64, 0:n], in1=cs[1][0:64, 0:n], op=ALU.mult)
        mk.I("dve", "tensor_tensor", out=dstv, in0=rp[1][0:64, 0:n], in1=rp[2][0:64, 0:n], op=ALU.add)

    for it in range(ntile):
        t0 = it * TTs
        if hT_d is not None:
            mk.dma("sp", hT[:, :, 0:TTs], hT_d[it // 2][:, :, (it % 2) * TTs:(it % 2 + 1) * TTs])
        else:
            norm_mod_tile(mk, C, x_rows[t0:t0 + TTs, :], TTs, xT, hT, C.rstd, gm1, ci)
        proj_norm(TTs, 512, gkv, lambda q: CKVT[:, q, t0:t0 + TTs])
        wt = C.wt[C.wt_rr % len(C.wt)]
        C.wt_rr += 1
        load_weight_tile(mk, wt, Wd["w_m"][:, 1024:1088], 16, 64)
        b = mk.bank()
        for kc in range(16):
            mk.I("pe", "matmul", out=b[0:64, 0:TTs], lhsT=wt[:, kc, 0:64], rhs=hT[:, kc, 0:TTs], start=(kc == 0), stop=(kc == 15))
        do_rope(b[0:64, 0:TTs], TTs, Wd["cos"][:, t0:t0 + TTs], Wd["sin"][:, t0:t0 + TTs], KPET[0:64, t0:t0 + TTs])
    for s in range(nctx // 128):
        xs = C.xstage[s % 2]
        mk.dma("sp", xs[:, 0:512], Wd["cache_ckv"][s * 128:(s + 1) * 128, :])
        mk.dma("sp", xs[:, 512:576], Wd["cache_kpe"][s * 128:(s + 1) * 128, :])
        b = mk.bank()
        for q in range(4):
            mk.I("pe", "transpose", out=b[:, q * 128:(q + 1) * 128], in_=xs[:, q * 128:(q + 1) * 128], identity=C.ident.v)
        mk.I("dve", "tensor_copy", out=CKVT[:, :, T + s * 128:T + (s + 1) * 128], in_=b[:, 0:512].rearrange("p (q t) -> p q t", q=4))
        b2 = mk.bank()
        mk.I("pe", "transpose", out=b2[0:64, 0:128], in_=xs[:, 512:576], identity=C.ident.v)
        mk.I("act", "activation", out=KPET[0:64, T + s * 128:T + (s + 1) * 128], in_=b2[0:64, 0:128], func=AF.Copy)
    for slot in range(4):
        g, hf = slot // 2, slot % 2
        norm_mod_tile(mk, C, xq[g, hf * TQ:(hf + 1) * TQ, :], TQ, xT, hT, C.rstd, gm1, ci)
        proj_norm(TQ, 0, gq, lambda q: QN[:, q, slot * TQ:(slot + 1) * TQ])
    mk.release(mark2)
    for h in range(NH):
        markh = mk.mark()
        KN = mk.alloc([NK], BF16, "KNh")
        V = mk.alloc([nkt, 128], BF16, "Vh")
        sqk = mk.alloc([512], F32, "sqk")
        kss = mk.alloc([512], F32, "kss")
        qn = mk.alloc([TQ], BF16, "qn")
        qr = mk.alloc([TQ], BF16, "qr")
        negm = mk.alloc([TQ], BF16, "negm")
        mrow = mk.alloc([TQ], F32, "mrow")
        PTb = [mk.alloc([TQ], BF16, f"PT{i}") for i in range(3)]
        rs = mk.alloc([TQ], F32, "rs")
        oo = mk.alloc([TQ], F32, "oo")
        cs = [mk.alloc([TQ], F32, f"csq{i}") for i in range(2)]
        rp = [mk.alloc([TQ], F32, f"rpq{i}") for i in range(3)]
        for k0 in range(0, NK, 512):
            n = min(512, NK - k0)
            b = mk.bank()
            for kc in range(4):
                mk.I("pe", "matmul", out=b[:, 0:n], lhsT=wkv[:, kc, h * 256:h * 256 + 128], rhs=CKVT[:, kc, k0:k0 + n], start=(kc == 0), stop=(kc == 3))
            mk.I("act", "activation", out=KN[:, k0:k0 + n], in_=b[:, 0:n], func=AF.Copy)
            bs = mk.bank()
            mk.I("act", "activation", out=sqk[:, 0:n], in_=b[:, 0:n], func=AF.Square)
            mk.I("pe", "matmul", out=bs[0:1, 0:n], lhsT=C.ones[:, 0:1], rhs=sqk[:, 0:n], start=True, stop=False)
            mk.I("act", "activation", out=kss[0:64, 0:n], in_=KPET[0:64, k0:k0 + n], func=AF.Square)
            mk.I("pe", "matmul", out=bs[0:1, 0:n], lhsT=C.ones[0:64, 0:1], rhs=kss[0:64, 0:n], start=False, stop=True)
            if k0 == 0:
                mk.I("dve", "tensor_reduce", out=kmax2[0:1, h:h + 1], in_=bs[0:1, 0:n], axis=AX.X, op=ALU.max)
            else:
                mk.I("dve", "tensor_reduce", out=kss[0:1, 0:1], in_=bs[0:1, 0:n], axis=AX.X, op=ALU.max)
                mk.I("dve", "tensor_tensor", out=kmax2[0:1, h:h + 1], in0=kmax2[0:1, h:h + 1], in1=kss[0:1, 0:1], op=ALU.max)
        for kt in range(nkt):
            b = mk.bank()
            for kc in range(4):
                mk.I("pe", "matmul", out=b[:, 0:128], lhsT=CKVT[:, kc, kt * 128:(kt + 1) * 128], rhs=wkv[:, kc, h * 256 + 128:h * 256 + 256], start=(kc == 0), stop=(kc == 3))
            mk.I("dve", "tensor_copy", out=V[:, kt, :], in_=b[:, 0:128])
        for slot in range(4):
            g, hf = slot // 2, slot % 2
            qs = slice(slot * TQ, (slot + 1) * TQ)
            b = mk.bank()
            for kc in range(4):
                mk.I("pe", "matmul", out=b[:, 0:TQ], lhsT=wq[:, kc, h * 192:h * 192 + 128], rhs=QN[:, kc, qs], start=(kc == 0), stop=(kc == 3))
            mk.I("act", "activation", out=qn.v, in_=b[:, 0:TQ], func=AF.Copy)
            mk.I("act", "activation", out=C.sq[0][:, 0:TQ], in_=b[:, 0:TQ], func=AF.Square)
            b2 = mk.bank()
            for kc in range(4):
                mk.I("pe", "matmul", out=b2[0:64, 0:TQ], lhsT=wq[:, kc, h * 192 + 128:h * 192 + 192], rhs=QN[:, kc, qs], start=(kc == 0), stop=(kc == 3))
            mk.I("act", "activation", out=C.sq[1][0:64, 0:TQ], in_=b2[0:64, 0:TQ], func=AF.Square)
            do_rope(b2[0:64, 0:TQ], TQ, Wd["cosq"][:, qs], Wd["sinq"][:, qs], qr[0:64, :])
            bm = mk.bank()
            mk.I("pe", "matmul", out=bm[0:1, 0:TQ], lhsT=C.ones[:, 0:1], rhs=C.sq[0][:, 0:TQ], start=True, stop=False)
            mk.I("pe", "matmul", out=bm[0:1, 0:TQ], lhsT=C.ones[0:64, 0:1], rhs=C.sq[1][0:64, 0:TQ], start=False, stop=True)
            mk.I("act", "activation", out=mrow[0:1, :], in_=bm[0:1, 0:TQ], func=AF.Sqrt, scale=kmax2[0:1, h:h + 1])
            mk.I("dve", "tensor_scalar", out=negm[0:1, :], in0=mrow[0:1, :], scalar1=-1.0, scalar2=None, op0=ALU.mult)
            bo = mk.reserve()
            bsum = mk.reserve()
            for kt in range(nkt):
                ks = slice(kt * 128, (kt + 1) * 128)
                bs = mk.bank()
                mk.I("pe", "matmul", out=bs[:, 0:TQ], lhsT=KN[:, ks], rhs=qn.v, start=True, stop=False)
                mk.I("pe", "matmul", out=bs[:, 0:TQ], lhsT=KPET[0:64, ks], rhs=qr[0:64, :], start=False, stop=False)
                mk.I("pe", "matmul", out=bs[:, 0:TQ], lhsT=onesb[0:1, :], rhs=negm[0:1, :], start=False, stop=True)
                PT = PTb[kt % 3]
                mk.I("act", "activation", out=PT.v, in_=bs[:, 0:TQ], func=AF.Exp, scale=scale)
                mk.I("pe", "matmul", out=bo[:, 0:TQ], lhsT=V[:, kt, :], rhs=PT.v, start=(kt == 0), stop=(kt == nkt - 1))
                mk.I("pe", "matmul", out=bsum[:, 0:TQ], lhsT=onesb.v, rhs=PT.v, start=(kt == 0), stop=(kt == nkt - 1))
            mk.I("dve", "reciprocal", out=rs.v, in_=bsum[:, 0:TQ])
            mk.I("dve", "tensor_tensor", out=oo.v, in0=bo[:, 0:TQ], in1=rs.v, op=ALU.mult)
            mk.dma("sp", dst[g, h, :, hf * TQ:(hf + 1) * TQ], oo.v)
            mk.unreserve(bo)
            mk.unreserve(bsum)
        mk.release(markh)
    mk.release(mark0)


def build_fused():
    nc = bass.Bass("TRN2", target_bir_lowering=False)
    DEBUG["nc"] = nc
    DEBUG["done"] = set()
    D = {}

    def din(name, shape):
        D[name] = nc.dram_tensor(name, list(shape), F32, kind="ExternalInput").ap()

    def dout(name, shape):
        D[name] = nc.dram_tensor(name, list(shape), F32, kind="ExternalOutput").ap()

    for name, shape in (("xp", [2, 256, 2048]), ("xs", [4096, 2048]), ("x2", [3, 514, 2048]), ("condT", [128, 16, 2]),
                        ("w_ada", [2048, 12288]), ("b_adaT", [128, 96]), ("norm1T", [128, 16]), ("ident", [128, 128]),
                        ("tri0", [128, 128]), ("tri1", [128, 128]), ("ms0", [128, 128]), ("ms1", [128, 128]),
                        ("s0p", [2, 8, 128, 128]), ("s0h", [8, 2, 1, 128, 128]), ("w_g_p", [2048, 4096]), ("w_ab_p", [2048, 32]),
                        ("cw_p", [128, 24, 3]), ("dtb_p", [128, 16]), ("alog_p", [128, 16]), ("gng", [128, 1]),
                        ("w_ab_h", [8, 2048, 4]), ("dtb_h", [8, 128, 2]), ("alog_h", [8, 128, 2]),
                        ("w_m", [2048, 1088]), ("gq", [128, 4]), ("gkv", [128, 4]), ("wq_p", [512, 1536]), ("wkv_p", [512, 2048]),
                        ("cos", [64, 4096]), ("sin", [64, 4096]), ("cosq", [64, 1028]), ("sinq", [64, 1028]), ("rot", [64, 64]),
                        ("cache_ckv", [256, 512]), ("cache_kpe", [256, 64]), ("sel", [128, 4]), ("hmask", [128, 4]),
                        ("w_out", [2048, 2048]), ("norm2T", [128, 16]), ("w_up", [2048, 11264]), ("fcwT", [128, 88, 3]),
                        ("fcbT", [128, 88]), ("w_down", [5632, 2048]), ("fnormT", [128, 16])):
        din(name, shape)
    for name, shape in (("y", [3, 512, 2048]), ("new_state", [2, 2, 8, 128, 128]), ("new_ckv", [2, 256, 512]), ("new_kpe", [2, 256, 64])):
        dout(name, shape)
    mixp_d = nc.dram_tensor("mixp_d", [2, 16, 128, 256], F32).ap()
    mixs_d = nc.dram_tensor("mixs_d", [2, 16, 128, 514], F32).ap()
    hT_d = nc.dram_tensor("hT_d", [8, 128, 16, 512], BF16).ap()
    with ExitStack() as st:
        mk = MK(nc, st)
        C = Ctx()
        alloc_common(mk, C, D)
        n1g = mk.alloc([16], F32, "n1g")
        gm1 = mk.alloc([16, 2], F32, "gm1")
        gm2 = mk.alloc([16, 2], F32, "gm2")
        n2g = mk.alloc([16], F32, "n2g")
        fng = mk.alloc([16], F32, "fng")
        fcw = mk.alloc([88, 3], F32, "fcw")
        fcb = mk.alloc([88], F32, "fcb")
        hm = mk.alloc([4], F32, "hm")
        sel = mk.alloc([4], F32, "sel")
        for t, nm in ((n1g, "norm1T"), (n2g, "norm2T"), (fng, "fnormT"), (fcw, "fcwT"), (fcb, "fcbT"), (hm, "hmask"), (sel, "sel")):
            mk.dma("sp", t.v, D[nm])
        adaln(mk, C, D["w_ada"], [0, 1, 2, 3, 4, 5], 2)
        for (gm, ng, lo) in ((gm1, n1g, 16), (gm2, n2g, 64)):
            mk.I("dve", "tensor_scalar", out=gm.v, in0=C.mod[:, lo:lo + 16, :], scalar1=1.0, scalar2=None, op0=ALU.add)
            mk.I("dve", "tensor_tensor", out=gm.v, in0=gm.v, in1=ng.v.rearrange("p (c o) -> p c o", o=1).bc([128, 16, 2]), op=ALU.mult)
        Wp = dict(w_g=D["w_g_p"], w_ab=D["w_ab_p"], cw=D["cw_p"], dtb=D["dtb_p"], alog=D["alog_p"], gng=D["gng"])
        Wmp = dict(w_m=D["w_m"], gq=D["gq"], gkv=D["gkv"], wq=D["wq_p"], wkv=D["wkv_p"], rot=D["rot"])
        for s in range(2):
            gdn_phase(mk, C, D["xp"][s], 256, 8, 0, gm1, Wp, D["s0p"], mixp_d[s, 0:8], D["new_state"][s])
            mla_phase(mk, C, D["xp"][s], 256, 8, 0, gm1, Wmp, 0, False, mixp_d[s, 8:16], D["new_ckv"][s], D["new_kpe"][s])
        for h in range(8):
            Ws = dict(w_g=D["w_g_p"][:, h * 512:(h + 1) * 512], w_ab=D["w_ab_h"][h], cw=D["cw_p"][:, 3 * h:3 * h + 3, :],
                      dtb=D["dtb_h"][h], alog=D["alog_h"][h], gng=D["gng"])
            gdn_phase(mk, C, D["xs"], 4096, 1, 1, gm1, Ws, D["s0h"][h], None, None, win=(sel, mixs_d[:, h]),
                      hcache=(hT_d, "write" if h == 0 else "read"))
        Wms = dict(w_m=D["w_m"], gq=D["gq"], gkv=D["gkv"], wq=D["wq_p"], wkv=D["wkv_p"], rot=D["rot"], cos=D["cos"], sin=D["sin"],
                   cosq=D["cosq"], sinq=D["sinq"], cache_ckv=D["cache_ckv"], cache_kpe=D["cache_kpe"])
        mla_fused(mk, C, D["xs"], 4096, 1, gm1, Wms, 256, D["x2"][1:3], mixs_d[:, 8:16], hT_d)
        mk.barrier()
        xT = mk.alloc([16, 514], F32, "xT")
        actin = mk.alloc([16, 514], BF16, "actin")
        actT = mk.alloc([44, 512], BF16, "actT")
        Ra = [mk.alloc([514], F32, f"Ra{i}") for i in range(2)]
        Rg = [mk.alloc([514], F32, f"Rg{i}") for i in range(2)]
        ta = [mk.alloc([512], F32, f"ta{i}") for i in range(2)]
        tg = [mk.alloc([512], F32, f"tg{i}") for i in range(2)]

        def load_mix(ti, actin_, W):
            if ti == 0:
                for s in range(2):
                    mk.dma("pool", actin_[:, :, s * 256:(s + 1) * 256], mixp_d[s].rearrange("c p w -> p c w"))
            else:
                mk.dma("pool", actin_[:, :, 0:W], mixs_d[ti - 1].rearrange("c p w -> p c w"))

        phase2_tiles(mk, C, D["x2"], D["y"], load_mix, n2g, fng, fcw, fcb, hm, gm2, xT, actin, actT, Ra, Rg, ta, tg, C.tmpx, C.rstd,
                     D["w_out"], D["w_up"], D["w_down"])
        mk.finalize()
        print("fused instructions:", mk.n_inst, {e: len(mk.ops[e]) for e in ENGS}, "sbuf words", mk.top)
    return nc


def fused_inputs(core, I):
    b, j = core // 4, core % 4
    d = phase1_inputs(core, I)
    for k in ("s0s", "w_g_s", "w_ab_s", "cw_s", "dtb_s", "alog_s", "wq_s", "wkv_s"):
        d.pop(k)
    p2 = phase2_inputs(core, I["x_prompt"], I["x_sample"], np.zeros((16, 256, 2048), np.float32), np.zeros((2, 4096, 2048), np.float32),
                       I["c"], I["c_ctx"], I["w_ada"][0], I["b_ada"][0], I["w_out"][0], I["norm2_g"][0], I["w_up"][0],
                       I["ffn_conv_w"][0], I["ffn_conv_b"][0], I["w_down"][0], I["final_norm_g"])
    for k in ("x2", "hmask", "w_out", "norm2T", "w_up", "fcwT", "fcbT", "w_down", "fnormT"):
        d[k] = p2[k]
    w_in = I["w_in"][0]
    dtb = I["gdn_dt_bias"][0]
    alog = I["gdn_a_log"][0]
    d["s0h"] = np.stack([I["state_gdn"][b, 0, :, h:h + 1] for h in range(8)], axis=0)
    d["w_ab_h"] = np.stack([w_in[:, [4096 + h, 4104 + h, 4112 + h, 4120 + h]] for h in range(8)], axis=0)
    d["dtb_h"] = np.stack([np.broadcast_to(dtb[:, h].reshape(1, 2), (128, 2)) for h in range(8)], axis=0)
    d["alog_h"] = np.stack([np.broadcast_to(alog[:, h].reshape(1, 2), (128, 2)) for h in range(8)], axis=0)
    cos2, sin2 = d["cos"], d["sin"]
    cosq = np.zeros((64, 1028), np.float32)
    sinq = np.zeros((64, 1028), np.float32)
    for g in range(2):
        lo = 1024 * j + 512 * g - 1
        a, e = max(lo, 0), min(lo + 514, 4096)
        cosq[:, g * 514 + a - lo:g * 514 + e - lo] = cos2[:, a:e]
        sinq[:, g * 514 + a - lo:g * 514 + e - lo] = sin2[:, a:e]
    d["cosq"], d["sinq"] = cosq, sinq
    sel = np.zeros((128, 4), np.float32)
    sel[:, j] = 1.0
    d["sel"] = sel
    return {k: np.ascontiguousarray(np.asarray(v, np.float32)) for k, v in d.items()}


def kernel_fused(**I):
    I = {k: np.asarray(v) for k, v in I.items()}
    if "f" not in _NC:
        _NC["f"] = build_fused()
    r = run_bass_kernel_spmd(_NC["f"], [fused_inputs(c, I) for c in range(NCORES)], core_ids=list(range(NCORES))).results
    yp = np.zeros((16, 256, 2048), np.float32)
    ys = np.zeros((2, 4096, 2048), np.float32)
    new_state = np.zeros((16, 1, 2, 8, 128, 128), np.float32)
    new_ckv = np.zeros((16, 1, 256, 512), np.float32)
    new_kpe = np.zeros((16, 1, 256, 64), np.float32)
    for c in range(NCORES):
        b, j = c // 4, c % 4
        y = r[c]["y"]
        yp[2 * c:2 * c + 2] = y[0].reshape(2, 256, 2048)
        ys[b, 1024 * j:1024 * j + 512] = y[1]
        ys[b, 1024 * j + 512:1024 * j + 1024] = y[2]
        for s in range(2):
            new_state[2 * c + s, 0] = r[c]["new_state"][s]
            new_ckv[2 * c + s, 0] = r[c]["new_ckv"][s]
            new_kpe[2 * c + s, 0] = r[c]["new_kpe"][s]
    return (yp, ys, new_state, new_ckv, new_kpe)


def kernel(**inputs):
    return kernel_fused(**inputs)
```

```python
import numpy as np
from contextlib import ExitStack
import concourse.bass as bass
import concourse.mybir as mybir
from concourse.bass_utils import run_bass_kernel_spmd

F32 = mybir.dt.float32
BF16 = mybir.dt.bfloat16
ALU = mybir.AluOpType
AF = mybir.ActivationFunctionType
AX = mybir.AxisListType
ENGS = ("pe", "act", "dve", "pool", "sp")
NCORES = 8
PENG = "dve"
GSTOP = {"v": 99}
NPAR = 4
EPS = 1e-6


class View:
    __slots__ = ("tile", "ap", "gen")

    def __init__(self, tile, ap, gen=None):
        self.tile = tile
        self.ap = ap
        self.gen = gen

    def __getitem__(self, idx):
        return View(self.tile, self.ap[idx], self.gen)

    def bc(self, shape):
        return View(self.tile, self.ap.to_broadcast(list(shape)), self.gen)

    def rearrange(self, s, **kw):
        return View(self.tile, self.ap.rearrange(s, **kw), self.gen)


class BankRef:
    def __init__(self, tile, gen):
        self.tile = tile
        self.gen = gen

    def __getitem__(self, idx):
        return View(self.tile, self.tile.h[idx], self.gen)


class Tile:
    def __init__(self, ap, name):
        self.h = ap
        self.name = name
        self.last_write = None
        self.reads = []
        self.dma_sem = None
        self.dma_count = 0

    def __getitem__(self, idx):
        return View(self, self.h[idx])

    @property
    def v(self):
        return View(self, self.h)


class MK:
    ARENA_F32 = 50688
    N_DMA_SEMS = 16

    def __init__(self, nc, stack):
        self.nc = nc
        self.stack = stack
        self.ops = {e: [] for e in ENGS}
        self.seq = {e: 0 for e in ENGS}
        self.sem = {e: stack.enter_context(nc.semaphore("sem_" + e)) for e in ("pe", "act", "dve", "pool")}
        self.waited = {e: {} for e in ENGS}
        self.dma_tiles = []
        self.n_inst = 0
        self.arena = stack.enter_context(nc.sbuf_tensor("arena", [128, self.ARENA_F32], F32))
        self.top = 0
        self.banks = [Tile(stack.enter_context(nc.psum_tensor(f"bank{i}", [128, 512], F32)), f"bank{i}")
                      for i in range(8)]
        for b in self.banks:
            b.is_bank = True
        self.bank_rr = 0
        self.reserved = []
        self.dma_sem_pool = {}
        self.dma_sem_rr = {}
        self.tcount = 0

    def alloc(self, free_shape, dtype=F32, name=None, parts=128):
        n = int(np.prod(free_shape))
        words = n if dtype == F32 else (n + 1) // 2
        words = (words + 7) // 8 * 8
        assert self.top + words <= self.ARENA_F32, f"SBUF arena overflow allocating {name} {free_shape}"
        ap = self.arena[0:parts, self.top:self.top + words]
        self.top += words
        if dtype != F32:
            ap = ap.bitcast(dtype)
        ap = ap[:, 0:n]
        if len(free_shape) == 2:
            ap = ap.rearrange("p (a b) -> p a b", a=free_shape[0])
        elif len(free_shape) == 3:
            ap = ap.rearrange("p (a b c) -> p a b c", a=free_shape[0], b=free_shape[1])
        self.tcount += 1
        return Tile(ap, name or f"t{self.tcount}")

    def mark(self):
        return self.top

    def release(self, mark):
        self.barrier()
        self.top = mark

    def bank(self):
        while True:
            b = self.banks[self.bank_rr % 8]
            self.bank_rr += 1
            if b not in self.reserved:
                b.gen = getattr(b, "gen", 0) + 1
                return BankRef(b, b.gen)

    def reserve(self):
        b = self.bank()
        self.reserved.append(b.tile)
        return b

    def unreserve(self, b):
        self.reserved.remove(b.tile)

    def _resolve(self, ev):
        if ev[0] == "dma":
            t = ev[1]
            return (t.dma_sem, t.dma_count, None)
        return (self.sem[ev[0]], ev[1], ev[0])

    def _wait(self, eng, ev):
        sem, val, src = self._resolve(ev)
        if src == eng and eng == "pe":
            return
        key = id(sem)
        if self.waited[eng].get(key, 0) >= val:
            return
        self.waited[eng][key] = val
        self.ops[eng].append(("w", sem, val))

    def _deps(self, eng, reads, writes):
        for t in reads:
            if t.last_write is not None:
                self._wait(eng, t.last_write)
            if getattr(t, "is_bank", False):
                for ev in t.reads:
                    if ev[0] != eng:
                        self._wait(eng, ev)
        for t in writes:
            if t.last_write is not None:
                self._wait(eng, t.last_write)
            for ev in t.reads:
                self._wait(eng, ev)

    def _commit(self, ev, reads, writes):
        for t in writes:
            t.last_write = ev
            t.reads = []
        for t in reads:
            if t in writes:
                continue
            t.reads.append(ev)
            if len(t.reads) > 24:
                best = {}
                for e in t.reads:
                    k = e[0] if e[0] != "dma" else ("dma", id(e[1]))
                    if k not in best or (e[0] != "dma" and e[1] > best[k][1]):
                        best[k] = e
                t.reads = list(best.values())

    def I(self, eng, meth, *args, reads=(), writes=(), **kw):
        rd, wr = list(reads), list(writes)
        real = {}
        for k, v in kw.items():
            if isinstance(v, View):
                assert v.gen is None or v.gen == v.tile.gen, f"stale PSUM bank handle used by {meth} ({k})"
                (wr if k in ("out", "accum_out") else rd).append(v.tile)
                real[k] = v.ap
            else:
                real[k] = v
        rargs = []
        for v in args:
            if isinstance(v, View):
                rd.append(v.tile)
                rargs.append(v.ap)
            else:
                rargs.append(v)
        self._deps(eng, rd, wr)
        self.seq[eng] += 1
        ev = (eng, self.seq[eng])
        self.ops[eng].append(("i", meth, rargs, real))
        self._commit(ev, rd, wr)
        self.n_inst += 1
        return ev

    def dma(self, q, out, in_, **kw):
        rd, wr = [], []
        st = None
        if isinstance(out, View):
            wr.append(out.tile)
            o = out.ap
            st = out.tile
        else:
            o = out
        if isinstance(in_, View):
            rd.append(in_.tile)
            i = in_.ap
            if st is None:
                st = in_.tile
        else:
            i = in_
        if st.dma_sem is None:
            st.dma_sem = {}
        if q not in st.dma_sem:
            pool = self.dma_sem_pool.setdefault(q, [])
            if len(pool) < self.N_DMA_SEMS:
                ds = Tile(None, "dsem_%s%d" % (q, len(pool)))
                ds.dma_sem = self.stack.enter_context(self.nc.semaphore("ds_%s%d" % (q, len(pool))))
                pool.append(ds)
                self.dma_tiles.append(ds)
            rr = self.dma_sem_rr.get(q, 0)
            self.dma_sem_rr[q] = rr + 1
            st.dma_sem[q] = pool[rr % self.N_DMA_SEMS]
        dsem = st.dma_sem[q]
        self._deps(q, rd, wr)
        if dsem.dma_count:
            self._wait(q, ("dma", dsem))
        dsem.dma_count += 16
        self.ops[q].append(("d", o, i, kw, dsem.dma_sem))
        ev = ("dma", dsem)
        self._commit(ev, rd, wr)
        self.n_inst += 1
        return ev

    def barrier(self):
        for e in ENGS:
            for src in ("pe", "act", "dve", "pool"):
                if src != e and self.seq[src] > 0:
                    self._wait(e, (src, self.seq[src]))
            for t in self.dma_tiles:
                if t.dma_count:
                    self._wait(e, ("dma", t))

    def finalize(self):
        for t in self.dma_tiles:
            self.ops["sp"].append(("w", t.dma_sem, t.dma_count))
        sem = self.sem

        def run(eng_name):
            def f(e):
                for op in self.ops[eng_name]:
                    if op[0] == "w":
                        e.wait_ge(op[1], op[2])
                    elif op[0] == "i":
                        ins = getattr(e, op[1])(*op[2], **op[3])
                        if eng_name in sem:
                            ins.then_inc(sem[eng_name], 1)
                    else:
                        e.dma_start(out=op[1], in_=op[2], **op[3]).then_inc(op[4], 16)
            return f

        with self.nc.Block() as block:
            block.tensor(run("pe"))
            block.scalar(run("act"))
            block.vector(run("dve"))
            block.gpsimd(run("pool"))
            block.sync(run("sp"))


class Ctx:
    pass


DEBUG = {"on": False, "nc": None, "done": set()}


def dump(mk, name, view, shape):
    if not DEBUG["on"] or name in DEBUG["done"]:
        return
    DEBUG["done"].add(name)
    ap = DEBUG["nc"].dram_tensor("dbg_" + name, list(shape), F32, kind="ExternalOutput").ap()
    stg = mk.alloc(list(shape[1:]), F32, "dbgs_" + name, parts=shape[0]) if False else None
    mk.dma("sp", ap, view)


def halves(W):
    if W <= 512:
        return [(0, W)]
    h = (W + 1) // 2
    return [(0, h), (h, W - h)]


def mm_group(mk, W, steps):
    outs = []
    for (c0, n) in halves(W):
        b = mk.bank()
        for i, (lhsT, rhs_fn) in enumerate(steps):
            mk.I("pe", "matmul", out=b[:, 0:n], lhsT=lhsT, rhs=rhs_fn(c0, n),
                 start=(i == 0), stop=(i == len(steps) - 1))
        outs.append((b, c0, n))
    return outs


def load_weight_tile(mk, wt, src_ap, KC, ncols):
    mk.dma("pool", wt[:, 0:KC, 0:ncols], src_ap.rearrange("(kc p) n -> p kc n", p=128))


def load_xT(mk, C, x_rows, W, xT, eng_rr):
    nsub = (W + 127) // 128
    for s in range(nsub):
        r0 = s * 128
        n = min(128, W - r0)
        xs = C.xstage[s % 2]
        mk.dma("sp", xs[0:n, :], x_rows[r0:r0 + n, :])
        for g in range(4):
            b = mk.bank()
            for q in range(4):
                c = g * 4 + q
                mk.I("pe", "transpose", out=b[:, q * 128:q * 128 + n], in_=xs[0:n, c * 128:(c + 1) * 128],
                     identity=C.ident[0:n, 0:n])
            src = b[:, 0:512].rearrange("p (q t) -> p q t", q=4)[:, :, 0:n]
            dst = xT[:, g * 4:(g + 1) * 4, r0:r0 + n]
            if (s * 4 + g) % 2 == 0:
                mk.I("dve", "tensor_copy", out=dst, in_=src)
            else:
                mk.I("act", "activation", out=dst, in_=src, func=AF.Copy)


def rms_rstd(mk, C, XT, nch, W, col0, dim, rstd):
    pieces = halves(W)
    banks = [mk.bank() for _ in pieces]
    use_b = hasattr(C, "sqb")
    for c in range(nch):
        if use_b:
            sq = C.sqb[C.sqb_rr % 4]
            C.sqb_rr += 1
            ones = C.onesb16
        else:
            sq = C.sq[c % 2]
            ones = C.ones
        mk.I("act", "activation", out=sq[:, 0:W], in_=XT[:, c, col0:col0 + W], func=AF.Square)
        for (b, (c0, n)) in zip(banks, pieces):
            mk.I("pe", "matmul", out=b[:, 0:n], lhsT=ones.v, rhs=sq[:, c0:c0 + n], start=(c == 0), stop=(c == nch - 1))
    for (b, (c0, n)) in zip(banks, pieces):
        mk.I("act", "activation", out=C.tmpn[:, c0:c0 + n], in_=b[:, 0:n], func=AF.Sqrt, scale=1.0 / dim, bias=C.epsb[:, 0:1])
        mk.I("dve", "reciprocal", out=rstd[:, c0:c0 + n], in_=C.tmpn[:, c0:c0 + n])


def adaln(mk, C, w_ada, which_list, ncond):
    b = mk.bank()
    for wh in which_list:
        for blk in range(4):
            col0 = wh * 2048 + blk * 512
            wt = C.wt[C.wt_rr % len(C.wt)]
            C.wt_rr += 1
            load_weight_tile(mk, wt, w_ada[:, col0:col0 + 512], 16, 512)
            for q in range(4):
                cc = wh * 16 + blk * 4 + q
                for kc in range(16):
                    mk.I("pe", "matmul", out=b[:, cc * ncond:(cc + 1) * ncond], lhsT=wt[:, kc, q * 128:(q + 1) * 128],
                         rhs=C.scond[:, kc, 0:ncond], start=(kc == 0), stop=(kc == 15))
    for wh in which_list:
        sl = slice(wh * 16, (wh + 1) * 16)
        mk.I("dve", "tensor_tensor", out=C.mod[:, sl, :],
             in0=b[:, wh * 16 * ncond:(wh + 1) * 16 * ncond].rearrange("p (c n) -> p c n", n=ncond),
             in1=C.b_ada[:, sl].rearrange("p (c o) -> p c o", o=1).bc([128, 16, ncond]), op=ALU.add)


P2_TILES = [dict(W=512, cond=0, segs=[(0, 256, 0), (256, 256, 0)], out0=0),
            dict(W=514, cond=1, segs=[(0, 514, 1)], out0=1),
            dict(W=514, cond=1, segs=[(0, 514, 1)], out0=1)]


def phase2_tiles(mk, C, x2, y, load_mix, n2g, fng, fcw, fcb, hm, gm2, xT, actin, actT, Ra, Rg, ta, tg, tmpx, rstd,
                 w_out, w_up, w_down):
    for ti, T in enumerate(P2_TILES):
        W, ci, out0 = T["W"], T["cond"], T["out0"]
        load_xT(mk, C, x2[ti], W, xT, 0)
        load_mix(ti, actin, W)
        for blk in range(4):
            wt = C.wt[C.wt_rr % len(C.wt)]
            C.wt_rr += 1
            load_weight_tile(mk, wt, w_out[:, blk * 512:(blk + 1) * 512], 16, 512)
            for q in range(4):
                cc = blk * 4 + q
                outs = mm_group(mk, W, [(wt[:, kc, q * 128:(q + 1) * 128],
                                        (lambda c0, n, kc=kc: actin[:, kc, c0:c0 + n])) for kc in range(16)])
                for (b, c0, n) in outs:
                    mk.I("dve", "scalar_tensor_tensor", out=xT[:, cc, c0:c0 + n], in0=b[:, 0:n],
                         scalar=C.mod[:, 32 + cc, ci:ci + 1], in1=xT[:, cc, c0:c0 + n], op0=ALU.mult, op1=ALU.add)
        rms_rstd(mk, C, xT, 16, W, 0, 2048.0, rstd)
        for cc in range(16):
            tx = tmpx[cc % 2]
            mk.I("dve", "scalar_tensor_tensor", out=tx[:, 0:W], in0=xT[:, cc, 0:W], scalar=gm2[:, cc, ci:ci + 1],
                 in1=rstd[:, 0:W], op0=ALU.mult, op1=ALU.mult)
            mk.I("act", "activation", out=actin[:, cc, 0:W], in_=tx[:, 0:W], func=AF.Identity,
                 bias=C.mod[:, 48 + cc, ci:ci + 1], scale=1.0)
        for jb in range(11):
            wa = C.wt[C.wt_rr % len(C.wt)]
            C.wt_rr += 1
            load_weight_tile(mk, wa, w_up[:, jb * 512:(jb + 1) * 512], 16, 512)
            wg = C.wt[C.wt_rr % len(C.wt)]
            C.wt_rr += 1
            load_weight_tile(mk, wg, w_up[:, 5632 + jb * 512:5632 + (jb + 1) * 512], 16, 512)
            for q in range(4):
                j = jb * 4 + q
                res = []
                for (wtile, R, chunk) in ((wa, Ra[j % 2], j), (wg, Rg[j % 2], 44 + j)):
                    outs = mm_group(mk, W, [(wtile[:, kc, q * 128:(q + 1) * 128],
                                            (lambda c0, n, kc=kc: actin[:, kc, c0:c0 + n])) for kc in range(16)])
                    for (b, c0, n) in outs:
                        mk.I("act", "activation", out=R[:, c0:c0 + n], in_=b[:, 0:n], func=AF.Copy)
                    res.append((R, chunk))
                for (R, chunk), tt in ((res[0], ta[j % 2]), (res[1], tg[j % 2])):
                    for (s0, L, halo) in T["segs"]:
                        if halo:
                            mk.I("dve", "tensor_scalar", out=R[:, s0:s0 + 1], in0=R[:, s0:s0 + 1],
                                 scalar1=hm[:, 2 * (ti - 1):2 * (ti - 1) + 1], scalar2=None, op0=ALU.mult)
                            mk.I("dve", "tensor_scalar", out=R[:, s0 + L - 1:s0 + L], in0=R[:, s0 + L - 1:s0 + L],
                                 scalar1=hm[:, 2 * (ti - 1) + 1:2 * (ti - 1) + 2], scalar2=None, op0=ALU.mult)
                            o0, n = s0 + 1, L - 2
                            mk.I("act", "activation", out=tt[:, 0:n], in_=R[:, o0:o0 + n], func=AF.Identity,
                                 scale=fcw[:, chunk, 1:2], bias=fcb[:, chunk:chunk + 1])
                            mk.I("dve", "scalar_tensor_tensor", out=tt[:, 0:n], in0=R[:, o0 - 1:o0 - 1 + n],
                                 scalar=fcw[:, chunk, 0:1], in1=tt[:, 0:n], op0=ALU.mult, op1=ALU.add)
                            mk.I("dve", "scalar_tensor_tensor", out=tt[:, 0:n], in0=R[:, o0 + 1:o0 + 1 + n],
                                 scalar=fcw[:, chunk, 2:3], in1=tt[:, 0:n], op0=ALU.mult, op1=ALU.add)
                        else:
                            mk.I("act", "activation", out=tt[:, s0:s0 + L], in_=R[:, s0:s0 + L], func=AF.Identity,
                                 scale=fcw[:, chunk, 1:2], bias=fcb[:, chunk:chunk + 1])
                            mk.I("dve", "scalar_tensor_tensor", out=tt[:, s0 + 1:s0 + L], in0=R[:, s0:s0 + L - 1],
                                 scalar=fcw[:, chunk, 0:1], in1=tt[:, s0 + 1:s0 + L], op0=ALU.mult, op1=ALU.add)
                            mk.I("dve", "scalar_tensor_tensor", out=tt[:, s0:s0 + L - 1], in0=R[:, s0 + 1:s0 + L],
                                 scalar=fcw[:, chunk, 2:3], in1=tt[:, s0:s0 + L - 1], op0=ALU.mult, op1=ALU.add)
                mk.I("act", "activation", out=ta[j % 2].v, in_=ta[j % 2].v, func=AF.Silu)
                mk.I("dve", "tensor_tensor", out=actT[:, j, :], in0=ta[j % 2].v, in1=tg[j % 2].v, op=ALU.mult)
        for cc in range(16):
            wt = C.wt[C.wt_rr % len(C.wt)]
            C.wt_rr += 1
            wv = wt.v.rearrange("p a b -> p (a b)")[:, 0:44 * 128].rearrange("p (k n) -> p k n", k=44)
            mk.dma("pool", wv, w_down[:, cc * 128:(cc + 1) * 128].rearrange("(kc p) n -> p kc n", p=128))
            b = mk.bank()
            for j in range(44):
                mk.I("pe", "matmul", out=b[:, 0:512], lhsT=wv[:, j, :], rhs=actT[:, j, :], start=(j == 0), stop=(j == 43))
            mk.I("dve", "scalar_tensor_tensor", out=xT[:, cc, out0:out0 + 512], in0=b[:, 0:512],
                 scalar=C.mod[:, 80 + cc, ci:ci + 1], in1=xT[:, cc, out0:out0 + 512], op0=ALU.mult, op1=ALU.add)
        rms_rstd(mk, C, xT, 16, 512, out0, 2048.0, rstd)
        for cc in range(16):
            mk.I("dve", "scalar_tensor_tensor", out=xT[:, cc, out0:out0 + 512], in0=xT[:, cc, out0:out0 + 512],
                 scalar=fng[:, cc:cc + 1], in1=rstd[:, 0:512], op0=ALU.mult, op1=ALU.mult)
        for s in range(4):
            ys = C.xstage[s % 2]
            for g in range(4):
                b = mk.bank()
                for q in range(4):
                    c = g * 4 + q
                    mk.I("pe", "transpose", out=b[:, q * 128:(q + 1) * 128],
                         in_=xT[:, c, out0 + s * 128:out0 + (s + 1) * 128], identity=C.ident.v)
                if g % 2 == 0:
                    mk.I("dve", "tensor_copy", out=ys[:, g * 512:(g + 1) * 512], in_=b[:, 0:512])
                else:
                    mk.I("act", "activation", out=ys[:, g * 512:(g + 1) * 512], in_=b[:, 0:512], func=AF.Copy)
            mk.dma("sp", y[ti, s * 128:(s + 1) * 128, :], ys.v)


def build_phase2():
    nc = bass.Bass("TRN2", target_bir_lowering=False)
    D = {}

    def din(name, shape, dt=F32):
        D[name] = nc.dram_tensor(name, list(shape), dt, kind="ExternalInput").ap()
        return D[name]

    x2 = din("x2", [3, 514, 2048])
    mix = din("mix", [3, 16, 128, 514])
    condT = din("condT", [128, 16, 2])
    hmask = din("hmask", [128, 4])
    w_ada = din("w_ada", [2048, 12288])
    b_adaT = din("b_adaT", [128, 96])
    w_out = din("w_out", [2048, 2048])
    norm2T = din("norm2T", [128, 16])
    w_up = din("w_up", [2048, 11264])
    fcwT = din("fcwT", [128, 88, 3])
    fcbT = din("fcbT", [128, 88])
    w_down = din("w_down", [5632, 2048])
    fnormT = din("fnormT", [128, 16])
    identD = din("ident", [128, 128])
    y = nc.dram_tensor("y", [3, 512, 2048], F32, kind="ExternalOutput").ap()

    with ExitStack() as st:
        mk = MK(nc, st)
        C = Ctx()
        C.ident = mk.alloc([128], F32, "ident")
        C.ones = mk.alloc([128], F32, "ones")
        C.epsb = mk.alloc([1], F32, "epsb")
        C.scond = mk.alloc([16, 2], BF16, "scond")
        condf = mk.alloc([16, 2], F32, "condf")
        C.b_ada = mk.alloc([96], F32, "b_ada")
        C.mod = mk.alloc([96, 2], F32, "mod")
        n2g = mk.alloc([16], F32, "n2g")
        fng = mk.alloc([16], F32, "fng")
        fcw = mk.alloc([88, 3], F32, "fcw")
        fcb = mk.alloc([88], F32, "fcb")
        hm = mk.alloc([4], F32, "hm")
        gm2 = mk.alloc([16, 2], F32, "gm2")
        C.xstage = [mk.alloc([2048], F32, f"xs{i}") for i in range(2)]
        C.sq = [mk.alloc([514], F32, f"sq{i}") for i in range(2)]
        C.tmpn = mk.alloc([514], F32, "tmpn")
        rstd = mk.alloc([514], F32, "rstd")
        C.wt = [mk.alloc([16, 512], BF16, f"wt{i}") for i in range(3)]
        C.wt_rr = 0
        xT = mk.alloc([16, 514], F32, "xT")
        actin = mk.alloc([16, 514], BF16, "actin")
        actT = mk.alloc([44, 512], BF16, "actT")
        Ra = [mk.alloc([514], F32, f"Ra{i}") for i in range(2)]
        Rg = [mk.alloc([514], F32, f"Rg{i}") for i in range(2)]
        ta = [mk.alloc([512], F32, f"ta{i}") for i in range(2)]
        tg = [mk.alloc([512], F32, f"tg{i}") for i in range(2)]
        tmpx = [mk.alloc([514], F32, f"tmpx{i}") for i in range(2)]

        mk.dma("sp", C.ident.v, identD)
        mk.I("dve", "memset", C.ones.v.ap, 1.0, writes=[C.ones])
        mk.I("dve", "memset", C.epsb.v.ap, EPS, writes=[C.epsb])
        mk.dma("sp", condf.v, condT)
        mk.dma("sp", C.b_ada.v, b_adaT)
        mk.dma("sp", n2g.v, norm2T)
        mk.dma("sp", fng.v, fnormT)
        mk.dma("sp", fcw.v, fcwT)
        mk.dma("sp", fcb.v, fcbT)
        mk.dma("sp", hm.v, hmask)
        mk.I("act", "activation", out=C.scond.v, in_=condf.v, func=AF.Silu)
        adaln(mk, C, w_ada, [2, 3, 4, 5], 2)
        mk.I("dve", "tensor_scalar", out=gm2.v, in0=C.mod[:, 64:80, :], scalar1=1.0, scalar2=None, op0=ALU.add)
        mk.I("dve", "tensor_tensor", out=gm2.v, in0=gm2.v,
             in1=n2g.v.rearrange("p (c o) -> p c o", o=1).bc([128, 16, 2]), op=ALU.mult)

        phase2_tiles(mk, C, x2, y, lambda ti, actin, W: mk.dma("pool", actin[:, :, 0:W], mix[ti, :, :, 0:W].rearrange("c p w -> p c w")),
                     n2g, fng, fcw, fcb, hm, gm2, xT, actin, actT, Ra, Rg, ta, tg, tmpx, rstd, w_out, w_up, w_down)
        mk.finalize()
        print("phase2 instructions:", mk.n_inst, {e: len(mk.ops[e]) for e in ENGS}, "sbuf words", mk.top)
    return nc


def colT(v, n):
    return np.ascontiguousarray(np.asarray(v, np.float32).reshape(n, 128).T)


def phase2_inputs(core, x_prompt, x_sample, mix_p, mix_s, c, c_ctx, w_ada, b_ada, w_out, norm2_g, w_up,
                  ffn_conv_w, ffn_conv_b, w_down, final_norm_g):
    b, j = core // 4, core % 4
    x2 = np.zeros((3, 514, 2048), np.float32)
    mix = np.zeros((3, 514, 2048), np.float32)
    x2[0, :512] = x_prompt[2 * core:2 * core + 2].reshape(512, 2048)
    mix[0, :512] = mix_p[2 * core:2 * core + 2].reshape(512, 2048)
    hmask = np.zeros((128, 4), np.float32)
    for g in range(2):
        lo = 1024 * j + 512 * g - 1
        hi = lo + 514
        a, e = max(lo, 0), min(hi, 4096)
        x2[1 + g, a - lo:e - lo] = x_sample[b, a:e]
        mix[1 + g, a - lo:e - lo] = mix_s[b, a:e]
        hmask[:, 2 * g] = 1.0 if lo >= 0 else 0.0
        hmask[:, 2 * g + 1] = 1.0 if hi <= 4096 else 0.0
    mixT = np.ascontiguousarray(mix.reshape(3, 514, 16, 128).transpose(0, 2, 3, 1))
    cond = np.stack([c_ctx, c[b]], axis=0)
    condT = np.ascontiguousarray(cond.reshape(2, 16, 128).transpose(2, 1, 0))
    return dict(x2=x2, mix=mixT, condT=condT, hmask=hmask, w_ada=w_ada, b_adaT=colT(b_ada, 96), w_out=w_out,
                norm2T=colT(norm2_g, 16), w_up=w_up,
                fcwT=np.ascontiguousarray(ffn_conv_w.reshape(3, 88, 128).transpose(2, 1, 0)),
                fcbT=colT(ffn_conv_b, 88), w_down=w_down, fnormT=colT(final_norm_g, 16),
                ident=np.eye(128, dtype=np.float32))


def norm_mod_tile(mk, C, x_rows, W, xT, hT, rstd, gm1, ci):
    load_xT(mk, C, x_rows, W, xT, 0)
    rms_rstd(mk, C, xT, 16, W, 0, 2048.0, rstd)
    for cc in range(16):
        tx = C.tmpx[cc % 2]
        mk.I("dve", "scalar_tensor_tensor", out=tx[:, 0:W], in0=xT[:, cc, 0:W], scalar=gm1[:, cc, ci:ci + 1],
             in1=rstd[:, 0:W], op0=ALU.mult, op1=ALU.mult)
        mk.I("act", "activation", out=hT[:, cc, 0:W], in_=tx[:, 0:W], func=AF.Identity,
             bias=C.mod[:, cc, ci:ci + 1], scale=1.0)


def bcast_sum_rstd(mk, C, srcs, W, dim, rstd, eps=EPS):
    b = mk.bank()
    for i, (s, P) in enumerate(srcs):
        sq = C.sq[i % 2]
        mk.I("act", "activation", out=sq[0:P, 0:W], in_=s, func=AF.Square)
        mk.I("pe", "matmul", out=b[:, 0:W], lhsT=C.ones[0:P, :], rhs=sq[0:P, 0:W], start=(i == 0), stop=(i == len(srcs) - 1))
    mk.I("act", "activation", out=C.tmpn[:, 0:W], in_=b[:, 0:W], func=AF.Sqrt, scale=1.0 / dim, bias=C.epsb[:, 0:1] if eps == EPS else C.zerob[:, 0:1])
    mk.I("dve", "reciprocal", out=rstd[:, 0:W], in_=C.tmpn[:, 0:W])


def gdn_unit(mk, C, G, h, d, c, k):
    NH = G.NH
    cols = slice(c * 128, (c + 1) * 128)
    gi = d * NH + h
    g_col = G.Gt[:, c, gi:gi + 1]
    beta_col = G.Bt[:, c, gi:gi + 1]
    nbeta_col = G.NBt[:, c, gi:gi + 1]
    TRI = C.tri[d]
    MS = C.ms[d]
    S = G.S[h][d]
    u = C.gu
    kT = G.KT[h][:, cols]
    qT = G.QT[h][:, cols]
    bk = mk.bank()
    mk.I("pe", "matmul", out=bk[:, 0:128], lhsT=kT, rhs=C.identb.v, start=True, stop=True)
    bv = mk.bank()
    mk.I("pe", "matmul", out=bv[:, 0:128], lhsT=G.VT[h][:, cols], rhs=C.identb.v, start=True, stop=True)
    ktm = u.ktm[k]
    vb = u.vb[k]
    mk.I("act", "activation", out=ktm.v, in_=bk[:, 0:128], func=AF.Copy)
    mk.I("dve", "tensor_scalar", out=vb.v, in0=bv[:, 0:128], scalar1=beta_col, scalar2=None, op0=ALU.mult)
    yield
    bg = mk.bank()
    mk.I("pe", "matmul", out=bg[:, 0:1], lhsT=TRI.v, rhs=g_col, start=True, stop=True)
    mk.I("pe", "matmul", out=bg[:, 128:256], lhsT=g_col.bc([128, 128]), rhs=TRI.v, start=True, stop=True)
    Gc = u.Gc[k]
    Gb = u.Gb[k]
    mk.I("dve", "tensor_copy", out=Gc[:, 0:1], in_=bg[:, 0:1])
    mk.I("act", "activation", out=Gb.v, in_=bg[:, 128:256], func=AF.Copy)
    gtot = Gb[:, 127:128] if d == 0 else Gb[:, 0:1]
    yield
    Dm, DTm = u.Dm[k], u.DTm[k]
    mk.I("dve", "tensor_scalar", out=Dm.v, in0=Gb.v, scalar1=Gc[:, 0:1], scalar2=0.0, op0=ALU.subtract, op1=ALU.max)
    mk.I("act", "activation", out=Dm.v, in_=Dm.v, func=AF.Exp, scale=-1.0)
    mk.I(PENG, "tensor_tensor", out=Dm.v, in0=Dm.v, in1=MS.v, op=ALU.mult)
    mk.I("dve", "tensor_scalar", out=DTm.v, in0=Gb.v, scalar1=Gc[:, 0:1], scalar2=0.0, op0=ALU.subtract, op1=ALU.min)
    mk.I("act", "activation", out=DTm.v, in_=DTm.v, func=AF.Exp)
    mk.I(PENG, "tensor_tensor", out=DTm.v, in0=DTm.v, in1=TRI.v, op=ALU.mult)
    yield
    bkk = mk.bank()
    kTc = u.kTc[k]
    mk.I("dve", "tensor_copy", out=kTc.v, in_=kT)
    mk.I("pe", "matmul", out=bkk[:, 0:128], lhsT=kT, rhs=kTc.v, start=True, stop=True)
    P, PT = u.P[k], u.PT[k]
    mk.I("dve", "scalar_tensor_tensor", out=P[0].v, in0=bkk[:, 0:128], scalar=nbeta_col, in1=Dm.v, op0=ALU.mult, op1=ALU.mult)
    bt = mk.bank()
    mk.I("pe", "transpose", out=bt[:, 0:128], in_=P[0].v, identity=C.ident.v)
    mk.I("act", "activation", out=PT[0].v, in_=bt[:, 0:128], func=AF.Copy)
    TT = u.TT[k]
    mk.I("dve", "tensor_tensor", out=TT.v, in0=bt[:, 0:128], in1=C.ident.v, op=ALU.add)
    yield
    cur = 0
    for lev in range(1, 7):
        nxt = 1 - cur
        b1 = mk.bank()
        mk.I("pe", "matmul", out=b1[:, 0:128], lhsT=PT[cur].v, rhs=P[cur].v, start=True, stop=True)
        if lev < 6:
            mk.I("pe", "matmul", out=b1[:, 128:256], lhsT=P[cur].v, rhs=PT[cur].v, start=True, stop=True)
        mk.I("act", "activation", out=P[nxt].v, in_=b1[:, 0:128], func=AF.Copy)
        if lev < 6:
            mk.I("dve", "tensor_copy", out=PT[nxt].v, in_=b1[:, 128:256])
        b2 = mk.bank()
        mk.I("pe", "matmul", out=b2[:, 0:128], lhsT=P[nxt].v, rhs=TT.v, start=True, stop=True)
        mk.I("dve", "tensor_tensor", out=TT.v, in0=b2[:, 0:128], in1=TT.v, op=ALU.add)
        cur = nxt
        yield
    yield
    sc = u.sc[k]
    mk.I("act", "activation", out=sc[:, 0:1], in_=Gc[:, 0:1], func=AF.Exp)
    mk.I("dve", "tensor_tensor", out=sc[:, 1:2], in0=sc[:, 0:1], in1=beta_col, op=ALU.mult)
    mk.I("act", "activation", out=sc[:, 2:3], in_=Gc[:, 0:1], func=AF.Exp, scale=-1.0, bias=gtot)
    mk.I("act", "activation", out=sc[:, 3:4], in_=gtot, func=AF.Exp)
    kbg, kdec = u.kbg[k], u.kdec[k]
    mk.I("act", "activation", out=kbg.v, in_=ktm.v, func=AF.Identity, scale=sc[:, 1:2])
    mk.I("dve", "tensor_scalar", out=kdec.v, in0=ktm.v, scalar1=sc[:, 2:3], scalar2=None, op0=ALU.mult)
    bu = mk.bank()
    mk.I("pe", "matmul", out=bu[:, 0:128], lhsT=TT.v, rhs=vb.v, start=True, stop=True)
    mk.I("pe", "matmul", out=bu[:, 128:256], lhsT=kbg.v, rhs=TT.v, start=True, stop=True)
    uu, wT = u.uu[k], u.wT[k]
    mk.I("act", "activation", out=uu.v, in_=bu[:, 0:128], func=AF.Copy)
    mk.I("dve", "tensor_copy", out=wT.v, in_=bu[:, 128:256])
    yield
    bq = mk.bank()
    mk.I("pe", "matmul", out=bq[:, 0:128], lhsT=kT, rhs=qT, start=True, stop=True)
    intraT, qgT, eGb = u.intraT[k], u.qgT[k], u.eGb[k]
    mk.I("dve", "tensor_tensor", out=intraT.v, in0=bq[:, 0:128], in1=DTm.v, op=ALU.mult)
    mk.I("act", "activation", out=eGb.v, in_=Gb.v, func=AF.Exp)
    mk.I(PENG, "tensor_tensor", out=qgT.v, in0=qT, in1=eGb.v, op=ALU.mult)
    yield
    b3 = mk.bank()
    mk.I("pe", "matmul", out=b3[:, 0:128], lhsT=wT.v, rhs=S.v, start=True, stop=True)
    vnew = u.vnew[k]
    mk.I("dve", "tensor_tensor", out=vnew.v, in0=uu.v, in1=b3[:, 0:128], op=ALU.subtract)
    b4 = mk.bank()
    mk.I("pe", "matmul", out=b4[:, 0:128], lhsT=S.v, rhs=qgT.v, start=True, stop=False)
    mk.I("pe", "matmul", out=b4[:, 0:128], lhsT=vnew.v, rhs=intraT.v, start=False, stop=True)
    mk.I("pe", "matmul", out=b4[:, 128:256], lhsT=kdec.v, rhs=vnew.v, start=True, stop=True)
    ocols = slice(c * 128 + G.pad, (c + 1) * 128 + G.pad)
    mk.I("dve", "tensor_tensor", out=G.OT[h][:, ocols], in0=G.OT[h][:, ocols], in1=b4[:, 0:128], op=ALU.add)
    mk.I("dve", "scalar_tensor_tensor", out=S.v, in0=S.v, scalar=sc[:, 3:4], in1=b4[:, 128:256], op0=ALU.mult, op1=ALU.add)
    if h == 0 and c == 0 and d == 0:
        for nm, vv in (("Gb", Gb), ("Dm", Dm), ("DTm", DTm), ("X", P[0] if False else None), ("TT", TT), ("vb", vb), ("kbg", kbg), ("kdec", kdec),
                       ("uu", uu), ("wT", wT), ("intraT", intraT), ("qgT", qgT), ("vnew", vnew), ("S", S)):
            if vv is not None:
                dump(mk, nm, vv.v, [128, 128])
        dump(mk, "sc", sc[:, 0:4], [128, 4])
        dump(mk, "Gc", Gc.v, [128, 1])


def conv_out(mk, C, G, h, comp, R, n, tok0):
    t = C.ct[C.ct_rr % 2]
    C.ct_rr += 1
    ch = h * 3 + comp
    mk.I("act", "activation", out=t[:, 0:n], in_=R[:, 1:1 + n], func=AF.Identity, scale=G.cw[:, ch, 1:2])
    mk.I("dve", "scalar_tensor_tensor", out=t[:, 0:n], in0=R[:, 0:n], scalar=G.cw[:, ch, 0:1], in1=t[:, 0:n], op0=ALU.mult, op1=ALU.add)
    mk.I("dve", "scalar_tensor_tensor", out=t[:, 0:n], in0=R[:, 2:2 + n], scalar=G.cw[:, ch, 2:3], in1=t[:, 0:n], op0=ALU.mult, op1=ALU.add)
    j0 = 1 if tok0 < 0 else 0
    if n - j0 <= 0:
        return
    dsl = slice(tok0 + j0, tok0 + n)
    if comp == 2:
        mk.I("act", "activation", out=G.VT[h][:, dsl], in_=t[:, j0:n], func=AF.Silu)
        return
    mk.I("act", "activation", out=t[:, 0:n], in_=t[:, 0:n], func=AF.Silu)
    bcast_sum_rstd(mk, C, [(t[:, 0:n], 128)], n, 1.0, C.rstd)
    dst = (G.QT if comp == 0 else G.KT)[h]
    mk.I("dve", "scalar_tensor_tensor", out=dst[:, dsl], in0=t[:, j0:n], scalar=(128.0 ** -0.5 if comp == 0 else 1.0),
         in1=C.rstd[:, j0:n], op0=ALU.mult, op1=ALU.mult)


def gdn_phase(mk, C, x_rows, T, NH, ci, gm1, Wd, s0_ap, mix_out, state_out, win=None, hcache=None):
    mark = mk.mark()
    G = Ctx()
    G.NH = NH
    C.ct = [mk.alloc([512], F32, f"ct{i}") for i in range(2)]
    C.ct_rr = 0
    C.R = [mk.alloc([516], F32, f"R{i}") for i in range(2)]
    C.R_rr = 0
    pad = G.pad = 1 if win is not None else 0
    TTs = min(512, T)
    ntile = T // TTs
    nch = T // 128
    G.QT = [mk.alloc([T], BF16, f"QT{h}") for h in range(NH)]
    G.KT = [mk.alloc([T], BF16, f"KT{h}") for h in range(NH)]
    G.VT = [mk.alloc([T], BF16, f"VT{h}") for h in range(NH)]
    G.ZT = [mk.alloc([T + 2 * pad], BF16, f"ZT{h}") for h in range(NH)]
    G.OT = [mk.alloc([T + 2 * pad], F32, f"OT{h}") for h in range(NH)]
    G.Gt = mk.alloc([nch, 2 * NH], F32, "Gt")
    G.Bt = mk.alloc([nch, 2 * NH], F32, "Bt")
    G.NBt = mk.alloc([nch, 2 * NH], F32, "NBt")
    G.cw = mk.alloc([NH * 3, 3], F32, "cw")
    G.halo = [[mk.alloc([2], F32, f"halo{h}_{c}") for c in range(3)] for h in range(NH)]
    G.S = [[mk.alloc([128], F32, f"S{h}_{d}") for d in range(2)] for h in range(NH)]
    wab = mk.alloc([16, 4 * NH], BF16, "wab")
    dtb = mk.alloc([2 * NH], F32, "dtb")
    nA = mk.alloc([2 * NH], F32, "nA")
    gng = mk.alloc([1], F32, "gng")
    sm = [mk.alloc([2 * NH], F32, f"sm{i}") for i in range(4)]
    mk.dma("sp", G.cw.v, Wd["cw"])
    mk.dma("sp", dtb.v, Wd["dtb"])
    mk.dma("sp", nA.v, Wd["alog"])
    mk.dma("sp", gng.v, Wd["gng"])
    mk.dma("pool", wab.v, Wd["w_ab"].rearrange("(kc p) n -> p kc n", p=128))
    mk.I("act", "activation", out=nA.v, in_=nA.v, func=AF.Exp)
    mk.I("dve", "tensor_scalar", out=nA.v, in0=nA.v, scalar1=-1.0, scalar2=None, op0=ALU.mult)
    for h in range(NH):
        mk.I(PENG, "memset", G.OT[h].v.ap, 0.0, writes=[G.OT[h]])
        if pad:
            mk.I(PENG, "memset", G.ZT[h].v.ap, 0.0, writes=[G.ZT[h]])
        for c in range(3):
            mk.I(PENG, "memset", G.halo[h][c].v.ap, 0.0, writes=[G.halo[h][c]])
        for d in range(2):
            mk.dma("sp", G.S[h][d].v, s0_ap[d, h])
    mark_w = mk.mark()
    hT = mk.alloc([16, TTs], BF16, "hT")
    xT = mk.alloc([16, TTs], F32, "xTg")
    for it in range(ntile):
        t0 = it * TTs
        if hcache is not None and hcache[1] == "read":
            mk.dma("sp", hT.v, hcache[0][it])
        else:
            norm_mod_tile(mk, C, x_rows[t0:t0 + TTs, :], TTs, xT, hT, C.rstd, gm1, ci)
            if hcache is not None:
                mk.dma("sp", hcache[0][it], hT.v)
        for h in range(NH):
            wt = C.wt[C.wt_rr % len(C.wt)]
            C.wt_rr += 1
            load_weight_tile(mk, wt, Wd["w_g"][:, h * 512:(h + 1) * 512], 16, 512)
            for comp in range(4):
                b = mk.bank()
                for kc in range(16):
                    mk.I("pe", "matmul", out=b[:, 0:TTs], lhsT=wt[:, kc, comp * 128:(comp + 1) * 128], rhs=hT[:, kc, 0:TTs],
                         start=(kc == 0), stop=(kc == 15))
                if comp == 3:
                    mk.I("act", "activation", out=G.ZT[h][:, pad + t0:pad + t0 + TTs], in_=b[:, 0:TTs], func=AF.Silu)
                    continue
                R = C.R[C.R_rr % 2]
                C.R_rr += 1
                halo = G.halo[h][comp]
                mk.I("dve", "tensor_copy", out=R[:, 0:2], in_=halo.v)
                mk.I("act", "activation", out=R[:, 2:2 + TTs], in_=b[:, 0:TTs], func=AF.Copy)
                mk.I("dve", "tensor_copy", out=halo.v, in_=R[:, TTs:TTs + 2])
                conv_out(mk, C, G, h, comp, R, TTs, t0 - 1)
                if it == ntile - 1:
                    R2 = C.R[C.R_rr % 2]
                    C.R_rr += 1
                    mk.I("dve", "tensor_copy", out=R2[:, 0:2], in_=halo.v)
                    mk.I("dve", "memset", R2[:, 2:3].ap, 0.0, writes=[R2])
                    conv_out(mk, C, G, h, comp, R2, 1, T - 1)
        for s in range(TTs // 128 if GSTOP['v'] >= 1 else 0):
            c = (t0 + s * 128) // 128
            b = mk.bank()
            for kc in range(16):
                mk.I("pe", "matmul", out=b[:, 0:4 * NH], lhsT=hT[:, kc, s * 128:(s + 1) * 128], rhs=wab[:, kc, :],
                     start=(kc == 0), stop=(kc == 15))
            xs, ax, ee, rr_ = sm
            mk.I("dve", "tensor_tensor", out=xs.v, in0=b[:, 0:2 * NH], in1=dtb.v, op=ALU.add)
            mk.I("dve", "tensor_scalar", out=ax.v, in0=xs.v, scalar1=-1.0, scalar2=None, op0=ALU.mult)
            mk.I("dve", "tensor_tensor", out=ax.v, in0=ax.v, in1=xs.v, op=ALU.max)
            mk.I("act", "activation", out=ee.v, in_=ax.v, func=AF.Exp, scale=-1.0)
            mk.I("act", "activation", out=ee.v, in_=ee.v, func=AF.Ln, bias=C.oneb[:, 0:1], scale=1.0)
            mk.I("dve", "tensor_scalar", out=rr_.v, in0=xs.v, scalar1=0.0, scalar2=None, op0=ALU.max)
            mk.I("dve", "tensor_tensor", out=ee.v, in0=ee.v, in1=rr_.v, op=ALU.add)
            mk.I("dve", "tensor_tensor", out=G.Gt[:, c, :], in0=ee.v, in1=nA.v, op=ALU.mult)
            mk.I("act", "activation", out=G.Bt[:, c, :], in_=b[:, 2 * NH:4 * NH], func=AF.Sigmoid)
            mk.I("dve", "tensor_scalar", out=G.NBt[:, c, :], in0=G.Bt[:, c, :], scalar1=-1.0, scalar2=None, op0=ALU.mult)
    mk.release(mark_w)
    u = C.gu = Ctx()
    for nm in ("ktm", "Gb", "Dm", "DTm", "TT", "vb", "kbg", "kdec", "uu", "wT", "intraT", "qgT", "eGb", "vnew"):
        setattr(u, nm, [mk.alloc([128], F32, f"{nm}{k}") for k in range(NPAR)])
    u.P = [[mk.alloc([128], F32, f"P{k}{i}") for i in range(2)] for k in range(NPAR)]
    u.PT = [[mk.alloc([128], F32, f"PT{k}{i}") for i in range(2)] for k in range(NPAR)]
    u.Gc = [mk.alloc([1], F32, f"Gc{k}") for k in range(NPAR)]
    u.kTc = [mk.alloc([128], BF16, f"kTc{k}") for k in range(NPAR)]
    u.sc = [mk.alloc([8], F32, f"sc{k}") for k in range(NPAR)]
    pending = [(h, d, (step if d == 0 else nch - 1 - step)) for step in range(nch) for h in range(NH) for d in range(2)]
    active = []
    free = list(range(NPAR))
    while pending or active:
        while pending and free:
            h_, d_, c_ = pending.pop(0)
            k_ = free.pop(0)
            active.append((gdn_unit(mk, C, G, h_, d_, c_, k_), k_))
        for item in list(active):
            try:
                next(item[0])
            except StopIteration:
                active.remove(item)
                free.append(item[1])
    for h in range(NH):
        if state_out is not None:
            for d in range(2):
                mk.dma("sp", state_out[d, h], G.S[h][d].v)
        if win is not None:
            sel, dst = win
            for g in range(2):
                for hf in range(2):
                    ow = C.ct[0]
                    zw = C.ct[1]
                    for jj in range(4):
                        c0 = 1024 * jj + 512 * g + 257 * hf
                        if jj == 0:
                            mk.I("dve", "tensor_scalar", out=ow[:, 0:257], in0=G.OT[h][:, c0:c0 + 257], scalar1=sel[:, 0:1], scalar2=None, op0=ALU.mult)
                            mk.I("dve", "tensor_scalar", out=zw[:, 0:257], in0=G.ZT[h][:, c0:c0 + 257], scalar1=sel[:, 0:1], scalar2=None, op0=ALU.mult)
                        else:
                            mk.I("dve", "scalar_tensor_tensor", out=ow[:, 0:257], in0=G.OT[h][:, c0:c0 + 257], scalar=sel[:, jj:jj + 1],
                                 in1=ow[:, 0:257], op0=ALU.mult, op1=ALU.add)
                            mk.I("dve", "scalar_tensor_tensor", out=zw[:, 0:257], in0=G.ZT[h][:, c0:c0 + 257], scalar=sel[:, jj:jj + 1],
                                 in1=zw[:, 0:257], op0=ALU.mult, op1=ALU.add)
                    bcast_sum_rstd(mk, C, [(ow[:, 0:257], 128)], 257, 128.0, C.rstd)
                    mk.I("dve", "scalar_tensor_tensor", out=ow[:, 0:257], in0=ow[:, 0:257], scalar=gng[:, 0:1],
                         in1=C.rstd[:, 0:257], op0=ALU.mult, op1=ALU.mult)
                    mk.I("dve", "tensor_tensor", out=ow[:, 0:257], in0=ow[:, 0:257], in1=zw[:, 0:257], op=ALU.mult)
                    mk.dma("sp", dst[g, :, 257 * hf:257 * hf + 257], ow[:, 0:257])
            continue
        for it in range(ntile):
            t0 = it * TTs
            bcast_sum_rstd(mk, C, [(G.OT[h][:, t0:t0 + TTs], 128)], TTs, 128.0, C.rstd)
            o = C.ct[C.ct_rr % 2]
            C.ct_rr += 1
            mk.I("dve", "scalar_tensor_tensor", out=o[:, 0:TTs], in0=G.OT[h][:, t0:t0 + TTs], scalar=gng[:, 0:1],
                 in1=C.rstd[:, 0:TTs], op0=ALU.mult, op1=ALU.mult)
            mk.I("dve", "tensor_tensor", out=o[:, 0:TTs], in0=o[:, 0:TTs], in1=G.ZT[h][:, t0:t0 + TTs], op=ALU.mult)
            mk.dma("sp", mix_out[h, :, t0:t0 + TTs], o[:, 0:TTs])
    mk.release(mark)


def mla_phase(mk, C, x_rows, T, NH, ci, gm1, Wd, nctx, rope, mix_out, ckv_out, kpe_out):
    mark0 = mk.mark()
    TTs = min(256, T)
    ntile = T // TTs
    NK = T + nctx
    nkt = NK // 128
    scale = 192.0 ** -0.5
    gq = mk.alloc([4], F32, "gq")
    gkv = mk.alloc([4], F32, "gkv")
    wq = mk.alloc([4, NH * 192], BF16, "wq")
    wkv = mk.alloc([4, NH * 256], BF16, "wkv")
    rot = mk.alloc([64], F32, "rot")
    onesb = mk.alloc([128], BF16, "onesb")
    kmax2 = mk.alloc([NH], F32, "kmax2")
    KPET = mk.alloc([NK], BF16, "KPET")
    KN = [mk.alloc([NK], BF16, f"KN{h}") for h in range(NH)]
    V = [mk.alloc([nkt, 128], BF16, f"V{h}") for h in range(NH)]
    mk.dma("sp", gq.v, Wd["gq"])
    mk.dma("sp", gkv.v, Wd["gkv"])
    mk.dma("pool", wq.v, Wd["wq"].rearrange("(kc p) n -> p kc n", p=128))
    mk.dma("pool", wkv.v, Wd["wkv"].rearrange("(kc p) n -> p kc n", p=128))
    mk.dma("sp", rot[0:64, :], Wd["rot"])
    mk.I("dve", "memset", onesb.v.ap, 1.0, writes=[onesb])
    mark1 = mk.mark()
    CKVT = mk.alloc([4, NK], BF16, "CKVT")
    mark2 = mk.mark()

    def work_tiles():
        W_ = Ctx()
        W_.hT = mk.alloc([16, TTs], BF16, "hTm")
        W_.xT = mk.alloc([16, TTs], F32, "xTm")
        W_.raw = mk.alloc([4, TTs], F32, "rawm")
        W_.cs = [mk.alloc([TTs], F32, f"cs{i}") for i in range(2)]
        W_.rp = [mk.alloc([TTs], F32, f"rp{i}") for i in range(3)]
        return W_

    def proj_norm(W_, wcol0, gvec, dst_fn, f32_out=None):
        wt = C.wt[C.wt_rr % len(C.wt)]
        C.wt_rr += 1
        load_weight_tile(mk, wt, Wd["w_m"][:, wcol0:wcol0 + 512], 16, 512)
        for q in range(4):
            b = mk.bank()
            for kc in range(16):
                mk.I("pe", "matmul", out=b[:, 0:TTs], lhsT=wt[:, kc, q * 128:(q + 1) * 128], rhs=W_.hT[:, kc, 0:TTs],
                     start=(kc == 0), stop=(kc == 15))
            mk.I("act", "activation", out=W_.raw[:, q, :], in_=b[:, 0:TTs], func=AF.Copy)
        bcast_sum_rstd(mk, C, [(W_.raw[:, q, :], 128) for q in range(4)], TTs, 512.0, C.rstd)
        for q in range(4):
            if f32_out is not None:
                mk.I("dve", "scalar_tensor_tensor", out=W_.raw[:, q, :], in0=W_.raw[:, q, :], scalar=gvec[:, q:q + 1],
                     in1=C.rstd[:, 0:TTs], op0=ALU.mult, op1=ALU.mult)
                mk.I("act", "activation", out=dst_fn(q), in_=W_.raw[:, q, :], func=AF.Copy)
            else:
                mk.I("dve", "scalar_tensor_tensor", out=dst_fn(q), in0=W_.raw[:, q, :], scalar=gvec[:, q:q + 1],
                     in1=C.rstd[:, 0:TTs], op0=ALU.mult, op1=ALU.mult)

    def do_rope(W_, src_bank_view, n, t0, dst):
        x = W_.rp[0]
        mk.I("act", "activation", out=x[0:64, 0:n], in_=src_bank_view, func=AF.Copy)
        if not rope:
            mk.I("dve", "tensor_copy", out=dst, in_=x[0:64, 0:n])
            return
        mk.dma("sp", W_.cs[0][0:64, 0:n], Wd["cos"][:, t0:t0 + n])
        mk.dma("sp", W_.cs[1][0:64, 0:n], Wd["sin"][:, t0:t0 + n])
        b = mk.bank()
        mk.I("pe", "matmul", out=b[0:64, 0:n], lhsT=rot[0:64, :], rhs=x[0:64, 0:n], start=True, stop=True)
        mk.I("dve", "tensor_tensor", out=W_.rp[1][0:64, 0:n], in0=x[0:64, 0:n], in1=W_.cs[0][0:64, 0:n], op=ALU.mult)
        mk.I("dve", "tensor_tensor", out=W_.rp[2][0:64, 0:n], in0=b[0:64, 0:n], in1=W_.cs[1][0:64, 0:n], op=ALU.mult)
        mk.I("dve", "tensor_tensor", out=dst, in0=W_.rp[1][0:64, 0:n], in1=W_.rp[2][0:64, 0:n], op=ALU.add)

    W_ = work_tiles()
    for it in range(ntile):
        t0 = it * TTs
        norm_mod_tile(mk, C, x_rows[t0:t0 + TTs, :], TTs, W_.xT, W_.hT, C.rstd, gm1, ci)
        proj_norm(W_, 512, gkv, lambda q: CKVT[:, q, t0:t0 + TTs], f32_out=(ckv_out is not None) or True)
        if ckv_out is not None:
            for s in range(TTs // 128):
                ys = C.xstage[s % 2]
                b = mk.bank()
                for q in range(4):
                    mk.I("pe", "transpose", out=b[:, q * 128:(q + 1) * 128], in_=W_.raw[:, q, s * 128:(s + 1) * 128], identity=C.ident.v)
                mk.I("dve", "tensor_copy", out=ys[:, 0:512], in_=b[:, 0:512])
                mk.dma("sp", ckv_out[t0 + s * 128:t0 + (s + 1) * 128, :], ys[:, 0:512])
        wt = C.wt[C.wt_rr % len(C.wt)]
        C.wt_rr += 1
        load_weight_tile(mk, wt, Wd["w_m"][:, 1024:1088], 16, 64)
        b = mk.bank()
        for kc in range(16):
            mk.I("pe", "matmul", out=b[0:64, 0:TTs], lhsT=wt[:, kc, 0:64], rhs=W_.hT[:, kc, 0:TTs], start=(kc == 0), stop=(kc == 15))
        do_rope(W_, b[0:64, 0:TTs], TTs, t0, KPET[0:64, t0:t0 + TTs])
        if kpe_out is not None:
            for s in range(TTs // 128):
                ys = C.xstage[s % 2]
                b2 = mk.bank()
                mk.I("pe", "transpose", out=b2[:, 0:64], in_=W_.rp[0][0:64, s * 128:(s + 1) * 128], identity=C.ident[0:64, 0:64])
                mk.I("dve", "tensor_copy", out=ys[:, 0:64], in_=b2[:, 0:64])
                mk.dma("sp", kpe_out[t0 + s * 128:t0 + (s + 1) * 128, :], ys[:, 0:64])
    for s in range(nctx // 128):
        xs = C.xstage[s % 2]
        mk.dma("sp", xs[:, 0:512], Wd["cache_ckv"][s * 128:(s + 1) * 128, :])
        mk.dma("sp", xs[:, 512:576], Wd["cache_kpe"][s * 128:(s + 1) * 128, :])
        b = mk.bank()
        for q in range(4):
            mk.I("pe", "transpose", out=b[:, q * 128:(q + 1) * 128], in_=xs[:, q * 128:(q + 1) * 128], identity=C.ident.v)
        mk.I("dve", "tensor_copy", out=CKVT[:, :, T + s * 128:T + (s + 1) * 128], in_=b[:, 0:512].rearrange("p (q t) -> p q t", q=4))
        b2 = mk.bank()
        mk.I("pe", "transpose", out=b2[0:64, 0:128], in_=xs[:, 512:576], identity=C.ident.v)
        mk.I("act", "activation", out=KPET[0:64, T + s * 128:T + (s + 1) * 128], in_=b2[0:64, 0:128], func=AF.Copy)
    mk.release(mark2)
    sqk = mk.alloc([512], F32, "sqk")
    kss = mk.alloc([512], F32, "kss")
    for h in range(NH):
        for k0 in range(0, NK, 512):
            n = min(512, NK - k0)
            b = mk.bank()
            for kc in range(4):
                mk.I("pe", "matmul", out=b[:, 0:n], lhsT=wkv[:, kc, h * 256:h * 256 + 128], rhs=CKVT[:, kc, k0:k0 + n],
                     start=(kc == 0), stop=(kc == 3))
            mk.I("act", "activation", out=KN[h][:, k0:k0 + n], in_=b[:, 0:n], func=AF.Copy)
            bs = mk.bank()
            mk.I("act", "activation", out=sqk[:, 0:n], in_=b[:, 0:n], func=AF.Square)
            mk.I("pe", "matmul", out=bs[0:1, 0:n], lhsT=C.ones[:, 0:1], rhs=sqk[:, 0:n], start=True, stop=False)
            mk.I("act", "activation", out=kss[0:64, 0:n], in_=KPET[0:64, k0:k0 + n], func=AF.Square)
            mk.I("pe", "matmul", out=bs[0:1, 0:n], lhsT=C.ones[0:64, 0:1], rhs=kss[0:64, 0:n], start=False, stop=True)
            if k0 == 0:
                mk.I("dve", "tensor_reduce", out=kmax2[0:1, h:h + 1], in_=bs[0:1, 0:n], axis=AX.X, op=ALU.max)
            else:
                mk.I("dve", "tensor_reduce", out=kss[0:1, 0:1], in_=bs[0:1, 0:n], axis=AX.X, op=ALU.max)
                mk.I("dve", "tensor_tensor", out=kmax2[0:1, h:h + 1], in0=kmax2[0:1, h:h + 1], in1=kss[0:1, 0:1], op=ALU.max)
        for kt in range(nkt):
            b = mk.bank()
            for kc in range(4):
                mk.I("pe", "matmul", out=b[:, 0:128], lhsT=CKVT[:, kc, kt * 128:(kt + 1) * 128], rhs=wkv[:, kc, h * 256 + 128:h * 256 + 256],
                     start=(kc == 0), stop=(kc == 3))
            mk.I("dve", "tensor_copy", out=V[h][:, kt, :], in_=b[:, 0:128])
    mk.release(mark1)
    W_ = work_tiles()
    QN = mk.alloc([4, TTs], BF16, "QN")
    qn = mk.alloc([TTs], BF16, "qn")
    qr = mk.alloc([TTs], BF16, "qr")
    negm = mk.alloc([TTs], BF16, "negm")
    mrow = mk.alloc([TTs], F32, "mrow")
    PTb = [mk.alloc([TTs], BF16, f"PT{i}") for i in range(3)]
    rs = mk.alloc([TTs], F32, "rs")
    oo = mk.alloc([TTs], F32, "oo")
    for it in range(ntile):
        t0 = it * TTs
        norm_mod_tile(mk, C, x_rows[t0:t0 + TTs, :], TTs, W_.xT, W_.hT, C.rstd, gm1, ci)
        proj_norm(W_, 0, gq, lambda q: QN[:, q, :])
        for h in range(NH):
            b = mk.bank()
            for kc in range(4):
                mk.I("pe", "matmul", out=b[:, 0:TTs], lhsT=wq[:, kc, h * 192:h * 192 + 128], rhs=QN[:, kc, :], start=(kc == 0), stop=(kc == 3))
            mk.I("act", "activation", out=qn.v, in_=b[:, 0:TTs], func=AF.Copy)
            mk.I("act", "activation", out=C.sq[0][:, 0:TTs], in_=b[:, 0:TTs], func=AF.Square)
            b2 = mk.bank()
            for kc in range(4):
                mk.I("pe", "matmul", out=b2[0:64, 0:TTs], lhsT=wq[:, kc, h * 192 + 128:h * 192 + 192], rhs=QN[:, kc, :], start=(kc == 0), stop=(kc == 3))
            mk.I("act", "activation", out=C.sq[1][0:64, 0:TTs], in_=b2[0:64, 0:TTs], func=AF.Square)
            do_rope(W_, b2[0:64, 0:TTs], TTs, t0, qr[0:64, :])
            bm = mk.bank()
            mk.I("pe", "matmul", out=bm[0:1, 0:TTs], lhsT=C.ones[:, 0:1], rhs=C.sq[0][:, 0:TTs], start=True, stop=False)
            mk.I("pe", "matmul", out=bm[0:1, 0:TTs], lhsT=C.ones[0:64, 0:1], rhs=C.sq[1][0:64, 0:TTs], start=False, stop=True)
            mk.I("act", "activation", out=mrow[0:1, :], in_=bm[0:1, 0:TTs], func=AF.Sqrt, scale=kmax2[0:1, h:h + 1])
            mk.I("dve", "tensor_scalar", out=negm[0:1, :], in0=mrow[0:1, :], scalar1=-1.0, scalar2=None, op0=ALU.mult)
            bo = mk.reserve()
            bsum = mk.reserve()
            for kt in range(nkt):
                ks = slice(kt * 128, (kt + 1) * 128)
                bs = mk.bank()
                mk.I("pe", "matmul", out=bs[:, 0:TTs], lhsT=KN[h][:, ks], rhs=qn.v, start=True, stop=False)
                mk.I("pe", "matmul", out=bs[:, 0:TTs], lhsT=KPET[0:64, ks], rhs=qr[0:64, :], start=False, stop=False)
                mk.I("pe", "matmul", out=bs[:, 0:TTs], lhsT=onesb[0:1, :], rhs=negm[0:1, :], start=False, stop=True)
                PT = PTb[kt % 3]
                mk.I("act", "activation", out=PT.v, in_=bs[:, 0:TTs], func=AF.Exp, scale=scale)
                mk.I("pe", "matmul", out=bo[:, 0:TTs], lhsT=V[h][:, kt, :], rhs=PT.v, start=(kt == 0), stop=(kt == nkt - 1))
                mk.I("pe", "matmul", out=bsum[:, 0:TTs], lhsT=onesb.v, rhs=PT.v, start=(kt == 0), stop=(kt == nkt - 1))
            mk.I("dve", "reciprocal", out=rs.v, in_=bsum[:, 0:TTs])
            mk.I("dve", "tensor_tensor", out=oo.v, in0=bo[:, 0:TTs], in1=rs.v, op=ALU.mult)
            mk.dma("sp", mix_out[h, :, t0:t0 + TTs], oo.v)
            mk.unreserve(bo)
            mk.unreserve(bsum)
    mk.release(mark0)


def alloc_common(mk, C, D):
    C.ident = mk.alloc([128], F32, "ident")
    C.identb = mk.alloc([128], BF16, "identb")
    C.ones = mk.alloc([128], F32, "ones")
    C.epsb = mk.alloc([1], F32, "epsb")
    C.oneb = mk.alloc([1], F32, "oneb")
    C.tri = [mk.alloc([128], F32, f"tri{d}") for d in range(2)]
    C.ms = [mk.alloc([128], F32, f"ms{d}") for d in range(2)]
    C.scond = mk.alloc([16, 2], BF16, "scond")
    C.condf = mk.alloc([16, 2], F32, "condf")
    C.b_ada = mk.alloc([96], F32, "b_ada")
    C.mod = mk.alloc([96, 2], F32, "mod")
    C.xstage = [mk.alloc([2048], F32, f"xs{i}") for i in range(2)]
    C.sq = [mk.alloc([514], F32, f"sq{i}") for i in range(2)]
    C.sqb = [mk.alloc([514], BF16, f"sqb{i}") for i in range(4)]
    C.sqb_rr = 0
    C.onesb16 = mk.alloc([128], BF16, "onesb16")
    C.tmpn = mk.alloc([514], F32, "tmpn")
    C.rstd = mk.alloc([514], F32, "rstd")
    C.wt = [mk.alloc([16, 512], BF16, f"wt{i}") for i in range(2)]
    C.wt_rr = 0
    C.tmpx = [mk.alloc([514], F32, f"tmpx{i}") for i in range(2)]
    mk.dma("sp", C.ident.v, D["ident"])
    mk.dma("sp", C.tri[0].v, D["tri0"])
    mk.dma("sp", C.tri[1].v, D["tri1"])
    mk.dma("sp", C.ms[0].v, D["ms0"])
    mk.dma("sp", C.ms[1].v, D["ms1"])
    mk.I("dve", "tensor_copy", out=C.identb.v, in_=C.ident.v)
    mk.I("dve", "memset", C.ones.v.ap, 1.0, writes=[C.ones])
    mk.I("dve", "memset", C.onesb16.v.ap, 1.0, writes=[C.onesb16])
    mk.I("dve", "memset", C.epsb.v.ap, EPS, writes=[C.epsb])
    mk.I("dve", "memset", C.oneb.v.ap, 1.0, writes=[C.oneb])
    mk.dma("sp", C.condf.v, D["condT"])
    mk.dma("sp", C.b_ada.v, D["b_adaT"])
    mk.I("act", "activation", out=C.scond.v, in_=C.condf.v, func=AF.Silu)


def build_phase1(parts=('pg', 'pm', 'sg', 'sm')):
    nc = bass.Bass("TRN2", target_bir_lowering=False)
    DEBUG["nc"] = nc
    DEBUG["done"] = set()
    D = {}

    def din(name, shape):
        D[name] = nc.dram_tensor(name, list(shape), F32, kind="ExternalInput").ap()

    def dout(name, shape):
        D[name] = nc.dram_tensor(name, list(shape), F32, kind="ExternalOutput").ap()

    for name, shape in (("xp", [2, 256, 2048]), ("xs", [4096, 2048]), ("condT", [128, 16, 2]), ("w_ada", [2048, 12288]),
                        ("b_adaT", [128, 96]), ("norm1T", [128, 16]), ("ident", [128, 128]), ("tri0", [128, 128]),
                        ("tri1", [128, 128]), ("ms0", [128, 128]), ("ms1", [128, 128]), ("s0p", [2, 8, 128, 128]),
                        ("s0s", [2, 2, 1, 128, 128]), ("w_g_p", [2048, 4096]), ("w_ab_p", [2048, 32]), ("cw_p", [128, 24, 3]),
                        ("dtb_p", [128, 16]), ("alog_p", [128, 16]), ("gng", [128, 1]), ("w_g_s", [2, 2048, 512]),
                        ("w_ab_s", [2, 2048, 4]), ("cw_s", [2, 128, 3, 3]), ("dtb_s", [2, 128, 2]), ("alog_s", [2, 128, 2]),
                        ("w_m", [2048, 1088]), ("gq", [128, 4]), ("gkv", [128, 4]), ("wq_p", [512, 1536]), ("wkv_p", [512, 2048]),
                        ("wq_s", [512, 384]), ("wkv_s", [512, 512]), ("cos", [64, 4096]), ("sin", [64, 4096]), ("rot", [64, 64]),
                        ("cache_ckv", [256, 512]), ("cache_kpe", [256, 64])):
        din(name, shape)
    for name, shape in (("mixp", [2, 16, 128, 256]), ("mixs", [4, 128, 4096]), ("new_state", [2, 2, 8, 128, 128]),
                        ("new_ckv", [2, 256, 512]), ("new_kpe", [2, 256, 64])):
        dout(name, shape)
    with ExitStack() as st:
        mk = MK(nc, st)
        C = Ctx()
        alloc_common(mk, C, D)
        n1g = mk.alloc([16], F32, "n1g")
        gm1 = mk.alloc([16, 2], F32, "gm1")
        mk.dma("sp", n1g.v, D["norm1T"])
        adaln(mk, C, D["w_ada"], [0, 1], 2)
        mk.I("dve", "tensor_scalar", out=gm1.v, in0=C.mod[:, 16:32, :], scalar1=1.0, scalar2=None, op0=ALU.add)
        mk.I("dve", "tensor_tensor", out=gm1.v, in0=gm1.v,
             in1=n1g.v.rearrange("p (c o) -> p c o", o=1).bc([128, 16, 2]), op=ALU.mult)
        Wp = dict(w_g=D["w_g_p"], w_ab=D["w_ab_p"], cw=D["cw_p"], dtb=D["dtb_p"], alog=D["alog_p"], gng=D["gng"])
        Wmp = dict(w_m=D["w_m"], gq=D["gq"], gkv=D["gkv"], wq=D["wq_p"], wkv=D["wkv_p"], rot=D["rot"])
        for s in range(2):
            if 'pg' in parts:
                gdn_phase(mk, C, D["xp"][s], 256, 8, 0, gm1, Wp, D["s0p"], D["mixp"][s, 0:8], D["new_state"][s])
            if 'pm' in parts:
                mla_phase(mk, C, D["xp"][s], 256, 8, 0, gm1, Wmp, 0, False, D["mixp"][s, 8:16], D["new_ckv"][s], D["new_kpe"][s])
        for lh in range(2):
            Ws = dict(w_g=D["w_g_s"][lh], w_ab=D["w_ab_s"][lh], cw=D["cw_s"][lh], dtb=D["dtb_s"][lh], alog=D["alog_s"][lh], gng=D["gng"])
            if 'sg' in parts:
                gdn_phase(mk, C, D["xs"], 4096, 1, 1, gm1, Ws, D["s0s"][lh], D["mixs"][lh:lh + 1], None)
        Wms = dict(w_m=D["w_m"], gq=D["gq"], gkv=D["gkv"], wq=D["wq_s"], wkv=D["wkv_s"], rot=D["rot"], cos=D["cos"], sin=D["sin"],
                   cache_ckv=D["cache_ckv"], cache_kpe=D["cache_kpe"])
        if 'sm' in parts:
            mla_phase(mk, C, D["xs"], 4096, 2, 1, gm1, Wms, 256, True, D["mixs"][2:4], None, None)
        mk.finalize()
        print("phase1 instructions:", mk.n_inst, {e: len(mk.ops[e]) for e in ENGS})
    return nc


def rope_tables():
    rows = 4096 // 64
    row = np.repeat(np.arange(rows, dtype=np.float32), 64)
    col = np.tile(np.arange(64, dtype=np.float32), rows)
    inv = (np.float32(10000.0) ** (-np.arange(16, dtype=np.float32) / np.float32(16))).astype(np.float32)
    ang = np.concatenate([row[:, None] * inv, col[:, None] * inv], axis=-1).astype(np.float32)
    cos, sin = np.cos(ang).astype(np.float32), np.sin(ang).astype(np.float32)
    cos2 = np.ascontiguousarray(np.concatenate([cos, cos], axis=1).T)
    sin2 = np.ascontiguousarray(np.concatenate([sin, sin], axis=1).T)
    rot = np.zeros((64, 64), np.float32)
    for m in range(32):
        rot[m + 32, m] = -1.0
        rot[m, m + 32] = 1.0
    return cos2, sin2, rot


def phase1_inputs(core, I):
    b, j = core // 4, core % 4
    w_in = I["w_in"][0]
    cw = I["gdn_conv_w"][0]
    dtb = I["gdn_dt_bias"][0]
    alog = I["gdn_a_log"][0]

    def wg(h):
        return np.concatenate([w_in[:, c0 + h * 128:c0 + (h + 1) * 128] for c0 in (0, 1024, 2048, 3072)], axis=1)

    def cwh(h):
        return np.stack([cw[:, comp * 1024 + h * 128:comp * 1024 + (h + 1) * 128].T for comp in range(3)], axis=1)

    cos2, sin2, rot = rope_tables()
    tri0 = np.triu(np.ones((128, 128), np.float32))
    tri1 = np.tril(np.ones((128, 128), np.float32))
    ms0 = np.tril(np.ones((128, 128), np.float32), -1)
    ms1 = np.triu(np.ones((128, 128), np.float32), 1)
    cond = np.stack([I["c_ctx"], I["c"][b]], axis=0)
    hg = [2 * j, 2 * j + 1]
    wq = I["mla_w_q_b"][0]
    wkv = I["mla_w_kv_b"][0]
    d = dict(
        xp=np.ascontiguousarray(I["x_prompt"][2 * core:2 * core + 2]), xs=np.ascontiguousarray(I["x_sample"][b]),
        condT=np.ascontiguousarray(cond.reshape(2, 16, 128).transpose(2, 1, 0)), w_ada=I["w_ada"][0],
        b_adaT=colT(I["b_ada"][0], 96), norm1T=colT(I["norm1_g"][0], 16), ident=np.eye(128, dtype=np.float32),
        tri0=tri0, tri1=tri1, ms0=ms0, ms1=ms1, s0p=np.zeros((2, 8, 128, 128), np.float32),
        s0s=np.ascontiguousarray(np.stack([I["state_gdn"][b, 0, :, h:h + 1] for h in hg], axis=0)),
        w_g_p=np.concatenate([wg(h) for h in range(8)], axis=1), w_ab_p=np.ascontiguousarray(w_in[:, 4096:4128]),
        cw_p=np.ascontiguousarray(np.concatenate([cwh(h) for h in range(8)], axis=1)),
        dtb_p=np.ascontiguousarray(np.broadcast_to(dtb.reshape(1, 16), (128, 16))),
        alog_p=np.ascontiguousarray(np.broadcast_to(alog.reshape(1, 16), (128, 16))),
        gng=np.ascontiguousarray(I["gdn_norm_g"][0].reshape(128, 1)),
        w_g_s=np.stack([wg(h) for h in hg], axis=0),
        w_ab_s=np.stack([w_in[:, [4096 + h, 4104 + h, 4112 + h, 4120 + h]] for h in hg], axis=0),
        cw_s=np.stack([cwh(h) for h in hg], axis=0),
        dtb_s=np.stack([np.broadcast_to(dtb[:, h].reshape(1, 2), (128, 2)) for h in hg], axis=0),
        alog_s=np.stack([np.broadcast_to(alog[:, h].reshape(1, 2), (128, 2)) for h in hg], axis=0),
        w_m=np.ascontiguousarray(w_in[:, 4128:5216]), gq=colT(I["mla_q_norm_g"][0], 4), gkv=colT(I["mla_kv_norm_g"][0], 4),
        wq_p=wq, wkv_p=wkv, wq_s=np.ascontiguousarray(wq[:, hg[0] * 192:(hg[1] + 1) * 192]),
        wkv_s=np.ascontiguousarray(wkv[:, hg[0] * 256:(hg[1] + 1) * 256]), cos=cos2, sin=sin2, rot=rot,
        cache_ckv=np.ascontiguousarray(I["cache_mla_ckv"][b, 0]), cache_kpe=np.ascontiguousarray(I["cache_mla_kpe"][b, 0]))
    return {k: np.ascontiguousarray(np.asarray(v, np.float32)) for k, v in d.items()}


_NC = {}


def kernel_twolaunch(**I):
    I = {k: np.asarray(v) for k, v in I.items()}
    if "p1" not in _NC:
        _NC["p1"] = build_phase1()
        _NC["p2"] = build_phase2()
    r1 = run_bass_kernel_spmd(_NC["p1"], [phase1_inputs(c, I) for c in range(NCORES)], core_ids=list(range(NCORES))).results
    mix_p = np.zeros((16, 256, 2048), np.float32)
    mix_s = np.zeros((2, 4096, 2048), np.float32)
    new_state = np.zeros((16, 1, 2, 8, 128, 128), np.float32)
    new_ckv = np.zeros((16, 1, 256, 512), np.float32)
    new_kpe = np.zeros((16, 1, 256, 64), np.float32)
    for c in range(NCORES):
        b, j = c // 4, c % 4
        r = r1[c]
        for s in range(2):
            mix_p[2 * c + s] = r["mixp"][s].transpose(2, 0, 1).reshape(256, 2048)
            new_state[2 * c + s, 0] = r["new_state"][s]
            new_ckv[2 * c + s, 0] = r["new_ckv"][s]
            new_kpe[2 * c + s, 0] = r["new_kpe"][s]
        for lh in range(2):
            hgl = 2 * j + lh
            mix_s[b, :, hgl * 128:(hgl + 1) * 128] = r["mixs"][lh].T
            mix_s[b, :, 1024 + hgl * 128:1024 + (hgl + 1) * 128] = r["mixs"][2 + lh].T
    r2 = run_bass_kernel_spmd(_NC["p2"], [phase2_inputs(c, I["x_prompt"], I["x_sample"], mix_p, mix_s, I["c"], I["c_ctx"],
                                                        I["w_ada"][0], I["b_ada"][0], I["w_out"][0], I["norm2_g"][0],
                                                        I["w_up"][0], I["ffn_conv_w"][0], I["ffn_conv_b"][0], I["w_down"][0],
                                                        I["final_norm_g"]) for c in range(NCORES)],
                              core_ids=list(range(NCORES))).results
    yp = np.zeros((16, 256, 2048), np.float32)
    ys = np.zeros((2, 4096, 2048), np.float32)
    for c in range(NCORES):
        b, j = c // 4, c % 4
        y = r2[c]["y"]
        yp[2 * c:2 * c + 2] = y[0].reshape(2, 256, 2048)
        ys[b, 1024 * j:1024 * j + 512] = y[1]
        ys[b, 1024 * j + 512:1024 * j + 1024] = y[2]
    return (yp, ys, new_state, new_ckv, new_kpe)


def mla_fused(mk, C, x_rows, T, ci, gm1, Wd, nctx, xq, dst, hT_d=None):
    NH = 8
    mark0 = mk.mark()
    TTs = 256
    TQ = 257
    ntile = T // TTs
    NK = T + nctx
    nkt = NK // 128
    scale = 192.0 ** -0.5
    gq = mk.alloc([4], F32, "gq")
    gkv = mk.alloc([4], F32, "gkv")
    wq = mk.alloc([4, NH * 192], BF16, "wq")
    wkv = mk.alloc([4, NH * 256], BF16, "wkv")
    rot = mk.alloc([64], F32, "rot")
    onesb = mk.alloc([128], BF16, "onesb")
    kmax2 = mk.alloc([NH], F32, "kmax2")
    KPET = mk.alloc([NK], BF16, "KPET")
    QN = mk.alloc([4, 4 * TQ], BF16, "QNall")
    mk.dma("sp", gq.v, Wd["gq"])
    mk.dma("sp", gkv.v, Wd["gkv"])
    mk.dma("pool", wq.v, Wd["wq"].rearrange("(kc p) n -> p kc n", p=128))
    mk.dma("pool", wkv.v, Wd["wkv"].rearrange("(kc p) n -> p kc n", p=128))
    mk.dma("sp", rot[0:64, :], Wd["rot"])
    mk.I("dve", "memset", onesb.v.ap, 1.0, writes=[onesb])
    CKVT = mk.alloc([4, NK], BF16, "CKVT")
    mark2 = mk.mark()
    hT = mk.alloc([16, TQ], BF16, "hTm")
    xT = mk.alloc([16, TQ], F32, "xTm")
    raw = mk.alloc([4, TQ], F32, "rawm")
    cs = [mk.alloc([TQ], F32, f"cs{i}") for i in range(2)]
    rp = [mk.alloc([TQ], F32, f"rp{i}") for i in range(3)]

    def proj_norm(W, wcol0, gvec, dst_fn):
        wt = C.wt[C.wt_rr % len(C.wt)]
        C.wt_rr += 1
        load_weight_tile(mk, wt, Wd["w_m"][:, wcol0:wcol0 + 512], 16, 512)
        for q in range(4):
            b = mk.bank()
            for kc in range(16):
                mk.I("pe", "matmul", out=b[:, 0:W], lhsT=wt[:, kc, q * 128:(q + 1) * 128], rhs=hT[:, kc, 0:W], start=(kc == 0), stop=(kc == 15))
            mk.I("act", "activation", out=raw[:, q, 0:W], in_=b[:, 0:W], func=AF.Copy)
        bcast_sum_rstd(mk, C, [(raw[:, q, 0:W], 128) for q in range(4)], W, 512.0, C.rstd)
        for q in range(4):
            mk.I("dve", "scalar_tensor_tensor", out=dst_fn(q), in0=raw[:, q, 0:W], scalar=gvec[:, q:q + 1], in1=C.rstd[:, 0:W], op0=ALU.mult, op1=ALU.mult)

    def do_rope(src, n, cos_ap, sin_ap, dstv):
        x = rp[0]
        mk.I("act", "activation", out=x[0:64, 0:n], in_=src, func=AF.Copy)
        mk.dma("sp", cs[0][0:64, 0:n], cos_ap)
        mk.dma("sp", cs[1][0:64, 0:n], sin_ap)
        b = mk.bank()
        mk.I("pe", "matmul", out=b[0:64, 0:n], lhsT=rot[0:64, :], rhs=x[0:64, 0:n], start=True, stop=True)
        mk.I("dve", "tensor_tensor", out=rp[1][0:64, 0:n], in0=x[0:64, 0:n], in1=cs[0][0:64, 0:n], op=ALU.mult)
        mk.I("dve", "tensor_tensor", out=rp[2][0:64, 0:n], in0=b[0:64, 0:n], in1=cs[1][0:64, 0:n], op=ALU.mult)
        mk.I("dve", "tensor_tensor", out=dstv, in0=rp[1][0:64, 0:n], in1=rp[2][0:64, 0:n], op=ALU.add)

    for it in range(ntile):
        t0 = it * TTs
        if hT_d is not None:
            mk.dma("sp", hT[:, :, 0:TTs], hT_d[it // 2][:, :, (it % 2) * TTs:(it % 2 + 1) * TTs])
        else:
            norm_mod_tile(mk, C, x_rows[t0:t0 + TTs, :], TTs, xT, hT, C.rstd, gm1, ci)
        proj_norm(TTs, 512, gkv, lambda q: CKVT[:, q, t0:t0 + TTs])
        wt = C.wt[C.wt_rr % len(C.wt)]
        C.wt_rr += 1
        load_weight_tile(mk, wt, Wd["w_m"][:, 1024:1088], 16, 64)
        b = mk.bank()
        for kc in range(16):
            mk.I("pe", "matmul", out=b[0:64, 0:TTs], lhsT=wt[:, kc, 0:64], rhs=hT[:, kc, 0:TTs], start=(kc == 0), stop=(kc == 15))
        do_rope(b[0:64, 0:TTs], TTs, Wd["cos"][:, t0:t0 + TTs], Wd["sin"][:, t0:t0 + TTs], KPET[0:64, t0:t0 + TTs])
    for s in range(nctx // 128):
        xs = C.xstage[s % 2]
        mk.dma("sp", xs[:, 0:512], Wd["cache_ckv"][s * 128:(s + 1) * 128, :])
        mk.dma("sp", xs[:, 512:576], Wd["cache_kpe"][s * 128:(s + 1) * 128, :])
        b = mk.bank()
        for q in range(4):
            mk.I("pe", "transpose", out=b[:, q * 128:(q + 1) * 128], in_=xs[:, q * 128:(q + 1) * 128], identity=C.ident.v)
        mk.I("dve", "tensor_copy", out=CKVT[:, :, T + s * 128:T + (s + 1) * 128], in_=b[:, 0:512].rearrange("p (q t) -> p q t", q=4))
        b2 = mk.bank()
        mk.I("pe", "transpose", out=b2[0:64, 0:128], in_=xs[:, 512:576], identity=C.ident.v)
        mk.I("act", "activation", out=KPET[0:64, T + s * 128:T + (s + 1) * 128], in_=b2[0:64, 0:128], func=AF.Copy)
    for slot in range(4):
        g, hf = slot // 2, slot % 2
        norm_mod_tile(mk, C, xq[g, hf * TQ:(hf + 1) * TQ, :], TQ, xT, hT, C.rstd, gm1, ci)
        proj_norm(TQ, 0, gq, lambda q: QN[:, q, slot * TQ:(slot + 1) * TQ])
    mk.release(mark2)
    for h in range(NH):
        markh = mk.mark()
        KN = mk.alloc([NK], BF16, "KNh")
        V = mk.alloc([nkt, 128], BF16, "Vh")
        sqk = mk.alloc([512], F32, "sqk")
        kss = mk.alloc([512], F32, "kss")
        qn = mk.alloc([TQ], BF16, "qn")
        qr = mk.alloc([TQ], BF16, "qr")
        negm = mk.alloc([TQ], BF16, "negm")
        mrow = mk.alloc([TQ], F32, "mrow")
        PTb = [mk.alloc([TQ], BF16, f"PT{i}") for i in range(3)]
        rs = mk.alloc([TQ], F32, "rs")
        oo = mk.alloc([TQ], F32, "oo")
        cs = [mk.alloc([TQ], F32, f"csq{i}") for i in range(2)]
        rp = [mk.alloc([TQ], F32, f"rpq{i}") for i in range(3)]
        for k0 in range(0, NK, 512):
            n = min(512, NK - k0)
            b = mk.bank()
            for kc in range(4):
                mk.I("pe", "matmul", out=b[:, 0:n], lhsT=wkv[:, kc, h * 256:h * 256 + 128], rhs=CKVT[:, kc, k0:k0 + n], start=(kc == 0), stop=(kc == 3))
            mk.I("act", "activation", out=KN[:, k0:k0 + n], in_=b[:, 0:n], func=AF.Copy)
            bs = mk.bank()
            mk.I("act", "activation", out=sqk[:, 0:n], in_=b[:, 0:n], func=AF.Square)
            mk.I("pe", "matmul", out=bs[0:1, 0:n], lhsT=C.ones[:, 0:1], rhs=sqk[:, 0:n], start=True, stop=False)
            mk.I("act", "activation", out=kss[0:64, 0:n], in_=KPET[0:64, k0:k0 + n], func=AF.Square)
            mk.I("pe", "matmul", out=bs[0:1, 0:n], lhsT=C.ones[0:64, 0:1], rhs=kss[0:64, 0:n], start=False, stop=True)
            if k0 == 0:
                mk.I("dve", "tensor_reduce", out=kmax2[0:1, h:h + 1], in_=bs[0:1, 0:n], axis=AX.X, op=ALU.max)
            else:
                mk.I("dve", "tensor_reduce", out=kss[0:1, 0:1], in_=bs[0:1, 0:n], axis=AX.X, op=ALU.max)
                mk.I("dve", "tensor_tensor", out=kmax2[0:1, h:h + 1], in0=kmax2[0:1, h:h + 1], in1=kss[0:1, 0:1], op=ALU.max)
        for kt in range(nkt):
            b = mk.bank()
            for kc in range(4):
                mk.I("pe", "matmul", out=b[:, 0:128], lhsT=CKVT[:, kc, kt * 128:(kt + 1) * 128], rhs=wkv[:, kc, h * 256 + 128:h * 256 + 256], start=(kc == 0), stop=(kc == 3))
            mk.I("dve", "tensor_copy", out=V[:, kt, :], in_=b[:, 0:128])
        for slot in range(4):
            g, hf = slot // 2, slot % 2
            qs = slice(slot * TQ, (slot + 1) * TQ)
            b = mk.bank()
            for kc in range(4):
                mk.I("pe", "matmul", out=b[:, 0:TQ], lhsT=wq[:, kc, h * 192:h * 192 + 128], rhs=QN[:, kc, qs], start=(kc == 0), stop=(kc == 3))
            mk.I("act", "activation", out=qn.v, in_=b[:, 0:TQ], func=AF.Copy)
            mk.I("act", "activation", out=C.sq[0][:, 0:TQ], in_=b[:, 0:TQ], func=AF.Square)
            b2 = mk.bank()
            for kc in range(4):
                mk.I("pe", "matmul", out=b2[0:64, 0:TQ], lhsT=wq[:, kc, h * 192 + 128:h * 192 + 192], rhs=QN[:, kc, qs], start=(kc == 0), stop=(kc == 3))
            mk.I("act", "activation", out=C.sq[1][0:64, 0:TQ], in_=b2[0:64, 0:TQ], func=AF.Square)
            do_rope(b2[0:64, 0:TQ], TQ, Wd["cosq"][:, qs], Wd["sinq"][:, qs], qr[0:64, :])
            bm = mk.bank()
            mk.I("pe", "matmul", out=bm[0:1, 0:TQ], lhsT=C.ones[:, 0:1], rhs=C.sq[0][:, 0:TQ], start=True, stop=False)
            mk.I("pe", "matmul", out=bm[0:1, 0:TQ], lhsT=C.ones[0:64, 0:1], rhs=C.sq[1][0:64, 0:TQ], start=False, stop=True)
            mk.I("act", "activation", out=mrow[0:1, :], in_=bm[0:1, 0:TQ], func=AF.Sqrt, scale=kmax2[0:1, h:h + 1])
            mk.I("dve", "tensor_scalar", out=negm[0:1, :], in0=mrow[0:1, :], scalar1=-1.0, scalar2=None, op0=ALU.mult)
            bo = mk.reserve()
            bsum = mk.reserve()
            for kt in range(nkt):
                ks = slice(kt * 128, (kt + 1) * 128)
                bs = mk.bank()
                mk.I("pe", "matmul", out=bs[:, 0:TQ], lhsT=KN[:, ks], rhs=qn.v, start=True, stop=False)
                mk.I("pe", "matmul", out=bs[:, 0:TQ], lhsT=KPET[0:64, ks], rhs=qr[0:64, :], start=False, stop=False)
                mk.I("pe", "matmul", out=bs[:, 0:TQ], lhsT=onesb[0:1, :], rhs=negm[0:1, :], start=False, stop=True)
                PT = PTb[kt % 3]
                mk.I("act", "activation", out=PT.v, in_=bs[:, 0:TQ], func=AF.Exp, scale=scale)
                mk.I("pe", "matmul", out=bo[:, 0:TQ], lhsT=V[:, kt, :], rhs=PT.v, start=(kt == 0), stop=(kt == nkt - 1))
                mk.I("pe", "matmul", out=bsum[:, 0:TQ], lhsT=onesb.v, rhs=PT.v, start=(kt == 0), stop=(kt == nkt - 1))
            mk.I("dve", "reciprocal", out=rs.v, in_=bsum[:, 0:TQ])
            mk.I("dve", "tensor_tensor", out=oo.v, in0=bo[:, 0:TQ], in1=rs.v, op=ALU.mult)
            mk.dma("sp", dst[g, h, :, hf * TQ:(hf + 1) * TQ], oo.v)
            mk.unreserve(bo)
            mk.unreserve(bsum)
        mk.release(markh)
    mk.release(mark0)


def build_fused():
    nc = bass.Bass("TRN2", target_bir_lowering=False)
    DEBUG["nc"] = nc
    DEBUG["done"] = set()
    D = {}

    def din(name, shape):
        D[name] = nc.dram_tensor(name, list(shape), F32, kind="ExternalInput").ap()

    def dout(name, shape):
        D[name] = nc.dram_tensor(name, list(shape), F32, kind="ExternalOutput").ap()

    for name, shape in (("xp", [2, 256, 2048]), ("xs", [4096, 2048]), ("x2", [3, 514, 2048]), ("condT", [128, 16, 2]),
                        ("w_ada", [2048, 12288]), ("b_adaT", [128, 96]), ("norm1T", [128, 16]), ("ident", [128, 128]),
                        ("tri0", [128, 128]), ("tri1", [128, 128]), ("ms0", [128, 128]), ("ms1", [128, 128]),
                        ("s0p", [2, 8, 128, 128]), ("s0h", [8, 2, 1, 128, 128]), ("w_g_p", [2048, 4096]), ("w_ab_p", [2048, 32]),
                        ("cw_p", [128, 24, 3]), ("dtb_p", [128, 16]), ("alog_p", [128, 16]), ("gng", [128, 1]),
                        ("w_ab_h", [8, 2048, 4]), ("dtb_h", [8, 128, 2]), ("alog_h", [8, 128, 2]),
                        ("w_m", [2048, 1088]), ("gq", [128, 4]), ("gkv", [128, 4]), ("wq_p", [512, 1536]), ("wkv_p", [512, 2048]),
                        ("cos", [64, 4096]), ("sin", [64, 4096]), ("cosq", [64, 1028]), ("sinq", [64, 1028]), ("rot", [64, 64]),
                        ("cache_ckv", [256, 512]), ("cache_kpe", [256, 64]), ("sel", [128, 4]), ("hmask", [128, 4]),
                        ("w_out", [2048, 2048]), ("norm2T", [128, 16]), ("w_up", [2048, 11264]), ("fcwT", [128, 88, 3]),
                        ("fcbT", [128, 88]), ("w_down", [5632, 2048]), ("fnormT", [128, 16])):
        din(name, shape)
    for name, shape in (("y", [3, 512, 2048]), ("new_state", [2, 2, 8, 128, 128]), ("new_ckv", [2, 256, 512]), ("new_kpe", [2, 256, 64])):
        dout(name, shape)
    mixp_d = nc.dram_tensor("mixp_d", [2, 16, 128, 256], F32).ap()
    mixs_d = nc.dram_tensor("mixs_d", [2, 16, 128, 514], F32).ap()
    hT_d = nc.dram_tensor("hT_d", [8, 128, 16, 512], BF16).ap()
    with ExitStack() as st:
        mk = MK(nc, st)
        C = Ctx()
        alloc_common(mk, C, D)
        n1g = mk.alloc([16], F32, "n1g")
        gm1 = mk.alloc([16, 2], F32, "gm1")
        gm2 = mk.alloc([16, 2], F32, "gm2")
        n2g = mk.alloc([16], F32, "n2g")
        fng = mk.alloc([16], F32, "fng")
        fcw = mk.alloc([88, 3], F32, "fcw")
        fcb = mk.alloc([88], F32, "fcb")
        hm = mk.alloc([4], F32, "hm")
        sel = mk.alloc([4], F32, "sel")
        for t, nm in ((n1g, "norm1T"), (n2g, "norm2T"), (fng, "fnormT"), (fcw, "fcwT"), (fcb, "fcbT"), (hm, "hmask"), (sel, "sel")):
            mk.dma("sp", t.v, D[nm])
        adaln(mk, C, D["w_ada"], [0, 1, 2, 3, 4, 5], 2)
        for (gm, ng, lo) in ((gm1, n1g, 16), (gm2, n2g, 64)):
            mk.I("dve", "tensor_scalar", out=gm.v, in0=C.mod[:, lo:lo + 16, :], scalar1=1.0, scalar2=None, op0=ALU.add)
            mk.I("dve", "tensor_tensor", out=gm.v, in0=gm.v, in1=ng.v.rearrange("p (c o) -> p c o", o=1).bc([128, 16, 2]), op=ALU.mult)
        Wp = dict(w_g=D["w_g_p"], w_ab=D["w_ab_p"], cw=D["cw_p"], dtb=D["dtb_p"], alog=D["alog_p"], gng=D["gng"])
        Wmp = dict(w_m=D["w_m"], gq=D["gq"], gkv=D["gkv"], wq=D["wq_p"], wkv=D["wkv_p"], rot=D["rot"])
        for s in range(2):
            gdn_phase(mk, C, D["xp"][s], 256, 8, 0, gm1, Wp, D["s0p"], mixp_d[s, 0:8], D["new_state"][s])
            mla_phase(mk, C, D["xp"][s], 256, 8, 0, gm1, Wmp, 0, False, mixp_d[s, 8:16], D["new_ckv"][s], D["new_kpe"][s])
        for h in range(8):
            Ws = dict(w_g=D["w_g_p"][:, h * 512:(h + 1) * 512], w_ab=D["w_ab_h"][h], cw=D["cw_p"][:, 3 * h:3 * h + 3, :],
                      dtb=D["dtb_h"][h], alog=D["alog_h"][h], gng=D["gng"])
            gdn_phase(mk, C, D["xs"], 4096, 1, 1, gm1, Ws, D["s0h"][h], None, None, win=(sel, mixs_d[:, h]),
                      hcache=(hT_d, "write" if h == 0 else "read"))
        Wms = dict(w_m=D["w_m"], gq=D["gq"], gkv=D["gkv"], wq=D["wq_p"], wkv=D["wkv_p"], rot=D["rot"], cos=D["cos"], sin=D["sin"],
                   cosq=D["cosq"], sinq=D["sinq"], cache_ckv=D["cache_ckv"], cache_kpe=D["cache_kpe"])
        mla_fused(mk, C, D["xs"], 4096, 1, gm1, Wms, 256, D["x2"][1:3], mixs_d[:, 8:16], hT_d)
        mk.barrier()
        C.wt.append(mk.alloc([16, 512], BF16, "wt2"))
        xT = mk.alloc([16, 514], F32, "xT")
        actin = mk.alloc([16, 514], BF16, "actin")
        actT = mk.alloc([44, 512], BF16, "actT")
        Ra = [mk.alloc([514], F32, f"Ra{i}") for i in range(2)]
        Rg = [mk.alloc([514], F32, f"Rg{i}") for i in range(2)]
        ta = [mk.alloc([512], F32, f"ta{i}") for i in range(2)]
        tg = [mk.alloc([512], F32, f"tg{i}") for i in range(2)]

        def load_mix(ti, actin_, W):
            if ti == 0:
                for s in range(2):
                    mk.dma("pool", actin_[:, :, s * 256:(s + 1) * 256], mixp_d[s].rearrange("c p w -> p c w"))
            else:
                mk.dma("pool", actin_[:, :, 0:W], mixs_d[ti - 1].rearrange("c p w -> p c w"))

        phase2_tiles(mk, C, D["x2"], D["y"], load_mix, n2g, fng, fcw, fcb, hm, gm2, xT, actin, actT, Ra, Rg, ta, tg, C.tmpx, C.rstd,
                     D["w_out"], D["w_up"], D["w_down"])
        mk.finalize()
        print("fused instructions:", mk.n_inst, {e: len(mk.ops[e]) for e in ENGS}, "sbuf words", mk.top)
    return nc


def fused_inputs(core, I):
    b, j = core // 4, core % 4
    d = phase1_inputs(core, I)
    for k in ("s0s", "w_g_s", "w_ab_s", "cw_s", "dtb_s", "alog_s", "wq_s", "wkv_s"):
        d.pop(k)
    p2 = phase2_inputs(core, I["x_prompt"], I["x_sample"], np.zeros((16, 256, 2048), np.float32), np.zeros((2, 4096, 2048), np.float32),
                       I["c"], I["c_ctx"], I["w_ada"][0], I["b_ada"][0], I["w_out"][0], I["norm2_g"][0], I["w_up"][0],
                       I["ffn_conv_w"][0], I["ffn_conv_b"][0], I["w_down"][0], I["final_norm_g"])
    for k in ("x2", "hmask", "w_out", "norm2T", "w_up", "fcwT", "fcbT", "w_down", "fnormT"):
        d[k] = p2[k]
    w_in = I["w_in"][0]
    dtb = I["gdn_dt_bias"][0]
    alog = I["gdn_a_log"][0]
    d["s0h"] = np.stack([I["state_gdn"][b, 0, :, h:h + 1] for h in range(8)], axis=0)
    d["w_ab_h"] = np.stack([w_in[:, [4096 + h, 4104 + h, 4112 + h, 4120 + h]] for h in range(8)], axis=0)
    d["dtb_h"] = np.stack([np.broadcast_to(dtb[:, h].reshape(1, 2), (128, 2)) for h in range(8)], axis=0)
    d["alog_h"] = np.stack([np.broadcast_to(alog[:, h].reshape(1, 2), (128, 2)) for h in range(8)], axis=0)
    cos2, sin2 = d["cos"], d["sin"]
    cosq = np.zeros((64, 1028), np.float32)
    sinq = np.zeros((64, 1028), np.float32)
    for g in range(2):
        lo = 1024 * j + 512 * g - 1
        a, e = max(lo, 0), min(lo + 514, 4096)
        cosq[:, g * 514 + a - lo:g * 514 + e - lo] = cos2[:, a:e]
        sinq[:, g * 514 + a - lo:g * 514 + e - lo] = sin2[:, a:e]
    d["cosq"], d["sinq"] = cosq, sinq
    sel = np.zeros((128, 4), np.float32)
    sel[:, j] = 1.0
    d["sel"] = sel
    return {k: np.ascontiguousarray(np.asarray(v, np.float32)) for k, v in d.items()}


def kernel_fused(**I):
    I = {k: np.asarray(v) for k, v in I.items()}
    if "f" not in _NC:
        _NC["f"] = build_fused()
    r = run_bass_kernel_spmd(_NC["f"], [fused_inputs(c, I) for c in range(NCORES)], core_ids=list(range(NCORES))).results
    yp = np.zeros((16, 256, 2048), np.float32)
    ys = np.zeros((2, 4096, 2048), np.float32)
    new_state = np.zeros((16, 1, 2, 8, 128, 128), np.float32)
    new_ckv = np.zeros((16, 1, 256, 512), np.float32)
    new_kpe = np.zeros((16, 1, 256, 64), np.float32)
    for c in range(NCORES):
        b, j = c // 4, c % 4
        y = r[c]["y"]
        yp[2 * c:2 * c + 2] = y[0].reshape(2, 256, 2048)
        ys[b, 1024 * j:1024 * j + 512] = y[1]
        ys[b, 1024 * j + 512:1024 * j + 1024] = y[2]
        for s in range(2):
            new_state[2 * c + s, 0] = r[c]["new_state"][s]
            new_ckv[2 * c + s, 0] = r[c]["new_ckv"][s]
            new_kpe[2 * c + s, 0] = r[c]["new_kpe"][s]
    return (yp, ys, new_state, new_ckv, new_kpe)


def kernel(**inputs):
    return kernel_fused(**inputs)
```
